# Optimizing a Trainium2 kernel written in Bass

```python
import math
import jax
import jax.numpy as jnp
from jax import lax
import numpy as np

D_MODEL = 1024
BATCH = 4
SEQ = 8192
DEPTH = 2
DEC_BATCH = 32
DEC_SEQ = 64
PAST_LEN = 1024

CHUNK = 64
QBLOCK = 128
HEAD_DIM = 64
N_FOX_HEADS = 6
N_MLA_HEADS = 6
N_DIFF_HEADS = 4
MLA_Q_RANK = 384
MLA_KV_RANK = 256
MLA_NOPE_DIM = 64
MLA_ROPE_DIM = 32
MLA_V_DIM = 64
DIFF_QK_DIM = 32
DIFF_V_DIM = 64
FOX_WIDTH = N_FOX_HEADS * HEAD_DIM
MLA_WIDTH = N_MLA_HEADS * MLA_V_DIM
DIFF_WIDTH = N_DIFF_HEADS * DIFF_V_DIM
D_MIX = FOX_WIDTH + MLA_WIDTH + DIFF_WIDTH
D_FF = 4 * D_MODEL
CONV_WIDTH = 3
PLE_DIM = 256
ROPE_THETA = 10000.0
RMS_EPS = 1e-6
NEG_INF = -1e30

OFF_FOX_Q = 0
OFF_FOX_K = OFF_FOX_Q + FOX_WIDTH
OFF_FOX_V = OFF_FOX_K + FOX_WIDTH
OFF_FOX_F = OFF_FOX_V + FOX_WIDTH
OFF_MLA_CQ = OFF_FOX_F + N_FOX_HEADS
OFF_MLA_CKV = OFF_MLA_CQ + MLA_Q_RANK
OFF_MLA_KR = OFF_MLA_CKV + MLA_KV_RANK
OFF_DIFF_Q = OFF_MLA_KR + MLA_ROPE_DIM
OFF_DIFF_K = OFF_DIFF_Q + N_DIFF_HEADS * 2 * DIFF_QK_DIM
OFF_DIFF_V = OFF_DIFF_K + N_DIFF_HEADS * 2 * DIFF_QK_DIM
IN_WIDTH = OFF_DIFF_V + DIFF_WIDTH

kernel_name = "hybrid_fox_mla_diff_streaming_step"


def _normal(k, shape, scale):
    return scale * jax.random.normal(k, shape, jnp.float32)


def rmsnorm(x, g):
    x32 = x.astype(jnp.float32)
    y = x32 * lax.rsqrt(jnp.mean(x32 * x32, axis=-1, keepdims=True) + RMS_EPS)
    return (y * g.astype(jnp.float32)).astype(x.dtype)


def rope(x, pos):
    half = MLA_ROPE_DIM // 2
    inv_freq = ROPE_THETA ** (-jnp.arange(half, dtype=jnp.float32) / half)
    ang = pos.astype(jnp.float32)[:, None] * inv_freq[None, :]
    shape = (1, pos.shape[0]) + (1,) * (x.ndim - 3) + (half,)
    cos = jnp.cos(ang).reshape(shape)
    sin = jnp.sin(ang).reshape(shape)
    x32 = x.astype(jnp.float32)
    x1, x2 = x32[..., :half], x32[..., half:]
    return jnp.concatenate([x1 * cos - x2 * sin, x1 * sin + x2 * cos], axis=-1).astype(x.dtype)


def alibi_slopes(n):
    return jnp.asarray(2.0 ** (-8.0 * np.arange(1, n + 1) / n), dtype=jnp.float32)


def masked_softmax(s, mask):
    return jax.nn.softmax(jnp.where(mask, s.astype(jnp.float32), NEG_INF), axis=-1)


def token_projections(xn, pos, prm):
    B, T, _ = xn.shape
    z = xn @ prm["w_in"]
    fox_q = z[..., OFF_FOX_Q:OFF_FOX_K].reshape(B, T, N_FOX_HEADS, HEAD_DIM)
    fox_k = z[..., OFF_FOX_K:OFF_FOX_V].reshape(B, T, N_FOX_HEADS, HEAD_DIM)
    fox_v = z[..., OFF_FOX_V:OFF_FOX_F].reshape(B, T, N_FOX_HEADS, HEAD_DIM)
    fox_logf = jax.nn.log_sigmoid((z[..., OFF_FOX_F:OFF_MLA_CQ] + prm["b_forget"]).astype(jnp.float32))
    c_q = rmsnorm(z[..., OFF_MLA_CQ:OFF_MLA_CKV], prm["mla_q_norm"])
    q_mla = (c_q @ prm["w_mla_uq"]).reshape(B, T, N_MLA_HEADS, MLA_NOPE_DIM + MLA_ROPE_DIM)
    mla_ckv = rmsnorm(z[..., OFF_MLA_CKV:OFF_MLA_KR], prm["mla_kv_norm"])
    mla_krope = rope(z[..., OFF_MLA_KR:OFF_DIFF_Q], pos)
    diff_q = z[..., OFF_DIFF_Q:OFF_DIFF_K].reshape(B, T, N_DIFF_HEADS, 2 * DIFF_QK_DIM)
    diff_k = z[..., OFF_DIFF_K:OFF_DIFF_V].reshape(B, T, N_DIFF_HEADS, 2 * DIFF_QK_DIM)
    diff_v = z[..., OFF_DIFF_V:IN_WIDTH].reshape(B, T, N_DIFF_HEADS, DIFF_V_DIM)
    query = dict(fox_q=fox_q, mla_q_nope=q_mla[..., :MLA_NOPE_DIM],
                 mla_q_rope=rope(q_mla[..., MLA_NOPE_DIM:], pos), diff_q=diff_q)
    rows = (fox_k, fox_v, fox_logf, mla_ckv, mla_krope, diff_k, diff_v)
    return query, rows


def key_side(rows, prm):
    fox_k, fox_v, fox_logf, mla_ckv, mla_krope, diff_k, diff_v = rows
    B, Tk = fox_k.shape[:2]
    return dict(
        fox_k=fox_k, fox_v=fox_v,
        fox_c=jnp.cumsum(fox_logf.astype(jnp.float32), axis=1),
        mla_k_nope=(mla_ckv @ prm["w_mla_uk"]).reshape(B, Tk, N_MLA_HEADS, MLA_NOPE_DIM),
        mla_k_rope=mla_krope,
        mla_v=(mla_ckv @ prm["w_mla_uv"]).reshape(B, Tk, N_MLA_HEADS, MLA_V_DIM),
        diff_k=diff_k, diff_v=diff_v)


def attend_block(q, k, qpos, kpos, lam, lam_init, diff_subln):
    B, Tq = q["fox_q"].shape[:2]
    causal = kpos[None, :] <= qpos[:, None]
    chunk_causal = (kpos // CHUNK)[None, :] <= (qpos // CHUNK)[:, None]
    decay = jnp.transpose(q["fox_c"], (0, 2, 1))[:, :, :, None] - jnp.transpose(k["fox_c"], (0, 2, 1))[:, :, None, :]
    s_a = jnp.einsum("bqhd,bkhd->bhqk", q["fox_q"], k["fox_k"]).astype(jnp.float32) * (HEAD_DIM ** -0.5) + decay
    p_a = masked_softmax(s_a, causal).astype(k["fox_v"].dtype)
    o_a = jnp.einsum("bhqk,bkhd->bqhd", p_a, k["fox_v"])
    s_b = (jnp.einsum("bqhd,bkhd->bhqk", q["mla_q_nope"], k["mla_k_nope"])
           + jnp.einsum("bqhr,bkr->bhqk", q["mla_q_rope"], k["mla_k_rope"])).astype(jnp.float32)
    s_b = s_b * ((MLA_NOPE_DIM + MLA_ROPE_DIM) ** -0.5)
    p_b = masked_softmax(s_b, chunk_causal).astype(k["mla_v"].dtype)
    o_b = jnp.einsum("bhqk,bkhd->bqhd", p_b, k["mla_v"])
    dist = jnp.abs(qpos[:, None] - kpos[None, :]).astype(jnp.float32)
    alibi = -alibi_slopes(N_DIFF_HEADS)[:, None, None] * dist[None]
    q1, q2 = q["diff_q"][..., :DIFF_QK_DIM], q["diff_q"][..., DIFF_QK_DIM:]
    k1, k2 = k["diff_k"][..., :DIFF_QK_DIM], k["diff_k"][..., DIFF_QK_DIM:]
    s1 = jnp.einsum("bqhd,bkhd->bhqk", q1, k1).astype(jnp.float32) * (DIFF_QK_DIM ** -0.5) + alibi
    s2 = jnp.einsum("bqhd,bkhd->bhqk", q2, k2).astype(jnp.float32) * (DIFF_QK_DIM ** -0.5) + alibi
    p_c = masked_softmax(s1, chunk_causal) - lam * masked_softmax(s2, chunk_causal)
    o_c = jnp.einsum("bhqk,bkhd->bqhd", p_c.astype(k["diff_v"].dtype), k["diff_v"])
    o_c = rmsnorm(o_c, diff_subln) * (1.0 - lam_init)
    return jnp.concatenate([o_a.reshape(B, Tq, FOX_WIDTH), o_b.reshape(B, Tq, MLA_WIDTH),
                            o_c.reshape(B, Tq, DIFF_WIDTH)], axis=-1)


def sweep_query_blocks(query, keys, kpos, lam, lam_init, diff_subln):
    B, S = query["fox_q"].shape[:2]

    def one_block(start):
        qb = {name: lax.dynamic_slice_in_dim(a, start, QBLOCK, axis=1) for name, a in query.items()}
        qpos = start + jnp.arange(QBLOCK)
        return attend_block(qb, keys, qpos, kpos, lam, lam_init, diff_subln)

    out = lax.map(one_block, jnp.arange(S // QBLOCK) * QBLOCK)
    return jnp.transpose(out, (1, 0, 2, 3)).reshape(B, S, D_MIX)


def conv_ffn(xn, conv_left, prm):
    u = xn @ prm["w_ffn_up"]
    gate, val = u[..., :D_FF], u[..., D_FF:]
    T = gate.shape[1]
    g_ext = jnp.concatenate([conv_left.astype(gate.dtype), gate], axis=1)
    w = prm["ffn_conv_w"]
    conv = sum(w[j] * g_ext[:, j:j + T] for j in range(CONV_WIDTH)) + prm["ffn_conv_b"]
    out = (jax.nn.gelu(conv, approximate=True) * val) @ prm["w_ffn_down"]
    return out, g_ext[:, -(CONV_WIDTH - 1):]


def trunk_layer(h, p, pos, cache_rows, conv_left, prm, lam_init):
    T = h.shape[1]
    query, new_rows = token_projections(rmsnorm(h, prm["norm_mix_pre"]), pos, prm)
    if cache_rows is None:
        key_rows = new_rows
    else:
        key_rows = tuple(jnp.concatenate([c.astype(n.dtype), n], axis=1) for c, n in zip(cache_rows, new_rows))
    keys = key_side(key_rows, prm)
    query["fox_c"] = keys["fox_c"][:, -T:]
    kpos = jnp.arange(keys["fox_k"].shape[1])
    lam = (jnp.exp(jnp.sum(prm["lq1"].astype(jnp.float32) * prm["lk1"].astype(jnp.float32)))
           - jnp.exp(jnp.sum(prm["lq2"].astype(jnp.float32) * prm["lk2"].astype(jnp.float32))) + lam_init)
    if cache_rows is None:
        mix = sweep_query_blocks(query, keys, kpos, lam, lam_init, prm["diff_subln"])
    else:
        mix = attend_block(query, keys, pos, kpos, lam, lam_init, prm["diff_subln"])
    h = h + rmsnorm(mix @ prm["w_out"], prm["norm_mix_post"])
    ffn_out, conv_state = conv_ffn(rmsnorm(h, prm["norm_ffn_pre"]), conv_left, prm)
    h = h + rmsnorm(ffn_out, prm["norm_ffn_post"])
    gate = jax.nn.sigmoid(rmsnorm(h, prm["norm_ple_pre"]) @ prm["w_ple_gate"])
    h = h + rmsnorm((p @ prm["w_ple_proj"]) * gate, prm["norm_ple_post"])
    return h, new_rows, conv_state


def stack_layers(per_layer):
    return tuple(jnp.stack(a, axis=0) for a in zip(*per_layer))


def setup_inputs(seed: int = 0) -> dict:
    key = jax.random.key(seed)
    ks = iter(jax.random.split(key, 48))
    L = DEPTH
    gain = lambda shape: 1.0 + _normal(next(ks), shape, 0.05)
    return {
        "x_prompt": _normal(next(ks), (BATCH, SEQ, D_MODEL), 1.0),
        "x_sample": _normal(next(ks), (DEC_BATCH, DEC_SEQ, D_MODEL), 1.0),
        "cache_fox_k": _normal(next(ks), (L, DEC_BATCH, PAST_LEN, N_FOX_HEADS, HEAD_DIM), 1.0),
        "cache_fox_v": _normal(next(ks), (L, DEC_BATCH, PAST_LEN, N_FOX_HEADS, HEAD_DIM), 1.0),
        "cache_fox_logf": jax.nn.log_sigmoid(3.0 + _normal(next(ks), (L, DEC_BATCH, PAST_LEN, N_FOX_HEADS), 1.0)),
        "cache_mla_ckv": _normal(next(ks), (L, DEC_BATCH, PAST_LEN, MLA_KV_RANK), 1.0),
        "cache_mla_krope": _normal(next(ks), (L, DEC_BATCH, PAST_LEN, MLA_ROPE_DIM), 1.0),
        "cache_diff_k": _normal(next(ks), (L, DEC_BATCH, PAST_LEN, N_DIFF_HEADS, 2 * DIFF_QK_DIM), 1.0),
        "cache_diff_v": _normal(next(ks), (L, DEC_BATCH, PAST_LEN, N_DIFF_HEADS, DIFF_V_DIM), 1.0),
        "state_ffn_conv": _normal(next(ks), (L, DEC_BATCH, CONV_WIDTH - 1, D_FF), 1.0),
        "p_prompt": _normal(next(ks), (L, BATCH, SEQ, PLE_DIM), 1.0),
        "p_sample": _normal(next(ks), (L, DEC_BATCH, DEC_SEQ, PLE_DIM), 1.0),
        "w_in": _normal(next(ks), (L, D_MODEL, IN_WIDTH), D_MODEL ** -0.5),
        "b_forget": 3.0 + _normal(next(ks), (L, N_FOX_HEADS), 0.1),
        "mla_q_norm": gain((L, MLA_Q_RANK)),
        "w_mla_uq": _normal(next(ks), (L, MLA_Q_RANK, N_MLA_HEADS * (MLA_NOPE_DIM + MLA_ROPE_DIM)), MLA_Q_RANK ** -0.5),
        "mla_kv_norm": gain((L, MLA_KV_RANK)),
        "w_mla_uk": _normal(next(ks), (L, MLA_KV_RANK, N_MLA_HEADS * MLA_NOPE_DIM), MLA_KV_RANK ** -0.5),
        "w_mla_uv": _normal(next(ks), (L, MLA_KV_RANK, N_MLA_HEADS * MLA_V_DIM), MLA_KV_RANK ** -0.5),
        "diff_lambda_q1": _normal(next(ks), (L, DIFF_QK_DIM), 0.1),
        "diff_lambda_k1": _normal(next(ks), (L, DIFF_QK_DIM), 0.1),
        "diff_lambda_q2": _normal(next(ks), (L, DIFF_QK_DIM), 0.1),
        "diff_lambda_k2": _normal(next(ks), (L, DIFF_QK_DIM), 0.1),
        "diff_subln": gain((L, DIFF_V_DIM)),
        "w_out": _normal(next(ks), (L, D_MIX, D_MODEL), D_MIX ** -0.5),
        "norm_mix_pre": gain((L, D_MODEL)),
        "norm_mix_post": gain((L, D_MODEL)),
        "norm_ffn_pre": gain((L, D_MODEL)),
        "norm_ffn_post": gain((L, D_MODEL)),
        "norm_ple_pre": gain((L, D_MODEL)),
        "norm_ple_post": gain((L, D_MODEL)),
        "w_ffn_up": _normal(next(ks), (L, D_MODEL, 2 * D_FF), D_MODEL ** -0.5),
        "ffn_conv_w": _normal(next(ks), (L, CONV_WIDTH, D_FF), CONV_WIDTH ** -0.5),
        "ffn_conv_b": _normal(next(ks), (L, D_FF), 0.02),
        "w_ffn_down": _normal(next(ks), (L, D_FF, D_MODEL), D_FF ** -0.5),
        "w_ple_gate": _normal(next(ks), (L, D_MODEL, D_MODEL), D_MODEL ** -0.5),
        "w_ple_proj": _normal(next(ks), (L, PLE_DIM, D_MODEL), PLE_DIM ** -0.5),
    }


def reference(x_prompt, x_sample, cache_fox_k, cache_fox_v, cache_fox_logf, cache_mla_ckv,
              cache_mla_krope, cache_diff_k, cache_diff_v, state_ffn_conv, p_prompt, p_sample,
              w_in, b_forget, mla_q_norm, w_mla_uq, mla_kv_norm, w_mla_uk, w_mla_uv,
              diff_lambda_q1, diff_lambda_k1, diff_lambda_q2, diff_lambda_k2, diff_subln, w_out,
              norm_mix_pre, norm_mix_post, norm_ffn_pre, norm_ffn_post, norm_ple_pre, norm_ple_post,
              w_ffn_up, ffn_conv_w, ffn_conv_b, w_ffn_down, w_ple_gate, w_ple_proj):
    B, S, _ = x_prompt.shape
    P = cache_fox_k.shape[2]
    T = x_sample.shape[1]
    pos_p = jnp.arange(S)
    pos_s = P + jnp.arange(T)
    hp, hs = x_prompt, x_sample
    rows_prompt, rows_sample, conv_prompt, conv_sample = [], [], [], []
    for l in range(DEPTH):
        prm = dict(w_in=w_in[l], b_forget=b_forget[l], mla_q_norm=mla_q_norm[l], w_mla_uq=w_mla_uq[l],
                   mla_kv_norm=mla_kv_norm[l], w_mla_uk=w_mla_uk[l], w_mla_uv=w_mla_uv[l],
                   lq1=diff_lambda_q1[l], lk1=diff_lambda_k1[l], lq2=diff_lambda_q2[l], lk2=diff_lambda_k2[l],
                   diff_subln=diff_subln[l], w_out=w_out[l],
                   norm_mix_pre=norm_mix_pre[l], norm_mix_post=norm_mix_post[l],
                   norm_ffn_pre=norm_ffn_pre[l], norm_ffn_post=norm_ffn_post[l],
                   norm_ple_pre=norm_ple_pre[l], norm_ple_post=norm_ple_post[l],
                   w_ffn_up=w_ffn_up[l], ffn_conv_w=ffn_conv_w[l], ffn_conv_b=ffn_conv_b[l],
                   w_ffn_down=w_ffn_down[l], w_ple_gate=w_ple_gate[l], w_ple_proj=w_ple_proj[l])
        lam_init = 0.8 - 0.6 * math.exp(-0.3 * l)
        zeros_left = jnp.zeros((B, CONV_WIDTH - 1, D_FF), hp.dtype)
        hp, rp, cp = trunk_layer(hp, p_prompt[l], pos_p, None, zeros_left, prm, lam_init)
        cache_l = (cache_fox_k[l], cache_fox_v[l], cache_fox_logf[l], cache_mla_ckv[l],
                   cache_mla_krope[l], cache_diff_k[l], cache_diff_v[l])
        hs, rs, cs = trunk_layer(hs, p_sample[l], pos_s, cache_l, state_ffn_conv[l], prm, lam_init)
        rows_prompt.append(rp)
        rows_sample.append(rs)
        conv_prompt.append(cp)
        conv_sample.append(cs)
    fox_k_p, fox_v_p, fox_logf_p, mla_ckv_p, mla_krope_p, diff_k_p, diff_v_p = stack_layers(rows_prompt)
    fox_k_s, fox_v_s, fox_logf_s, mla_ckv_s, mla_krope_s, diff_k_s, diff_v_s = stack_layers(rows_sample)
    conv_p = jnp.stack(conv_prompt, axis=0)
    conv_s = jnp.stack(conv_sample, axis=0)
    return (hp, hs, fox_k_p, fox_v_p, fox_logf_p, mla_ckv_p, mla_krope_p, diff_k_p, diff_v_p, conv_p,
            fox_k_s, fox_v_s, fox_logf_s, mla_ckv_s, mla_krope_s, diff_k_s, diff_v_s, conv_s)
```

```python
import contextlib
import math
import numpy as np
import concourse.bass as bass
import concourse.mybir as mybir
from concourse.bass_utils import run_bass_kernel_spmd

F32 = mybir.dt.float32
BF16 = mybir.dt.bfloat16
AF = mybir.ActivationFunctionType
ALU = mybir.AluOpType

D = 1024
DFF = 4096
PLE = 256
PAST = 1024
TS = 64
NCOLS = 2632
O_FQ, O_CQ, O_DQ, O_FK, O_FF, O_FV, O_CKV, O_KR, O_KRS, O_DK, O_DV = (
    0, 384, 768, 1024, 1408, 1416, 1800, 2056, 2088, 2120, 2376)
EPS = 1e-6
import os
DEBUG = bool(int(os.environ.get('KDEBUG', '0')))
_LAST = None
NCORES = 8


class Prog:
    ENG = ('pe', 'act', 'dve', 'pool', 'sp')
    LIM = 30000
    LIMD = 16 * 1800

    def __init__(self, nc):
        self.nc = nc
        self.stack = contextlib.ExitStack()
        self.rec = {e: [] for e in self.ENG}
        self.seq = {e: 0 for e in self.ENG}
        self.dcount = {}
        self.waited = {e: {} for e in self.ENG}
        self.lastw = {}
        self.readers = {}

    def sb(self, name, shape, dtype):
        return self.stack.enter_context(self.nc.sbuf_tensor(name, list(shape), dtype))

    def ps(self, name, shape, dtype):
        return self.stack.enter_context(self.nc.psum_tensor(name, list(shape), dtype))

    def _deps(self, reads, writes, eng):
        deps = []
        for k in reads:
            d = self.lastw.get(k)
            if d is not None:
                deps.append(d)
            if k.startswith('ps'):
                r = self.readers.get(k)
                if r:
                    deps.extend((sk, v) for sk, v in r.items() if sk != ('E', eng))
        for k in writes:
            d = self.lastw.get(k)
            if d is not None:
                deps.append(d)
            r = self.readers.get(k)
            if r:
                deps.extend(r.items())
        return deps

    def _emit_waits(self, eng, deps):
        need = {}
        w = self.waited[eng]
        for (sk, v) in deps:
            if sk[0] == 'E':
                if sk[1] == 'pe' and eng == 'pe':
                    continue
            else:
                v = self.dcount[sk[1]]
            if w.get(sk, 0) >= v:
                continue
            if need.get(sk, 0) < v:
                need[sk] = v
        for sk, v in need.items():
            self.rec[eng].append(('w', sk, v, w.get(sk, 0)))
            w[sk] = v

    def _record(self, dep, reads, writes):
        sk, v = dep
        for k in reads:
            self.readers.setdefault(k, {})[sk] = v
        for k in writes:
            self.lastw[k] = dep
            self.readers[k] = {}

    def op(self, eng, meth, args, kw, reads=(), writes=()):
        self._emit_waits(eng, self._deps(reads, writes, eng))
        self.seq[eng] += 1
        n = self.seq[eng]
        self.rec[eng].append(('o', (meth, args, kw), n))
        self._record((('E', eng), n), reads, writes)

    def dma(self, eng, out, in_, sem, reads=(), writes=(), **kw):
        self._emit_waits(eng, self._deps(reads, writes, eng))
        c = self.dcount.get(sem, 0) + 16
        self.dcount[sem] = c
        self.rec[eng].append(('d', out, in_, kw, sem, c))
        self._record((('D', sem), c), reads, writes)

    def barrier(self):
        deps = [(('D', n), c) for n, c in self.dcount.items()]
        deps += [(('E', e), self.seq[e]) for e in ('pe', 'act', 'dve', 'pool') if self.seq[e] > 0]
        for e in self.ENG:
            self._emit_waits(e, [d for d in deps if d[0] != ('E', e)])
        self.lastw = {}
        self.readers = {}

    def finish(self):
        nc = self.nc
        deps = [(('D', n), c) for n, c in self.dcount.items()]
        deps += [(('E', e), self.seq[e]) for e in ('pe', 'act', 'dve', 'pool') if self.seq[e] > 0]
        self._emit_waits('sp', deps)
        sig = {e: set() for e in self.ENG}
        for e in self.ENG:
            for r in self.rec[e]:
                if r[0] == 'w' and r[1][0] == 'E':
                    sig[r[1][1]].add(r[2])
        rank, esems = {}, {}
        for e in self.ENG:
            s = sorted(sig[e])
            rank[e] = {n: i for i, n in enumerate(s)}
            nep = (len(s) + self.LIM - 1) // self.LIM
            esems[e] = [self.stack.enter_context(nc.semaphore("es_%s_%d" % (e, k))) for k in range(nep)]
        dsems = {}
        for n, c in self.dcount.items():
            nep = (c + self.LIMD - 1) // self.LIMD
            dsems[n] = [self.stack.enter_context(nc.semaphore("ds_%s_%d" % (n, k))) for k in range(nep)]
        self.nsem = sum(len(v) for v in esems.values()) + sum(len(v) for v in dsems.values())
        self.ninst = sum(len(v) for v in self.rec.values())
        LIM, LIMD = self.LIM, self.LIMD

        def run(eng, e):
            for r in self.rec[eng]:
                if r[0] == 'w':
                    sk, v, prev = r[1], r[2], r[3]
                    if sk[0] == 'E':
                        i = rank[sk[1]][v]
                        e.wait_ge(esems[sk[1]][i // LIM], (i % LIM) + 1)
                    else:
                        k = (v - 1) // LIMD
                        if k > 0 and prev < k * LIMD:
                            e.wait_ge(dsems[sk[1]][k - 1], LIMD)
                        e.wait_ge(dsems[sk[1]][k], v - k * LIMD)
                elif r[0] == 'o':
                    meth, args, kw = r[1]
                    ins = getattr(e, meth)(*args, **kw)
                    i = rank[eng].get(r[2])
                    if i is not None:
                        ins.then_inc(esems[eng][i // LIM], 1)
                else:
                    _, out, in_, kw, sem, c = r
                    k = (c - 1) // LIMD
                    e.dma_start(out=out, in_=in_, **kw).then_inc(dsems[sem][k], 16)

        with nc.Block() as block:
            @block.tensor
            def _(e):
                run('pe', e)

            @block.scalar
            def _(e):
                run('act', e)

            @block.vector
            def _(e):
                run('dve', e)

            @block.gpsimd
            def _(e):
                run('pool', e)

            @block.sync
            def _(e):
                run('sp', e)
        self.stack.close()


class Arena:
    def __init__(self, t, nwords):
        self.t, self.n, self.off = t, nwords, 0

    def reset(self):
        self.off = 0

    def alloc(self, free_shape, dtype, parts=128):
        nel = 1
        for s in free_shape:
            nel *= s
        nw = (nel * (2 if dtype == BF16 else 4) + 3) // 4
        nw = (nw + 7) // 8 * 8
        o = self.off
        self.off += nw
        assert self.off <= self.n, ("arena overflow", self.off, self.n)
        v = self.t[0:parts, o:o + nw]
        if dtype != F32:
            v = v.bitcast(dtype)
        v = v[:, 0:nel]
        if len(free_shape) == 2:
            v = v.rearrange("p (a b) -> p a b", b=free_shape[1])
        elif len(free_shape) == 3:
            v = v.rearrange("p (a b c) -> p a b c", b=free_shape[1], c=free_shape[2])
        return v


def build(S, NBS, L):
    nc = bass.Bass("TRN2", target_bir_lowering=False)
    P = Prog(nc)
    NS = NBS * TS
    TK = PAST + TS
    NKB_S = (TK + 127) // 128
    NT = S // 128
    NQT = S // 512
    assert S % 512 == 0 and NS % 128 == 0

    def din(name, shape):
        return nc.dram_tensor(name, list(shape), F32, kind="ExternalInput").ap()

    def dout(name, shape):
        return nc.dram_tensor(name, list(shape), F32, kind="ExternalOutput").ap()

    def dscr(name, shape, dt=BF16):
        return nc.dram_tensor(name, list(shape), dt, kind="Internal").ap()

    xp = din("xp", [S, D]); xs = din("xs", [NS, D])
    pp = din("pp", [L, S, PLE]); pss = din("pss", [L, NS, PLE])
    c_fk = din("c_fk", [L, NBS, PAST, 384]); c_fv = din("c_fv", [L, NBS, PAST, 384])
    c_lf = din("c_lf", [L, NBS, PAST, 6]); c_ckv = din("c_ckv", [L, NBS, PAST, 256])
    c_kr = din("c_kr", [L, NBS, PAST, 32]); c_dk = din("c_dk", [L, NBS, PAST, 256])
    c_dv = din("c_dv", [L, NBS, PAST, 256]); c_cs = din("c_cs", [L, NBS, 2, DFF])
    w_in = din("w_in", [L, D, NCOLS]); w_uq = din("w_uq", [L, 384, 768])
    w_uk = din("w_uk", [L, 256, 384]); w_uv = din("w_uv", [L, 256, 384])
    w_out = din("w_out", [L, D, D]); w_up = din("w_up", [L, D, 2 * DFF])
    w_down = din("w_down", [L, DFF, D]); w_gate = din("w_gate", [L, D, D])
    w_proj = din("w_proj", [L, PLE, D])
    b_f = din("b_f", [L, 6]); g_q = din("g_q", [L, 384]); g_kv = din("g_kv", [L, 256])
    lam4 = din("lam4", [L, 4, 32]); g_sub = din("g_sub", [L, 64])
    g_norm = din("g_norm", [L, 6, D])
    cw = din("cw", [L, 3, DFF]); cb = din("cb", [L, DFF])
    k_masks = din("k_masks", [6, 128, 128]); k_U = din("k_U", [128, 128]); k_E = din("k_E", [128, 128])
    k_cs_p = din("k_cs_p", [S, 64]); k_cst_p = din("k_cst_p", [64, S])
    k_cs_s = din("k_cs_s", [NS, 64]); k_cst_s = din("k_cst_s", [64, NS])
    NAL = 4 * NQT + NKB_S
    k_al = din("k_al", [128, 4 * NAL])

    y_p = dout("y_p", [S, D]); y_s = dout("y_s", [NS, D])
    o_fk = [dout("o_fk_p", [L, S, 384]), dout("o_fk_s", [L, NS, 384])]
    o_fv = [dout("o_fv_p", [L, S, 384]), dout("o_fv_s", [L, NS, 384])]
    o_lf = [dout("o_lf_p", [L, S, 6]), dout("o_lf_s", [L, NS, 6])]
    o_ckv = [dout("o_ckv_p", [L, S, 256]), dout("o_ckv_s", [L, NS, 256])]
    o_kr = [dout("o_kr_p", [L, S, 32]), dout("o_kr_s", [L, NS, 32])]
    o_dk = [dout("o_dk_p", [L, S, 256]), dout("o_dk_s", [L, NS, 256])]
    o_dv = [dout("o_dv_p", [L, S, 256]), dout("o_dv_s", [L, NS, 256])]
    o_cv = [dout("o_cv_p", [L, 1, 2, DFF]), dout("o_cv_s", [L, NBS, 2, DFF])]

    wb_in = dscr("wb_in", [L, D, NCOLS]); wb_uq = dscr("wb_uq", [L, 384, 768])
    wb_uk = dscr("wb_uk", [L, 256, 384]); wb_uv = dscr("wb_uv", [L, 256, 384])
    wb_out = dscr("wb_out", [L, D, D]); wb_up = dscr("wb_up", [L, D, 2 * DFF])
    wb_down = dscr("wb_down", [L, DFF, D]); wb_gate = dscr("wb_gate", [L, D, D])
    wb_proj = dscr("wb_proj", [L, PLE, D])

    class G:
        pass
    grp = []
    for gi, (nseq, tk, tq) in enumerate([(1, S, S), (NBS, TK, TS)]):
        g = G()
        g.gi, g.nseq, g.tk, g.tq = gi, nseq, tk, tq
        g.ntok = nseq * tq
        g.qT_fox = dscr("qT_fox%d" % gi, [nseq, 384, tq]); g.kT_fox = dscr("kT_fox%d" % gi, [nseq, 384, tk])
        g.v_fox = dscr("v_fox%d" % gi, [nseq, tk, 6 * 65])
        g.qT_mla = dscr("qT_mla%d" % gi, [nseq, 576, tq]); g.kT_rope = dscr("kT_rope%d" % gi, [nseq, 32, tk])
        g.kT_nope = dscr("kT_nope%d" % gi, [nseq, 384, tk]); g.v_mla = dscr("v_mla%d" % gi, [nseq, tk, 6 * 65])
        g.qT_diff = dscr("qT_diff%d" % gi, [nseq, 256, tq]); g.kT_diff = dscr("kT_diff%d" % gi, [nseq, 256, tk])
        g.v_diff = dscr("v_diff%d" % gi, [nseq, tk, 4 * 65])
        g.mixT = (nc.dram_tensor("mixT%d" % gi, [D, g.ntok], BF16, kind="ExternalOutput").ap() if DEBUG
                  else dscr("mixT%d" % gi, [D, g.ntok]))
        g.hbuf = dscr("hbuf%d" % gi, [g.ntok, D], F32)
        g.x = xp if gi == 0 else xs
        g.y = y_p if gi == 0 else y_s
        g.pin = pp if gi == 0 else pss
        g.cs = k_cs_p if gi == 0 else k_cs_s
        g.cst = k_cst_p if gi == 0 else k_cst_s
        grp.append(g)

    ARW = 50300
    arena_t = P.sb("arena", [128, ARW], F32)
    ar = Arena(arena_t, ARW)
    pers_t = P.sb("pers", [128, 2600], F32)
    pers = Arena(pers_t, 2600)
    banks = [P.ps("bank%d" % i, [128, 512], F32) for i in range(8)]

    def bk(i):
        return banks[i][:, :]

    def bkb(i):
        return banks[i][:, :].bitcast(BF16)

    def pk(i):
        return 'ps%d' % i

    ident = pers.alloc([128], BF16)
    ones_f = pers.alloc([128], F32)
    U_f = pers.alloc([128], F32)
    E_f = pers.alloc([128], F32)
    masks = pers.alloc([6, 128], BF16)
    al_t = pers.alloc([4 * NAL], F32)
    eps_t = pers.alloc([1], F32)
    zero6 = pers.alloc([6], F32)
    ctok_p = pers.alloc([NT, 6], F32)
    ctok_s = pers.alloc([NBS, NKB_S, 6], F32)
    cref = pers.alloc([512], F32)


    def setup():
        A('pool', 'memset', ident, 1.0, w=['ident'])
        A('pool', 'affine_select', out=ident, in_=ident, pattern=[[-1, 128]], compare_op=ALU.is_equal,
                                             fill=0.0, base=0, channel_multiplier=1, r=['ident'], w=['ident'])
        A('pool', 'memset', ones_f, 1.0, w=['ones'])
        A('pool', 'memset', eps_t, EPS, w=['eps'])
        A('pool', 'memset', zero6, 0.0, w=['zero6'])
        P.dma('sp', U_f, k_U[:, :], 'c0', writes=['U'])
        P.dma('sp', E_f, k_E[:, :], 'c0', writes=['E'])
        P.dma('sp', al_t, k_al[:, :], 'c0', writes=['al'])
        ar.reset()
        mtmp = ar.alloc([6, 128], F32)
        P.dma('sp', mtmp, k_masks.rearrange("m k q -> k m q"), 'c0', writes=['mtmp'])
        A('dve', 'tensor_copy', out=masks, in_=mtmp, r=['mtmp'], w=['masks'])

    _op = P.op

    def A(eng, meth, *args, r=(), w=(), **kw):
        _op(eng, meth, args, kw, reads=r, writes=w)

    def prep_weights():
        ar.reset()
        NB = 3
        stg = [ar.alloc([4096], F32) for _ in range(NB)]
        stb = [ar.alloc([4096], BF16) for _ in range(NB)]
        gcol = ar.alloc([L, 3, 8], F32)
        for l in range(L):
            for wi, gidx in enumerate((0, 2, 4)):
                P.dma('sp', gcol[:, l, wi, :], g_norm[l, gidx].rearrange("(k p) -> p k", p=128), 'c0',
                      writes=['gcol'], allow_slow_non_contiguous=True)
        cnt = [0]

        def conv(src, dst, rows, cols, gain=None):
            for rk in range(rows // 128):
                for c0 in range(0, cols, 4096):
                    c1 = min(cols, c0 + 4096)
                    i = cnt[0] % NB
                    cnt[0] += 1
                    sv, bv = stg[i][:, 0:c1 - c0], stb[i][:, 0:c1 - c0]
                    P.dma('sp', sv, src[rk * 128:(rk + 1) * 128, c0:c1], 'wl%d' % i, writes=['stg%d' % i])
                    eng = ('dve', 'act', 'pool')[cnt[0] % 3] if gain is None else 'dve'
                    if gain is None:
                        if eng == 'act':
                            A('act', 'activation', out=bv, in_=sv, func=AF.Copy,
                              r=['stg%d' % i], w=['stb%d' % i])
                        else:
                            A(eng, 'tensor_copy', out=bv, in_=sv,
                              r=['stg%d' % i], w=['stb%d' % i])
                    else:
                        gc = gain[:, rk:rk + 1]
                        A('dve', 'tensor_scalar_mul', out=bv, in0=sv, scalar1=gc,
                          r=['stg%d' % i, 'gcol'], w=['stb%d' % i])
                    P.dma('pool', dst[rk * 128:(rk + 1) * 128, c0:c1], bv, 'ws%d' % i, reads=['stb%d' % i])

        for l in range(L):
            conv(w_in[l], wb_in[l], D, NCOLS, gcol[:, l, 0, :])
            conv(w_uq[l], wb_uq[l], 384, 768)
            conv(w_uk[l], wb_uk[l], 256, 384)
            conv(w_uv[l], wb_uv[l], 256, 384)
            conv(w_out[l], wb_out[l], D, D)
            conv(w_up[l], wb_up[l], D, 2 * DFF, gcol[:, l, 1, :])
            conv(w_down[l], wb_down[l], DFF, D)
            conv(w_gate[l], wb_gate[l], D, D, gcol[:, l, 2, :])
            conv(w_proj[l], wb_proj[l], PLE, D)

    def rstd_from_ss(ss_ap, n, key):
        A('act', 'activation', out=ss_ap, in_=ss_ap, func=AF.Ln, bias=eps_t[0:ss_ap.shape[0], 0:1],
                                        scale=1.0 / n, r=[key, 'eps'], w=[key])
        A('act', 'activation', out=ss_ap, in_=ss_ap, func=AF.Exp, scale=-0.5, r=[key], w=[key])

    def token_rows(g, t0, n):
        out = []
        if g.gi == 0:
            return [(0, t0, n, 0)]
        t = t0
        while t < t0 + n:
            out.append((t // TS, PAST + (t % TS), TS, t - t0))
            t += TS
        return out

    def phase_p1(l):
        ar.reset()
        wi = ar.alloc([8, NCOLS], BF16)
        wq = ar.alloc([3, 768], BF16)
        wk = ar.alloc([2, 384], BF16)
        wv = ar.alloc([2, 384], BF16)
        P.dma('sp', wi, wb_in[l].rearrange("(k p) n -> p k n", p=128), 'c0', writes=['wi'])
        P.dma('sp', wq, wb_uq[l].rearrange("(k p) n -> p k n", p=128), 'c0', writes=['wq'])
        P.dma('sp', wk, wb_uk[l].rearrange("(k p) n -> p k n", p=128), 'c0', writes=['wk'])
        P.dma('sp', wv, wb_uv[l].rearrange("(k p) n -> p k n", p=128), 'c0', writes=['wv'])
        gq_c = ar.alloc([3], F32); gkv_c = ar.alloc([2], F32)
        P.dma('sp', gq_c, g_q[l].rearrange("(k p) -> p k", p=128), 'c0', writes=['gqc'], allow_slow_non_contiguous=True)
        P.dma('sp', gkv_c, g_kv[l].rearrange("(k p) -> p k", p=128), 'c0', writes=['gkvc'], allow_slow_non_contiguous=True)
        gkv_b = ar.alloc([256], F32); bf_b = ar.alloc([6], F32)
        P.dma('sp', gkv_b, g_kv[l:l + 1, :].to_broadcast([128, 256]), 'c0', writes=['gkvb'])
        P.dma('sp', bf_b, b_f[l:l + 1, :].to_broadcast([128, 6]), 'c0', writes=['bfb'])
        h_t = ar.alloc([4, D], F32)
        ssn = ar.alloc([8], F32)
        xn = [ar.alloc([D], BF16) for _ in range(2)]
        xnT = ar.alloc([8, 512], BF16)
        cs_t = ar.alloc([4, 64], F32)
        cst_t = ar.alloc([512], F32, parts=64)
        ost_a = [ar.alloc([390], F32) for _ in range(2)]
        ost_b = [ar.alloc([384], F32) for _ in range(2)]
        ost_c = [ar.alloc([320], F32) for _ in range(2)]
        ost_d = [ar.alloc([512], F32) for _ in range(2)]
        vst = ar.alloc([4, 16, 65], BF16)
        fst = [ar.alloc([512], BF16) for _ in range(3)]
        raw = ar.alloc([3, 512], F32)
        sq = [ar.alloc([512], F32) for _ in range(2)]
        rsb = ar.alloc([512], F32)
        cqnT = ar.alloc([3, 512], BF16)
        ckvnT = ar.alloc([2, 512], BF16)
        t1 = ar.alloc([512], F32); t2 = ar.alloc([512], F32)
        sm = ar.alloc([64], F32)
        lf_t = ar.alloc([4, 6], F32)
        A('pool', 'memset', vst, 1.0, w=['vst'])
        rr = [0]

        for g in grp:
            ntile = (g.ntok + 511) // 512
            for ti in range(ntile):
                t0 = ti * 512
                W = min(512, g.ntok - t0)
                nsub = W // 128
                src = g.x if l == 0 else g.hbuf
                P.dma('sp', h_t[:, 0:nsub, :], src[t0:t0 + W, :].rearrange("(j p) d -> p j d", p=128), 'p1h', writes=['h_t'])
                P.dma('sp', cs_t[:, 0:nsub, :], g.cs[t0:t0 + W, :].rearrange("(j p) d -> p j d", p=128), 'p1c', writes=['cs_t'])
                P.dma('sp', cst_t[:, 0:W], g.cst[:, t0:t0 + W], 'p1c', writes=['cst_t'])
                A('pool', 'memset', ssn, 0.0, w=['ssn'])
                for j in range(nsub):
                    x_b = xn[j % 2]; xk = 'xn%d' % (j % 2)
                    A('act', 'activation', out=x_b, in_=h_t[:, j, :], func=AF.Square,
                                                                  accum_out=ssn[:, j:j + 1], r=['h_t', 'ssn'], w=[xk, 'ssn'])
                for j in range(nsub):
                    pass
                rstd_from_ss(ssn[:, 0:nsub], D, 'ssn')
                for j in range(nsub):
                    x_b = xn[j % 2]; xk = 'xn%d' % (j % 2)
                    A('dve', 'tensor_scalar_mul', out=x_b, in0=h_t[:, j, :], scalar1=ssn[:, j:j + 1],
                      r=['h_t', 'ssn'], w=[xk])
                    pb = 6 + (j % 2)
                    for kc in range(8):
                        A('pe', 'transpose', bkb(pb)[:, kc * 128:(kc + 1) * 128],
                                                                             x_b[:, kc * 128:(kc + 1) * 128], ident,
                          r=[xk, 'ident'], w=[pk(pb)])
                    A('act', 'activation', out=xnT[:, :, j * 128:(j + 1) * 128],
                                                                in_=bkb(pb).rearrange("p (a b) -> p a b", b=128), func=AF.Copy,
                      r=[pk(pb)], w=['xnT'])
                for j in range(nsub):
                    rows = token_rows(g, t0 + j * 128, 128)
                    oa, ob, oc_, od = ost_a[j % 2], ost_b[j % 2], ost_c[j % 2], ost_d[j % 2]
                    ka, kb_, kc_, kd = 'oa%d' % (j % 2), 'ob%d' % (j % 2), 'oc%d' % (j % 2), 'od%d' % (j % 2)
                    for ci, (c0, c1) in enumerate([(O_FK, O_FK + 390), (O_FV, O_FV + 384), (O_CKV, O_CKV + 320), (O_DK, O_DK + 512)]):
                        pb = rr[0] % 3
                        rr[0] += 1
                        for kc in range(8):
                            A('pe', 'matmul',
                                bk(pb)[:, 0:c1 - c0], lhsT=xnT[:, kc, j * 128:(j + 1) * 128], rhs=wi[:, kc, c0:c1],
                                start=(kc == 0), stop=(kc == 7), r=['xnT', 'wi'], w=[pk(pb)])
                        if ci == 0:
                            A('act', 'activation', out=oa, in_=bk(pb)[:, 0:390], func=AF.Copy,
                              r=[pk(pb)], w=[ka])
                            A('dve', 'tensor_tensor', out=sm[:, 0:6], in0=oa[:, 384:390], in1=bf_b, op=ALU.add,
                              r=[ka, 'bfb'], w=['sm'])
                            A('act', 'activation', out=sm[:, 0:6], in_=sm[:, 0:6], func=AF.Exp, scale=-1.0, r=['sm'], w=['sm'])
                            A('act', 'activation', out=sm[:, 0:6], in_=sm[:, 0:6], func=AF.Ln, bias=1.0, scale=1.0, r=['sm'], w=['sm'])
                            A('dve', 'tensor_scalar_mul', out=lf_t[:, j, :], in0=sm[:, 0:6], scalar1=-1.0, r=['sm'], w=['lf_t'])
                            for (sq_, pos0, cnt_, p0) in rows:
                                d0 = (sq_ * g.tq + pos0 - (g.tk - g.tq))
                                P.dma('pool', o_fk[g.gi][l, d0:d0 + cnt_, :], oa[p0:p0 + cnt_, 0:384], 'so', reads=[ka])
                                P.dma('pool', o_lf[g.gi][l, d0:d0 + cnt_, :], lf_t[p0:p0 + cnt_, j, :], 'so', reads=['lf_t'])
                            if g.gi == 0:
                                blk = t0 // 128 + j
                                prev = zero6 if blk == 0 else ctok_p[:, blk - 1, :]
                                A('pe', 'matmul', bk(5)[:, 0:6], lhsT=U_f, rhs=lf_t[:, j, :], start=True, stop=False,
                                  r=['U', 'lf_t'], w=[pk(5)])
                                A('pe', 'matmul', bk(5)[:, 0:6], lhsT=E_f, rhs=prev, start=False, stop=True,
                                  r=['E', 'ctok', 'zero6'], w=[pk(5)])
                                A('dve', 'tensor_copy', out=ctok_p[:, blk, :], in_=bk(5)[:, 0:6], r=[pk(5)], w=['ctok'])
                            else:
                                for (sq_, pos0, cnt_, p0) in rows:
                                    A('pe', 'matmul', bk(5)[0:64, 0:6], lhsT=U_f[p0:p0 + 64, p0:p0 + 64],
                                                                          rhs=lf_t[p0:p0 + 64, j, :], start=True, stop=False,
                                      r=['U', 'lf_t'], w=[pk(5)])
                                    A('pe', 'matmul', bk(5)[0:64, 0:6], lhsT=E_f[:, 0:64], rhs=ctok_s[:, sq_, NKB_S - 2, :],
                                                                       start=False, stop=True, r=['E', 'ctoks'], w=[pk(5)])
                                    A('dve', 'tensor_copy', out=ctok_s[0:64, sq_, NKB_S - 1, :], in_=bk(5)[0:64, 0:6],
                                      r=[pk(5)], w=['ctoks'])
                        elif ci == 1:
                            A('act', 'activation', out=ob, in_=bk(pb)[:, 0:384], func=AF.Copy, r=[pk(pb)], w=[kb_])
                            A('dve', 'tensor_copy', out=vst[:, j, 0:6, 0:64],
                                                                         in_=bk(pb)[:, 0:384].rearrange("p (h d) -> p h d", d=64),
                              r=[pk(pb)], w=['vst'])
                            for (sq_, pos0, cnt_, p0) in rows:
                                d0 = (sq_ * g.tq + pos0 - (g.tk - g.tq))
                                P.dma('pool', o_fv[g.gi][l, d0:d0 + cnt_, :], ob[p0:p0 + cnt_, :], 'so', reads=[kb_])
                        elif ci == 2:
                            A('pool', 'memset', sm[:, 8:9], 0.0, w=['sm8'])
                            A('act', 'activation', out=t1[:, 0:256], in_=bk(pb)[:, 0:256], func=AF.Square,
                                                                   accum_out=sm[:, 8:9], r=[pk(pb), 'sm8'], w=['t1', 'sm8'])
                            rstd_from_ss(sm[:, 8:9], 256, 'sm8')
                            A('dve', 'scalar_tensor_tensor', out=oc_[:, 0:256], in0=bk(pb)[:, 0:256], scalar=sm[:, 8:9],
                                                                                       in1=gkv_b, op0=ALU.mult, op1=ALU.mult,
                              r=[pk(pb), 'sm8', 'gkvb'], w=[kc_])
                            A('dve', 'tensor_tensor', out=t1[:, 256:288], in0=bk(pb)[:, 256:288], in1=cs_t[:, j, 0:32], op=ALU.mult,
                              r=[pk(pb), 'cs_t'], w=['t1'])
                            A('dve', 'tensor_tensor', out=t1[:, 288:320], in0=bk(pb)[:, 288:320], in1=cs_t[:, j, 32:64], op=ALU.mult,
                              r=[pk(pb), 'cs_t'], w=['t1'])
                            A('dve', 'tensor_tensor', out=oc_[:, 256:288], in0=t1[:, 256:288], in1=t1[:, 288:320], op=ALU.add,
                              r=['t1'], w=[kc_])
                            for (sq_, pos0, cnt_, p0) in rows:
                                d0 = (sq_ * g.tq + pos0 - (g.tk - g.tq))
                                P.dma('pool', o_ckv[g.gi][l, d0:d0 + cnt_, :], oc_[p0:p0 + cnt_, 0:256], 'so', reads=[kc_])
                                P.dma('pool', o_kr[g.gi][l, d0:d0 + cnt_, :], oc_[p0:p0 + cnt_, 256:288], 'so', reads=[kc_])
                        else:
                            A('act', 'activation', out=od, in_=bk(pb)[:, 0:512], func=AF.Copy, r=[pk(pb)], w=[kd])
                            A('dve', 'tensor_copy', out=vst[:, j, 12:16, 0:64],
                                                                         in_=bk(pb)[:, 256:512].rearrange("p (h d) -> p h d", d=64),
                              r=[pk(pb)], w=['vst'])
                            for (sq_, pos0, cnt_, p0) in rows:
                                d0 = (sq_ * g.tq + pos0 - (g.tk - g.tq))
                                P.dma('pool', o_dk[g.gi][l, d0:d0 + cnt_, :], od[p0:p0 + cnt_, 0:256], 'so', reads=[kd])
                                P.dma('pool', o_dv[g.gi][l, d0:d0 + cnt_, :], od[p0:p0 + cnt_, 256:512], 'so', reads=[kd])
                trows = token_rows(g, t0, W)

                def fm(c0, m, dst_fn, post=None):
                    pb = 3 + rr[0] % 2
                    rr[0] += 1
                    for kc in range(8):
                        A('pe', 'matmul', bk(pb)[0:m, 0:W], lhsT=wi[:, kc, c0:c0 + m], rhs=xnT[:, kc, 0:W],
                                                                  start=(kc == 0), stop=(kc == 7), r=['xnT', 'wi'], w=[pk(pb)])
                    return pb

                def store_fm(stage, skey, m, dst_fn, is_key):
                    for (sq_, pos0, cnt_, p0) in trows:
                        col = pos0 if is_key else pos0 - (g.tk - g.tq)
                        P.dma('pool', dst_fn(sq_)[:, col:col + cnt_], stage[0:m, p0:p0 + cnt_], 'sf', reads=[skey])

                def simple_fm(c0, nchunk, dst, is_key):
                    for c in range(nchunk):
                        pb = fm(c0 + c * 128, 128, None)
                        si = rr[0] % 3
                        st, sk = fst[si], 'fst%d' % si
                        eng = 'act' if c % 2 == 0 else 'dve'
                        if eng == 'act':
                            A('act', 'activation', out=st[:, 0:W], in_=bk(pb)[:, 0:W], func=AF.Copy, r=[pk(pb)], w=[sk])
                        else:
                            A('dve', 'tensor_copy', out=st[:, 0:W], in_=bk(pb)[:, 0:W], r=[pk(pb)], w=[sk])
                        store_fm(st, sk, 128, lambda s_, c=c: dst[s_, c * 128:(c + 1) * 128, :], is_key)

                simple_fm(O_FQ, 3, g.qT_fox, False)
                simple_fm(O_FK, 3, g.kT_fox, True)
                simple_fm(O_DQ, 2, g.qT_diff, False)
                simple_fm(O_DK, 2, g.kT_diff, True)

                def norm_fm(c0, nchunk, n, gcol_, gkey, outT, okey):
                    for c in range(nchunk):
                        pb = fm(c0 + c * 128, 128, None)
                        A('act', 'activation', out=raw[:, c, 0:W], in_=bk(pb)[:, 0:W], func=AF.Copy, r=[pk(pb)], w=['raw'])
                        s_ = sq[c % 2]
                        A('dve', 'tensor_tensor', out=s_[:, 0:W], in0=raw[:, c, 0:W], in1=raw[:, c, 0:W], op=ALU.mult,
                          r=['raw'], w=['sq%d' % (c % 2)])
                        A('pe', 'matmul', bk(5)[:, 0:W], lhsT=ones_f, rhs=s_[:, 0:W], start=(c == 0), stop=(c == nchunk - 1),
                          r=['ones', 'sq%d' % (c % 2)], w=[pk(5)])
                    A('act', 'activation', out=rsb[:, 0:W], in_=bk(5)[:, 0:W], func=AF.Ln, bias=eps_t[:, 0:1], scale=1.0 / n,
                      r=[pk(5), 'eps'], w=['rsb'])
                    A('act', 'activation', out=rsb[:, 0:W], in_=rsb[:, 0:W], func=AF.Exp, scale=-0.5, r=['rsb'], w=['rsb'])
                    for c in range(nchunk):
                        A('dve', 'scalar_tensor_tensor', out=outT[:, c, 0:W], in0=raw[:, c, 0:W], scalar=gcol_[:, c:c + 1],
                                                                        in1=rsb[:, 0:W], op0=ALU.mult, op1=ALU.mult,
                          r=['raw', gkey, 'rsb'], w=[okey])

                norm_fm(O_CQ, 3, 384, gq_c, 'gqc', cqnT, 'cqnT')
                norm_fm(O_CKV, 2, 256, gkv_c, 'gkvc', ckvnT, 'ckvnT')

                def rope_rows(pb, st, skey, b0=0):
                    A('dve', 'tensor_tensor', out=t1[b0:b0 + 32, 0:W], in0=bk(pb)[b0:b0 + 32, 0:W], in1=cst_t[0:32, 0:W], op=ALU.mult,
                      r=[pk(pb), 'cst_t'], w=['t1'])
                    A('dve', 'tensor_tensor', out=t2[b0:b0 + 32, 0:W], in0=bk(pb)[b0 + 32:b0 + 64, 0:W], in1=cst_t[32:64, 0:W], op=ALU.mult,
                      r=[pk(pb), 'cst_t'], w=['t2'])
                    A('dve', 'tensor_tensor', out=st[b0:b0 + 32, 0:W], in0=t1[b0:b0 + 32, 0:W], in1=t2[b0:b0 + 32, 0:W], op=ALU.add,
                      r=['t1', 't2'], w=[skey])

                pb = fm(O_KR, 64, None)
                si = rr[0] % 3
                st, sk = fst[si], 'fst%d' % si
                rope_rows(pb, st, sk)
                store_fm(st, sk, 32, lambda s_: g.kT_rope[s_, :, :], True)
                for hh in range(6):
                    pb = 3 + rr[0] % 2
                    rr[0] += 1
                    for kc in range(3):
                        A('pe', 'matmul', bk(pb)[:, 0:W], lhsT=wq[:, kc, hh * 128:(hh + 1) * 128], rhs=cqnT[:, kc, 0:W],
                                                                         start=(kc == 0), stop=(kc == 2), r=['wq', 'cqnT'], w=[pk(pb)])
                    si = rr[0] % 3
                    st, sk = fst[si], 'fst%d' % si
                    rope_rows(pb, st, sk, 64)
                    A('dve', 'tensor_copy', out=st[0:64, 0:W], in_=bk(pb)[0:64, 0:W], r=[pk(pb)], w=[sk])
                    store_fm(st, sk, 96, lambda s_, hh=hh: g.qT_mla[s_, hh * 96:(hh + 1) * 96, :], False)
                for c in range(3):
                    pb = 3 + rr[0] % 2
                    rr[0] += 1
                    for kc in range(2):
                        A('pe', 'matmul', bk(pb)[:, 0:W], lhsT=wk[:, kc, c * 128:(c + 1) * 128], rhs=ckvnT[:, kc, 0:W],
                                                                       start=(kc == 0), stop=(kc == 1), r=['wk', 'ckvnT'], w=[pk(pb)])
                    si = rr[0] % 3
                    st, sk = fst[si], 'fst%d' % si
                    A('act', 'activation', out=st[:, 0:W], in_=bk(pb)[:, 0:W], func=AF.Copy, r=[pk(pb)], w=[sk])
                    store_fm(st, sk, 128, lambda s_, c=c: g.kT_nope[s_, c * 128:(c + 1) * 128, :], True)
                for j in range(nsub):
                    pb = rr[0] % 3
                    rr[0] += 1
                    for kc in range(2):
                        A('pe', 'matmul', bk(pb)[:, 0:384], lhsT=ckvnT[:, kc, j * 128:(j + 1) * 128], rhs=wv[:, kc, :],
                                                                       start=(kc == 0), stop=(kc == 1), r=['wv', 'ckvnT'], w=[pk(pb)])
                    A('dve', 'tensor_copy', out=vst[:, j, 6:12, 0:64], in_=bk(pb)[:, 0:384].rearrange("p (h d) -> p h d", d=64),
                      r=[pk(pb)], w=['vst'])
                for j in range(nsub):
                    for (sq_, pos0, cnt_, p0) in token_rows(g, t0 + j * 128, 128):
                        P.dma('pool', g.v_fox[sq_, pos0:pos0 + cnt_, :].rearrange("t (h c) -> t h c", c=65), vst[p0:p0 + cnt_, j, 0:6, :], 'sv', reads=['vst'])
                        P.dma('pool', g.v_mla[sq_, pos0:pos0 + cnt_, :].rearrange("t (h c) -> t h c", c=65), vst[p0:p0 + cnt_, j, 6:12, :], 'sv', reads=['vst'])
                        P.dma('pool', g.v_diff[sq_, pos0:pos0 + cnt_, :].rearrange("t (h c) -> t h c", c=65), vst[p0:p0 + cnt_, j, 12:16, :], 'sv', reads=['vst'])

    def phase_cache(l):
        ar.reset()
        g = grp[1]
        wk = ar.alloc([2, 384], BF16); wv = ar.alloc([2, 384], BF16)
        P.dma('sp', wk, wb_uk[l].rearrange("(k p) n -> p k n", p=128), 'c0', writes=['wk'])
        P.dma('sp', wv, wb_uv[l].rearrange("(k p) n -> p k n", p=128), 'c0', writes=['wv'])
        ld = [ar.alloc([8, 384], F32) for _ in range(2)]
        lb = [ar.alloc([8, 384], BF16) for _ in range(2)]
        kTs = [ar.alloc([1024], BF16) for _ in range(2)]
        vst = ar.alloc([8, 6, 65], BF16)
        ckT = ar.alloc([2, 1024], BF16)
        lf_c = ar.alloc([8, 6], F32)
        A('pool', 'memset', vst, 1.0, w=['vst'])
        rr = [0]
        for b in range(NBS):
            def load(src, ncol):
                i = rr[0] % 2
                rr[0] += 1
                P.dma('sp', ld[i][:, :, 0:ncol], src[l, b].rearrange("(j p) d -> p j d", p=128), 'cl%d' % i, writes=['ld%d' % i])
                A('dve', 'tensor_copy', out=lb[i][:, :, 0:ncol], in_=ld[i][:, :, 0:ncol], r=['ld%d' % i], w=['lb%d' % i])
                return i

            def transp(i, c0, m, dstT, dkey, dst_dram):
                pb = 6 + rr[0] % 2
                rr[0] += 1
                for j in range(8):
                    A('pe', 'transpose', bkb(pb)[0:m, j * 128:(j + 1) * 128], lb[i][:, j, c0:c0 + m], ident,
                      r=['lb%d' % i, 'ident'], w=[pk(pb)])
                A('act', 'activation', out=dstT[0:m, :], in_=bkb(pb)[0:m, :], func=AF.Copy, r=[pk(pb)], w=[dkey])
                if dst_dram is not None:
                    P.dma('pool', dst_dram, dstT[0:m, :], 'sc', reads=[dkey])

            def vstore(i, nh, dst):
                A('dve', 'tensor_copy', out=vst[:, :, 0:nh, 0:64], in_=lb[i][:, :, 0:nh * 64].rearrange("p j (h d) -> p j h d", d=64),
                  r=['lb%d' % i], w=['vst'])
                P.dma('pool', dst[b, 0:PAST, :].rearrange("(j p) (h c) -> p j h c", p=128, c=65), vst[:, :, 0:nh, :], 'sc', reads=['vst'])

            i = load(c_fk, 384)
            for c in range(3):
                kt = kTs[rr[0] % 2]; kk = 'kTs%d' % (rr[0] % 2)
                transp(i, c * 128, 128, kt, kk, g.kT_fox[b, c * 128:(c + 1) * 128, 0:PAST])
            i = load(c_dk, 256)
            for c in range(2):
                kt = kTs[rr[0] % 2]; kk = 'kTs%d' % (rr[0] % 2)
                transp(i, c * 128, 128, kt, kk, g.kT_diff[b, c * 128:(c + 1) * 128, 0:PAST])
            i = load(c_kr, 32)
            kt = kTs[rr[0] % 2]; kk = 'kTs%d' % (rr[0] % 2)
            transp(i, 0, 32, kt, kk, g.kT_rope[b, :, 0:PAST])
            i = load(c_fv, 384)
            vstore(i, 6, g.v_fox)
            i = load(c_dv, 256)
            vstore(i, 4, g.v_diff)
            i = load(c_ckv, 256)
            for c in range(2):
                transp(i, c * 128, 128, ckT[:, c, :], 'ckT', None)
            for c in range(3):
                for hf in range(2):
                    pb = 3 + rr[0] % 2
                    rr[0] += 1
                    for kc in range(2):
                        A('pe', 'matmul', bk(pb)[:, 0:512], lhsT=wk[:, kc, c * 128:(c + 1) * 128],
                                                                               rhs=ckT[:, kc, hf * 512:(hf + 1) * 512], start=(kc == 0), stop=(kc == 1),
                          r=['wk', 'ckT'], w=[pk(pb)])
                    kt = kTs[rr[0] % 2]; kk = 'kTs%d' % (rr[0] % 2)
                    A('act', 'activation', out=kt[:, 0:512], in_=bk(pb)[:, 0:512], func=AF.Copy, r=[pk(pb)], w=[kk])
                    P.dma('pool', g.kT_nope[b, c * 128:(c + 1) * 128, hf * 512:(hf + 1) * 512], kt[:, 0:512], 'sc', reads=[kk])
            for j in range(8):
                pb = rr[0] % 3
                rr[0] += 1
                for kc in range(2):
                    A('pe', 'matmul', bk(pb)[:, 0:384], lhsT=ckT[:, kc, j * 128:(j + 1) * 128], rhs=wv[:, kc, :],
                                                                   start=(kc == 0), stop=(kc == 1), r=['wv', 'ckT'], w=[pk(pb)])
                A('dve', 'tensor_copy', out=vst[:, j, 0:6, 0:64], in_=bk(pb)[:, 0:384].rearrange("p (h d) -> p h d", d=64),
                  r=[pk(pb)], w=['vst'])
            P.dma('pool', g.v_mla[b, 0:PAST, :].rearrange("(j p) (h c) -> p j h c", p=128, c=65), vst[:, :, 0:6, :], 'sc', reads=['vst'])
            P.dma('sp', lf_c, c_lf[l, b].rearrange("(j p) h -> p j h", p=128), 'cl2', writes=['lf_c'])
            for j in range(8):
                prev = zero6 if j == 0 else ctok_s[:, b, j - 1, :]
                A('pe', 'matmul', bk(5)[:, 0:6], lhsT=U_f, rhs=lf_c[:, j, :], start=True, stop=False, r=['U', 'lf_c'], w=[pk(5)])
                A('pe', 'matmul', bk(5)[:, 0:6], lhsT=E_f, rhs=prev, start=False, stop=True, r=['E', 'ctoks', 'zero6'], w=[pk(5)])
                A('dve', 'tensor_copy', out=ctok_s[:, b, j, :], in_=bk(5)[:, 0:6], r=[pk(5)], w=['ctoks'])

    def phase_p2(l):
        ar.reset()
        lam_init = 0.8 - 0.6 * math.exp(-0.3 * l)
        lq = ar.alloc([4, 32], F32, parts=64)
        lt = ar.alloc([8], F32, parts=64)
        negl = ar.alloc([1], F32, parts=64)
        gsub = ar.alloc([1], F32, parts=64)
        P.dma('sp', lq, lam4[l:l + 1].to_broadcast([64, 4, 32]), 'c0', writes=['lq'])
        P.dma('sp', gsub, g_sub[l].rearrange("(d o) -> d o", o=1), 'c0', writes=['gsub'], allow_slow_non_contiguous=True)
        A('dve', 'tensor_tensor', out=lq[:, 0, :], in0=lq[:, 0, :], in1=lq[:, 1, :], op=ALU.mult, r=['lq'], w=['lq'])
        A('dve', 'tensor_tensor', out=lq[:, 2, :], in0=lq[:, 2, :], in1=lq[:, 3, :], op=ALU.mult, r=['lq'], w=['lq'])
        A('dve', 'reduce_sum', out=lt[:, 0:1], in_=lq[:, 0, :], axis=mybir.AxisListType.X, r=['lq'], w=['lt'])
        A('dve', 'reduce_sum', out=lt[:, 1:2], in_=lq[:, 2, :], axis=mybir.AxisListType.X, r=['lq'], w=['lt'])
        A('act', 'activation', out=lt[:, 0:2], in_=lt[:, 0:2], func=AF.Exp, r=['lt'], w=['lt'])
        A('dve', 'tensor_tensor', out=lt[:, 2:3], in0=lt[:, 1:2], in1=lt[:, 0:1], op=ALU.subtract, r=['lt'], w=['lt'])
        A('dve', 'tensor_scalar_add', out=negl, in0=lt[:, 2:3], scalar1=-lam_init, r=['lt'], w=['negl'])
        A('dve', 'tensor_scalar_mul', out=gsub, in0=gsub, scalar1=(1.0 - lam_init), r=['gsub'], w=['gsub'])

        NKBmax = max(g.tk for g in grp)
        NKBmax = (NKBmax + 127) // 128
        TKmax = max(g.tk for g in grp); TQmax = max(g.tq for g in grp)
        slots = []
        for s in range(2):
            slots.append((ar.alloc([TKmax], BF16, parts=96), ar.alloc([TQmax], BF16, parts=96), ar.alloc([NKBmax, 65], BF16)))
        pT = [ar.alloc([512], BF16) for _ in range(4)]
        rl = ar.alloc([512], F32, parts=65)
        bcs = ar.alloc([512], F32, parts=64)
        on2 = ar.alloc([512], F32, parts=64)
        oc = ar.alloc([512], F32, parts=64)
        sqd = ar.alloc([512], F32, parts=64)
        rs2 = ar.alloc([512], F32, parts=64)
        mst = [ar.alloc([512], BF16, parts=64) for _ in range(2)]
        fbias = ar.alloc([NKBmax], F32)

        units = []
        for g in grp:
            for sq_ in range(g.nseq):
                for h in range(6):
                    units.append(dict(g=g, s=sq_, kind='fox', h=h, rows=64, scale=0.125, mask=0, row0=64 * h,
                                      kT=[(g.kT_fox[sq_, 64 * h:64 * h + 64, :], 0, 64)], qT=g.qT_fox[sq_, 64 * h:64 * h + 64, :],
                                      v=g.v_fox[sq_, :, 65 * h:65 * h + 65]))
                for h in range(6):
                    units.append(dict(g=g, s=sq_, kind='mla', h=h, rows=96, scale=96 ** -0.5, mask=1, row0=384 + 64 * h,
                                      kT=[(g.kT_nope[sq_, 64 * h:64 * h + 64, :], 0, 64), (g.kT_rope[sq_, :, :], 64, 32)],
                                      qT=g.qT_mla[sq_, 96 * h:96 * h + 96, :], v=g.v_mla[sq_, :, 65 * h:65 * h + 65]))
                for h in range(4):
                    for s2 in range(2):
                        r0 = 64 * h + 32 * s2
                        units.append(dict(g=g, s=sq_, kind='diff', h=h, s2=s2, rows=32, scale=32 ** -0.5, mask=2 + h, row0=768 + 64 * h,
                                          kT=[(g.kT_diff[sq_, r0:r0 + 32, :], 0, 32)], qT=g.qT_diff[sq_, r0:r0 + 32, :],
                                          v=g.v_diff[sq_, :, 65 * h:65 * h + 65]))

        def load_unit(ui):
            u = units[ui]
            g = u['g']
            kt, qt, vt = slots[ui % 2]
            sk = 'u%d' % (ui % 2)
            for (src, p0, n) in u['kT']:
                P.dma('sp', kt[p0:p0 + n, 0:g.tk], src, 'ul%d' % (ui % 2), writes=[sk])
            P.dma('sp', qt[0:u['rows'], 0:g.tq], u['qT'], 'ul%d' % (ui % 2), writes=[sk])
            nkb = g.tk // 128
            for k0 in range(0, nkb, 16):
                k1 = min(nkb, k0 + 16)
                P.dma('sp', vt[:, k0:k1, :], u['v'][k0 * 128:k1 * 128, :].rearrange("(j p) c -> p j c", p=128), 'ul%d' % (ui % 2), writes=[sk])
            rem = g.tk - nkb * 128
            if rem:
                P.dma('sp', vt[0:rem, nkb, :], u['v'][nkb * 128:g.tk, :], 'ul%d' % (ui % 2), writes=[sk])

        steps = []
        grpno = 0
        for ui, u in enumerate(units):
            g = u['g']
            if g.gi == 0:
                for i in range(NQT):
                    nkb = 4 * i + 4
                    for kb in range(nkb):
                        j = kb - 4 * i
                        c0 = 128 * j if j > 0 else 0
                        steps.append(dict(ui=ui, i=i, kb=kb, nk=128, c0=c0, W=512, first=(kb == 0), last=(kb == nkb - 1),
                                          diag=(j if j >= 0 else None), grp=grpno, q0=512 * i, al=(kb - 4 * i - 2) + (4 * NQT - 2)))
                    grpno += 1
            else:
                nkb = NKB_S
                for kb in range(nkb):
                    nk = min(128, g.tk - 128 * kb)
                    steps.append(dict(ui=ui, i=0, kb=kb, nk=nk, c0=0, W=TS, first=(kb == 0), last=(kb == nkb - 1),
                                      diag=(0 if kb == nkb - 1 else None), grp=grpno, q0=0, al=4 * NQT + kb))
                    grpno += 1 if kb == nkb - 1 else 0

        last_step_of = {}
        for n_, st_ in enumerate(steps):
            last_step_of[st_['ui']] = n_
        for ui_ in range(min(2, len(units))):
            load_unit(ui_)

        fb_state = [None]

        def emit_S(n):
            st = steps[n]
            u = units[st['ui']]
            kt, qt, vt = slots[st['ui'] % 2]
            sk = 'u%d' % (st['ui'] % 2)
            pb = n % 3
            rows, nk, c0, W, kb = u['rows'], st['nk'], st['c0'], st['W'], st['kb']
            qc = st['q0'] if u['g'].gi == 0 else 0
            A('pe', 'matmul', bk(pb)[0:nk, c0:W], lhsT=kt[0:rows, kb * 128:kb * 128 + nk], rhs=qt[0:rows, qc + c0:qc + W],
                                       start=True, stop=True, r=[sk], w=[pk(pb)])

        def fox_bias(st, u):
            key = (st['ui'], st['i'])
            if fb_state[0] == key:
                return
            fb_state[0] = key
            g, h = u['g'], u['h']
            if g.gi == 0:
                blk = 4 * st['i'] + 1
                A('pe', 'matmul', bk(5)[:, 0:1], lhsT=E_f, rhs=ctok_p[:, blk, h:h + 1], start=True, stop=True,
                  r=['E', 'ctok'], w=[pk(5)])
                A('dve', 'tensor_copy', out=cref[:, 0:1], in_=bk(5)[:, 0:1], r=[pk(5)], w=['cref'])
                A('dve', 'tensor_scalar', out=fbias[:, 0:NT], in0=ctok_p[:, :, h], scalar1=-1.0, scalar2=cref[:, 0:1],
                                                   op0=ALU.mult, op1=ALU.add, r=['ctok', 'cref'], w=['fbias'])
            else:
                sq_ = u['s']
                A('pe', 'matmul', bk(5)[:, 0:1], lhsT=E_f, rhs=ctok_s[:, sq_, NKB_S - 2, h:h + 1], start=True, stop=True,
                  r=['E', 'ctoks'], w=[pk(5)])
                A('dve', 'tensor_copy', out=cref[:, 0:1], in_=bk(5)[:, 0:1], r=[pk(5)], w=['cref'])
                A('dve', 'tensor_scalar', out=fbias[:, 0:NKB_S], in0=ctok_s[:, sq_, :, h], scalar1=-1.0, scalar2=cref[:, 0:1],
                                                   op0=ALU.mult, op1=ALU.add, r=['ctoks', 'cref'], w=['fbias'])

        def emit_rest(n):
            st = steps[n]
            u = units[st['ui']]
            g = u['g']
            kt, qt, vt = slots[st['ui'] % 2]
            sk = 'u%d' % (st['ui'] % 2)
            pb = n % 3
            nk, c0, W, kb = st['nk'], st['c0'], st['W'], st['kb']
            pt = pT[n % 4]; ptk = 'pT%d' % (n % 4)
            ob = 3 + st['grp'] % 2
            if u['kind'] == 'fox':
                fox_bias(st, u)
                bias = fbias[0:nk, kb:kb + 1]
                A('act', 'activation', out=pt[0:nk, c0:W], in_=bk(pb)[0:nk, c0:W], func=AF.Exp, bias=bias, scale=u['scale'],
                  r=[pk(pb), 'fbias'], w=[ptk])
            elif u['kind'] == 'diff':
                ai = u['h'] * NAL + st['al']
                bias = al_t[0:nk, ai:ai + 1]
                A('act', 'activation', out=pt[0:nk, c0:W], in_=bk(pb)[0:nk, c0:W], func=AF.Exp, bias=bias, scale=u['scale'],
                  r=[pk(pb), 'al'], w=[ptk])
            else:
                A('act', 'activation', out=pt[0:nk, c0:W], in_=bk(pb)[0:nk, c0:W], func=AF.Exp, scale=u['scale'],
                  r=[pk(pb)], w=[ptk])
            if st['diag'] is not None:
                mw = min(128, W - c0)
                m = masks[0:nk, u['mask'], 0:mw]
                A('dve', 'tensor_tensor', out=pt[0:nk, c0:c0 + mw], in0=pt[0:nk, c0:c0 + mw], in1=m, op=ALU.mult,
                  r=[ptk, 'masks'], w=[ptk])
            A('pe', 'matmul', bk(ob)[0:65, c0:W], lhsT=vt[0:nk, kb, :], rhs=pt[0:nk, c0:W], start=st['first'], stop=st['last'],
              r=[sk, ptk], w=[pk(ob)])
            if last_step_of[st['ui']] == n and st['ui'] + 2 < len(units):
                load_unit(st['ui'] + 2)
            if not st['last']:
                return
            A('dve', 'reciprocal', out=rl[64:65, 0:W], in_=bk(ob)[64:65, 0:W], r=[pk(ob)], w=['rl'])
            A('pe', 'matmul', bk(5)[0:64, 0:W], lhsT=ones_f[64:65, 0:64], rhs=rl[64:65, 0:W], start=True, stop=True,
              r=['ones', 'rl'], w=[pk(5)])
            A('dve', 'tensor_copy', out=bcs[:, 0:W], in_=bk(5)[0:64, 0:W], r=[pk(5)], w=['bcs'])
            tok0 = (u['s'] * g.tq if g.gi == 1 else 0) + st['q0']
            if u['kind'] != 'diff':
                ms = mst[st['grp'] % 2]; msk = 'mst%d' % (st['grp'] % 2)
                A('dve', 'tensor_tensor', out=ms[:, 0:W], in0=bk(ob)[0:64, 0:W], in1=bcs[:, 0:W], op=ALU.mult,
                  r=[pk(ob), 'bcs'], w=[msk])
                P.dma('pool', g.mixT[u['row0']:u['row0'] + 64, tok0:tok0 + W], ms[:, 0:W], 'sm', reads=[msk])
            elif u['s2'] == 0:
                A('dve', 'tensor_tensor', out=on1s[st['i']][:, 0:W], in0=bk(ob)[0:64, 0:W], in1=bcs[:, 0:W], op=ALU.mult,
                  r=[pk(ob), 'bcs'], w=['on1%d' % st['i']])
            else:
                A('dve', 'tensor_tensor', out=on2[:, 0:W], in0=bk(ob)[0:64, 0:W], in1=bcs[:, 0:W], op=ALU.mult,
                  r=[pk(ob), 'bcs'], w=['on2'])
                A('dve', 'scalar_tensor_tensor', out=oc[:, 0:W], in0=on2[:, 0:W], scalar=negl[:, 0:1], in1=on1s[st['i']][:, 0:W],
                                                          op0=ALU.mult, op1=ALU.add, r=['on2', 'negl', 'on1%d' % st['i']], w=['oc'])
                A('dve', 'tensor_tensor', out=sqd[:, 0:W], in0=oc[:, 0:W], in1=oc[:, 0:W], op=ALU.mult, r=['oc'], w=['sqd'])
                A('pe', 'matmul', bk(6)[0:64, 0:W], lhsT=ones_f[0:64, 0:64], rhs=sqd[:, 0:W], start=True, stop=True,
                  r=['ones', 'sqd'], w=[pk(6)])
                A('act', 'activation', out=rs2[:, 0:W], in_=bk(6)[0:64, 0:W], func=AF.Ln, bias=eps_t[0:64, 0:1], scale=1.0 / 64,
                  r=[pk(6), 'eps'], w=['rs2'])
                A('act', 'activation', out=rs2[:, 0:W], in_=rs2[:, 0:W], func=AF.Exp, scale=-0.5, r=['rs2'], w=['rs2'])
                ms = mst[st['grp'] % 2]; msk = 'mst%d' % (st['grp'] % 2)
                A('dve', 'scalar_tensor_tensor', out=ms[:, 0:W], in0=oc[:, 0:W], scalar=gsub[:, 0:1], in1=rs2[:, 0:W],
                                                          op0=ALU.mult, op1=ALU.mult, r=['oc', 'gsub', 'rs2'], w=[msk])
                P.dma('pool', g.mixT[u['row0']:u['row0'] + 64, tok0:tok0 + W], ms[:, 0:W], 'sm', reads=[msk])

        on1s = [ar.alloc([512], F32, parts=64) for _ in range(max(NQT, 1))]

        LA = 2
        for n in range(len(steps) + LA):
            if n < len(steps):
                emit_S(n)
            if n - LA >= 0:
                emit_rest(n - LA)

    def phase_p3(l):
        ar.reset()
        wdn = ar.alloc([32, D], BF16)
        wms = ar.alloc([8, D], BF16)
        wpj = ar.alloc([2, D], BF16)
        wus = [ar.alloc([8, 512], BF16) for _ in range(2)]
        P.dma('sp', wdn, wb_down[l].rearrange("(c p) n -> p c n", p=128), 'c0', writes=['wdn'])
        P.dma('sp', wpj, wb_proj[l].rearrange("(c p) n -> p c n", p=128), 'c0', writes=['wpj'])
        gb = [ar.alloc([D], F32) for _ in range(3)]
        for i, gi_ in enumerate((1, 3, 5)):
            P.dma('sp', gb[i], g_norm[l, gi_:gi_ + 1, :].to_broadcast([128, D]), 'c0', writes=['gb%d' % i])
        cwt = ar.alloc([3, 32], F32); cbt = ar.alloc([32], F32)
        for jj in range(3):
            P.dma('sp', cwt[:, jj, :], cw[l, jj].rearrange("(c p) -> p c", p=128), 'c0', writes=['cwt'], allow_slow_non_contiguous=True)
        P.dma('sp', cbt, cb[l].rearrange("(c p) -> p c", p=128), 'c0', writes=['cbt'], allow_slow_non_contiguous=True)
        aT = ar.alloc([32, 512], BF16)
        h_t = ar.alloc([4, D], F32)
        XT = ar.alloc([8, 512], BF16)
        ytmp = ar.alloc([D], F32)
        tpl = ar.alloc([D], F32)
        nseg_max = max(1, NBS)
        gext = [ar.alloc([516 + 2 * nseg_max], F32) for _ in range(2)]
        tcv = ar.alloc([512], F32)
        ugl = ar.alloc([512], F32)
        xnb = [ar.alloc([D], BF16)] * 2
        p_t = ar.alloc([4, PLE], F32)
        p_b = ar.alloc([PLE], BF16)
        pTt = ar.alloc([2, 128], BF16)
        ss = ar.alloc([16], F32)
        carry = ar.alloc([32, nseg_max, 2], F32)
        cso = ar.alloc([32, nseg_max, 2], F32)
        rr = [0]

        def resid_norm(j, src_ap, src_keys, gbi, sscol):
            A('pool', 'memset', ss[:, sscol:sscol + 1], 0.0, w=['ss%d' % sscol])
            A('act', 'activation', out=tpl_junk, in_=src_ap, func=AF.Square, accum_out=ss[:, sscol:sscol + 1],
              r=src_keys + ['ss%d' % sscol], w=['junk', 'ss%d' % sscol])
            rstd_from_ss(ss[:, sscol:sscol + 1], D, 'ss%d' % sscol)
            A('dve', 'scalar_tensor_tensor', out=src_ap, in0=src_ap, scalar=ss[:, sscol:sscol + 1], in1=gb[gbi],
                                                      op0=ALU.mult, op1=ALU.mult, r=src_keys + ['ss%d' % sscol, 'gb%d' % gbi], w=src_keys)
            A('dve', 'tensor_tensor', out=h_t[:, j, :], in0=h_t[:, j, :], in1=src_ap, op=ALU.add, r=src_keys + ['h%d' % j], w=['h%d' % j])

        def prenorm_T(j, sscol):
            x_b = xnb[0]; xk = 'xnb'
            A('pool', 'memset', ss[:, sscol:sscol + 1], 0.0, w=['ss%d' % sscol])
            A('act', 'activation', out=tpl_junk, in_=h_t[:, j, :], func=AF.Square, accum_out=ss[:, sscol:sscol + 1],
              r=['h%d' % j, 'ss%d' % sscol], w=['junk', 'ss%d' % sscol])
            rstd_from_ss(ss[:, sscol:sscol + 1], D, 'ss%d' % sscol)
            A('dve', 'tensor_scalar_mul', out=x_b, in0=h_t[:, j, :], scalar1=ss[:, sscol:sscol + 1], r=['h%d' % j, 'ss%d' % sscol], w=[xk])
            pb = 2 + (j % 2)
            for kc in range(8):
                A('pe', 'transpose', bkb(pb)[:, kc * 128:(kc + 1) * 128], x_b[:, kc * 128:(kc + 1) * 128], ident,
                  r=[xk, 'ident'], w=[pk(pb)])
            A('act', 'activation', out=XT[:, :, j * 128:(j + 1) * 128], in_=bkb(pb).rearrange("p (a b) -> p a b", b=128), func=AF.Copy,
              r=[pk(pb)], w=['XT%d' % j])

        tpl_junk = ar.alloc([D], BF16)

        for g in grp:
            ntile = (g.ntok + 511) // 512
            nseg = 1 if g.gi == 0 else None
            if g.gi == 0:
                A('pool', 'memset', carry, 0.0, w=['carry'])
            else:
                for b in range(NBS):
                    for jj in range(2):
                        P.dma('sp', carry[:, :, b, jj], c_cs[l, b, jj].rearrange("(c p) -> p c", p=128), 'c0', writes=['carry'],
                              allow_slow_non_contiguous=True)
            for ti in range(ntile):
                t0 = ti * 512
                W = min(512, g.ntok - t0)
                nsub = W // 128
                nsg = 1 if g.gi == 0 else W // TS
                segW = W // nsg
                src = g.x if l == 0 else g.hbuf
                dst = g.y if l == L - 1 else g.hbuf
                P.dma('sp', XT[:, :, 0:W], g.mixT[:, t0:t0 + W].rearrange("(k p) t -> p k t", p=128), 'p3m',
                      writes=['XT%d' % j for j in range(nsub)])
                P.dma('sp', h_t[:, 0:nsub, :], src[t0:t0 + W, :].rearrange("(j p) d -> p j d", p=128), 'p3h', writes=['h%d' % j for j in range(nsub)])
                P.dma('sp', p_t[:, 0:nsub, :], g.pin[l, t0:t0 + W, :].rearrange("(j p) d -> p j d", p=128), 'p3p', writes=['p_t'])
                P.dma('sp', wms, wb_out[l].rearrange("(k p) n -> p k n", p=128), 'p3w', writes=['wms'])
                for j in range(nsub):
                    for hf in range(2):
                        pb = hf
                        for kc in range(8):
                            A('pe', 'matmul', bk(hf)[:, 0:512], lhsT=XT[:, kc, j * 128:(j + 1) * 128],
                                                                           rhs=wms[:, kc, hf * 512:(hf + 1) * 512], start=(kc == 0), stop=(kc == 7),
                              r=['XT%d' % j, 'wms'], w=[pk(hf)])
                        A('act', 'activation', out=ytmp[:, hf * 512:(hf + 1) * 512], in_=bk(hf)[:, 0:512], func=AF.Copy,
                          r=[pk(hf)], w=['ytmp'])
                    resid_norm(j, ytmp, ['ytmp'], 0, 0)
                for j in range(nsub):
                    prenorm_T(j, 1)
                xkeys = ['XT%d' % j for j in range(nsub)]
                for c in range(32):
                    if c % 2 == 0:
                        gi2 = (c // 2) % 2
                        P.dma('sp', wus[gi2], wb_up[l][:, (c // 2) * 512:(c // 2 + 1) * 512].rearrange("(k p) n -> p k n", p=128),
                              'p3u%d' % gi2, writes=['wus%d' % gi2])
                    wu = wus[(c // 2) % 2]; wuk = 'wus%d' % ((c // 2) % 2)
                    off = (c % 2) * 256
                    pg, pv = 4 + 2 * (c % 2), 5 + 2 * (c % 2)
                    for kc in range(8):
                        A('pe', 'matmul', bk(pg)[:, 0:W], lhsT=wu[:, kc, off:off + 128], rhs=XT[:, kc, 0:W],
                                                                                 start=(kc == 0), stop=(kc == 7), r=[wuk] + xkeys, w=[pk(pg)])
                    for kc in range(8):
                        A('pe', 'matmul', bk(pv)[:, 0:W], lhsT=wu[:, kc, off + 128:off + 256], rhs=XT[:, kc, 0:W],
                                                                                 start=(kc == 0), stop=(kc == 7), r=[wuk] + xkeys, w=[pk(pv)])
                    ge = gext[c % 2]; gk = 'gext%d' % (c % 2)
                    gv = ge[:, 0:nsg * (segW + 2)].rearrange("p (s w) -> p s w", w=segW + 2)
                    A('pool', 'tensor_copy', out=gv[:, :, 0:2], in_=carry[:, c, 0:nsg, :], r=['carry'], w=[gk])
                    A('act', 'activation', out=gv[:, :, 2:2 + segW], in_=bk(pg)[:, 0:W].rearrange("p (s w) -> p s w", w=segW),
                                                                  func=AF.Copy, r=[pk(pg)], w=[gk])
                    A('pool', 'tensor_copy', out=carry[:, c, 0:nsg, :], in_=gv[:, :, segW:segW + 2], r=[gk], w=['carry'])
                    if ti == ntile - 1:
                        A('pool', 'tensor_copy', out=cso[:, c, 0:nsg, :], in_=gv[:, :, segW:segW + 2], r=[gk], w=['cso'])
                    tv = tcv[:, 0:W].rearrange("p (s w) -> p s w", w=segW)
                    A('dve', 'tensor_scalar', out=tv, in0=gv[:, :, 0:segW], scalar1=cwt[:, 0, c:c + 1], scalar2=cbt[:, c:c + 1],
                                                                          op0=ALU.mult, op1=ALU.add, r=[gk, 'cwt', 'cbt'], w=['tcv'])
                    A('dve', 'scalar_tensor_tensor', out=tv, in0=gv[:, :, 1:1 + segW], scalar=cwt[:, 1, c:c + 1], in1=tv,
                                                                                 op0=ALU.mult, op1=ALU.add, r=[gk, 'cwt', 'tcv'], w=['tcv'])
                    A('dve', 'scalar_tensor_tensor', out=tv, in0=gv[:, :, 2:2 + segW], scalar=cwt[:, 2, c:c + 1], in1=tv,
                                                                                 op0=ALU.mult, op1=ALU.add, r=[gk, 'cwt', 'tcv'], w=['tcv'])
                    A('act', 'activation', out=ugl[:, 0:W], in_=tcv[:, 0:W], func=AF.Gelu_apprx_tanh, r=['tcv'], w=['ugl'])
                    A('dve', 'tensor_tensor', out=aT[:, c, 0:W], in0=ugl[:, 0:W], in1=bk(pv)[:, 0:W], op=ALU.mult,
                      r=['ugl', pk(pv)], w=['aT'])
                if ti == ntile - 1:
                    for b in range(nsg if g.gi == 1 else 1):
                        for jj in range(2):
                            P.dma('pool', o_cv[g.gi][l, b, jj].rearrange("(c p) -> p c", p=128), cso[:, :, b, jj], 'so', reads=['cso'],
                                  allow_slow_non_contiguous=True)
                P.dma('sp', wms, wb_gate[l].rearrange("(k p) n -> p k n", p=128), 'p3w', writes=['wms'])
                for j in range(nsub):
                    for hf in range(2):
                        for c in range(32):
                            A('pe', 'matmul', bk(hf)[:, 0:512], lhsT=aT[:, c, j * 128:(j + 1) * 128],
                                                                         rhs=wdn[:, c, hf * 512:(hf + 1) * 512], start=(c == 0), stop=(c == 31),
                              r=['aT', 'wdn'], w=[pk(hf)])
                        A('act', 'activation', out=ytmp[:, hf * 512:(hf + 1) * 512], in_=bk(hf)[:, 0:512], func=AF.Copy,
                          r=[pk(hf)], w=['ytmp'])
                    resid_norm(j, ytmp, ['ytmp'], 1, 2)
                    prenorm_T(j, 3)
                    A('dve', 'tensor_copy', out=p_b, in_=p_t[:, j, :], r=['p_t'], w=['p_b'])
                    for kc in range(2):
                        A('pe', 'transpose', bkb(3)[:, kc * 128:(kc + 1) * 128], p_b[:, kc * 128:(kc + 1) * 128], ident,
                          r=['p_b', 'ident'], w=[pk(3)])
                    A('dve', 'tensor_copy', out=pTt, in_=bkb(3)[:, 0:256].rearrange("p (a b) -> p a b", b=128), r=[pk(3)], w=['pTt'])
                    for hf in range(2):
                        pg, pp_ = 4 + hf, 6 + hf
                        for kc in range(8):
                            A('pe', 'matmul', bk(pg)[:, 0:512], lhsT=XT[:, kc, j * 128:(j + 1) * 128],
                                                                                 rhs=wms[:, kc, hf * 512:(hf + 1) * 512], start=(kc == 0), stop=(kc == 7),
                              r=['XT%d' % j, 'wms'], w=[pk(pg)])
                        for kc in range(2):
                            A('pe', 'matmul', bk(pp_)[:, 0:512], lhsT=pTt[:, kc, :], rhs=wpj[:, kc, hf * 512:(hf + 1) * 512],
                                                                               start=(kc == 0), stop=(kc == 1), r=['pTt', 'wpj'], w=[pk(pp_)])
                        A('act', 'activation', out=ytmp[:, hf * 512:(hf + 1) * 512], in_=bk(pg)[:, 0:512], func=AF.Sigmoid,
                          r=[pk(pg)], w=['ytmp'])
                        A('dve', 'tensor_tensor', out=tpl[:, hf * 512:(hf + 1) * 512], in0=ytmp[:, hf * 512:(hf + 1) * 512],
                                                                            in1=bk(pp_)[:, 0:512], op=ALU.mult, r=['ytmp', pk(pp_)], w=['tpl'])
                    resid_norm(j, tpl, ['tpl'], 2, 4)
                    P.dma('pool', dst[t0 + j * 128:t0 + (j + 1) * 128, :], h_t[:, j, :], 'sh', reads=['h%d' % j], writes=['hb'])

    setup()
    prep_weights()
    P.barrier()
    for l in range(L):
        phase_cache(l)
        P.barrier()
        phase_p1(l)
        P.barrier()
        phase_p2(l)
        P.barrier()
        phase_p3(l)
        P.barrier()
    P.finish()
    return nc, P


_CACHE = {}


def _host_consts(S, NBS):
    NS = NBS * TS
    NQT = S // 512
    TK = PAST + TS
    NKB_S = (TK + 127) // 128
    k = np.arange(128)[:, None]
    q = np.arange(128)[None, :]
    slopes = 2.0 ** (-8.0 * np.arange(1, 5) / 4)
    masks = np.zeros((6, 128, 128), np.float32)
    masks[0] = (k <= q)
    chunk = ((k // 64) <= (q // 64)).astype(np.float32)
    masks[1] = chunk
    for h in range(4):
        masks[2 + h] = chunk * np.exp(-2.0 * slopes[h] * np.maximum(k - q, 0))
    U = (k <= q).astype(np.float32)
    E = np.zeros((128, 128), np.float32)
    E[127, :] = 1.0
    half = 16
    inv_freq = (10000.0 ** (-np.arange(half, dtype=np.float32) / half)).astype(np.float32)

    def cs(pos):
        ang = pos.astype(np.float32)[:, None] * inv_freq[None, :]
        c, s = np.cos(ang), np.sin(ang)
        return np.concatenate([c, c, -s, s], axis=1).astype(np.float32)
    cs_p = cs(np.arange(S))
    cs_s = cs(PAST + (np.arange(NS) % TS))
    NAL = 4 * NQT + NKB_S
    al = np.zeros((128, 4, NAL), np.float32)
    p = np.arange(128)
    for h in range(4):
        for e in range(4 * NQT):
            d = e - (4 * NQT - 2)
            al[:, h, e] = slopes[h] * (p + 128 * d)
        for kb in range(NKB_S):
            al[:, h, 4 * NQT + kb] = slopes[h] * (p + 128 * kb - (PAST + TS // 2))
    return dict(k_masks=masks, k_U=U, k_E=E, k_cs_p=cs_p, k_cst_p=np.ascontiguousarray(cs_p.T),
                k_cs_s=cs_s, k_cst_s=np.ascontiguousarray(cs_s.T), k_al=np.ascontiguousarray(al.reshape(128, 4 * NAL)))


def kernel(x_prompt, x_sample, cache_fox_k, cache_fox_v, cache_fox_logf, cache_mla_ckv,
           cache_mla_krope, cache_diff_k, cache_diff_v, state_ffn_conv, p_prompt, p_sample,
           w_in, b_forget, mla_q_norm, w_mla_uq, mla_kv_norm, w_mla_uk, w_mla_uv,
           diff_lambda_q1, diff_lambda_k1, diff_lambda_q2, diff_lambda_k2, diff_subln, w_out,
           norm_mix_pre, norm_mix_post, norm_ffn_pre, norm_ffn_post, norm_ple_pre, norm_ple_post,
           w_ffn_up, ffn_conv_w, ffn_conv_b, w_ffn_down, w_ple_gate, w_ple_proj):
    f = lambda a: np.ascontiguousarray(np.asarray(a, dtype=np.float32))
    B, S, _ = x_prompt.shape
    DB = x_sample.shape[0]
    L = w_in.shape[0]
    NBS = DB // NCORES
    NS = NBS * TS
    assert B * 2 == NCORES or B <= NCORES
    key = (S, NBS, L)
    if key not in _CACHE:
        _CACHE[key] = build(S, NBS, L)
    nc, _ = _CACHE[key]
    w_in = f(w_in)
    kr = w_in[:, :, 1798:1830]
    krs = np.concatenate([kr[:, :, 16:32], kr[:, :, 0:16]], axis=2)
    w_in_p = np.concatenate([w_in[:, :, 0:384], w_in[:, :, 1158:1542], w_in[:, :, 1830:2086], w_in[:, :, 384:768],
                             w_in[:, :, 1152:1158], np.zeros((L, D, 2), np.float32), w_in[:, :, 768:1152],
                             w_in[:, :, 1542:1798], kr, krs, w_in[:, :, 2086:2342], w_in[:, :, 2342:2598]], axis=2)
    assert w_in_p.shape[2] == NCOLS
    uq = f(w_mla_uq).reshape(L, 384, 6, 96)
    uq_p = np.concatenate([uq[..., 0:64], uq[..., 64:96], uq[..., 80:96], uq[..., 64:80]], axis=3).reshape(L, 384, 768)
    up = f(w_ffn_up)
    up_p = np.stack([up[:, :, :DFF].reshape(L, D, 32, 128), up[:, :, DFF:].reshape(L, D, 32, 128)], axis=3).reshape(L, D, 2 * DFF)
    shared = dict(
        w_in=np.ascontiguousarray(w_in_p), w_uq=np.ascontiguousarray(uq_p), w_uk=f(w_mla_uk), w_uv=f(w_mla_uv),
        w_out=f(w_out), w_up=np.ascontiguousarray(up_p), w_down=f(w_ffn_down), w_gate=f(w_ple_gate), w_proj=f(w_ple_proj),
        b_f=f(b_forget), g_q=f(mla_q_norm), g_kv=f(mla_kv_norm),
        lam4=np.ascontiguousarray(np.stack([f(diff_lambda_q1), f(diff_lambda_k1), f(diff_lambda_q2), f(diff_lambda_k2)], axis=1)),
        g_sub=f(diff_subln),
        g_norm=np.ascontiguousarray(np.stack([f(norm_mix_pre), f(norm_mix_post), f(norm_ffn_pre), f(norm_ffn_post),
                                              f(norm_ple_pre), f(norm_ple_post)], axis=1)),
        cw=f(ffn_conv_w), cb=f(ffn_conv_b))
    shared.update(_host_consts(S, NBS))
    xp, xs, pp, ps_ = f(x_prompt), f(x_sample), f(p_prompt), f(p_sample)
    cfk, cfv, clf = f(cache_fox_k), f(cache_fox_v), f(cache_fox_logf)
    cckv, ckr, cdk, cdv, ccs = f(cache_mla_ckv), f(cache_mla_krope), f(cache_diff_k), f(cache_diff_v), f(state_ffn_conv)
    in_maps = []
    for c in range(NCORES):
        b = c % B
        sb = slice(c * NBS, (c + 1) * NBS)
        m = dict(shared)
        m.update(xp=xp[b], xs=np.ascontiguousarray(xs[sb].reshape(NS, D)), pp=np.ascontiguousarray(pp[:, b]),
                 pss=np.ascontiguousarray(ps_[:, sb].reshape(L, NS, PLE)),
                 c_fk=np.ascontiguousarray(cfk[:, sb].reshape(L, NBS, PAST, 384)),
                 c_fv=np.ascontiguousarray(cfv[:, sb].reshape(L, NBS, PAST, 384)),
                 c_lf=np.ascontiguousarray(clf[:, sb]), c_ckv=np.ascontiguousarray(cckv[:, sb]),
                 c_kr=np.ascontiguousarray(ckr[:, sb]),
                 c_dk=np.ascontiguousarray(cdk[:, sb].reshape(L, NBS, PAST, 256)),
                 c_dv=np.ascontiguousarray(cdv[:, sb].reshape(L, NBS, PAST, 256)),
                 c_cs=np.ascontiguousarray(ccs[:, sb]))
        in_maps.append(m)
    res = run_bass_kernel_spmd(nc, in_maps, core_ids=list(range(NCORES)))
    R = res.results
    global _LAST
    _LAST = R
    pc = [R[b] for b in range(B)]
    y_prompt = np.stack([r["y_p"] for r in pc], 0)
    y_sample = np.concatenate([r["y_s"].reshape(NBS, TS, D) for r in R], 0)

    def pstack(name, tail):
        return np.stack([r[name] for r in pc], 1).reshape((L, B, S) + tail)

    def sstack(name, tail):
        return np.concatenate([r[name].reshape((L, NBS, TS) + tail) for r in R], 1)

    outs = (y_prompt, y_sample,
            pstack("o_fk_p", (6, 64)), pstack("o_fv_p", (6, 64)), pstack("o_lf_p", (6,)), pstack("o_ckv_p", (256,)),
            pstack("o_kr_p", (32,)), pstack("o_dk_p", (4, 64)), pstack("o_dv_p", (4, 64)),
            np.stack([r["o_cv_p"][:, 0] for r in pc], 1),
            sstack("o_fk_s", (6, 64)), sstack("o_fv_s", (6, 64)), sstack("o_lf_s", (6,)), sstack("o_ckv_s", (256,)),
            sstack("o_kr_s", (32,)), sstack("o_dk_s", (4, 64)), sstack("o_dv_s", (4, 64)),
            np.concatenate([r["o_cv_s"] for r in R], 1))
    return tuple(np.ascontiguousarray(o, dtype=np.float32) for o in outs)
```

```python
import contextlib
import math
import numpy as np
import concourse.bass as bass
import concourse.mybir as mybir
from concourse.bass_utils import run_bass_kernel_spmd

F32 = mybir.dt.float32
BF16 = mybir.dt.bfloat16
AF = mybir.ActivationFunctionType
ALU = mybir.AluOpType

D = 1024
DFF = 4096
PLE = 256
PAST = 1024
TS = 64
NCOLS = 2632
O_FQ, O_CQ, O_DQ, O_FK, O_FF, O_FV, O_CKV, O_KR, O_KRS, O_DK, O_DV = (
    0, 384, 768, 1024, 1408, 1416, 1800, 2056, 2088, 2120, 2376)
EPS = 1e-6
import os
DEBUG = bool(int(os.environ.get('KDEBUG', '0')))
OPT_STQ = bool(int(os.environ.get('KV_STQ', '0')))
_LAST = None
NCORES = 8


class Prog:
    ENG = ('pe', 'act', 'dve', 'pool', 'sp')
    LIM = 30000
    LIMD = 16 * 1800

    def __init__(self, nc):
        self.nc = nc
        self.stack = contextlib.ExitStack()
        self.rec = {e: [] for e in self.ENG}
        self.seq = {e: 0 for e in self.ENG}
        self.dcount = {}
        self.waited = {e: {} for e in self.ENG}
        self.lastw = {}
        self.readers = {}

    def sb(self, name, shape, dtype):
        return self.stack.enter_context(self.nc.sbuf_tensor(name, list(shape), dtype))

    def ps(self, name, shape, dtype):
        return self.stack.enter_context(self.nc.psum_tensor(name, list(shape), dtype))

    def _deps(self, reads, writes, eng):
        deps = []
        for k in reads:
            d = self.lastw.get(k)
            if d is not None:
                deps.append(d)
            if k.startswith('ps'):
                r = self.readers.get(k)
                if r:
                    deps.extend((sk, v) for sk, v in r.items() if sk != ('E', eng))
        for k in writes:
            d = self.lastw.get(k)
            if d is not None:
                deps.append(d)
            r = self.readers.get(k)
            if r:
                deps.extend(r.items())
        return deps

    def _emit_waits(self, eng, deps):
        need = {}
        w = self.waited[eng]
        for (sk, v) in deps:
            if sk[0] == 'E':
                if sk[1] == 'pe' and eng == 'pe':
                    continue
            else:
                v = self.dcount[sk[1]]
            if w.get(sk, 0) >= v:
                continue
            if need.get(sk, 0) < v:
                need[sk] = v
        for sk, v in need.items():
            self.rec[eng].append(('w', sk, v, w.get(sk, 0)))
            w[sk] = v

    def _record(self, dep, reads, writes):
        sk, v = dep
        for k in reads:
            self.readers.setdefault(k, {})[sk] = v
        for k in writes:
            self.lastw[k] = dep
            self.readers[k] = {}

    def op(self, eng, meth, args, kw, reads=(), writes=()):
        self._emit_waits(eng, self._deps(reads, writes, eng))
        self.seq[eng] += 1
        n = self.seq[eng]
        self.rec[eng].append(('o', (meth, args, kw), n))
        self._record((('E', eng), n), reads, writes)

    def dma(self, eng, out, in_, sem, reads=(), writes=(), **kw):
        self._emit_waits(eng, self._deps(reads, writes, eng))
        c = self.dcount.get(sem, 0) + 16
        self.dcount[sem] = c
        self.rec[eng].append(('d', out, in_, kw, sem, c))
        self._record((('D', sem), c), reads, writes)

    def barrier(self):
        deps = [(('D', n), c) for n, c in self.dcount.items()]
        deps += [(('E', e), self.seq[e]) for e in ('pe', 'act', 'dve', 'pool') if self.seq[e] > 0]
        for e in self.ENG:
            self._emit_waits(e, [d for d in deps if d[0] != ('E', e)])
        self.lastw = {}
        self.readers = {}

    def finish(self):
        nc = self.nc
        deps = [(('D', n), c) for n, c in self.dcount.items()]
        deps += [(('E', e), self.seq[e]) for e in ('pe', 'act', 'dve', 'pool') if self.seq[e] > 0]
        self._emit_waits('sp', deps)
        sig = {e: set() for e in self.ENG}
        for e in self.ENG:
            for r in self.rec[e]:
                if r[0] == 'w' and r[1][0] == 'E':
                    sig[r[1][1]].add(r[2])
        rank, esems = {}, {}
        for e in self.ENG:
            s = sorted(sig[e])
            rank[e] = {n: i for i, n in enumerate(s)}
            nep = (len(s) + self.LIM - 1) // self.LIM
            esems[e] = [self.stack.enter_context(nc.semaphore("es_%s_%d" % (e, k))) for k in range(nep)]
        dsems = {}
        for n, c in self.dcount.items():
            nep = (c + self.LIMD - 1) // self.LIMD
            dsems[n] = [self.stack.enter_context(nc.semaphore("ds_%s_%d" % (n, k))) for k in range(nep)]
        self.nsem = sum(len(v) for v in esems.values()) + sum(len(v) for v in dsems.values())
        self.ninst = sum(len(v) for v in self.rec.values())
        LIM, LIMD = self.LIM, self.LIMD

        def run(eng, e):
            for r in self.rec[eng]:
                if r[0] == 'w':
                    sk, v, prev = r[1], r[2], r[3]
                    if sk[0] == 'E':
                        i = rank[sk[1]][v]
                        e.wait_ge(esems[sk[1]][i // LIM], (i % LIM) + 1)
                    else:
                        k = (v - 1) // LIMD
                        if k > 0 and prev < k * LIMD:
                            e.wait_ge(dsems[sk[1]][k - 1], LIMD)
                        e.wait_ge(dsems[sk[1]][k], v - k * LIMD)
                elif r[0] == 'o':
                    meth, args, kw = r[1]
                    ins = getattr(e, meth)(*args, **kw)
                    i = rank[eng].get(r[2])
                    if i is not None:
                        ins.then_inc(esems[eng][i // LIM], 1)
                else:
                    _, out, in_, kw, sem, c = r
                    k = (c - 1) // LIMD
                    e.dma_start(out=out, in_=in_, **kw).then_inc(dsems[sem][k], 16)

        with nc.Block() as block:
            @block.tensor
            def _(e):
                run('pe', e)

            @block.scalar
            def _(e):
                run('act', e)

            @block.vector
            def _(e):
                run('dve', e)

            @block.gpsimd
            def _(e):
                run('pool', e)

            @block.sync
            def _(e):
                run('sp', e)
        self.stack.close()


class Arena:
    def __init__(self, t, nwords):
        self.t, self.n, self.off = t, nwords, 0

    def reset(self):
        self.off = 0

    def alloc(self, free_shape, dtype, parts=128):
        nel = 1
        for s in free_shape:
            nel *= s
        nw = (nel * (2 if dtype == BF16 else 4) + 3) // 4
        nw = (nw + 7) // 8 * 8
        o = self.off
        self.off += nw
        assert self.off <= self.n, ("arena overflow", self.off, self.n)
        v = self.t[0:parts, o:o + nw]
        if dtype != F32:
            v = v.bitcast(dtype)
        v = v[:, 0:nel]
        if len(free_shape) == 2:
            v = v.rearrange("p (a b) -> p a b", b=free_shape[1])
        elif len(free_shape) == 3:
            v = v.rearrange("p (a b c) -> p a b c", b=free_shape[1], c=free_shape[2])
        return v


def build(S, NBS, L):
    nc = bass.Bass("TRN2", target_bir_lowering=False)
    P = Prog(nc)
    NS = NBS * TS
    TK = PAST + TS
    NKB_S = (TK + 127) // 128
    NT = S // 128
    NQT = S // 512
    assert S % 512 == 0 and NS % 128 == 0

    def din(name, shape):
        return nc.dram_tensor(name, list(shape), F32, kind="ExternalInput").ap()

    def dout(name, shape):
        return nc.dram_tensor(name, list(shape), F32, kind="ExternalOutput").ap()

    def dscr(name, shape, dt=BF16):
        return nc.dram_tensor(name, list(shape), dt, kind="Internal").ap()

    xp = din("xp", [S, D]); xs = din("xs", [NS, D])
    pp = din("pp", [L, S, PLE]); pss = din("pss", [L, NS, PLE])
    c_fk = din("c_fk", [L, NBS, PAST, 384]); c_fv = din("c_fv", [L, NBS, PAST, 384])
    c_lf = din("c_lf", [L, NBS, PAST, 6]); c_ckv = din("c_ckv", [L, NBS, PAST, 256])
    c_kr = din("c_kr", [L, NBS, PAST, 32]); c_dk = din("c_dk", [L, NBS, PAST, 256])
    c_dv = din("c_dv", [L, NBS, PAST, 256]); c_cs = din("c_cs", [L, NBS, 2, DFF])
    w_in = din("w_in", [L, D, NCOLS]); w_uq = din("w_uq", [L, 384, 768])
    w_uk = din("w_uk", [L, 256, 384]); w_uv = din("w_uv", [L, 256, 384])
    w_out = din("w_out", [L, D, D]); w_up = din("w_up", [L, D, 2 * DFF])
    w_down = din("w_down", [L, DFF, D]); w_gate = din("w_gate", [L, D, D])
    w_proj = din("w_proj", [L, PLE, D])
    b_f = din("b_f", [L, 6]); g_q = din("g_q", [L, 384]); g_kv = din("g_kv", [L, 256])
    lam4 = din("lam4", [L, 4, 32]); g_sub = din("g_sub", [L, 64])
    g_norm = din("g_norm", [L, 6, D])
    cw = din("cw", [L, 3, DFF]); cb = din("cb", [L, DFF])
    k_masks = din("k_masks", [6, 128, 128]); k_U = din("k_U", [128, 128]); k_E = din("k_E", [128, 128])
    k_cs_p = din("k_cs_p", [S, 64]); k_cst_p = din("k_cst_p", [64, S])
    k_cs_s = din("k_cs_s", [NS, 64]); k_cst_s = din("k_cst_s", [64, NS])
    NAL = 4 * NQT + NKB_S
    k_al = din("k_al", [128, 4 * NAL])

    y_p = dout("y_p", [S, D]); y_s = dout("y_s", [NS, D])
    o_fk = [dout("o_fk_p", [L, S, 384]), dout("o_fk_s", [L, NS, 384])]
    o_fv = [dout("o_fv_p", [L, S, 384]), dout("o_fv_s", [L, NS, 384])]
    o_lf = [dout("o_lf_p", [L, S, 6]), dout("o_lf_s", [L, NS, 6])]
    o_ckv = [dout("o_ckv_p", [L, S, 256]), dout("o_ckv_s", [L, NS, 256])]
    o_kr = [dout("o_kr_p", [L, S, 32]), dout("o_kr_s", [L, NS, 32])]
    o_dk = [dout("o_dk_p", [L, S, 256]), dout("o_dk_s", [L, NS, 256])]
    o_dv = [dout("o_dv_p", [L, S, 256]), dout("o_dv_s", [L, NS, 256])]
    o_cv = [dout("o_cv_p", [L, 1, 2, DFF]), dout("o_cv_s", [L, NBS, 2, DFF])]

    wb_in = dscr("wb_in", [L, D, NCOLS]); wb_uq = dscr("wb_uq", [L, 384, 768])
    wb_uk = dscr("wb_uk", [L, 256, 384]); wb_uv = dscr("wb_uv", [L, 256, 384])
    wb_out = dscr("wb_out", [L, D, D]); wb_up = dscr("wb_up", [L, D, 2 * DFF])
    wb_down = dscr("wb_down", [L, DFF, D]); wb_gate = dscr("wb_gate", [L, D, D])
    wb_proj = dscr("wb_proj", [L, PLE, D])

    class G:
        pass
    grp = []
    for gi, (nseq, tk, tq) in enumerate([(1, S, S), (NBS, TK, TS)]):
        g = G()
        g.gi, g.nseq, g.tk, g.tq = gi, nseq, tk, tq
        g.ntok = nseq * tq
        g.qT_fox = dscr("qT_fox%d" % gi, [nseq, 384, tq]); g.kT_fox = dscr("kT_fox%d" % gi, [nseq, 384, tk])
        g.v_fox = dscr("v_fox%d" % gi, [nseq, tk, 6 * 65])
        g.qT_mla = dscr("qT_mla%d" % gi, [nseq, 576, tq]); g.kT_rope = dscr("kT_rope%d" % gi, [nseq, 32, tk])
        g.kT_nope = dscr("kT_nope%d" % gi, [nseq, 384, tk]); g.v_mla = dscr("v_mla%d" % gi, [nseq, tk, 6 * 65])
        g.qT_diff = dscr("qT_diff%d" % gi, [nseq, 256, tq]); g.kT_diff = dscr("kT_diff%d" % gi, [nseq, 256, tk])
        g.v_diff = dscr("v_diff%d" % gi, [nseq, tk, 4 * 65])
        g.mixT = (nc.dram_tensor("mixT%d" % gi, [D, g.ntok], BF16, kind="ExternalOutput").ap() if DEBUG
                  else dscr("mixT%d" % gi, [D, g.ntok]))
        g.hbuf = dscr("hbuf%d" % gi, [g.ntok, D], F32)
        g.x = xp if gi == 0 else xs
        g.y = y_p if gi == 0 else y_s
        g.pin = pp if gi == 0 else pss
        g.cs = k_cs_p if gi == 0 else k_cs_s
        g.cst = k_cst_p if gi == 0 else k_cst_s
        grp.append(g)

    ARW = 50300
    arena_t = P.sb("arena", [128, ARW], F32)
    ar = Arena(arena_t, ARW)
    pers_t = P.sb("pers", [128, 2600], F32)
    pers = Arena(pers_t, 2600)
    banks = [P.ps("bank%d" % i, [128, 512], F32) for i in range(8)]

    def bk(i):
        return banks[i][:, :]

    def bkb(i):
        return banks[i][:, :].bitcast(BF16)

    def pk(i):
        return 'ps%d' % i

    ident = pers.alloc([128], BF16)
    ones_f = pers.alloc([128], F32)
    U_f = pers.alloc([128], F32)
    E_f = pers.alloc([128], F32)
    masks = pers.alloc([6, 128], BF16)
    al_t = pers.alloc([4 * NAL], F32)
    eps_t = pers.alloc([1], F32)
    zero6 = pers.alloc([6], F32)
    ctok_p = pers.alloc([NT, 6], F32)
    ctok_s = pers.alloc([NBS, NKB_S, 6], F32)
    cref = pers.alloc([512], F32)


    def setup():
        A('pool', 'memset', ident, 1.0, w=['ident'])
        A('pool', 'affine_select', out=ident, in_=ident, pattern=[[-1, 128]], compare_op=ALU.is_equal,
                                             fill=0.0, base=0, channel_multiplier=1, r=['ident'], w=['ident'])
        A('pool', 'memset', ones_f, 1.0, w=['ones'])
        A('pool', 'memset', eps_t, EPS, w=['eps'])
        A('pool', 'memset', zero6, 0.0, w=['zero6'])
        P.dma('sp', U_f, k_U[:, :], 'c0', writes=['U'])
        P.dma('sp', E_f, k_E[:, :], 'c0', writes=['E'])
        P.dma('sp', al_t, k_al[:, :], 'c0', writes=['al'])
        ar.reset()
        mtmp = ar.alloc([6, 128], F32)
        P.dma('sp', mtmp, k_masks.rearrange("m k q -> k m q"), 'c0', writes=['mtmp'])
        A('dve', 'tensor_copy', out=masks, in_=mtmp, r=['mtmp'], w=['masks'])

    _op = P.op

    def A(eng, meth, *args, r=(), w=(), **kw):
        _op(eng, meth, args, kw, reads=r, writes=w)

    def prep_weights():
        ar.reset()
        NB = 3
        stg = [ar.alloc([4096], F32) for _ in range(NB)]
        stb = [ar.alloc([4096], BF16) for _ in range(NB)]
        gcol = ar.alloc([L, 3, 8], F32)
        for l in range(L):
            for wi, gidx in enumerate((0, 2, 4)):
                P.dma('sp', gcol[:, l, wi, :], g_norm[l, gidx].rearrange("(k p) -> p k", p=128), 'c0',
                      writes=['gcol'], allow_slow_non_contiguous=True)
        cnt = [0]

        def conv(src, dst, rows, cols, gain=None):
            for rk in range(rows // 128):
                for c0 in range(0, cols, 4096):
                    c1 = min(cols, c0 + 4096)
                    i = cnt[0] % NB
                    cnt[0] += 1
                    sv, bv = stg[i][:, 0:c1 - c0], stb[i][:, 0:c1 - c0]
                    P.dma('sp', sv, src[rk * 128:(rk + 1) * 128, c0:c1], 'wl%d' % i, writes=['stg%d' % i])
                    eng = ('dve', 'act', 'pool')[cnt[0] % 3] if gain is None else 'dve'
                    if gain is None:
                        if eng == 'act':
                            A('act', 'activation', out=bv, in_=sv, func=AF.Copy,
                              r=['stg%d' % i], w=['stb%d' % i])
                        else:
                            A(eng, 'tensor_copy', out=bv, in_=sv,
                              r=['stg%d' % i], w=['stb%d' % i])
                    else:
                        gc = gain[:, rk:rk + 1]
                        A('dve', 'tensor_scalar_mul', out=bv, in0=sv, scalar1=gc,
                          r=['stg%d' % i, 'gcol'], w=['stb%d' % i])
                    P.dma('pool', dst[rk * 128:(rk + 1) * 128, c0:c1], bv, 'ws%d' % i, reads=['stb%d' % i])

        for l in range(L):
            conv(w_in[l], wb_in[l], D, NCOLS, gcol[:, l, 0, :])
            conv(w_uq[l], wb_uq[l], 384, 768)
            conv(w_uk[l], wb_uk[l], 256, 384)
            conv(w_uv[l], wb_uv[l], 256, 384)
            conv(w_out[l], wb_out[l], D, D)
            conv(w_up[l], wb_up[l], D, 2 * DFF, gcol[:, l, 1, :])
            conv(w_down[l], wb_down[l], DFF, D)
            conv(w_gate[l], wb_gate[l], D, D, gcol[:, l, 2, :])
            conv(w_proj[l], wb_proj[l], PLE, D)

    def rstd_from_ss(ss_ap, n, key):
        A('act', 'activation', out=ss_ap, in_=ss_ap, func=AF.Ln, bias=eps_t[0:ss_ap.shape[0], 0:1],
                                        scale=1.0 / n, r=[key, 'eps'], w=[key])
        A('act', 'activation', out=ss_ap, in_=ss_ap, func=AF.Exp, scale=-0.5, r=[key], w=[key])

    def token_rows(g, t0, n):
        out = []
        if g.gi == 0:
            return [(0, t0, n, 0)]
        t = t0
        while t < t0 + n:
            out.append((t // TS, PAST + (t % TS), TS, t - t0))
            t += TS
        return out

    def phase_p1(l):
        ar.reset()
        wi = ar.alloc([8, NCOLS], BF16)
        wq = ar.alloc([3, 768], BF16)
        wk = ar.alloc([2, 384], BF16)
        wv = ar.alloc([2, 384], BF16)
        P.dma('sp', wi, wb_in[l].rearrange("(k p) n -> p k n", p=128), 'c0', writes=['wi'])
        P.dma('sp', wq, wb_uq[l].rearrange("(k p) n -> p k n", p=128), 'c0', writes=['wq'])
        P.dma('sp', wk, wb_uk[l].rearrange("(k p) n -> p k n", p=128), 'c0', writes=['wk'])
        P.dma('sp', wv, wb_uv[l].rearrange("(k p) n -> p k n", p=128), 'c0', writes=['wv'])
        gq_c = ar.alloc([3], F32); gkv_c = ar.alloc([2], F32)
        P.dma('sp', gq_c, g_q[l].rearrange("(k p) -> p k", p=128), 'c0', writes=['gqc'], allow_slow_non_contiguous=True)
        P.dma('sp', gkv_c, g_kv[l].rearrange("(k p) -> p k", p=128), 'c0', writes=['gkvc'], allow_slow_non_contiguous=True)
        gkv_b = ar.alloc([256], F32); bf_b = ar.alloc([6], F32)
        P.dma('sp', gkv_b, g_kv[l:l + 1, :].to_broadcast([128, 256]), 'c0', writes=['gkvb'])
        P.dma('sp', bf_b, b_f[l:l + 1, :].to_broadcast([128, 6]), 'c0', writes=['bfb'])
        h_ts = [ar.alloc([4, D], F32) for _ in range(2)]
        ssn = ar.alloc([8], F32)
        tcount = [0]
        sq_rr = [0]

        def stq():
            sq_rr[0] += 1
            return ('sp', 'pool')[sq_rr[0] % 2] if OPT_STQ else 'pool'
        xn = [ar.alloc([D], BF16) for _ in range(2)]
        xnT = ar.alloc([8, 512], BF16)
        cs_t = ar.alloc([4, 64], F32)
        cst_t = ar.alloc([512], F32, parts=64)
        ost_a = [ar.alloc([390], F32) for _ in range(2)]
        ost_b = [ar.alloc([384], F32) for _ in range(2)]
        ost_c = [ar.alloc([320], F32) for _ in range(2)]
        ost_d = [ar.alloc([512], F32) for _ in range(2)]
        vst = ar.alloc([4, 16, 65], BF16)
        fst = [ar.alloc([512], BF16) for _ in range(3)]
        raw = ar.alloc([3, 512], F32)
        sq = [ar.alloc([512], F32) for _ in range(2)]
        rsb = ar.alloc([512], F32)
        cqnT = ar.alloc([3, 512], BF16)
        ckvnT = ar.alloc([2, 512], BF16)
        t1 = ar.alloc([512], F32); t2 = ar.alloc([512], F32)
        sm = ar.alloc([64], F32)
        lf_t = ar.alloc([4, 6], F32)
        A('pool', 'memset', vst, 1.0, w=['vst'])
        rr = [0]

        for g in grp:
            ntile = (g.ntok + 511) // 512
            for ti in range(ntile):
                t0 = ti * 512
                W = min(512, g.ntok - t0)
                nsub = W // 128
                src = g.x if l == 0 else g.hbuf
                h_t = h_ts[tcount[0] % 2]
                hkey = 'h_t%d' % (tcount[0] % 2)
                tcount[0] += 1
                P.dma('sp', h_t[:, 0:nsub, :], src[t0:t0 + W, :].rearrange("(j p) d -> p j d", p=128), 'p1h' + hkey, writes=[hkey])
                P.dma('sp', cs_t[:, 0:nsub, :], g.cs[t0:t0 + W, :].rearrange("(j p) d -> p j d", p=128), 'p1c', writes=['cs_t'])
                P.dma('sp', cst_t[:, 0:W], g.cst[:, t0:t0 + W], 'p1c', writes=['cst_t'])
                A('pool', 'memset', ssn, 0.0, w=['ssn'])
                for j in range(nsub):
                    x_b = xn[j % 2]; xk = 'xn%d' % (j % 2)
                    A('act', 'activation', out=x_b, in_=h_t[:, j, :], func=AF.Square,
                                                                  accum_out=ssn[:, j:j + 1], r=[hkey, 'ssn'], w=[xk, 'ssn'])
                for j in range(nsub):
                    pass
                rstd_from_ss(ssn[:, 0:nsub], D, 'ssn')
                for j in range(nsub):
                    x_b = xn[j % 2]; xk = 'xn%d' % (j % 2)
                    A('dve', 'tensor_scalar_mul', out=x_b, in0=h_t[:, j, :], scalar1=ssn[:, j:j + 1],
                      r=[hkey, 'ssn'], w=[xk])
                    pb = 6 + (j % 2)
                    for kc in range(8):
                        A('pe', 'transpose', bkb(pb)[:, kc * 128:(kc + 1) * 128],
                                                                             x_b[:, kc * 128:(kc + 1) * 128], ident,
                          r=[xk, 'ident'], w=[pk(pb)])
                    A('act', 'activation', out=xnT[:, :, j * 128:(j + 1) * 128],
                                                                in_=bkb(pb).rearrange("p (a b) -> p a b", b=128), func=AF.Copy,
                      r=[pk(pb)], w=['xnT'])
                for j in range(nsub):
                    rows = token_rows(g, t0 + j * 128, 128)
                    oa, ob, oc_, od = ost_a[j % 2], ost_b[j % 2], ost_c[j % 2], ost_d[j % 2]
                    ka, kb_, kc_, kd = 'oa%d' % (j % 2), 'ob%d' % (j % 2), 'oc%d' % (j % 2), 'od%d' % (j % 2)
                    for ci, (c0, c1) in enumerate([(O_FK, O_FK + 390), (O_FV, O_FV + 384), (O_CKV, O_CKV + 320), (O_DK, O_DK + 512)]):
                        pb = rr[0] % 3
                        rr[0] += 1
                        for kc in range(8):
                            A('pe', 'matmul',
                                bk(pb)[:, 0:c1 - c0], lhsT=xnT[:, kc, j * 128:(j + 1) * 128], rhs=wi[:, kc, c0:c1],
                                start=(kc == 0), stop=(kc == 7), r=['xnT', 'wi'], w=[pk(pb)])
                        if ci == 0:
                            A('act', 'activation', out=oa, in_=bk(pb)[:, 0:390], func=AF.Copy,
                              r=[pk(pb)], w=[ka])
                            A('dve', 'tensor_tensor', out=sm[:, 0:6], in0=oa[:, 384:390], in1=bf_b, op=ALU.add,
                              r=[ka, 'bfb'], w=['sm'])
                            A('act', 'activation', out=sm[:, 0:6], in_=sm[:, 0:6], func=AF.Exp, scale=-1.0, r=['sm'], w=['sm'])
                            A('act', 'activation', out=sm[:, 0:6], in_=sm[:, 0:6], func=AF.Ln, bias=1.0, scale=1.0, r=['sm'], w=['sm'])
                            A('dve', 'tensor_scalar_mul', out=lf_t[:, j, :], in0=sm[:, 0:6], scalar1=-1.0, r=['sm'], w=['lf_t'])
                            for (sq_, pos0, cnt_, p0) in rows:
                                d0 = (sq_ * g.tq + pos0 - (g.tk - g.tq))
                                P.dma(stq(), o_fk[g.gi][l, d0:d0 + cnt_, :], oa[p0:p0 + cnt_, 0:384], 'so', reads=[ka])
                                P.dma(stq(), o_lf[g.gi][l, d0:d0 + cnt_, :], lf_t[p0:p0 + cnt_, j, :], 'so', reads=['lf_t'])
                            if g.gi == 0:
                                blk = t0 // 128 + j
                                prev = zero6 if blk == 0 else ctok_p[:, blk - 1, :]
                                A('pe', 'matmul', bk(5)[:, 0:6], lhsT=U_f, rhs=lf_t[:, j, :], start=True, stop=False,
                                  r=['U', 'lf_t'], w=[pk(5)])
                                A('pe', 'matmul', bk(5)[:, 0:6], lhsT=E_f, rhs=prev, start=False, stop=True,
                                  r=['E', 'ctok', 'zero6'], w=[pk(5)])
                                A('dve', 'tensor_copy', out=ctok_p[:, blk, :], in_=bk(5)[:, 0:6], r=[pk(5)], w=['ctok'])
                            else:
                                for (sq_, pos0, cnt_, p0) in rows:
                                    A('pe', 'matmul', bk(5)[0:64, 0:6], lhsT=U_f[p0:p0 + 64, p0:p0 + 64],
                                                                          rhs=lf_t[p0:p0 + 64, j, :], start=True, stop=False,
                                      r=['U', 'lf_t'], w=[pk(5)])
                                    A('pe', 'matmul', bk(5)[0:64, 0:6], lhsT=E_f[:, 0:64], rhs=ctok_s[:, sq_, NKB_S - 2, :],
                                                                       start=False, stop=True, r=['E', 'ctoks'], w=[pk(5)])
                                    A('dve', 'tensor_copy', out=ctok_s[0:64, sq_, NKB_S - 1, :], in_=bk(5)[0:64, 0:6],
                                      r=[pk(5)], w=['ctoks'])
                        elif ci == 1:
                            A('act', 'activation', out=ob, in_=bk(pb)[:, 0:384], func=AF.Copy, r=[pk(pb)], w=[kb_])
                            A('dve', 'tensor_copy', out=vst[:, j, 0:6, 0:64],
                                                                         in_=bk(pb)[:, 0:384].rearrange("p (h d) -> p h d", d=64),
                              r=[pk(pb)], w=['vst'])
                            for (sq_, pos0, cnt_, p0) in rows:
                                d0 = (sq_ * g.tq + pos0 - (g.tk - g.tq))
                                P.dma(stq(), o_fv[g.gi][l, d0:d0 + cnt_, :], ob[p0:p0 + cnt_, :], 'so', reads=[kb_])
                        elif ci == 2:
                            A('pool', 'memset', sm[:, 8:9], 0.0, w=['sm8'])
                            A('act', 'activation', out=t1[:, 0:256], in_=bk(pb)[:, 0:256], func=AF.Square,
                                                                   accum_out=sm[:, 8:9], r=[pk(pb), 'sm8'], w=['t1', 'sm8'])
                            rstd_from_ss(sm[:, 8:9], 256, 'sm8')
                            A('dve', 'scalar_tensor_tensor', out=oc_[:, 0:256], in0=bk(pb)[:, 0:256], scalar=sm[:, 8:9],
                                                                                       in1=gkv_b, op0=ALU.mult, op1=ALU.mult,
                              r=[pk(pb), 'sm8', 'gkvb'], w=[kc_])
                            A('dve', 'tensor_tensor', out=t1[:, 256:288], in0=bk(pb)[:, 256:288], in1=cs_t[:, j, 0:32], op=ALU.mult,
                              r=[pk(pb), 'cs_t'], w=['t1'])
                            A('dve', 'tensor_tensor', out=t1[:, 288:320], in0=bk(pb)[:, 288:320], in1=cs_t[:, j, 32:64], op=ALU.mult,
                              r=[pk(pb), 'cs_t'], w=['t1'])
                            A('dve', 'tensor_tensor', out=oc_[:, 256:288], in0=t1[:, 256:288], in1=t1[:, 288:320], op=ALU.add,
                              r=['t1'], w=[kc_])
                            for (sq_, pos0, cnt_, p0) in rows:
                                d0 = (sq_ * g.tq + pos0 - (g.tk - g.tq))
                                P.dma(stq(), o_ckv[g.gi][l, d0:d0 + cnt_, :], oc_[p0:p0 + cnt_, 0:256], 'so', reads=[kc_])
                                P.dma(stq(), o_kr[g.gi][l, d0:d0 + cnt_, :], oc_[p0:p0 + cnt_, 256:288], 'so', reads=[kc_])
                        else:
                            A('act', 'activation', out=od, in_=bk(pb)[:, 0:512], func=AF.Copy, r=[pk(pb)], w=[kd])
                            A('dve', 'tensor_copy', out=vst[:, j, 12:16, 0:64],
                                                                         in_=bk(pb)[:, 256:512].rearrange("p (h d) -> p h d", d=64),
                              r=[pk(pb)], w=['vst'])
                            for (sq_, pos0, cnt_, p0) in rows:
                                d0 = (sq_ * g.tq + pos0 - (g.tk - g.tq))
                                P.dma(stq(), o_dk[g.gi][l, d0:d0 + cnt_, :], od[p0:p0 + cnt_, 0:256], 'so', reads=[kd])
                                P.dma(stq(), o_dv[g.gi][l, d0:d0 + cnt_, :], od[p0:p0 + cnt_, 256:512], 'so', reads=[kd])
                trows = token_rows(g, t0, W)

                def fm(c0, m, dst_fn, post=None):
                    pb = 3 + rr[0] % 2
                    rr[0] += 1
                    for kc in range(8):
                        A('pe', 'matmul', bk(pb)[0:m, 0:W], lhsT=wi[:, kc, c0:c0 + m], rhs=xnT[:, kc, 0:W],
                                                                  start=(kc == 0), stop=(kc == 7), r=['xnT', 'wi'], w=[pk(pb)])
                    return pb

                def store_fm(stage, skey, m, dst_fn, is_key):
                    for (sq_, pos0, cnt_, p0) in trows:
                        col = pos0 if is_key else pos0 - (g.tk - g.tq)
                        P.dma(stq(), dst_fn(sq_)[:, col:col + cnt_], stage[0:m, p0:p0 + cnt_], 'sf', reads=[skey])

                def simple_fm(c0, nchunk, dst, is_key):
                    for c in range(nchunk):
                        pb = fm(c0 + c * 128, 128, None)
                        si = rr[0] % 3
                        st, sk = fst[si], 'fst%d' % si
                        eng = 'act' if c % 2 == 0 else 'dve'
                        if eng == 'act':
                            A('act', 'activation', out=st[:, 0:W], in_=bk(pb)[:, 0:W], func=AF.Copy, r=[pk(pb)], w=[sk])
                        else:
                            A('dve', 'tensor_copy', out=st[:, 0:W], in_=bk(pb)[:, 0:W], r=[pk(pb)], w=[sk])
                        store_fm(st, sk, 128, lambda s_, c=c: dst[s_, c * 128:(c + 1) * 128, :], is_key)

                simple_fm(O_FQ, 3, g.qT_fox, False)
                simple_fm(O_FK, 3, g.kT_fox, True)
                simple_fm(O_DQ, 2, g.qT_diff, False)
                simple_fm(O_DK, 2, g.kT_diff, True)

                def norm_fm(c0, nchunk, n, gcol_, gkey, outT, okey):
                    for c in range(nchunk):
                        pb = fm(c0 + c * 128, 128, None)
                        A('act', 'activation', out=raw[:, c, 0:W], in_=bk(pb)[:, 0:W], func=AF.Copy, r=[pk(pb)], w=['raw'])
                        s_ = sq[c % 2]
                        A('dve', 'tensor_tensor', out=s_[:, 0:W], in0=raw[:, c, 0:W], in1=raw[:, c, 0:W], op=ALU.mult,
                          r=['raw'], w=['sq%d' % (c % 2)])
                        A('pe', 'matmul', bk(5)[:, 0:W], lhsT=ones_f, rhs=s_[:, 0:W], start=(c == 0), stop=(c == nchunk - 1),
                          r=['ones', 'sq%d' % (c % 2)], w=[pk(5)])
                    A('act', 'activation', out=rsb[:, 0:W], in_=bk(5)[:, 0:W], func=AF.Ln, bias=eps_t[:, 0:1], scale=1.0 / n,
                      r=[pk(5), 'eps'], w=['rsb'])
                    A('act', 'activation', out=rsb[:, 0:W], in_=rsb[:, 0:W], func=AF.Exp, scale=-0.5, r=['rsb'], w=['rsb'])
                    for c in range(nchunk):
                        A('dve', 'scalar_tensor_tensor', out=outT[:, c, 0:W], in0=raw[:, c, 0:W], scalar=gcol_[:, c:c + 1],
                                                                        in1=rsb[:, 0:W], op0=ALU.mult, op1=ALU.mult,
                          r=['raw', gkey, 'rsb'], w=[okey])

                norm_fm(O_CQ, 3, 384, gq_c, 'gqc', cqnT, 'cqnT')
                norm_fm(O_CKV, 2, 256, gkv_c, 'gkvc', ckvnT, 'ckvnT')

                def rope_rows(pb, st, skey, b0=0):
                    A('dve', 'tensor_tensor', out=t1[b0:b0 + 32, 0:W], in0=bk(pb)[b0:b0 + 32, 0:W], in1=cst_t[0:32, 0:W], op=ALU.mult,
                      r=[pk(pb), 'cst_t'], w=['t1'])
                    A('dve', 'tensor_tensor', out=t2[b0:b0 + 32, 0:W], in0=bk(pb)[b0 + 32:b0 + 64, 0:W], in1=cst_t[32:64, 0:W], op=ALU.mult,
                      r=[pk(pb), 'cst_t'], w=['t2'])
                    A('dve', 'tensor_tensor', out=st[b0:b0 + 32, 0:W], in0=t1[b0:b0 + 32, 0:W], in1=t2[b0:b0 + 32, 0:W], op=ALU.add,
                      r=['t1', 't2'], w=[skey])

                pb = fm(O_KR, 64, None)
                si = rr[0] % 3
                st, sk = fst[si], 'fst%d' % si
                rope_rows(pb, st, sk)
                store_fm(st, sk, 32, lambda s_: g.kT_rope[s_, :, :], True)
                for hh in range(6):
                    pb = 3 + rr[0] % 2
                    rr[0] += 1
                    for kc in range(3):
                        A('pe', 'matmul', bk(pb)[:, 0:W], lhsT=wq[:, kc, hh * 128:(hh + 1) * 128], rhs=cqnT[:, kc, 0:W],
                                                                         start=(kc == 0), stop=(kc == 2), r=['wq', 'cqnT'], w=[pk(pb)])
                    si = rr[0] % 3
                    st, sk = fst[si], 'fst%d' % si
                    rope_rows(pb, st, sk, 64)
                    A('dve', 'tensor_copy', out=st[0:64, 0:W], in_=bk(pb)[0:64, 0:W], r=[pk(pb)], w=[sk])
                    store_fm(st, sk, 96, lambda s_, hh=hh: g.qT_mla[s_, hh * 96:(hh + 1) * 96, :], False)
                for c in range(3):
                    pb = 3 + rr[0] % 2
                    rr[0] += 1
                    for kc in range(2):
                        A('pe', 'matmul', bk(pb)[:, 0:W], lhsT=wk[:, kc, c * 128:(c + 1) * 128], rhs=ckvnT[:, kc, 0:W],
                                                                       start=(kc == 0), stop=(kc == 1), r=['wk', 'ckvnT'], w=[pk(pb)])
                    si = rr[0] % 3
                    st, sk = fst[si], 'fst%d' % si
                    A('act', 'activation', out=st[:, 0:W], in_=bk(pb)[:, 0:W], func=AF.Copy, r=[pk(pb)], w=[sk])
                    store_fm(st, sk, 128, lambda s_, c=c: g.kT_nope[s_, c * 128:(c + 1) * 128, :], True)
                for j in range(nsub):
                    pb = rr[0] % 3
                    rr[0] += 1
                    for kc in range(2):
                        A('pe', 'matmul', bk(pb)[:, 0:384], lhsT=ckvnT[:, kc, j * 128:(j + 1) * 128], rhs=wv[:, kc, :],
                                                                       start=(kc == 0), stop=(kc == 1), r=['wv', 'ckvnT'], w=[pk(pb)])
                    A('dve', 'tensor_copy', out=vst[:, j, 6:12, 0:64], in_=bk(pb)[:, 0:384].rearrange("p (h d) -> p h d", d=64),
                      r=[pk(pb)], w=['vst'])
                for j in range(nsub):
                    for (sq_, pos0, cnt_, p0) in token_rows(g, t0 + j * 128, 128):
                        P.dma(stq(), g.v_fox[sq_, pos0:pos0 + cnt_, :].rearrange("t (h c) -> t h c", c=65), vst[p0:p0 + cnt_, j, 0:6, :], 'sv', reads=['vst'])
                        P.dma(stq(), g.v_mla[sq_, pos0:pos0 + cnt_, :].rearrange("t (h c) -> t h c", c=65), vst[p0:p0 + cnt_, j, 6:12, :], 'sv', reads=['vst'])
                        P.dma(stq(), g.v_diff[sq_, pos0:pos0 + cnt_, :].rearrange("t (h c) -> t h c", c=65), vst[p0:p0 + cnt_, j, 12:16, :], 'sv', reads=['vst'])

    def phase_cache(l):
        ar.reset()
        g = grp[1]
        wk = ar.alloc([2, 384], BF16); wv = ar.alloc([2, 384], BF16)
        P.dma('sp', wk, wb_uk[l].rearrange("(k p) n -> p k n", p=128), 'c0', writes=['wk'])
        P.dma('sp', wv, wb_uv[l].rearrange("(k p) n -> p k n", p=128), 'c0', writes=['wv'])
        ld = [ar.alloc([8, 384], F32) for _ in range(2)]
        lb = [ar.alloc([8, 384], BF16) for _ in range(2)]
        kTs = [ar.alloc([1024], BF16) for _ in range(2)]
        vst = ar.alloc([8, 6, 65], BF16)
        ckT = ar.alloc([2, 1024], BF16)
        lf_c = ar.alloc([8, 6], F32)
        A('pool', 'memset', vst, 1.0, w=['vst'])
        rr = [0]
        for b in range(NBS):
            def load(src, ncol):
                i = rr[0] % 2
                rr[0] += 1
                P.dma('sp', ld[i][:, :, 0:ncol], src[l, b].rearrange("(j p) d -> p j d", p=128), 'cl%d' % i, writes=['ld%d' % i])
                A('dve', 'tensor_copy', out=lb[i][:, :, 0:ncol], in_=ld[i][:, :, 0:ncol], r=['ld%d' % i], w=['lb%d' % i])
                return i

            def transp(i, c0, m, dstT, dkey, dst_dram):
                pb = 6 + rr[0] % 2
                rr[0] += 1
                for j in range(8):
                    A('pe', 'transpose', bkb(pb)[0:m, j * 128:(j + 1) * 128], lb[i][:, j, c0:c0 + m], ident,
                      r=['lb%d' % i, 'ident'], w=[pk(pb)])
                A('act', 'activation', out=dstT[0:m, :], in_=bkb(pb)[0:m, :], func=AF.Copy, r=[pk(pb)], w=[dkey])
                if dst_dram is not None:
                    P.dma('pool', dst_dram, dstT[0:m, :], 'sc', reads=[dkey])

            def vstore(i, nh, dst):
                A('dve', 'tensor_copy', out=vst[:, :, 0:nh, 0:64], in_=lb[i][:, :, 0:nh * 64].rearrange("p j (h d) -> p j h d", d=64),
                  r=['lb%d' % i], w=['vst'])
                P.dma('pool', dst[b, 0:PAST, :].rearrange("(j p) (h c) -> p j h c", p=128, c=65), vst[:, :, 0:nh, :], 'sc', reads=['vst'])

            i = load(c_fk, 384)
            for c in range(3):
                kt = kTs[rr[0] % 2]; kk = 'kTs%d' % (rr[0] % 2)
                transp(i, c * 128, 128, kt, kk, g.kT_fox[b, c * 128:(c + 1) * 128, 0:PAST])
            i = load(c_dk, 256)
            for c in range(2):
                kt = kTs[rr[0] % 2]; kk = 'kTs%d' % (rr[0] % 2)
                transp(i, c * 128, 128, kt, kk, g.kT_diff[b, c * 128:(c + 1) * 128, 0:PAST])
            i = load(c_kr, 32)
            kt = kTs[rr[0] % 2]; kk = 'kTs%d' % (rr[0] % 2)
            transp(i, 0, 32, kt, kk, g.kT_rope[b, :, 0:PAST])
            i = load(c_fv, 384)
            vstore(i, 6, g.v_fox)
            i = load(c_dv, 256)
            vstore(i, 4, g.v_diff)
            i = load(c_ckv, 256)
            for c in range(2):
                transp(i, c * 128, 128, ckT[:, c, :], 'ckT', None)
            for c in range(3):
                for hf in range(2):
                    pb = 3 + rr[0] % 2
                    rr[0] += 1
                    for kc in range(2):
                        A('pe', 'matmul', bk(pb)[:, 0:512], lhsT=wk[:, kc, c * 128:(c + 1) * 128],
                                                                               rhs=ckT[:, kc, hf * 512:(hf + 1) * 512], start=(kc == 0), stop=(kc == 1),
                          r=['wk', 'ckT'], w=[pk(pb)])
                    kt = kTs[rr[0] % 2]; kk = 'kTs%d' % (rr[0] % 2)
                    A('act', 'activation', out=kt[:, 0:512], in_=bk(pb)[:, 0:512], func=AF.Copy, r=[pk(pb)], w=[kk])
                    P.dma('pool', g.kT_nope[b, c * 128:(c + 1) * 128, hf * 512:(hf + 1) * 512], kt[:, 0:512], 'sc', reads=[kk])
            for j in range(8):
                pb = rr[0] % 3
                rr[0] += 1
                for kc in range(2):
                    A('pe', 'matmul', bk(pb)[:, 0:384], lhsT=ckT[:, kc, j * 128:(j + 1) * 128], rhs=wv[:, kc, :],
                                                                   start=(kc == 0), stop=(kc == 1), r=['wv', 'ckT'], w=[pk(pb)])
                A('dve', 'tensor_copy', out=vst[:, j, 0:6, 0:64], in_=bk(pb)[:, 0:384].rearrange("p (h d) -> p h d", d=64),
                  r=[pk(pb)], w=['vst'])
            P.dma('pool', g.v_mla[b, 0:PAST, :].rearrange("(j p) (h c) -> p j h c", p=128, c=65), vst[:, :, 0:6, :], 'sc', reads=['vst'])
            P.dma('sp', lf_c, c_lf[l, b].rearrange("(j p) h -> p j h", p=128), 'cl2', writes=['lf_c'])
            for j in range(8):
                prev = zero6 if j == 0 else ctok_s[:, b, j - 1, :]
                A('pe', 'matmul', bk(5)[:, 0:6], lhsT=U_f, rhs=lf_c[:, j, :], start=True, stop=False, r=['U', 'lf_c'], w=[pk(5)])
                A('pe', 'matmul', bk(5)[:, 0:6], lhsT=E_f, rhs=prev, start=False, stop=True, r=['E', 'ctoks', 'zero6'], w=[pk(5)])
                A('dve', 'tensor_copy', out=ctok_s[:, b, j, :], in_=bk(5)[:, 0:6], r=[pk(5)], w=['ctoks'])

    def phase_p2(l):
        ar.reset()
        lam_init = 0.8 - 0.6 * math.exp(-0.3 * l)
        lq = ar.alloc([4, 32], F32, parts=64)
        lt = ar.alloc([8], F32, parts=64)
        negl = ar.alloc([1], F32, parts=64)
        gsub = ar.alloc([1], F32, parts=64)
        P.dma('sp', lq, lam4[l:l + 1].to_broadcast([64, 4, 32]), 'c0', writes=['lq'])
        P.dma('sp', gsub, g_sub[l].rearrange("(d o) -> d o", o=1), 'c0', writes=['gsub'], allow_slow_non_contiguous=True)
        A('dve', 'tensor_tensor', out=lq[:, 0, :], in0=lq[:, 0, :], in1=lq[:, 1, :], op=ALU.mult, r=['lq'], w=['lq'])
        A('dve', 'tensor_tensor', out=lq[:, 2, :], in0=lq[:, 2, :], in1=lq[:, 3, :], op=ALU.mult, r=['lq'], w=['lq'])
        A('dve', 'reduce_sum', out=lt[:, 0:1], in_=lq[:, 0, :], axis=mybir.AxisListType.X, r=['lq'], w=['lt'])
        A('dve', 'reduce_sum', out=lt[:, 1:2], in_=lq[:, 2, :], axis=mybir.AxisListType.X, r=['lq'], w=['lt'])
        A('act', 'activation', out=lt[:, 0:2], in_=lt[:, 0:2], func=AF.Exp, r=['lt'], w=['lt'])
        A('dve', 'tensor_tensor', out=lt[:, 2:3], in0=lt[:, 1:2], in1=lt[:, 0:1], op=ALU.subtract, r=['lt'], w=['lt'])
        A('dve', 'tensor_scalar_add', out=negl, in0=lt[:, 2:3], scalar1=-lam_init, r=['lt'], w=['negl'])
        A('dve', 'tensor_scalar_mul', out=gsub, in0=gsub, scalar1=(1.0 - lam_init), r=['gsub'], w=['gsub'])

        NKBmax = max(g.tk for g in grp)
        NKBmax = (NKBmax + 127) // 128
        TKmax = max(g.tk for g in grp); TQmax = max(g.tq for g in grp)
        slots = []
        for s in range(2):
            slots.append((ar.alloc([TKmax], BF16), ar.alloc([TQmax], BF16), ar.alloc([NKBmax, 65], BF16)))
            A('pool', 'memset', slots[s][0], 0.0, w=['u%d' % s])
            A('pool', 'memset', slots[s][1], 0.0, w=['u%d' % s])
        slot_hw = [0, 0]
        pT = [ar.alloc([512], BF16) for _ in range(4)]
        rl = ar.alloc([512], F32, parts=65)
        bcs = ar.alloc([512], F32, parts=64)
        on2 = ar.alloc([512], F32, parts=64)
        oc = ar.alloc([512], F32, parts=64)
        sqd = ar.alloc([512], F32, parts=64)
        rs2 = ar.alloc([512], F32, parts=64)
        mst = [ar.alloc([512], BF16, parts=64) for _ in range(2)]
        fbias = ar.alloc([NKBmax], F32)

        units = []
        for g in grp:
            for sq_ in range(g.nseq):
                for h in range(6):
                    units.append(dict(g=g, s=sq_, kind='fox', h=h, rows=64, scale=0.125, mask=0, row0=64 * h,
                                      kT=[(g.kT_fox[sq_, 64 * h:64 * h + 64, :], 0, 64)], qT=g.qT_fox[sq_, 64 * h:64 * h + 64, :],
                                      v=g.v_fox[sq_, :, 65 * h:65 * h + 65]))
                for h in range(6):
                    units.append(dict(g=g, s=sq_, kind='mla', h=h, rows=96, scale=96 ** -0.5, mask=1, row0=384 + 64 * h,
                                      kT=[(g.kT_nope[sq_, 64 * h:64 * h + 64, :], 0, 64), (g.kT_rope[sq_, :, :], 64, 32)],
                                      qT=g.qT_mla[sq_, 96 * h:96 * h + 96, :], v=g.v_mla[sq_, :, 65 * h:65 * h + 65]))
                for h in range(4):
                    for s2 in range(2):
                        r0 = 64 * h + 32 * s2
                        units.append(dict(g=g, s=sq_, kind='diff', h=h, s2=s2, rows=32, scale=32 ** -0.5, mask=2 + h, row0=768 + 64 * h,
                                          kT=[(g.kT_diff[sq_, r0:r0 + 32, :], 0, 32)], qT=g.qT_diff[sq_, r0:r0 + 32, :],
                                          v=g.v_diff[sq_, :, 65 * h:65 * h + 65]))

        def load_unit(ui):
            u = units[ui]
            g = u['g']
            kt, qt, vt = slots[ui % 2]
            sk = 'u%d' % (ui % 2)
            if u['rows'] < slot_hw[ui % 2]:
                a0 = u['rows']
                while a0 < slot_hw[ui % 2]:
                    a1 = min(128, a0 + (32 if a0 % 64 else 64))
                    A('pool', 'memset', qt[a0:a1, :], 0.0, w=[sk])
                    a0 = a1
            slot_hw[ui % 2] = u['rows']
            for (src, p0, n) in u['kT']:
                P.dma('sp', kt[p0:p0 + n, 0:g.tk], src, 'ul%d' % (ui % 2), writes=[sk])
            P.dma('sp', qt[0:u['rows'], 0:g.tq], u['qT'], 'ul%d' % (ui % 2), writes=[sk])
            nkb = g.tk // 128
            for k0 in range(0, nkb, 16):
                k1 = min(nkb, k0 + 16)
                P.dma('sp', vt[:, k0:k1, :], u['v'][k0 * 128:k1 * 128, :].rearrange("(j p) c -> p j c", p=128), 'ul%d' % (ui % 2), writes=[sk])
            rem = g.tk - nkb * 128
            if rem:
                P.dma('sp', vt[0:rem, nkb, :], u['v'][nkb * 128:g.tk, :], 'ul%d' % (ui % 2), writes=[sk])

        steps = []
        grpno = 0
        for ui, u in enumerate(units):
            g = u['g']
            if g.gi == 0:
                for i in range(NQT):
                    nkb = 4 * i + 4
                    for kb in range(nkb):
                        j = kb - 4 * i
                        c0 = 128 * j if j > 0 else 0
                        steps.append(dict(ui=ui, i=i, kb=kb, nk=128, c0=c0, W=512, first=(kb == 0), last=(kb == nkb - 1),
                                          diag=(j if j >= 0 else None), grp=grpno, q0=512 * i, al=(kb - 4 * i - 2) + (4 * NQT - 2)))
                    grpno += 1
            else:
                nkb = NKB_S
                for kb in range(nkb):
                    nk = min(128, g.tk - 128 * kb)
                    steps.append(dict(ui=ui, i=0, kb=kb, nk=nk, c0=0, W=TS, first=(kb == 0), last=(kb == nkb - 1),
                                      diag=(0 if kb == nkb - 1 else None), grp=grpno, q0=0, al=4 * NQT + kb))
                    grpno += 1 if kb == nkb - 1 else 0

        last_step_of = {}
        for n_, st_ in enumerate(steps):
            last_step_of[st_['ui']] = n_
        for ui_ in range(min(2, len(units))):
            load_unit(ui_)

        fb_state = [None]

        def emit_S(n):
            st = steps[n]
            u = units[st['ui']]
            kt, qt, vt = slots[st['ui'] % 2]
            sk = 'u%d' % (st['ui'] % 2)
            pb = n % 3
            rows, nk, c0, W, kb = u['rows'], st['nk'], st['c0'], st['W'], st['kb']
            qc = st['q0'] if u['g'].gi == 0 else 0
            A('pe', 'matmul', bk(pb)[0:nk, c0:W], lhsT=kt[:, kb * 128:kb * 128 + nk], rhs=qt[:, qc + c0:qc + W],
                                       start=True, stop=True, r=[sk], w=[pk(pb)])

        def fox_bias(st, u):
            key = (st['ui'], st['i'])
            if fb_state[0] == key:
                return
            fb_state[0] = key
            g, h = u['g'], u['h']
            if g.gi == 0:
                blk = 4 * st['i'] + 1
                A('pe', 'matmul', bk(5)[:, 0:1], lhsT=E_f, rhs=ctok_p[:, blk, h:h + 1], start=True, stop=True,
                  r=['E', 'ctok'], w=[pk(5)])
                A('dve', 'tensor_copy', out=cref[:, 0:1], in_=bk(5)[:, 0:1], r=[pk(5)], w=['cref'])
                A('dve', 'tensor_scalar', out=fbias[:, 0:NT], in0=ctok_p[:, :, h], scalar1=-1.0, scalar2=cref[:, 0:1],
                                                   op0=ALU.mult, op1=ALU.add, r=['ctok', 'cref'], w=['fbias'])
            else:
                sq_ = u['s']
                A('pe', 'matmul', bk(5)[:, 0:1], lhsT=E_f, rhs=ctok_s[:, sq_, NKB_S - 2, h:h + 1], start=True, stop=True,
                  r=['E', 'ctoks'], w=[pk(5)])
                A('dve', 'tensor_copy', out=cref[:, 0:1], in_=bk(5)[:, 0:1], r=[pk(5)], w=['cref'])
                A('dve', 'tensor_scalar', out=fbias[:, 0:NKB_S], in0=ctok_s[:, sq_, :, h], scalar1=-1.0, scalar2=cref[:, 0:1],
                                                   op0=ALU.mult, op1=ALU.add, r=['ctoks', 'cref'], w=['fbias'])

        def emit_rest(n):
            st = steps[n]
            u = units[st['ui']]
            g = u['g']
            kt, qt, vt = slots[st['ui'] % 2]
            sk = 'u%d' % (st['ui'] % 2)
            pb = n % 3
            nk, c0, W, kb = st['nk'], st['c0'], st['W'], st['kb']
            pt = pT[n % 4]; ptk = 'pT%d' % (n % 4)
            ob = 3 + st['grp'] % 2
            if u['kind'] == 'fox':
                fox_bias(st, u)
                bias = fbias[0:nk, kb:kb + 1]
                A('act', 'activation', out=pt[0:nk, c0:W], in_=bk(pb)[0:nk, c0:W], func=AF.Exp, bias=bias, scale=u['scale'],
                  r=[pk(pb), 'fbias'], w=[ptk])
            elif u['kind'] == 'diff':
                ai = u['h'] * NAL + st['al']
                bias = al_t[0:nk, ai:ai + 1]
                A('act', 'activation', out=pt[0:nk, c0:W], in_=bk(pb)[0:nk, c0:W], func=AF.Exp, bias=bias, scale=u['scale'],
                  r=[pk(pb), 'al'], w=[ptk])
            else:
                A('act', 'activation', out=pt[0:nk, c0:W], in_=bk(pb)[0:nk, c0:W], func=AF.Exp, scale=u['scale'],
                  r=[pk(pb)], w=[ptk])
            if st['diag'] is not None:
                mw = min(128, W - c0)
                m = masks[0:nk, u['mask'], 0:mw]
                A('dve', 'tensor_tensor', out=pt[0:nk, c0:c0 + mw], in0=pt[0:nk, c0:c0 + mw], in1=m, op=ALU.mult,
                  r=[ptk, 'masks'], w=[ptk])
            A('pe', 'matmul', bk(ob)[0:65, c0:W], lhsT=vt[0:nk, kb, :], rhs=pt[0:nk, c0:W], start=st['first'], stop=st['last'],
              r=[sk, ptk], w=[pk(ob)])
            if last_step_of[st['ui']] == n and st['ui'] + 2 < len(units):
                load_unit(st['ui'] + 2)
            if not st['last']:
                return
            A('dve', 'reciprocal', out=rl[64:65, 0:W], in_=bk(ob)[64:65, 0:W], r=[pk(ob)], w=['rl'])
            A('pe', 'matmul', bk(5)[0:64, 0:W], lhsT=ones_f[64:65, 0:64], rhs=rl[64:65, 0:W], start=True, stop=True,
              r=['ones', 'rl'], w=[pk(5)])
            A('dve', 'tensor_copy', out=bcs[:, 0:W], in_=bk(5)[0:64, 0:W], r=[pk(5)], w=['bcs'])
            tok0 = (u['s'] * g.tq if g.gi == 1 else 0) + st['q0']
            if u['kind'] != 'diff':
                ms = mst[st['grp'] % 2]; msk = 'mst%d' % (st['grp'] % 2)
                A('dve', 'tensor_tensor', out=ms[:, 0:W], in0=bk(ob)[0:64, 0:W], in1=bcs[:, 0:W], op=ALU.mult,
                  r=[pk(ob), 'bcs'], w=[msk])
                P.dma('pool', g.mixT[u['row0']:u['row0'] + 64, tok0:tok0 + W], ms[:, 0:W], 'sm', reads=[msk])
            elif u['s2'] == 0:
                A('dve', 'tensor_tensor', out=on1s[st['i']][:, 0:W], in0=bk(ob)[0:64, 0:W], in1=bcs[:, 0:W], op=ALU.mult,
                  r=[pk(ob), 'bcs'], w=['on1%d' % st['i']])
            else:
                A('dve', 'tensor_tensor', out=on2[:, 0:W], in0=bk(ob)[0:64, 0:W], in1=bcs[:, 0:W], op=ALU.mult,
                  r=[pk(ob), 'bcs'], w=['on2'])
                A('dve', 'scalar_tensor_tensor', out=oc[:, 0:W], in0=on2[:, 0:W], scalar=negl[:, 0:1], in1=on1s[st['i']][:, 0:W],
                                                          op0=ALU.mult, op1=ALU.add, r=['on2', 'negl', 'on1%d' % st['i']], w=['oc'])
                A('dve', 'tensor_tensor', out=sqd[:, 0:W], in0=oc[:, 0:W], in1=oc[:, 0:W], op=ALU.mult, r=['oc'], w=['sqd'])
                A('pe', 'matmul', bk(6)[0:64, 0:W], lhsT=ones_f[0:64, 0:64], rhs=sqd[:, 0:W], start=True, stop=True,
                  r=['ones', 'sqd'], w=[pk(6)])
                A('act', 'activation', out=rs2[:, 0:W], in_=bk(6)[0:64, 0:W], func=AF.Ln, bias=eps_t[0:64, 0:1], scale=1.0 / 64,
                  r=[pk(6), 'eps'], w=['rs2'])
                A('act', 'activation', out=rs2[:, 0:W], in_=rs2[:, 0:W], func=AF.Exp, scale=-0.5, r=['rs2'], w=['rs2'])
                ms = mst[st['grp'] % 2]; msk = 'mst%d' % (st['grp'] % 2)
                A('dve', 'scalar_tensor_tensor', out=ms[:, 0:W], in0=oc[:, 0:W], scalar=gsub[:, 0:1], in1=rs2[:, 0:W],
                                                          op0=ALU.mult, op1=ALU.mult, r=['oc', 'gsub', 'rs2'], w=[msk])
                P.dma('pool', g.mixT[u['row0']:u['row0'] + 64, tok0:tok0 + W], ms[:, 0:W], 'sm', reads=[msk])

        on1s = [ar.alloc([512], F32, parts=64) for _ in range(max(NQT, 1))]

        LA = 2
        for n in range(len(steps) + LA):
            if n < len(steps):
                emit_S(n)
            if n - LA >= 0:
                emit_rest(n - LA)

    def phase_p3(l):
        ar.reset()
        wdn = ar.alloc([32, D], BF16)
        wms = ar.alloc([8, D], BF16)
        wpj = ar.alloc([2, D], BF16)
        wus = [ar.alloc([8, 512], BF16) for _ in range(2)]
        P.dma('sp', wdn, wb_down[l].rearrange("(c p) n -> p c n", p=128), 'c0', writes=['wdn'])
        P.dma('sp', wpj, wb_proj[l].rearrange("(c p) n -> p c n", p=128), 'c0', writes=['wpj'])
        gb = [ar.alloc([D], F32) for _ in range(3)]
        for i, gi_ in enumerate((1, 3, 5)):
            P.dma('sp', gb[i], g_norm[l, gi_:gi_ + 1, :].to_broadcast([128, D]), 'c0', writes=['gb%d' % i])
        cwt = ar.alloc([3, 32], F32); cbt = ar.alloc([32], F32)
        for jj in range(3):
            P.dma('sp', cwt[:, jj, :], cw[l, jj].rearrange("(c p) -> p c", p=128), 'c0', writes=['cwt'], allow_slow_non_contiguous=True)
        P.dma('sp', cbt, cb[l].rearrange("(c p) -> p c", p=128), 'c0', writes=['cbt'], allow_slow_non_contiguous=True)
        aT = ar.alloc([32, 512], BF16)
        h_t = ar.alloc([4, D], F32)
        XT = ar.alloc([8, 512], BF16)
        ytmp = ar.alloc([D], F32)
        tpl = ar.alloc([D], F32)
        nseg_max = max(1, NBS)
        gext = [ar.alloc([516 + 2 * nseg_max], F32) for _ in range(2)]
        tcv = ar.alloc([512], F32)
        ugl = ar.alloc([512], F32)
        xnb = [ar.alloc([D], BF16)] * 2
        p_t = ar.alloc([4, PLE], F32)
        p_b = ar.alloc([PLE], BF16)
        pTt = ar.alloc([2, 128], BF16)
        ss = ar.alloc([16], F32)
        carry = ar.alloc([32, nseg_max, 2], F32)
        cso = ar.alloc([32, nseg_max, 2], F32)
        rr = [0]

        def resid_norm(j, src_ap, src_keys, gbi, sscol):
            A('pool', 'memset', ss[:, sscol:sscol + 1], 0.0, w=['ss%d' % sscol])
            A('act', 'activation', out=tpl_junk, in_=src_ap, func=AF.Square, accum_out=ss[:, sscol:sscol + 1],
              r=src_keys + ['ss%d' % sscol], w=['junk', 'ss%d' % sscol])
            rstd_from_ss(ss[:, sscol:sscol + 1], D, 'ss%d' % sscol)
            A('dve', 'scalar_tensor_tensor', out=src_ap, in0=src_ap, scalar=ss[:, sscol:sscol + 1], in1=gb[gbi],
                                                      op0=ALU.mult, op1=ALU.mult, r=src_keys + ['ss%d' % sscol, 'gb%d' % gbi], w=src_keys)
            A('dve', 'tensor_tensor', out=h_t[:, j, :], in0=h_t[:, j, :], in1=src_ap, op=ALU.add, r=src_keys + ['h%d' % j], w=['h%d' % j])

        def prenorm_T(j, sscol):
            x_b = xnb[0]; xk = 'xnb'
            A('pool', 'memset', ss[:, sscol:sscol + 1], 0.0, w=['ss%d' % sscol])
            A('act', 'activation', out=tpl_junk, in_=h_t[:, j, :], func=AF.Square, accum_out=ss[:, sscol:sscol + 1],
              r=['h%d' % j, 'ss%d' % sscol], w=['junk', 'ss%d' % sscol])
            rstd_from_ss(ss[:, sscol:sscol + 1], D, 'ss%d' % sscol)
            A('dve', 'tensor_scalar_mul', out=x_b, in0=h_t[:, j, :], scalar1=ss[:, sscol:sscol + 1], r=['h%d' % j, 'ss%d' % sscol], w=[xk])
            pb = 2 + (j % 2)
            for kc in range(8):
                A('pe', 'transpose', bkb(pb)[:, kc * 128:(kc + 1) * 128], x_b[:, kc * 128:(kc + 1) * 128], ident,
                  r=[xk, 'ident'], w=[pk(pb)])
            A('act', 'activation', out=XT[:, :, j * 128:(j + 1) * 128], in_=bkb(pb).rearrange("p (a b) -> p a b", b=128), func=AF.Copy,
              r=[pk(pb)], w=['XT%d' % j])

        tpl_junk = ar.alloc([D], BF16)

        for g in grp:
            ntile = (g.ntok + 511) // 512
            nseg = 1 if g.gi == 0 else None
            if g.gi == 0:
                A('pool', 'memset', carry, 0.0, w=['carry'])
            else:
                for b in range(NBS):
                    for jj in range(2):
                        P.dma('sp', carry[:, :, b, jj], c_cs[l, b, jj].rearrange("(c p) -> p c", p=128), 'c0', writes=['carry'],
                              allow_slow_non_contiguous=True)
            for ti in range(ntile):
                t0 = ti * 512
                W = min(512, g.ntok - t0)
                nsub = W // 128
                nsg = 1 if g.gi == 0 else W // TS
                segW = W // nsg
                src = g.x if l == 0 else g.hbuf
                dst = g.y if l == L - 1 else g.hbuf
                P.dma('sp', XT[:, :, 0:W], g.mixT[:, t0:t0 + W].rearrange("(k p) t -> p k t", p=128), 'p3m',
                      writes=['XT%d' % j for j in range(nsub)])
                P.dma('sp', h_t[:, 0:nsub, :], src[t0:t0 + W, :].rearrange("(j p) d -> p j d", p=128), 'p3h', writes=['h%d' % j for j in range(nsub)])
                P.dma('sp', p_t[:, 0:nsub, :], g.pin[l, t0:t0 + W, :].rearrange("(j p) d -> p j d", p=128), 'p3p', writes=['p_t'])
                P.dma('sp', wms, wb_out[l].rearrange("(k p) n -> p k n", p=128), 'p3w', writes=['wms'])
                for j in range(nsub):
                    for hf in range(2):
                        pb = hf
                        for kc in range(8):
                            A('pe', 'matmul', bk(hf)[:, 0:512], lhsT=XT[:, kc, j * 128:(j + 1) * 128],
                                                                           rhs=wms[:, kc, hf * 512:(hf + 1) * 512], start=(kc == 0), stop=(kc == 7),
                              r=['XT%d' % j, 'wms'], w=[pk(hf)])
                        A('act', 'activation', out=ytmp[:, hf * 512:(hf + 1) * 512], in_=bk(hf)[:, 0:512], func=AF.Copy,
                          r=[pk(hf)], w=['ytmp'])
                    resid_norm(j, ytmp, ['ytmp'], 0, 0)
                for j in range(nsub):
                    prenorm_T(j, 1)
                xkeys = ['XT%d' % j for j in range(nsub)]
                for c in range(32):
                    if c % 2 == 0:
                        gi2 = (c // 2) % 2
                        P.dma('sp', wus[gi2], wb_up[l][:, (c // 2) * 512:(c // 2 + 1) * 512].rearrange("(k p) n -> p k n", p=128),
                              'p3u%d' % gi2, writes=['wus%d' % gi2])
                    wu = wus[(c // 2) % 2]; wuk = 'wus%d' % ((c // 2) % 2)
                    off = (c % 2) * 256
                    pg, pv = 4 + 2 * (c % 2), 5 + 2 * (c % 2)
                    for kc in range(8):
                        A('pe', 'matmul', bk(pg)[:, 0:W], lhsT=wu[:, kc, off:off + 128], rhs=XT[:, kc, 0:W],
                                                                                 start=(kc == 0), stop=(kc == 7), r=[wuk] + xkeys, w=[pk(pg)])
                    for kc in range(8):
                        A('pe', 'matmul', bk(pv)[:, 0:W], lhsT=wu[:, kc, off + 128:off + 256], rhs=XT[:, kc, 0:W],
                                                                                 start=(kc == 0), stop=(kc == 7), r=[wuk] + xkeys, w=[pk(pv)])
                    ge = gext[c % 2]; gk = 'gext%d' % (c % 2)
                    gv = ge[:, 0:nsg * (segW + 2)].rearrange("p (s w) -> p s w", w=segW + 2)
                    A('pool', 'tensor_copy', out=gv[:, :, 0:2], in_=carry[:, c, 0:nsg, :], r=['carry'], w=[gk])
                    A('act', 'activation', out=gv[:, :, 2:2 + segW], in_=bk(pg)[:, 0:W].rearrange("p (s w) -> p s w", w=segW),
                                                                  func=AF.Copy, r=[pk(pg)], w=[gk])
                    A('pool', 'tensor_copy', out=carry[:, c, 0:nsg, :], in_=gv[:, :, segW:segW + 2], r=[gk], w=['carry'])
                    if ti == ntile - 1:
                        A('pool', 'tensor_copy', out=cso[:, c, 0:nsg, :], in_=gv[:, :, segW:segW + 2], r=[gk], w=['cso'])
                    tv = tcv[:, 0:W].rearrange("p (s w) -> p s w", w=segW)
                    A('dve', 'tensor_scalar', out=tv, in0=gv[:, :, 0:segW], scalar1=cwt[:, 0, c:c + 1], scalar2=cbt[:, c:c + 1],
                                                                          op0=ALU.mult, op1=ALU.add, r=[gk, 'cwt', 'cbt'], w=['tcv'])
                    A('dve', 'scalar_tensor_tensor', out=tv, in0=gv[:, :, 1:1 + segW], scalar=cwt[:, 1, c:c + 1], in1=tv,
                                                                                 op0=ALU.mult, op1=ALU.add, r=[gk, 'cwt', 'tcv'], w=['tcv'])
                    A('dve', 'scalar_tensor_tensor', out=tv, in0=gv[:, :, 2:2 + segW], scalar=cwt[:, 2, c:c + 1], in1=tv,
                                                                                 op0=ALU.mult, op1=ALU.add, r=[gk, 'cwt', 'tcv'], w=['tcv'])
                    A('act', 'activation', out=ugl[:, 0:W], in_=tcv[:, 0:W], func=AF.Gelu_apprx_tanh, r=['tcv'], w=['ugl'])
                    A('dve', 'tensor_tensor', out=aT[:, c, 0:W], in0=ugl[:, 0:W], in1=bk(pv)[:, 0:W], op=ALU.mult,
                      r=['ugl', pk(pv)], w=['aT'])
                if ti == ntile - 1:
                    for b in range(nsg if g.gi == 1 else 1):
                        for jj in range(2):
                            P.dma('pool', o_cv[g.gi][l, b, jj].rearrange("(c p) -> p c", p=128), cso[:, :, b, jj], 'so', reads=['cso'],
                                  allow_slow_non_contiguous=True)
                P.dma('sp', wms, wb_gate[l].rearrange("(k p) n -> p k n", p=128), 'p3w', writes=['wms'])
                for j in range(nsub):
                    for hf in range(2):
                        for c in range(32):
                            A('pe', 'matmul', bk(hf)[:, 0:512], lhsT=aT[:, c, j * 128:(j + 1) * 128],
                                                                         rhs=wdn[:, c, hf * 512:(hf + 1) * 512], start=(c == 0), stop=(c == 31),
                              r=['aT', 'wdn'], w=[pk(hf)])
                        A('act', 'activation', out=ytmp[:, hf * 512:(hf + 1) * 512], in_=bk(hf)[:, 0:512], func=AF.Copy,
                          r=[pk(hf)], w=['ytmp'])
                    resid_norm(j, ytmp, ['ytmp'], 1, 2)
                    prenorm_T(j, 3)
                    A('dve', 'tensor_copy', out=p_b, in_=p_t[:, j, :], r=['p_t'], w=['p_b'])
                    for kc in range(2):
                        A('pe', 'transpose', bkb(3)[:, kc * 128:(kc + 1) * 128], p_b[:, kc * 128:(kc + 1) * 128], ident,
                          r=['p_b', 'ident'], w=[pk(3)])
                    A('dve', 'tensor_copy', out=pTt, in_=bkb(3)[:, 0:256].rearrange("p (a b) -> p a b", b=128), r=[pk(3)], w=['pTt'])
                    for hf in range(2):
                        pg, pp_ = 4 + hf, 6 + hf
                        for kc in range(8):
                            A('pe', 'matmul', bk(pg)[:, 0:512], lhsT=XT[:, kc, j * 128:(j + 1) * 128],
                                                                                 rhs=wms[:, kc, hf * 512:(hf + 1) * 512], start=(kc == 0), stop=(kc == 7),
                              r=['XT%d' % j, 'wms'], w=[pk(pg)])
                        for kc in range(2):
                            A('pe', 'matmul', bk(pp_)[:, 0:512], lhsT=pTt[:, kc, :], rhs=wpj[:, kc, hf * 512:(hf + 1) * 512],
                                                                               start=(kc == 0), stop=(kc == 1), r=['pTt', 'wpj'], w=[pk(pp_)])
                        A('act', 'activation', out=ytmp[:, hf * 512:(hf + 1) * 512], in_=bk(pg)[:, 0:512], func=AF.Sigmoid,
                          r=[pk(pg)], w=['ytmp'])
                        A('dve', 'tensor_tensor', out=tpl[:, hf * 512:(hf + 1) * 512], in0=ytmp[:, hf * 512:(hf + 1) * 512],
                                                                            in1=bk(pp_)[:, 0:512], op=ALU.mult, r=['ytmp', pk(pp_)], w=['tpl'])
                    resid_norm(j, tpl, ['tpl'], 2, 4)
                    P.dma('pool', dst[t0 + j * 128:t0 + (j + 1) * 128, :], h_t[:, j, :], 'sh', reads=['h%d' % j], writes=['hb'])

    setup()
    prep_weights()
    P.barrier()
    for l in range(L):
        phase_cache(l)
        P.barrier()
        phase_p1(l)
        P.barrier()
        phase_p2(l)
        P.barrier()
        phase_p3(l)
        P.barrier()
    P.finish()
    return nc, P


_CACHE = {}


def _host_consts(S, NBS):
    NS = NBS * TS
    NQT = S // 512
    TK = PAST + TS
    NKB_S = (TK + 127) // 128
    k = np.arange(128)[:, None]
    q = np.arange(128)[None, :]
    slopes = 2.0 ** (-8.0 * np.arange(1, 5) / 4)
    masks = np.zeros((6, 128, 128), np.float32)
    masks[0] = (k <= q)
    chunk = ((k // 64) <= (q // 64)).astype(np.float32)
    masks[1] = chunk
    for h in range(4):
        masks[2 + h] = chunk * np.exp(-2.0 * slopes[h] * np.maximum(k - q, 0))
    U = (k <= q).astype(np.float32)
    E = np.zeros((128, 128), np.float32)
    E[127, :] = 1.0
    half = 16
    inv_freq = (10000.0 ** (-np.arange(half, dtype=np.float32) / half)).astype(np.float32)

    def cs(pos):
        ang = pos.astype(np.float32)[:, None] * inv_freq[None, :]
        c, s = np.cos(ang), np.sin(ang)
        return np.concatenate([c, c, -s, s], axis=1).astype(np.float32)
    cs_p = cs(np.arange(S))
    cs_s = cs(PAST + (np.arange(NS) % TS))
    NAL = 4 * NQT + NKB_S
    al = np.zeros((128, 4, NAL), np.float32)
    p = np.arange(128)
    for h in range(4):
        for e in range(4 * NQT):
            d = e - (4 * NQT - 2)
            al[:, h, e] = slopes[h] * (p + 128 * d)
        for kb in range(NKB_S):
            al[:, h, 4 * NQT + kb] = slopes[h] * (p + 128 * kb - (PAST + TS // 2))
    return dict(k_masks=masks, k_U=U, k_E=E, k_cs_p=cs_p, k_cst_p=np.ascontiguousarray(cs_p.T),
                k_cs_s=cs_s, k_cst_s=np.ascontiguousarray(cs_s.T), k_al=np.ascontiguousarray(al.reshape(128, 4 * NAL)))


def kernel(x_prompt, x_sample, cache_fox_k, cache_fox_v, cache_fox_logf, cache_mla_ckv,
           cache_mla_krope, cache_diff_k, cache_diff_v, state_ffn_conv, p_prompt, p_sample,
           w_in, b_forget, mla_q_norm, w_mla_uq, mla_kv_norm, w_mla_uk, w_mla_uv,
           diff_lambda_q1, diff_lambda_k1, diff_lambda_q2, diff_lambda_k2, diff_subln, w_out,
           norm_mix_pre, norm_mix_post, norm_ffn_pre, norm_ffn_post, norm_ple_pre, norm_ple_post,
           w_ffn_up, ffn_conv_w, ffn_conv_b, w_ffn_down, w_ple_gate, w_ple_proj):
    f = lambda a: np.ascontiguousarray(np.asarray(a, dtype=np.float32))
    B, S, _ = x_prompt.shape
    DB = x_sample.shape[0]
    L = w_in.shape[0]
    NBS = DB // NCORES
    NS = NBS * TS
    assert B * 2 == NCORES or B <= NCORES
    key = (S, NBS, L)
    if key not in _CACHE:
        _CACHE[key] = build(S, NBS, L)
    nc, _ = _CACHE[key]
    w_in = f(w_in)
    kr = w_in[:, :, 1798:1830]
    krs = np.concatenate([kr[:, :, 16:32], kr[:, :, 0:16]], axis=2)
    w_in_p = np.concatenate([w_in[:, :, 0:384], w_in[:, :, 1158:1542], w_in[:, :, 1830:2086], w_in[:, :, 384:768],
                             w_in[:, :, 1152:1158], np.zeros((L, D, 2), np.float32), w_in[:, :, 768:1152],
                             w_in[:, :, 1542:1798], kr, krs, w_in[:, :, 2086:2342], w_in[:, :, 2342:2598]], axis=2)
    assert w_in_p.shape[2] == NCOLS
    uq = f(w_mla_uq).reshape(L, 384, 6, 96)
    uq_p = np.concatenate([uq[..., 0:64], uq[..., 64:96], uq[..., 80:96], uq[..., 64:80]], axis=3).reshape(L, 384, 768)
    up = f(w_ffn_up)
    up_p = np.stack([up[:, :, :DFF].reshape(L, D, 32, 128), up[:, :, DFF:].reshape(L, D, 32, 128)], axis=3).reshape(L, D, 2 * DFF)
    shared = dict(
        w_in=np.ascontiguousarray(w_in_p), w_uq=np.ascontiguousarray(uq_p), w_uk=f(w_mla_uk), w_uv=f(w_mla_uv),
        w_out=f(w_out), w_up=np.ascontiguousarray(up_p), w_down=f(w_ffn_down), w_gate=f(w_ple_gate), w_proj=f(w_ple_proj),
        b_f=f(b_forget), g_q=f(mla_q_norm), g_kv=f(mla_kv_norm),
        lam4=np.ascontiguousarray(np.stack([f(diff_lambda_q1), f(diff_lambda_k1), f(diff_lambda_q2), f(diff_lambda_k2)], axis=1)),
        g_sub=f(diff_subln),
        g_norm=np.ascontiguousarray(np.stack([f(norm_mix_pre), f(norm_mix_post), f(norm_ffn_pre), f(norm_ffn_post),
                                              f(norm_ple_pre), f(norm_ple_post)], axis=1)),
        cw=f(ffn_conv_w), cb=f(ffn_conv_b))
    shared.update(_host_consts(S, NBS))
    xp, xs, pp, ps_ = f(x_prompt), f(x_sample), f(p_prompt), f(p_sample)
    cfk, cfv, clf = f(cache_fox_k), f(cache_fox_v), f(cache_fox_logf)
    cckv, ckr, cdk, cdv, ccs = f(cache_mla_ckv), f(cache_mla_krope), f(cache_diff_k), f(cache_diff_v), f(state_ffn_conv)
    in_maps = []
    for c in range(NCORES):
        b = c % B
        sb = slice(c * NBS, (c + 1) * NBS)
        m = dict(shared)
        m.update(xp=xp[b], xs=np.ascontiguousarray(xs[sb].reshape(NS, D)), pp=np.ascontiguousarray(pp[:, b]),
                 pss=np.ascontiguousarray(ps_[:, sb].reshape(L, NS, PLE)),
                 c_fk=np.ascontiguousarray(cfk[:, sb].reshape(L, NBS, PAST, 384)),
                 c_fv=np.ascontiguousarray(cfv[:, sb].reshape(L, NBS, PAST, 384)),
                 c_lf=np.ascontiguousarray(clf[:, sb]), c_ckv=np.ascontiguousarray(cckv[:, sb]),
                 c_kr=np.ascontiguousarray(ckr[:, sb]),
                 c_dk=np.ascontiguousarray(cdk[:, sb].reshape(L, NBS, PAST, 256)),
                 c_dv=np.ascontiguousarray(cdv[:, sb].reshape(L, NBS, PAST, 256)),
                 c_cs=np.ascontiguousarray(ccs[:, sb]))
        in_maps.append(m)
    res = run_bass_kernel_spmd(nc, in_maps, core_ids=list(range(NCORES)))
    R = res.results
    global _LAST
    _LAST = R
    pc = [R[b] for b in range(B)]
    y_prompt = np.stack([r["y_p"] for r in pc], 0)
    y_sample = np.concatenate([r["y_s"].reshape(NBS, TS, D) for r in R], 0)

    def pstack(name, tail):
        return np.stack([r[name] for r in pc], 1).reshape((L, B, S) + tail)

    def sstack(name, tail):
        return np.concatenate([r[name].reshape((L, NBS, TS) + tail) for r in R], 1)

    outs = (y_prompt, y_sample,
            pstack("o_fk_p", (6, 64)), pstack("o_fv_p", (6, 64)), pstack("o_lf_p", (6,)), pstack("o_ckv_p", (256,)),
            pstack("o_kr_p", (32,)), pstack("o_dk_p", (4, 64)), pstack("o_dv_p", (4, 64)),
            np.stack([r["o_cv_p"][:, 0] for r in pc], 1),
            sstack("o_fk_s", (6, 64)), sstack("o_fv_s", (6, 64)), sstack("o_lf_s", (6,)), sstack("o_ckv_s", (256,)),
            sstack("o_kr_s", (32,)), sstack("o_dk_s", (4, 64)), sstack("o_dv_s", (4, 64)),
            np.concatenate([r["o_cv_s"] for r in R], 1))
    return tuple(np.ascontiguousarray(o, dtype=np.float32) for o in outs)
```

```python
import contextlib
import math
import numpy as np
import concourse.bass as bass
import concourse.mybir as mybir
from concourse.bass_utils import run_bass_kernel_spmd

F32 = mybir.dt.float32
BF16 = mybir.dt.bfloat16
AF = mybir.ActivationFunctionType
ALU = mybir.AluOpType

D = 1024
DFF = 4096
PLE = 256
PAST = 1024
TS = 64
NCOLS = 2632
O_FQ, O_CQ, O_DQ, O_FK, O_FF, O_FV, O_CKV, O_KR, O_KRS, O_DK, O_DV = (
    0, 384, 768, 1024, 1408, 1416, 1800, 2056, 2088, 2120, 2376)
EPS = 1e-6
import os
DEBUG = bool(int(os.environ.get('KDEBUG', '0')))
OPT_STQ = bool(int(os.environ.get('KV_STQ', '0')))
OPT_P3PIPE = bool(int(os.environ.get('KV_P3PIPE', '0')))
_LAST = None
NCORES = 8


class Prog:
    ENG = ('pe', 'act', 'dve', 'pool', 'sp')
    LIM = 30000
    LIMD = 16 * 1800

    def __init__(self, nc):
        self.nc = nc
        self.stack = contextlib.ExitStack()
        self.rec = {e: [] for e in self.ENG}
        self.seq = {e: 0 for e in self.ENG}
        self.dcount = {}
        self.waited = {e: {} for e in self.ENG}
        self.lastw = {}
        self.readers = {}

    def sb(self, name, shape, dtype):
        return self.stack.enter_context(self.nc.sbuf_tensor(name, list(shape), dtype))

    def ps(self, name, shape, dtype):
        return self.stack.enter_context(self.nc.psum_tensor(name, list(shape), dtype))

    def _deps(self, reads, writes, eng):
        deps = []
        for k in reads:
            d = self.lastw.get(k)
            if d is not None:
                deps.append(d)
            if k.startswith('ps'):
                r = self.readers.get(k)
                if r:
                    deps.extend((sk, v) for sk, v in r.items() if sk != ('E', eng))
        for k in writes:
            d = self.lastw.get(k)
            if d is not None:
                deps.append(d)
            r = self.readers.get(k)
            if r:
                deps.extend(r.items())
        return deps

    def _emit_waits(self, eng, deps):
        need = {}
        w = self.waited[eng]
        for (sk, v) in deps:
            if sk[0] == 'E':
                if sk[1] == 'pe' and eng == 'pe':
                    continue
            else:
                v = self.dcount[sk[1]]
            if w.get(sk, 0) >= v:
                continue
            if need.get(sk, 0) < v:
                need[sk] = v
        for sk, v in need.items():
            self.rec[eng].append(('w', sk, v, w.get(sk, 0)))
            w[sk] = v

    def _record(self, dep, reads, writes):
        sk, v = dep
        for k in reads:
            self.readers.setdefault(k, {})[sk] = v
        for k in writes:
            self.lastw[k] = dep
            self.readers[k] = {}

    def op(self, eng, meth, args, kw, reads=(), writes=()):
        self._emit_waits(eng, self._deps(reads, writes, eng))
        self.seq[eng] += 1
        n = self.seq[eng]
        self.rec[eng].append(('o', (meth, args, kw), n))
        self._record((('E', eng), n), reads, writes)

    def dma(self, eng, out, in_, sem, reads=(), writes=(), **kw):
        self._emit_waits(eng, self._deps(reads, writes, eng))
        c = self.dcount.get(sem, 0) + 16
        self.dcount[sem] = c
        self.rec[eng].append(('d', out, in_, kw, sem, c))
        self._record((('D', sem), c), reads, writes)

    def barrier(self):
        deps = [(('D', n), c) for n, c in self.dcount.items()]
        deps += [(('E', e), self.seq[e]) for e in ('pe', 'act', 'dve', 'pool') if self.seq[e] > 0]
        for e in self.ENG:
            self._emit_waits(e, [d for d in deps if d[0] != ('E', e)])
        self.lastw = {}
        self.readers = {}

    def finish(self):
        nc = self.nc
        deps = [(('D', n), c) for n, c in self.dcount.items()]
        deps += [(('E', e), self.seq[e]) for e in ('pe', 'act', 'dve', 'pool') if self.seq[e] > 0]
        self._emit_waits('sp', deps)
        sig = {e: set() for e in self.ENG}
        for e in self.ENG:
            for r in self.rec[e]:
                if r[0] == 'w' and r[1][0] == 'E':
                    sig[r[1][1]].add(r[2])
        rank, esems = {}, {}
        for e in self.ENG:
            s = sorted(sig[e])
            rank[e] = {n: i for i, n in enumerate(s)}
            nep = (len(s) + self.LIM - 1) // self.LIM
            esems[e] = [self.stack.enter_context(nc.semaphore("es_%s_%d" % (e, k))) for k in range(nep)]
        dsems = {}
        for n, c in self.dcount.items():
            nep = (c + self.LIMD - 1) // self.LIMD
            dsems[n] = [self.stack.enter_context(nc.semaphore("ds_%s_%d" % (n, k))) for k in range(nep)]
        self.nsem = sum(len(v) for v in esems.values()) + sum(len(v) for v in dsems.values())
        self.ninst = sum(len(v) for v in self.rec.values())
        LIM, LIMD = self.LIM, self.LIMD

        def run(eng, e):
            for r in self.rec[eng]:
                if r[0] == 'w':
                    sk, v, prev = r[1], r[2], r[3]
                    if sk[0] == 'E':
                        i = rank[sk[1]][v]
                        e.wait_ge(esems[sk[1]][i // LIM], (i % LIM) + 1)
                    else:
                        k = (v - 1) // LIMD
                        if k > 0 and prev < k * LIMD:
                            e.wait_ge(dsems[sk[1]][k - 1], LIMD)
                        e.wait_ge(dsems[sk[1]][k], v - k * LIMD)
                elif r[0] == 'o':
                    meth, args, kw = r[1]
                    ins = getattr(e, meth)(*args, **kw)
                    i = rank[eng].get(r[2])
                    if i is not None:
                        ins.then_inc(esems[eng][i // LIM], 1)
                else:
                    _, out, in_, kw, sem, c = r
                    k = (c - 1) // LIMD
                    e.dma_start(out=out, in_=in_, **kw).then_inc(dsems[sem][k], 16)

        with nc.Block() as block:
            @block.tensor
            def _(e):
                run('pe', e)

            @block.scalar
            def _(e):
                run('act', e)

            @block.vector
            def _(e):
                run('dve', e)

            @block.gpsimd
            def _(e):
                run('pool', e)

            @block.sync
            def _(e):
                run('sp', e)
        self.stack.close()


class Arena:
    def __init__(self, t, nwords):
        self.t, self.n, self.off = t, nwords, 0

    def reset(self):
        self.off = 0

    def alloc(self, free_shape, dtype, parts=128):
        nel = 1
        for s in free_shape:
            nel *= s
        nw = (nel * (2 if dtype == BF16 else 4) + 3) // 4
        nw = (nw + 7) // 8 * 8
        o = self.off
        self.off += nw
        assert self.off <= self.n, ("arena overflow", self.off, self.n)
        v = self.t[0:parts, o:o + nw]
        if dtype != F32:
            v = v.bitcast(dtype)
        v = v[:, 0:nel]
        if len(free_shape) == 2:
            v = v.rearrange("p (a b) -> p a b", b=free_shape[1])
        elif len(free_shape) == 3:
            v = v.rearrange("p (a b c) -> p a b c", b=free_shape[1], c=free_shape[2])
        return v


def build(S, NBS, L):
    nc = bass.Bass("TRN2", target_bir_lowering=False)
    P = Prog(nc)
    NS = NBS * TS
    TK = PAST + TS
    NKB_S = (TK + 127) // 128
    NT = S // 128
    NQT = S // 512
    assert S % 512 == 0 and NS % 128 == 0

    def din(name, shape):
        return nc.dram_tensor(name, list(shape), F32, kind="ExternalInput").ap()

    def dout(name, shape):
        return nc.dram_tensor(name, list(shape), F32, kind="ExternalOutput").ap()

    def dscr(name, shape, dt=BF16):
        return nc.dram_tensor(name, list(shape), dt, kind="Internal").ap()

    xp = din("xp", [S, D]); xs = din("xs", [NS, D])
    pp = din("pp", [L, S, PLE]); pss = din("pss", [L, NS, PLE])
    c_fk = din("c_fk", [L, NBS, PAST, 384]); c_fv = din("c_fv", [L, NBS, PAST, 384])
    c_lf = din("c_lf", [L, NBS, PAST, 6]); c_ckv = din("c_ckv", [L, NBS, PAST, 256])
    c_kr = din("c_kr", [L, NBS, PAST, 32]); c_dk = din("c_dk", [L, NBS, PAST, 256])
    c_dv = din("c_dv", [L, NBS, PAST, 256]); c_cs = din("c_cs", [L, NBS, 2, DFF])
    w_in = din("w_in", [L, D, NCOLS]); w_uq = din("w_uq", [L, 384, 768])
    w_uk = din("w_uk", [L, 256, 384]); w_uv = din("w_uv", [L, 256, 384])
    w_out = din("w_out", [L, D, D]); w_up = din("w_up", [L, D, 2 * DFF])
    w_down = din("w_down", [L, DFF, D]); w_gate = din("w_gate", [L, D, D])
    w_proj = din("w_proj", [L, PLE, D])
    b_f = din("b_f", [L, 6]); g_q = din("g_q", [L, 384]); g_kv = din("g_kv", [L, 256])
    lam4 = din("lam4", [L, 4, 32]); g_sub = din("g_sub", [L, 64])
    g_norm = din("g_norm", [L, 6, D])
    cw = din("cw", [L, 3, DFF]); cb = din("cb", [L, DFF])
    k_masks = din("k_masks", [6, 128, 128]); k_U = din("k_U", [128, 128]); k_E = din("k_E", [128, 128])
    k_cs_p = din("k_cs_p", [S, 64]); k_cst_p = din("k_cst_p", [64, S])
    k_cs_s = din("k_cs_s", [NS, 64]); k_cst_s = din("k_cst_s", [64, NS])
    NAL = 4 * NQT + NKB_S
    k_al = din("k_al", [128, 4 * NAL])

    y_p = dout("y_p", [S, D]); y_s = dout("y_s", [NS, D])
    o_fk = [dout("o_fk_p", [L, S, 384]), dout("o_fk_s", [L, NS, 384])]
    o_fv = [dout("o_fv_p", [L, S, 384]), dout("o_fv_s", [L, NS, 384])]
    o_lf = [dout("o_lf_p", [L, S, 6]), dout("o_lf_s", [L, NS, 6])]
    o_ckv = [dout("o_ckv_p", [L, S, 256]), dout("o_ckv_s", [L, NS, 256])]
    o_kr = [dout("o_kr_p", [L, S, 32]), dout("o_kr_s", [L, NS, 32])]
    o_dk = [dout("o_dk_p", [L, S, 256]), dout("o_dk_s", [L, NS, 256])]
    o_dv = [dout("o_dv_p", [L, S, 256]), dout("o_dv_s", [L, NS, 256])]
    o_cv = [dout("o_cv_p", [L, 1, 2, DFF]), dout("o_cv_s", [L, NBS, 2, DFF])]

    wb_in = dscr("wb_in", [L, D, NCOLS]); wb_uq = dscr("wb_uq", [L, 384, 768])
    wb_uk = dscr("wb_uk", [L, 256, 384]); wb_uv = dscr("wb_uv", [L, 256, 384])
    wb_out = dscr("wb_out", [L, D, D]); wb_up = dscr("wb_up", [L, D, 2 * DFF])
    wb_down = dscr("wb_down", [L, DFF, D]); wb_gate = dscr("wb_gate", [L, D, D])
    wb_proj = dscr("wb_proj", [L, PLE, D])

    class G:
        pass
    grp = []
    for gi, (nseq, tk, tq) in enumerate([(1, S, S), (NBS, TK, TS)]):
        g = G()
        g.gi, g.nseq, g.tk, g.tq = gi, nseq, tk, tq
        g.ntok = nseq * tq
        g.qT_fox = dscr("qT_fox%d" % gi, [nseq, 384, tq]); g.kT_fox = dscr("kT_fox%d" % gi, [nseq, 384, tk])
        g.v_fox = dscr("v_fox%d" % gi, [nseq, tk, 6 * 65])
        g.qT_mla = dscr("qT_mla%d" % gi, [nseq, 576, tq]); g.kT_rope = dscr("kT_rope%d" % gi, [nseq, 32, tk])
        g.kT_nope = dscr("kT_nope%d" % gi, [nseq, 384, tk]); g.v_mla = dscr("v_mla%d" % gi, [nseq, tk, 6 * 65])
        g.qT_diff = dscr("qT_diff%d" % gi, [nseq, 256, tq]); g.kT_diff = dscr("kT_diff%d" % gi, [nseq, 256, tk])
        g.v_diff = dscr("v_diff%d" % gi, [nseq, tk, 4 * 65])
        g.mixT = (nc.dram_tensor("mixT%d" % gi, [D, g.ntok], BF16, kind="ExternalOutput").ap() if DEBUG
                  else dscr("mixT%d" % gi, [D, g.ntok]))
        g.hbuf = dscr("hbuf%d" % gi, [g.ntok, D], F32)
        g.x = xp if gi == 0 else xs
        g.y = y_p if gi == 0 else y_s
        g.pin = pp if gi == 0 else pss
        g.cs = k_cs_p if gi == 0 else k_cs_s
        g.cst = k_cst_p if gi == 0 else k_cst_s
        grp.append(g)

    ARW = 50300
    arena_t = P.sb("arena", [128, ARW], F32)
    ar = Arena(arena_t, ARW)
    pers_t = P.sb("pers", [128, 2600], F32)
    pers = Arena(pers_t, 2600)
    banks = [P.ps("bank%d" % i, [128, 512], F32) for i in range(8)]

    def bk(i):
        return banks[i][:, :]

    def bkb(i):
        return banks[i][:, :].bitcast(BF16)

    def pk(i):
        return 'ps%d' % i

    ident = pers.alloc([128], BF16)
    ones_f = pers.alloc([128], F32)
    U_f = pers.alloc([128], F32)
    E_f = pers.alloc([128], F32)
    masks = pers.alloc([6, 128], BF16)
    al_t = pers.alloc([4 * NAL], F32)
    eps_t = pers.alloc([1], F32)
    zero6 = pers.alloc([6], F32)
    ctok_p = pers.alloc([NT, 6], F32)
    ctok_s = pers.alloc([NBS, NKB_S, 6], F32)
    cref = pers.alloc([512], F32)


    def setup():
        A('pool', 'memset', ident, 1.0, w=['ident'])
        A('pool', 'affine_select', out=ident, in_=ident, pattern=[[-1, 128]], compare_op=ALU.is_equal,
                                             fill=0.0, base=0, channel_multiplier=1, r=['ident'], w=['ident'])
        A('pool', 'memset', ones_f, 1.0, w=['ones'])
        A('pool', 'memset', eps_t, EPS, w=['eps'])
        A('pool', 'memset', zero6, 0.0, w=['zero6'])
        P.dma('sp', U_f, k_U[:, :], 'c0', writes=['U'])
        P.dma('sp', E_f, k_E[:, :], 'c0', writes=['E'])
        P.dma('sp', al_t, k_al[:, :], 'c0', writes=['al'])
        ar.reset()
        mtmp = ar.alloc([6, 128], F32)
        P.dma('sp', mtmp, k_masks.rearrange("m k q -> k m q"), 'c0', writes=['mtmp'])
        A('dve', 'tensor_copy', out=masks, in_=mtmp, r=['mtmp'], w=['masks'])

    _op = P.op

    def A(eng, meth, *args, r=(), w=(), **kw):
        _op(eng, meth, args, kw, reads=r, writes=w)

    def prep_weights():
        ar.reset()
        NB = 3
        stg = [ar.alloc([4096], F32) for _ in range(NB)]
        stb = [ar.alloc([4096], BF16) for _ in range(NB)]
        gcol = ar.alloc([L, 3, 8], F32)
        for l in range(L):
            for wi, gidx in enumerate((0, 2, 4)):
                P.dma('sp', gcol[:, l, wi, :], g_norm[l, gidx].rearrange("(k p) -> p k", p=128), 'c0',
                      writes=['gcol'], allow_slow_non_contiguous=True)
        cnt = [0]

        def conv(src, dst, rows, cols, gain=None):
            for rk in range(rows // 128):
                for c0 in range(0, cols, 4096):
                    c1 = min(cols, c0 + 4096)
                    i = cnt[0] % NB
                    cnt[0] += 1
                    sv, bv = stg[i][:, 0:c1 - c0], stb[i][:, 0:c1 - c0]
                    P.dma('sp', sv, src[rk * 128:(rk + 1) * 128, c0:c1], 'wl%d' % i, writes=['stg%d' % i])
                    eng = ('dve', 'act', 'pool')[cnt[0] % 3] if gain is None else 'dve'
                    if gain is None:
                        if eng == 'act':
                            A('act', 'activation', out=bv, in_=sv, func=AF.Copy,
                              r=['stg%d' % i], w=['stb%d' % i])
                        else:
                            A(eng, 'tensor_copy', out=bv, in_=sv,
                              r=['stg%d' % i], w=['stb%d' % i])
                    else:
                        gc = gain[:, rk:rk + 1]
                        A('dve', 'tensor_scalar_mul', out=bv, in0=sv, scalar1=gc,
                          r=['stg%d' % i, 'gcol'], w=['stb%d' % i])
                    P.dma('pool', dst[rk * 128:(rk + 1) * 128, c0:c1], bv, 'ws%d' % i, reads=['stb%d' % i])

        for l in range(L):
            conv(w_in[l], wb_in[l], D, NCOLS, gcol[:, l, 0, :])
            conv(w_uq[l], wb_uq[l], 384, 768)
            conv(w_uk[l], wb_uk[l], 256, 384)
            conv(w_uv[l], wb_uv[l], 256, 384)
            conv(w_out[l], wb_out[l], D, D)
            conv(w_up[l], wb_up[l], D, 2 * DFF, gcol[:, l, 1, :])
            conv(w_down[l], wb_down[l], DFF, D)
            conv(w_gate[l], wb_gate[l], D, D, gcol[:, l, 2, :])
            conv(w_proj[l], wb_proj[l], PLE, D)

    def rstd_from_ss(ss_ap, n, key):
        A('act', 'activation', out=ss_ap, in_=ss_ap, func=AF.Ln, bias=eps_t[0:ss_ap.shape[0], 0:1],
                                        scale=1.0 / n, r=[key, 'eps'], w=[key])
        A('act', 'activation', out=ss_ap, in_=ss_ap, func=AF.Exp, scale=-0.5, r=[key], w=[key])

    def token_rows(g, t0, n):
        out = []
        if g.gi == 0:
            return [(0, t0, n, 0)]
        t = t0
        while t < t0 + n:
            out.append((t // TS, PAST + (t % TS), TS, t - t0))
            t += TS
        return out

    def phase_p1(l):
        ar.reset()
        wi = ar.alloc([8, NCOLS], BF16)
        wq = ar.alloc([3, 768], BF16)
        wk = ar.alloc([2, 384], BF16)
        wv = ar.alloc([2, 384], BF16)
        P.dma('sp', wi, wb_in[l].rearrange("(k p) n -> p k n", p=128), 'c0', writes=['wi'])
        P.dma('sp', wq, wb_uq[l].rearrange("(k p) n -> p k n", p=128), 'c0', writes=['wq'])
        P.dma('sp', wk, wb_uk[l].rearrange("(k p) n -> p k n", p=128), 'c0', writes=['wk'])
        P.dma('sp', wv, wb_uv[l].rearrange("(k p) n -> p k n", p=128), 'c0', writes=['wv'])
        gq_c = ar.alloc([3], F32); gkv_c = ar.alloc([2], F32)
        P.dma('sp', gq_c, g_q[l].rearrange("(k p) -> p k", p=128), 'c0', writes=['gqc'], allow_slow_non_contiguous=True)
        P.dma('sp', gkv_c, g_kv[l].rearrange("(k p) -> p k", p=128), 'c0', writes=['gkvc'], allow_slow_non_contiguous=True)
        gkv_b = ar.alloc([256], F32); bf_b = ar.alloc([6], F32)
        P.dma('sp', gkv_b, g_kv[l:l + 1, :].to_broadcast([128, 256]), 'c0', writes=['gkvb'])
        P.dma('sp', bf_b, b_f[l:l + 1, :].to_broadcast([128, 6]), 'c0', writes=['bfb'])
        h_ts = [ar.alloc([4, D], F32) for _ in range(2)]
        ssn = ar.alloc([8], F32)
        tcount = [0]
        sq_rr = [0]

        def stq():
            sq_rr[0] += 1
            return ('sp', 'pool')[sq_rr[0] % 2] if OPT_STQ else 'pool'
        xn = [ar.alloc([D], BF16) for _ in range(2)]
        xnT = ar.alloc([8, 512], BF16)
        cs_t = ar.alloc([4, 64], F32)
        cst_t = ar.alloc([512], F32, parts=64)
        ost_a = [ar.alloc([390], F32) for _ in range(2)]
        ost_b = [ar.alloc([384], F32) for _ in range(2)]
        ost_c = [ar.alloc([320], F32) for _ in range(2)]
        ost_d = [ar.alloc([512], F32) for _ in range(2)]
        vst = ar.alloc([4, 16, 65], BF16)
        fst = [ar.alloc([512], BF16) for _ in range(6)]
        fsi = [0]

        def next_fst():
            fsi[0] += 1
            return fsi[0] % 6
        raw = ar.alloc([3, 512], F32)
        sq = [ar.alloc([512], F32) for _ in range(2)]
        rsb = ar.alloc([512], F32)
        cqnT = ar.alloc([3, 512], BF16)
        ckvnT = ar.alloc([2, 512], BF16)
        t1 = ar.alloc([512], F32); t2 = ar.alloc([512], F32)
        sm = ar.alloc([64], F32)
        lf_t = ar.alloc([4, 6], F32)
        A('pool', 'memset', vst, 1.0, w=['vst'])
        rr = [0]

        for g in grp:
            ntile = (g.ntok + 511) // 512
            for ti in range(ntile):
                t0 = ti * 512
                W = min(512, g.ntok - t0)
                nsub = W // 128
                src = g.x if l == 0 else g.hbuf
                h_t = h_ts[tcount[0] % 2]
                hkey = 'h_t%d' % (tcount[0] % 2)
                tcount[0] += 1
                P.dma('sp', h_t[:, 0:nsub, :], src[t0:t0 + W, :].rearrange("(j p) d -> p j d", p=128), 'p1h' + hkey, writes=[hkey])
                P.dma('sp', cs_t[:, 0:nsub, :], g.cs[t0:t0 + W, :].rearrange("(j p) d -> p j d", p=128), 'p1c', writes=['cs_t'])
                P.dma('sp', cst_t[:, 0:W], g.cst[:, t0:t0 + W], 'p1c', writes=['cst_t'])
                A('pool', 'memset', ssn, 0.0, w=['ssn'])
                for j in range(nsub):
                    x_b = xn[j % 2]; xk = 'xn%d' % (j % 2)
                    A('act', 'activation', out=x_b, in_=h_t[:, j, :], func=AF.Square,
                                                                  accum_out=ssn[:, j:j + 1], r=[hkey, 'ssn'], w=[xk, 'ssn'])
                for j in range(nsub):
                    pass
                rstd_from_ss(ssn[:, 0:nsub], D, 'ssn')
                for j in range(nsub):
                    x_b = xn[j % 2]; xk = 'xn%d' % (j % 2)
                    A('dve', 'tensor_scalar_mul', out=x_b, in0=h_t[:, j, :], scalar1=ssn[:, j:j + 1],
                      r=[hkey, 'ssn'], w=[xk])
                    pb = 6 + (j % 2)
                    for kc in range(8):
                        A('pe', 'transpose', bkb(pb)[:, kc * 128:(kc + 1) * 128],
                                                                             x_b[:, kc * 128:(kc + 1) * 128], ident,
                          r=[xk, 'ident'], w=[pk(pb)])
                    A('act', 'activation', out=xnT[:, :, j * 128:(j + 1) * 128],
                                                                in_=bkb(pb).rearrange("p (a b) -> p a b", b=128), func=AF.Copy,
                      r=[pk(pb)], w=['xnT'])
                for j in range(nsub):
                    rows = token_rows(g, t0 + j * 128, 128)
                    oa, ob, oc_, od = ost_a[j % 2], ost_b[j % 2], ost_c[j % 2], ost_d[j % 2]
                    ka, kb_, kc_, kd = 'oa%d' % (j % 2), 'ob%d' % (j % 2), 'oc%d' % (j % 2), 'od%d' % (j % 2)
                    for ci, (c0, c1) in enumerate([(O_FK, O_FK + 390), (O_FV, O_FV + 384), (O_CKV, O_CKV + 320), (O_DK, O_DK + 512)]):
                        pb = rr[0] % 3
                        rr[0] += 1
                        for kc in range(8):
                            A('pe', 'matmul',
                                bk(pb)[:, 0:c1 - c0], lhsT=xnT[:, kc, j * 128:(j + 1) * 128], rhs=wi[:, kc, c0:c1],
                                start=(kc == 0), stop=(kc == 7), r=['xnT', 'wi'], w=[pk(pb)])
                        if ci == 0:
                            A('act', 'activation', out=oa, in_=bk(pb)[:, 0:390], func=AF.Copy,
                              r=[pk(pb)], w=[ka])
                            A('dve', 'tensor_tensor', out=sm[:, 0:6], in0=oa[:, 384:390], in1=bf_b, op=ALU.add,
                              r=[ka, 'bfb'], w=['sm'])
                            A('act', 'activation', out=sm[:, 0:6], in_=sm[:, 0:6], func=AF.Exp, scale=-1.0, r=['sm'], w=['sm'])
                            A('act', 'activation', out=sm[:, 0:6], in_=sm[:, 0:6], func=AF.Ln, bias=1.0, scale=1.0, r=['sm'], w=['sm'])
                            A('dve', 'tensor_scalar_mul', out=lf_t[:, j, :], in0=sm[:, 0:6], scalar1=-1.0, r=['sm'], w=['lf_t'])
                            for (sq_, pos0, cnt_, p0) in rows:
                                d0 = (sq_ * g.tq + pos0 - (g.tk - g.tq))
                                P.dma(stq(), o_fk[g.gi][l, d0:d0 + cnt_, :], oa[p0:p0 + cnt_, 0:384], 'so', reads=[ka])
                                P.dma(stq(), o_lf[g.gi][l, d0:d0 + cnt_, :], lf_t[p0:p0 + cnt_, j, :], 'so', reads=['lf_t'])
                        elif ci == 1:
                            A('act', 'activation', out=ob, in_=bk(pb)[:, 0:384], func=AF.Copy, r=[pk(pb)], w=[kb_])
                            A('dve', 'tensor_copy', out=vst[:, j, 0:6, 0:64],
                                                                         in_=bk(pb)[:, 0:384].rearrange("p (h d) -> p h d", d=64),
                              r=[pk(pb)], w=['vst'])
                            for (sq_, pos0, cnt_, p0) in rows:
                                d0 = (sq_ * g.tq + pos0 - (g.tk - g.tq))
                                P.dma(stq(), o_fv[g.gi][l, d0:d0 + cnt_, :], ob[p0:p0 + cnt_, :], 'so', reads=[kb_])
                        elif ci == 2:
                            A('pool', 'memset', sm[:, 8:9], 0.0, w=['sm8'])
                            A('act', 'activation', out=t1[:, 0:256], in_=bk(pb)[:, 0:256], func=AF.Square,
                                                                   accum_out=sm[:, 8:9], r=[pk(pb), 'sm8'], w=['t1', 'sm8'])
                            rstd_from_ss(sm[:, 8:9], 256, 'sm8')
                            A('dve', 'scalar_tensor_tensor', out=oc_[:, 0:256], in0=bk(pb)[:, 0:256], scalar=sm[:, 8:9],
                                                                                       in1=gkv_b, op0=ALU.mult, op1=ALU.mult,
                              r=[pk(pb), 'sm8', 'gkvb'], w=[kc_])
                            A('dve', 'tensor_tensor', out=t1[:, 256:288], in0=bk(pb)[:, 256:288], in1=cs_t[:, j, 0:32], op=ALU.mult,
                              r=[pk(pb), 'cs_t'], w=['t1'])
                            A('dve', 'tensor_tensor', out=t1[:, 288:320], in0=bk(pb)[:, 288:320], in1=cs_t[:, j, 32:64], op=ALU.mult,
                              r=[pk(pb), 'cs_t'], w=['t1'])
                            A('dve', 'tensor_tensor', out=oc_[:, 256:288], in0=t1[:, 256:288], in1=t1[:, 288:320], op=ALU.add,
                              r=['t1'], w=[kc_])
                            for (sq_, pos0, cnt_, p0) in rows:
                                d0 = (sq_ * g.tq + pos0 - (g.tk - g.tq))
                                P.dma(stq(), o_ckv[g.gi][l, d0:d0 + cnt_, :], oc_[p0:p0 + cnt_, 0:256], 'so', reads=[kc_])
                                P.dma(stq(), o_kr[g.gi][l, d0:d0 + cnt_, :], oc_[p0:p0 + cnt_, 256:288], 'so', reads=[kc_])
                        else:
                            A('act', 'activation', out=od, in_=bk(pb)[:, 0:512], func=AF.Copy, r=[pk(pb)], w=[kd])
                            A('dve', 'tensor_copy', out=vst[:, j, 12:16, 0:64],
                                                                         in_=bk(pb)[:, 256:512].rearrange("p (h d) -> p h d", d=64),
                              r=[pk(pb)], w=['vst'])
                            for (sq_, pos0, cnt_, p0) in rows:
                                d0 = (sq_ * g.tq + pos0 - (g.tk - g.tq))
                                P.dma(stq(), o_dk[g.gi][l, d0:d0 + cnt_, :], od[p0:p0 + cnt_, 0:256], 'so', reads=[kd])
                                P.dma(stq(), o_dv[g.gi][l, d0:d0 + cnt_, :], od[p0:p0 + cnt_, 256:512], 'so', reads=[kd])
                    if g.gi == 0:
                        blk = t0 // 128 + j
                        prev = zero6 if blk == 0 else ctok_p[:, blk - 1, :]
                        A('pe', 'matmul', bk(5)[:, 0:6], lhsT=U_f, rhs=lf_t[:, j, :], start=True, stop=False,
                          r=['U', 'lf_t'], w=[pk(5)])
                        A('pe', 'matmul', bk(5)[:, 0:6], lhsT=E_f, rhs=prev, start=False, stop=True,
                          r=['E', 'ctok', 'zero6'], w=[pk(5)])
                        A('dve', 'tensor_copy', out=ctok_p[:, blk, :], in_=bk(5)[:, 0:6], r=[pk(5)], w=['ctok'])
                    else:
                        for (sq_, pos0, cnt_, p0) in rows:
                            A('pe', 'matmul', bk(5)[0:64, 0:6], lhsT=U_f[p0:p0 + 64, p0:p0 + 64],
                                                                  rhs=lf_t[p0:p0 + 64, j, :], start=True, stop=False,
                              r=['U', 'lf_t'], w=[pk(5)])
                            A('pe', 'matmul', bk(5)[0:64, 0:6], lhsT=E_f[:, 0:64], rhs=ctok_s[:, sq_, NKB_S - 2, :],
                                                               start=False, stop=True, r=['E', 'ctoks'], w=[pk(5)])
                            A('dve', 'tensor_copy', out=ctok_s[0:64, sq_, NKB_S - 1, :], in_=bk(5)[0:64, 0:6],
                              r=[pk(5)], w=['ctoks'])
                trows = token_rows(g, t0, W)

                def fm(c0, m, dst_fn, post=None):
                    pb = 3 + rr[0] % 2
                    rr[0] += 1
                    for kc in range(8):
                        A('pe', 'matmul', bk(pb)[0:m, 0:W], lhsT=wi[:, kc, c0:c0 + m], rhs=xnT[:, kc, 0:W],
                                                                  start=(kc == 0), stop=(kc == 7), r=['xnT', 'wi'], w=[pk(pb)])
                    return pb

                def store_fm(stage, skey, m, dst_fn, is_key):
                    for (sq_, pos0, cnt_, p0) in trows:
                        col = pos0 if is_key else pos0 - (g.tk - g.tq)
                        P.dma(stq(), dst_fn(sq_)[:, col:col + cnt_], stage[0:m, p0:p0 + cnt_], 'sf', reads=[skey])

                def simple_fm(c0, nchunk, dst, is_key):
                    for c in range(nchunk):
                        pb = fm(c0 + c * 128, 128, None)
                        si = next_fst()
                        st, sk = fst[si], 'fst%d' % si
                        eng = 'act' if c % 2 == 0 else 'dve'
                        if eng == 'act':
                            A('act', 'activation', out=st[:, 0:W], in_=bk(pb)[:, 0:W], func=AF.Copy, r=[pk(pb)], w=[sk])
                        else:
                            A('dve', 'tensor_copy', out=st[:, 0:W], in_=bk(pb)[:, 0:W], r=[pk(pb)], w=[sk])
                        store_fm(st, sk, 128, lambda s_, c=c: dst[s_, c * 128:(c + 1) * 128, :], is_key)

                simple_fm(O_FQ, 3, g.qT_fox, False)
                simple_fm(O_FK, 3, g.kT_fox, True)
                simple_fm(O_DQ, 2, g.qT_diff, False)
                simple_fm(O_DK, 2, g.kT_diff, True)

                def norm_fm(c0, nchunk, n, gcol_, gkey, outT, okey):
                    for c in range(nchunk):
                        pb = fm(c0 + c * 128, 128, None)
                        A('act', 'activation', out=raw[:, c, 0:W], in_=bk(pb)[:, 0:W], func=AF.Copy, r=[pk(pb)], w=['raw'])
                        s_ = sq[c % 2]
                        A('dve', 'tensor_tensor', out=s_[:, 0:W], in0=raw[:, c, 0:W], in1=raw[:, c, 0:W], op=ALU.mult,
                          r=['raw'], w=['sq%d' % (c % 2)])
                        A('pe', 'matmul', bk(5)[:, 0:W], lhsT=ones_f, rhs=s_[:, 0:W], start=(c == 0), stop=(c == nchunk - 1),
                          r=['ones', 'sq%d' % (c % 2)], w=[pk(5)])
                    A('act', 'activation', out=rsb[:, 0:W], in_=bk(5)[:, 0:W], func=AF.Ln, bias=eps_t[:, 0:1], scale=1.0 / n,
                      r=[pk(5), 'eps'], w=['rsb'])
                    A('act', 'activation', out=rsb[:, 0:W], in_=rsb[:, 0:W], func=AF.Exp, scale=-0.5, r=['rsb'], w=['rsb'])
                    for c in range(nchunk):
                        A('dve', 'scalar_tensor_tensor', out=outT[:, c, 0:W], in0=raw[:, c, 0:W], scalar=gcol_[:, c:c + 1],
                                                                        in1=rsb[:, 0:W], op0=ALU.mult, op1=ALU.mult,
                          r=['raw', gkey, 'rsb'], w=[okey])

                norm_fm(O_CQ, 3, 384, gq_c, 'gqc', cqnT, 'cqnT')
                norm_fm(O_CKV, 2, 256, gkv_c, 'gkvc', ckvnT, 'ckvnT')

                def rope_rows(pb, st, skey, b0=0):
                    A('dve', 'tensor_tensor', out=t1[b0:b0 + 32, 0:W], in0=bk(pb)[b0:b0 + 32, 0:W], in1=cst_t[0:32, 0:W], op=ALU.mult,
                      r=[pk(pb), 'cst_t'], w=['t1'])
                    A('dve', 'tensor_tensor', out=t2[b0:b0 + 32, 0:W], in0=bk(pb)[b0 + 32:b0 + 64, 0:W], in1=cst_t[32:64, 0:W], op=ALU.mult,
                      r=[pk(pb), 'cst_t'], w=['t2'])
                    A('dve', 'tensor_tensor', out=st[b0:b0 + 32, 0:W], in0=t1[b0:b0 + 32, 0:W], in1=t2[b0:b0 + 32, 0:W], op=ALU.add,
                      r=['t1', 't2'], w=[skey])

                pb = fm(O_KR, 64, None)
                si = next_fst()
                st, sk = fst[si], 'fst%d' % si
                rope_rows(pb, st, sk)
                store_fm(st, sk, 32, lambda s_: g.kT_rope[s_, :, :], True)
                for hh in range(6):
                    pb = 3 + rr[0] % 2
                    rr[0] += 1
                    for kc in range(3):
                        A('pe', 'matmul', bk(pb)[:, 0:W], lhsT=wq[:, kc, hh * 128:(hh + 1) * 128], rhs=cqnT[:, kc, 0:W],
                                                                         start=(kc == 0), stop=(kc == 2), r=['wq', 'cqnT'], w=[pk(pb)])
                    si = next_fst()
                    st, sk = fst[si], 'fst%d' % si
                    rope_rows(pb, st, sk, 64)
                    A('dve', 'tensor_copy', out=st[0:64, 0:W], in_=bk(pb)[0:64, 0:W], r=[pk(pb)], w=[sk])
                    store_fm(st, sk, 96, lambda s_, hh=hh: g.qT_mla[s_, hh * 96:(hh + 1) * 96, :], False)
                for c in range(3):
                    pb = 3 + rr[0] % 2
                    rr[0] += 1
                    for kc in range(2):
                        A('pe', 'matmul', bk(pb)[:, 0:W], lhsT=wk[:, kc, c * 128:(c + 1) * 128], rhs=ckvnT[:, kc, 0:W],
                                                                       start=(kc == 0), stop=(kc == 1), r=['wk', 'ckvnT'], w=[pk(pb)])
                    si = next_fst()
                    st, sk = fst[si], 'fst%d' % si
                    A('act', 'activation', out=st[:, 0:W], in_=bk(pb)[:, 0:W], func=AF.Copy, r=[pk(pb)], w=[sk])
                    store_fm(st, sk, 128, lambda s_, c=c: g.kT_nope[s_, c * 128:(c + 1) * 128, :], True)
                for j in range(nsub):
                    pb = rr[0] % 3
                    rr[0] += 1
                    for kc in range(2):
                        A('pe', 'matmul', bk(pb)[:, 0:384], lhsT=ckvnT[:, kc, j * 128:(j + 1) * 128], rhs=wv[:, kc, :],
                                                                       start=(kc == 0), stop=(kc == 1), r=['wv', 'ckvnT'], w=[pk(pb)])
                    A('dve', 'tensor_copy', out=vst[:, j, 6:12, 0:64], in_=bk(pb)[:, 0:384].rearrange("p (h d) -> p h d", d=64),
                      r=[pk(pb)], w=['vst'])
                for j in range(nsub):
                    for (sq_, pos0, cnt_, p0) in token_rows(g, t0 + j * 128, 128):
                        P.dma(stq(), g.v_fox[sq_, pos0:pos0 + cnt_, :].rearrange("t (h c) -> t h c", c=65), vst[p0:p0 + cnt_, j, 0:6, :], 'sv', reads=['vst'])
                        P.dma(stq(), g.v_mla[sq_, pos0:pos0 + cnt_, :].rearrange("t (h c) -> t h c", c=65), vst[p0:p0 + cnt_, j, 6:12, :], 'sv', reads=['vst'])
                        P.dma(stq(), g.v_diff[sq_, pos0:pos0 + cnt_, :].rearrange("t (h c) -> t h c", c=65), vst[p0:p0 + cnt_, j, 12:16, :], 'sv', reads=['vst'])

    def phase_cache(l):
        ar.reset()
        g = grp[1]
        wk = ar.alloc([2, 384], BF16); wv = ar.alloc([2, 384], BF16)
        P.dma('sp', wk, wb_uk[l].rearrange("(k p) n -> p k n", p=128), 'c0', writes=['wk'])
        P.dma('sp', wv, wb_uv[l].rearrange("(k p) n -> p k n", p=128), 'c0', writes=['wv'])
        ld = [ar.alloc([8, 384], F32) for _ in range(2)]
        lb = [ar.alloc([8, 384], BF16) for _ in range(2)]
        kTs = [ar.alloc([1024], BF16) for _ in range(2)]
        vst = ar.alloc([8, 6, 65], BF16)
        ckT = ar.alloc([2, 1024], BF16)
        lf_c = ar.alloc([8, 6], F32)
        A('pool', 'memset', vst, 1.0, w=['vst'])
        rr = [0]
        for b in range(NBS):
            def load(src, ncol):
                i = rr[0] % 2
                rr[0] += 1
                P.dma('sp', ld[i][:, :, 0:ncol], src[l, b].rearrange("(j p) d -> p j d", p=128), 'cl%d' % i, writes=['ld%d' % i])
                A('dve', 'tensor_copy', out=lb[i][:, :, 0:ncol], in_=ld[i][:, :, 0:ncol], r=['ld%d' % i], w=['lb%d' % i])
                return i

            def transp(i, c0, m, dstT, dkey, dst_dram):
                pb = 6 + rr[0] % 2
                rr[0] += 1
                for j in range(8):
                    A('pe', 'transpose', bkb(pb)[0:m, j * 128:(j + 1) * 128], lb[i][:, j, c0:c0 + m], ident,
                      r=['lb%d' % i, 'ident'], w=[pk(pb)])
                A('act', 'activation', out=dstT[0:m, :], in_=bkb(pb)[0:m, :], func=AF.Copy, r=[pk(pb)], w=[dkey])
                if dst_dram is not None:
                    P.dma('pool', dst_dram, dstT[0:m, :], 'sc', reads=[dkey])

            def vstore(i, nh, dst):
                A('dve', 'tensor_copy', out=vst[:, :, 0:nh, 0:64], in_=lb[i][:, :, 0:nh * 64].rearrange("p j (h d) -> p j h d", d=64),
                  r=['lb%d' % i], w=['vst'])
                P.dma('pool', dst[b, 0:PAST, :].rearrange("(j p) (h c) -> p j h c", p=128, c=65), vst[:, :, 0:nh, :], 'sc', reads=['vst'])

            i = load(c_fk, 384)
            for c in range(3):
                kt = kTs[rr[0] % 2]; kk = 'kTs%d' % (rr[0] % 2)
                transp(i, c * 128, 128, kt, kk, g.kT_fox[b, c * 128:(c + 1) * 128, 0:PAST])
            i = load(c_dk, 256)
            for c in range(2):
                kt = kTs[rr[0] % 2]; kk = 'kTs%d' % (rr[0] % 2)
                transp(i, c * 128, 128, kt, kk, g.kT_diff[b, c * 128:(c + 1) * 128, 0:PAST])
            i = load(c_kr, 32)
            kt = kTs[rr[0] % 2]; kk = 'kTs%d' % (rr[0] % 2)
            transp(i, 0, 32, kt, kk, g.kT_rope[b, :, 0:PAST])
            i = load(c_fv, 384)
            vstore(i, 6, g.v_fox)
            i = load(c_dv, 256)
            vstore(i, 4, g.v_diff)
            i = load(c_ckv, 256)
            for c in range(2):
                transp(i, c * 128, 128, ckT[:, c, :], 'ckT', None)
            for c in range(3):
                for hf in range(2):
                    pb = 3 + rr[0] % 2
                    rr[0] += 1
                    for kc in range(2):
                        A('pe', 'matmul', bk(pb)[:, 0:512], lhsT=wk[:, kc, c * 128:(c + 1) * 128],
                                                                               rhs=ckT[:, kc, hf * 512:(hf + 1) * 512], start=(kc == 0), stop=(kc == 1),
                          r=['wk', 'ckT'], w=[pk(pb)])
                    kt = kTs[rr[0] % 2]; kk = 'kTs%d' % (rr[0] % 2)
                    A('act', 'activation', out=kt[:, 0:512], in_=bk(pb)[:, 0:512], func=AF.Copy, r=[pk(pb)], w=[kk])
                    P.dma('pool', g.kT_nope[b, c * 128:(c + 1) * 128, hf * 512:(hf + 1) * 512], kt[:, 0:512], 'sc', reads=[kk])
            for j in range(8):
                pb = rr[0] % 3
                rr[0] += 1
                for kc in range(2):
                    A('pe', 'matmul', bk(pb)[:, 0:384], lhsT=ckT[:, kc, j * 128:(j + 1) * 128], rhs=wv[:, kc, :],
                                                                   start=(kc == 0), stop=(kc == 1), r=['wv', 'ckT'], w=[pk(pb)])
                A('dve', 'tensor_copy', out=vst[:, j, 0:6, 0:64], in_=bk(pb)[:, 0:384].rearrange("p (h d) -> p h d", d=64),
                  r=[pk(pb)], w=['vst'])
            P.dma('pool', g.v_mla[b, 0:PAST, :].rearrange("(j p) (h c) -> p j h c", p=128, c=65), vst[:, :, 0:6, :], 'sc', reads=['vst'])
            P.dma('sp', lf_c, c_lf[l, b].rearrange("(j p) h -> p j h", p=128), 'cl2', writes=['lf_c'])
            for j in range(8):
                prev = zero6 if j == 0 else ctok_s[:, b, j - 1, :]
                A('pe', 'matmul', bk(5)[:, 0:6], lhsT=U_f, rhs=lf_c[:, j, :], start=True, stop=False, r=['U', 'lf_c'], w=[pk(5)])
                A('pe', 'matmul', bk(5)[:, 0:6], lhsT=E_f, rhs=prev, start=False, stop=True, r=['E', 'ctoks', 'zero6'], w=[pk(5)])
                A('dve', 'tensor_copy', out=ctok_s[:, b, j, :], in_=bk(5)[:, 0:6], r=[pk(5)], w=['ctoks'])

    def phase_p2(l):
        ar.reset()
        lam_init = 0.8 - 0.6 * math.exp(-0.3 * l)
        lq = ar.alloc([4, 32], F32, parts=64)
        lt = ar.alloc([8], F32, parts=64)
        negl = ar.alloc([1], F32, parts=64)
        gsub = ar.alloc([1], F32, parts=64)
        P.dma('sp', lq, lam4[l:l + 1].to_broadcast([64, 4, 32]), 'c0', writes=['lq'])
        P.dma('sp', gsub, g_sub[l].rearrange("(d o) -> d o", o=1), 'c0', writes=['gsub'], allow_slow_non_contiguous=True)
        A('dve', 'tensor_tensor', out=lq[:, 0, :], in0=lq[:, 0, :], in1=lq[:, 1, :], op=ALU.mult, r=['lq'], w=['lq'])
        A('dve', 'tensor_tensor', out=lq[:, 2, :], in0=lq[:, 2, :], in1=lq[:, 3, :], op=ALU.mult, r=['lq'], w=['lq'])
        A('dve', 'reduce_sum', out=lt[:, 0:1], in_=lq[:, 0, :], axis=mybir.AxisListType.X, r=['lq'], w=['lt'])
        A('dve', 'reduce_sum', out=lt[:, 1:2], in_=lq[:, 2, :], axis=mybir.AxisListType.X, r=['lq'], w=['lt'])
        A('act', 'activation', out=lt[:, 0:2], in_=lt[:, 0:2], func=AF.Exp, r=['lt'], w=['lt'])
        A('dve', 'tensor_tensor', out=lt[:, 2:3], in0=lt[:, 1:2], in1=lt[:, 0:1], op=ALU.subtract, r=['lt'], w=['lt'])
        A('dve', 'tensor_scalar_add', out=negl, in0=lt[:, 2:3], scalar1=-lam_init, r=['lt'], w=['negl'])
        A('dve', 'tensor_scalar_mul', out=gsub, in0=gsub, scalar1=(1.0 - lam_init), r=['gsub'], w=['gsub'])

        NKBmax = max(g.tk for g in grp)
        NKBmax = (NKBmax + 127) // 128
        TKmax = max(g.tk for g in grp); TQmax = max(g.tq for g in grp)
        slots = []
        for s in range(2):
            slots.append((ar.alloc([TKmax], BF16), ar.alloc([TQmax], BF16), ar.alloc([NKBmax, 128], BF16)))
            A('pool', 'memset', slots[s][2][:, :, 64:128], 1.0, w=['u%d' % s])
            A('pool', 'memset', slots[s][0], 0.0, w=['u%d' % s])
            A('pool', 'memset', slots[s][1], 0.0, w=['u%d' % s])
        slot_hw = [0, 0]
        pT = [ar.alloc([512], BF16) for _ in range(4)]
        rl = ar.alloc([512], F32)
        on2 = ar.alloc([512], F32, parts=64)
        oc = ar.alloc([512], F32, parts=64)
        sqd = ar.alloc([512], F32, parts=64)
        rs2 = ar.alloc([512], F32, parts=64)
        mst = [ar.alloc([512], BF16, parts=64) for _ in range(2)]
        fbias = ar.alloc([NKBmax], F32)
        crefp = ar.alloc([NT, 6], F32)
        crefs = ar.alloc([max(NBS, 1), 6], F32)
        A('pe', 'matmul', bk(5)[:, 0:NT * 6], lhsT=E_f, rhs=ctok_p.rearrange("p a b -> p (a b)"), start=True, stop=True,
          r=['E', 'ctok'], w=[pk(5)])
        A('dve', 'tensor_copy', out=crefp.rearrange("p a b -> p (a b)"), in_=bk(5)[:, 0:NT * 6], r=[pk(5)], w=['crefp'])
        for b_ in range(NBS):
            A('pe', 'matmul', bk(5)[:, 0:6], lhsT=E_f, rhs=ctok_s[:, b_, NKB_S - 2, :], start=True, stop=True, r=['E', 'ctoks'], w=[pk(5)])
            A('dve', 'tensor_copy', out=crefs[:, b_, :], in_=bk(5)[:, 0:6], r=[pk(5)], w=['crefs'])

        units = []
        for g in grp:
            for sq_ in range(g.nseq):
                for h in range(6):
                    units.append(dict(g=g, s=sq_, kind='fox', h=h, rows=64, scale=0.125, mask=0, row0=64 * h,
                                      kT=[(g.kT_fox[sq_, 64 * h:64 * h + 64, :], 0, 64)], qT=g.qT_fox[sq_, 64 * h:64 * h + 64, :],
                                      v=g.v_fox[sq_, :, 65 * h:65 * h + 65]))
                for h in range(6):
                    units.append(dict(g=g, s=sq_, kind='mla', h=h, rows=96, scale=96 ** -0.5, mask=1, row0=384 + 64 * h,
                                      kT=[(g.kT_nope[sq_, 64 * h:64 * h + 64, :], 0, 64), (g.kT_rope[sq_, :, :], 64, 32)],
                                      qT=g.qT_mla[sq_, 96 * h:96 * h + 96, :], v=g.v_mla[sq_, :, 65 * h:65 * h + 65]))
                for h in range(4):
                    for s2 in range(2):
                        r0 = 64 * h + 32 * s2
                        units.append(dict(g=g, s=sq_, kind='diff', h=h, s2=s2, rows=32, scale=32 ** -0.5, mask=2 + h, row0=768 + 64 * h,
                                          kT=[(g.kT_diff[sq_, r0:r0 + 32, :], 0, 32)], qT=g.qT_diff[sq_, r0:r0 + 32, :],
                                          v=g.v_diff[sq_, :, 65 * h:65 * h + 65]))

        def load_unit(ui):
            u = units[ui]
            g = u['g']
            kt, qt, vt = slots[ui % 2]
            sk = 'u%d' % (ui % 2)
            if u['rows'] < slot_hw[ui % 2]:
                a0 = u['rows']
                while a0 < slot_hw[ui % 2]:
                    a1 = min(128, a0 + (32 if a0 % 64 else 64))
                    A('pool', 'memset', qt[a0:a1, :], 0.0, w=[sk])
                    a0 = a1
            slot_hw[ui % 2] = u['rows']
            for (src, p0, n) in u['kT']:
                P.dma('sp', kt[p0:p0 + n, 0:g.tk], src, 'ul%d' % (ui % 2), writes=[sk])
            P.dma('sp', qt[0:u['rows'], 0:g.tq], u['qT'], 'ul%d' % (ui % 2), writes=[sk])
            nkb = g.tk // 128
            for k0 in range(0, nkb, 16):
                k1 = min(nkb, k0 + 16)
                P.dma('sp', vt[:, k0:k1, 0:64], u['v'][k0 * 128:k1 * 128, 0:64].rearrange("(j p) c -> p j c", p=128), 'ul%d' % (ui % 2), writes=[sk])
            rem = g.tk - nkb * 128
            if rem:
                P.dma('sp', vt[0:rem, nkb, 0:64], u['v'][nkb * 128:g.tk, 0:64], 'ul%d' % (ui % 2), writes=[sk])

        steps = []
        grpno = 0
        for ui, u in enumerate(units):
            g = u['g']
            if g.gi == 0:
                for i in range(NQT):
                    nkb = 4 * i + 4
                    for kb in range(nkb):
                        j = kb - 4 * i
                        c0 = 128 * j if j > 0 else 0
                        steps.append(dict(ui=ui, i=i, kb=kb, nk=128, c0=c0, W=512, first=(kb == 0), last=(kb == nkb - 1),
                                          diag=(j if j >= 0 else None), grp=grpno, q0=512 * i, al=(kb - 4 * i - 2) + (4 * NQT - 2)))
                    grpno += 1
            else:
                nkb = NKB_S
                for kb in range(nkb):
                    nk = min(128, g.tk - 128 * kb)
                    steps.append(dict(ui=ui, i=0, kb=kb, nk=nk, c0=0, W=TS, first=(kb == 0), last=(kb == nkb - 1),
                                      diag=(0 if kb == nkb - 1 else None), grp=grpno, q0=0, al=4 * NQT + kb))
                    grpno += 1 if kb == nkb - 1 else 0

        last_step_of = {}
        for n_, st_ in enumerate(steps):
            last_step_of[st_['ui']] = n_
        for ui_ in range(min(2, len(units))):
            load_unit(ui_)

        fb_state = [None]

        def emit_S(n):
            st = steps[n]
            u = units[st['ui']]
            kt, qt, vt = slots[st['ui'] % 2]
            sk = 'u%d' % (st['ui'] % 2)
            pb = (0, 1, 2, 5)[n % 4]
            rows, nk, c0, W, kb = u['rows'], st['nk'], st['c0'], st['W'], st['kb']
            qc = st['q0'] if u['g'].gi == 0 else 0
            A('pe', 'matmul', bk(pb)[0:nk, c0:W], lhsT=kt[:, kb * 128:kb * 128 + nk], rhs=qt[:, qc + c0:qc + W],
                                       start=True, stop=True, r=[sk], w=[pk(pb)])

        def fox_bias(st, u):
            key = (st['ui'], st['i'])
            if fb_state[0] == key:
                return
            fb_state[0] = key
            g, h = u['g'], u['h']
            if g.gi == 0:
                blk = 4 * st['i'] + 1
                A('dve', 'tensor_scalar', out=fbias[:, 0:NT], in0=ctok_p[:, :, h], scalar1=-1.0, scalar2=crefp[:, blk, h:h + 1],
                                                   op0=ALU.mult, op1=ALU.add, r=['ctok', 'crefp'], w=['fbias'])
            else:
                sq_ = u['s']
                A('dve', 'tensor_scalar', out=fbias[:, 0:NKB_S], in0=ctok_s[:, sq_, :, h], scalar1=-1.0, scalar2=crefs[:, sq_, h:h + 1],
                                                   op0=ALU.mult, op1=ALU.add, r=['ctoks', 'crefs'], w=['fbias'])

        def emit_rest(n):
            st = steps[n]
            u = units[st['ui']]
            g = u['g']
            kt, qt, vt = slots[st['ui'] % 2]
            sk = 'u%d' % (st['ui'] % 2)
            pb = (0, 1, 2, 5)[n % 4]
            nk, c0, W, kb = st['nk'], st['c0'], st['W'], st['kb']
            pt = pT[n % 4]; ptk = 'pT%d' % (n % 4)
            ob = (3, 4, 7)[st['grp'] % 3]
            if u['kind'] == 'fox':
                fox_bias(st, u)
                bias = fbias[0:nk, kb:kb + 1]
                A('act', 'activation', out=pt[0:nk, c0:W], in_=bk(pb)[0:nk, c0:W], func=AF.Exp, bias=bias, scale=u['scale'],
                  r=[pk(pb), 'fbias'], w=[ptk])
            elif u['kind'] == 'diff':
                ai = u['h'] * NAL + st['al']
                bias = al_t[0:nk, ai:ai + 1]
                A('act', 'activation', out=pt[0:nk, c0:W], in_=bk(pb)[0:nk, c0:W], func=AF.Exp, bias=bias, scale=u['scale'],
                  r=[pk(pb), 'al'], w=[ptk])
            else:
                A('act', 'activation', out=pt[0:nk, c0:W], in_=bk(pb)[0:nk, c0:W], func=AF.Exp, scale=u['scale'],
                  r=[pk(pb)], w=[ptk])
            if st['diag'] is not None:
                mw = min(128, W - c0)
                m = masks[0:nk, u['mask'], 0:mw]
                A('dve', 'tensor_tensor', out=pt[0:nk, c0:c0 + mw], in0=pt[0:nk, c0:c0 + mw], in1=m, op=ALU.mult,
                  r=[ptk, 'masks'], w=[ptk])
            A('pe', 'matmul', bk(ob)[:, c0:W], lhsT=vt[0:nk, kb, :], rhs=pt[0:nk, c0:W], start=st['first'], stop=st['last'],
              r=[sk, ptk], w=[pk(ob)])
            if last_step_of[st['ui']] == n and st['ui'] + 2 < len(units):
                load_unit(st['ui'] + 2)
            if not st['last']:
                return
            bcs = rl[64:128, :]
            A('dve', 'reciprocal', out=rl[64:128, 0:W], in_=bk(ob)[64:128, 0:W], r=[pk(ob)], w=['bcs'])
            tok0 = (u['s'] * g.tq if g.gi == 1 else 0) + st['q0']
            if u['kind'] != 'diff':
                ms = mst[st['grp'] % 2]; msk = 'mst%d' % (st['grp'] % 2)
                A('dve', 'tensor_tensor', out=ms[:, 0:W], in0=bk(ob)[0:64, 0:W], in1=bcs[:, 0:W], op=ALU.mult,
                  r=[pk(ob), 'bcs'], w=[msk])
                P.dma('pool', g.mixT[u['row0']:u['row0'] + 64, tok0:tok0 + W], ms[:, 0:W], 'sm', reads=[msk])
            elif u['s2'] == 0:
                A('dve', 'tensor_tensor', out=on1s[st['i']][:, 0:W], in0=bk(ob)[0:64, 0:W], in1=bcs[:, 0:W], op=ALU.mult,
                  r=[pk(ob), 'bcs'], w=['on1%d' % st['i']])
            else:
                A('dve', 'tensor_tensor', out=on2[:, 0:W], in0=bk(ob)[0:64, 0:W], in1=bcs[:, 0:W], op=ALU.mult,
                  r=[pk(ob), 'bcs'], w=['on2'])
                A('dve', 'scalar_tensor_tensor', out=oc[:, 0:W], in0=on2[:, 0:W], scalar=negl[:, 0:1], in1=on1s[st['i']][:, 0:W],
                                                          op0=ALU.mult, op1=ALU.add, r=['on2', 'negl', 'on1%d' % st['i']], w=['oc'])
                A('dve', 'tensor_tensor', out=sqd[:, 0:W], in0=oc[:, 0:W], in1=oc[:, 0:W], op=ALU.mult, r=['oc'], w=['sqd'])
                A('pe', 'matmul', bk(6)[0:64, 0:W], lhsT=ones_f[0:64, 0:64], rhs=sqd[:, 0:W], start=True, stop=True,
                  r=['ones', 'sqd'], w=[pk(6)])
                A('act', 'activation', out=rs2[:, 0:W], in_=bk(6)[0:64, 0:W], func=AF.Ln, bias=eps_t[0:64, 0:1], scale=1.0 / 64,
                  r=[pk(6), 'eps'], w=['rs2'])
                A('act', 'activation', out=rs2[:, 0:W], in_=rs2[:, 0:W], func=AF.Exp, scale=-0.5, r=['rs2'], w=['rs2'])
                ms = mst[st['grp'] % 2]; msk = 'mst%d' % (st['grp'] % 2)
                A('dve', 'scalar_tensor_tensor', out=ms[:, 0:W], in0=oc[:, 0:W], scalar=gsub[:, 0:1], in1=rs2[:, 0:W],
                                                          op0=ALU.mult, op1=ALU.mult, r=['oc', 'gsub', 'rs2'], w=[msk])
                P.dma('pool', g.mixT[u['row0']:u['row0'] + 64, tok0:tok0 + W], ms[:, 0:W], 'sm', reads=[msk])

        on1s = [ar.alloc([512], F32, parts=64) for _ in range(max(NQT, 1))]

        LA = 3
        for n in range(len(steps) + LA):
            if n < len(steps):
                emit_S(n)
            if n - LA >= 0:
                emit_rest(n - LA)

    def phase_p3(l):
        ar.reset()
        wdn = ar.alloc([32, D], BF16)
        wms = ar.alloc([8, D], BF16)
        wpj = ar.alloc([2, D], BF16)
        wus = [ar.alloc([8, 512], BF16) for _ in range(2)]
        P.dma('sp', wdn, wb_down[l].rearrange("(c p) n -> p c n", p=128), 'c0', writes=['wdn'])
        P.dma('sp', wpj, wb_proj[l].rearrange("(c p) n -> p c n", p=128), 'c0', writes=['wpj'])
        gb = [ar.alloc([D], F32) for _ in range(3)]
        for i, gi_ in enumerate((1, 3, 5)):
            P.dma('sp', gb[i], g_norm[l, gi_:gi_ + 1, :].to_broadcast([128, D]), 'c0', writes=['gb%d' % i])
        cwt = ar.alloc([3, 32], F32); cbt = ar.alloc([32], F32)
        for jj in range(3):
            P.dma('sp', cwt[:, jj, :], cw[l, jj].rearrange("(c p) -> p c", p=128), 'c0', writes=['cwt'], allow_slow_non_contiguous=True)
        P.dma('sp', cbt, cb[l].rearrange("(c p) -> p c", p=128), 'c0', writes=['cbt'], allow_slow_non_contiguous=True)
        aT = ar.alloc([32, 512], BF16)
        h_t = ar.alloc([4, D], F32)
        XT = ar.alloc([8, 512], BF16)
        ytmp = ar.alloc([D], F32)
        tpl = ar.alloc([D], F32)
        nseg_max = max(1, NBS)
        gext = [ar.alloc([516 + 2 * nseg_max], F32) for _ in range(2)]
        tcv = ar.alloc([512], F32)
        ugl = ar.alloc([512], F32)
        xnb = [ar.alloc([D], BF16)] * 2
        p_t = ar.alloc([4, PLE], F32)
        p_b = ar.alloc([PLE], BF16)
        pTt = ar.alloc([2, 128], BF16)
        ss = ar.alloc([16], F32)
        carry = ar.alloc([32, nseg_max, 2], F32)
        cso = ar.alloc([32, nseg_max, 2], F32)
        rr = [0]

        def resid_norm(j, src_ap, src_keys, gbi, sscol):
            A('pool', 'memset', ss[:, sscol:sscol + 1], 0.0, w=['ss%d' % sscol])
            A('act', 'activation', out=tpl_junk, in_=src_ap, func=AF.Square, accum_out=ss[:, sscol:sscol + 1],
              r=src_keys + ['ss%d' % sscol], w=['junk', 'ss%d' % sscol])
            rstd_from_ss(ss[:, sscol:sscol + 1], D, 'ss%d' % sscol)
            A('dve', 'scalar_tensor_tensor', out=src_ap, in0=src_ap, scalar=ss[:, sscol:sscol + 1], in1=gb[gbi],
                                                      op0=ALU.mult, op1=ALU.mult, r=src_keys + ['ss%d' % sscol, 'gb%d' % gbi], w=src_keys)
            A('dve', 'tensor_tensor', out=h_t[:, j, :], in0=h_t[:, j, :], in1=src_ap, op=ALU.add, r=src_keys + ['h%d' % j], w=['h%d' % j])

        def prenorm_T(j, sscol):
            x_b = xnb[0]; xk = 'xnb'
            A('pool', 'memset', ss[:, sscol:sscol + 1], 0.0, w=['ss%d' % sscol])
            A('act', 'activation', out=tpl_junk, in_=h_t[:, j, :], func=AF.Square, accum_out=ss[:, sscol:sscol + 1],
              r=['h%d' % j, 'ss%d' % sscol], w=['junk', 'ss%d' % sscol])
            rstd_from_ss(ss[:, sscol:sscol + 1], D, 'ss%d' % sscol)
            A('dve', 'tensor_scalar_mul', out=x_b, in0=h_t[:, j, :], scalar1=ss[:, sscol:sscol + 1], r=['h%d' % j, 'ss%d' % sscol], w=[xk])
            pb = 2 + (j % 2)
            for kc in range(8):
                A('pe', 'transpose', bkb(pb)[:, kc * 128:(kc + 1) * 128], x_b[:, kc * 128:(kc + 1) * 128], ident,
                  r=[xk, 'ident'], w=[pk(pb)])
            A('act', 'activation', out=XT[:, :, j * 128:(j + 1) * 128], in_=bkb(pb).rearrange("p (a b) -> p a b", b=128), func=AF.Copy,
              r=[pk(pb)], w=['XT%d' % j])

        tpl_junk = ar.alloc([D], BF16)

        for g in grp:
            ntile = (g.ntok + 511) // 512
            nseg = 1 if g.gi == 0 else None
            if g.gi == 0:
                A('pool', 'memset', carry, 0.0, w=['carry'])
            else:
                for b in range(NBS):
                    for jj in range(2):
                        P.dma('sp', carry[:, :, b, jj], c_cs[l, b, jj].rearrange("(c p) -> p c", p=128), 'c0', writes=['carry'],
                              allow_slow_non_contiguous=True)
            for ti in range(ntile):
                t0 = ti * 512
                W = min(512, g.ntok - t0)
                nsub = W // 128
                nsg = 1 if g.gi == 0 else W // TS
                segW = W // nsg
                src = g.x if l == 0 else g.hbuf
                dst = g.y if l == L - 1 else g.hbuf
                for j in range(nsub):
                    P.dma('sp', XT[:, :, j * 128:(j + 1) * 128], g.mixT[:, t0 + j * 128:t0 + (j + 1) * 128].rearrange("(k p) t -> p k t", p=128),
                          'p3m%d' % j, writes=['XT%d' % j])
                    P.dma('sp', h_t[:, j, :], src[t0 + j * 128:t0 + (j + 1) * 128, :], 'p3h%d' % j, writes=['h%d' % j])
                P.dma('sp', p_t[:, 0:nsub, :], g.pin[l, t0:t0 + W, :].rearrange("(j p) d -> p j d", p=128), 'p3p', writes=['p_t'])
                P.dma('sp', wms, wb_out[l].rearrange("(k p) n -> p k n", p=128), 'p3w', writes=['wms'])
                for j in range(nsub):
                    for hf in range(2):
                        pb = hf
                        for kc in range(8):
                            A('pe', 'matmul', bk(hf)[:, 0:512], lhsT=XT[:, kc, j * 128:(j + 1) * 128],
                                                                           rhs=wms[:, kc, hf * 512:(hf + 1) * 512], start=(kc == 0), stop=(kc == 7),
                              r=['XT%d' % j, 'wms'], w=[pk(hf)])
                        A('act', 'activation', out=ytmp[:, hf * 512:(hf + 1) * 512], in_=bk(hf)[:, 0:512], func=AF.Copy,
                          r=[pk(hf)], w=['ytmp'])
                    resid_norm(j, ytmp, ['ytmp'], 0, 0)
                for j in range(nsub):
                    prenorm_T(j, 1)
                xkeys = ['XT%d' % j for j in range(nsub)]
                for c in range(32):
                    if c % 2 == 0:
                        gi2 = (c // 2) % 2
                        P.dma('sp', wus[gi2], wb_up[l][:, (c // 2) * 512:(c // 2 + 1) * 512].rearrange("(k p) n -> p k n", p=128),
                              'p3u%d' % gi2, writes=['wus%d' % gi2])
                    wu = wus[(c // 2) % 2]; wuk = 'wus%d' % ((c // 2) % 2)
                    off = (c % 2) * 256
                    pg, pv = 4 + 2 * (c % 2), 5 + 2 * (c % 2)
                    for kc in range(8):
                        A('pe', 'matmul', bk(pg)[:, 0:W], lhsT=wu[:, kc, off:off + 128], rhs=XT[:, kc, 0:W],
                                                                                 start=(kc == 0), stop=(kc == 7), r=[wuk] + xkeys, w=[pk(pg)])
                    for kc in range(8):
                        A('pe', 'matmul', bk(pv)[:, 0:W], lhsT=wu[:, kc, off + 128:off + 256], rhs=XT[:, kc, 0:W],
                                                                                 start=(kc == 0), stop=(kc == 7), r=[wuk] + xkeys, w=[pk(pv)])
                    ge = gext[c % 2]; gk = 'gext%d' % (c % 2)
                    gv = ge[:, 0:nsg * (segW + 2)].rearrange("p (s w) -> p s w", w=segW + 2)
                    A('pool', 'tensor_copy', out=gv[:, :, 0:2], in_=carry[:, c, 0:nsg, :], r=['carry'], w=[gk])
                    A('act', 'activation', out=gv[:, :, 2:2 + segW], in_=bk(pg)[:, 0:W].rearrange("p (s w) -> p s w", w=segW),
                                                                  func=AF.Copy, r=[pk(pg)], w=[gk])
                    A('pool', 'tensor_copy', out=carry[:, c, 0:nsg, :], in_=gv[:, :, segW:segW + 2], r=[gk], w=['carry'])
                    if ti == ntile - 1:
                        A('pool', 'tensor_copy', out=cso[:, c, 0:nsg, :], in_=gv[:, :, segW:segW + 2], r=[gk], w=['cso'])
                    tv = tcv[:, 0:W].rearrange("p (s w) -> p s w", w=segW)
                    A('dve', 'tensor_scalar', out=tv, in0=gv[:, :, 0:segW], scalar1=cwt[:, 0, c:c + 1], scalar2=cbt[:, c:c + 1],
                                                                          op0=ALU.mult, op1=ALU.add, r=[gk, 'cwt', 'cbt'], w=['tcv'])
                    A('dve', 'scalar_tensor_tensor', out=tv, in0=gv[:, :, 1:1 + segW], scalar=cwt[:, 1, c:c + 1], in1=tv,
                                                                                 op0=ALU.mult, op1=ALU.add, r=[gk, 'cwt', 'tcv'], w=['tcv'])
                    A('dve', 'scalar_tensor_tensor', out=tv, in0=gv[:, :, 2:2 + segW], scalar=cwt[:, 2, c:c + 1], in1=tv,
                                                                                 op0=ALU.mult, op1=ALU.add, r=[gk, 'cwt', 'tcv'], w=['tcv'])
                    A('act', 'activation', out=ugl[:, 0:W], in_=tcv[:, 0:W], func=AF.Gelu_apprx_tanh, r=['tcv'], w=['ugl'])
                    A('dve', 'tensor_tensor', out=aT[:, c, 0:W], in0=ugl[:, 0:W], in1=bk(pv)[:, 0:W], op=ALU.mult,
                      r=['ugl', pk(pv)], w=['aT'])
                if ti == ntile - 1:
                    for b in range(nsg if g.gi == 1 else 1):
                        for jj in range(2):
                            P.dma('pool', o_cv[g.gi][l, b, jj].rearrange("(c p) -> p c", p=128), cso[:, :, b, jj], 'so', reads=['cso'],
                                  allow_slow_non_contiguous=True)
                P.dma('sp', wms, wb_gate[l].rearrange("(k p) n -> p k n", p=128), 'p3w', writes=['wms'])
                def st5_mm(j):
                    for hf in range(2):
                        for c in range(32):
                            A('pe', 'matmul', bk(hf)[:, 0:512], lhsT=aT[:, c, j * 128:(j + 1) * 128],
                                                                         rhs=wdn[:, c, hf * 512:(hf + 1) * 512], start=(c == 0), stop=(c == 31),
                              r=['aT', 'wdn'], w=[pk(hf)])

                def st5_ev(j):
                    for hf in range(2):
                        A('act', 'activation', out=ytmp[:, hf * 512:(hf + 1) * 512], in_=bk(hf)[:, 0:512], func=AF.Copy,
                          r=[pk(hf)], w=['ytmp'])
                    resid_norm(j, ytmp, ['ytmp'], 1, 2)

                def st6_rest(j):
                    prenorm_T(j, 3)
                    A('dve', 'tensor_copy', out=p_b, in_=p_t[:, j, :], r=['p_t'], w=['p_b'])
                    for kc in range(2):
                        A('pe', 'transpose', bkb(3)[:, kc * 128:(kc + 1) * 128], p_b[:, kc * 128:(kc + 1) * 128], ident,
                          r=['p_b', 'ident'], w=[pk(3)])
                    A('dve', 'tensor_copy', out=pTt, in_=bkb(3)[:, 0:256].rearrange("p (a b) -> p a b", b=128), r=[pk(3)], w=['pTt'])
                    for hf in range(2):
                        pg, pp_ = 4 + hf, 6 + hf
                        for kc in range(8):
                            A('pe', 'matmul', bk(pg)[:, 0:512], lhsT=XT[:, kc, j * 128:(j + 1) * 128],
                                                                                 rhs=wms[:, kc, hf * 512:(hf + 1) * 512], start=(kc == 0), stop=(kc == 7),
                              r=['XT%d' % j, 'wms'], w=[pk(pg)])
                        for kc in range(2):
                            A('pe', 'matmul', bk(pp_)[:, 0:512], lhsT=pTt[:, kc, :], rhs=wpj[:, kc, hf * 512:(hf + 1) * 512],
                                                                               start=(kc == 0), stop=(kc == 1), r=['pTt', 'wpj'], w=[pk(pp_)])
                        A('act', 'activation', out=tpl[:, hf * 512:(hf + 1) * 512], in_=bk(pg)[:, 0:512], func=AF.Sigmoid,
                          r=[pk(pg)], w=['tpl'])
                        A('dve', 'tensor_tensor', out=tpl[:, hf * 512:(hf + 1) * 512], in0=tpl[:, hf * 512:(hf + 1) * 512],
                                                                            in1=bk(pp_)[:, 0:512], op=ALU.mult, r=['tpl', pk(pp_)], w=['tpl'])
                    resid_norm(j, tpl, ['tpl'], 2, 4)
                    P.dma('pool', dst[t0 + j * 128:t0 + (j + 1) * 128, :], h_t[:, j, :], 'sh', reads=['h%d' % j], writes=['hb'])

                if OPT_P3PIPE:
                    st5_mm(0)
                for j in range(nsub):
                    if not OPT_P3PIPE:
                        st5_mm(j)
                    st5_ev(j)
                    if OPT_P3PIPE and j + 1 < nsub:
                        st5_mm(j + 1)
                    st6_rest(j)

    setup()
    prep_weights()
    P.barrier()
    for l in range(L):
        phase_cache(l)
        P.barrier()
        phase_p1(l)
        P.barrier()
        phase_p2(l)
        P.barrier()
        phase_p3(l)
        P.barrier()
    P.finish()
    return nc, P


_CACHE = {}


def _host_consts(S, NBS):
    NS = NBS * TS
    NQT = S // 512
    TK = PAST + TS
    NKB_S = (TK + 127) // 128
    k = np.arange(128)[:, None]
    q = np.arange(128)[None, :]
    slopes = 2.0 ** (-8.0 * np.arange(1, 5) / 4)
    masks = np.zeros((6, 128, 128), np.float32)
    masks[0] = (k <= q)
    chunk = ((k // 64) <= (q // 64)).astype(np.float32)
    masks[1] = chunk
    for h in range(4):
        masks[2 + h] = chunk * np.exp(-2.0 * slopes[h] * np.maximum(k - q, 0))
    U = (k <= q).astype(np.float32)
    E = np.zeros((128, 128), np.float32)
    E[127, :] = 1.0
    half = 16
    inv_freq = (10000.0 ** (-np.arange(half, dtype=np.float32) / half)).astype(np.float32)

    def cs(pos):
        ang = pos.astype(np.float32)[:, None] * inv_freq[None, :]
        c, s = np.cos(ang), np.sin(ang)
        return np.concatenate([c, c, -s, s], axis=1).astype(np.float32)
    cs_p = cs(np.arange(S))
    cs_s = cs(PAST + (np.arange(NS) % TS))
    NAL = 4 * NQT + NKB_S
    al = np.zeros((128, 4, NAL), np.float32)
    p = np.arange(128)
    for h in range(4):
        for e in range(4 * NQT):
            d = e - (4 * NQT - 2)
            al[:, h, e] = slopes[h] * (p + 128 * d)
        for kb in range(NKB_S):
            al[:, h, 4 * NQT + kb] = slopes[h] * (p + 128 * kb - (PAST + TS // 2))
    return dict(k_masks=masks, k_U=U, k_E=E, k_cs_p=cs_p, k_cst_p=np.ascontiguousarray(cs_p.T),
                k_cs_s=cs_s, k_cst_s=np.ascontiguousarray(cs_s.T), k_al=np.ascontiguousarray(al.reshape(128, 4 * NAL)))


def kernel(x_prompt, x_sample, cache_fox_k, cache_fox_v, cache_fox_logf, cache_mla_ckv,
           cache_mla_krope, cache_diff_k, cache_diff_v, state_ffn_conv, p_prompt, p_sample,
           w_in, b_forget, mla_q_norm, w_mla_uq, mla_kv_norm, w_mla_uk, w_mla_uv,
           diff_lambda_q1, diff_lambda_k1, diff_lambda_q2, diff_lambda_k2, diff_subln, w_out,
           norm_mix_pre, norm_mix_post, norm_ffn_pre, norm_ffn_post, norm_ple_pre, norm_ple_post,
           w_ffn_up, ffn_conv_w, ffn_conv_b, w_ffn_down, w_ple_gate, w_ple_proj):
    f = lambda a: np.ascontiguousarray(np.asarray(a, dtype=np.float32))
    B, S, _ = x_prompt.shape
    DB = x_sample.shape[0]
    L = w_in.shape[0]
    NBS = DB // NCORES
    NS = NBS * TS
    assert B * 2 == NCORES or B <= NCORES
    key = (S, NBS, L)
    if key not in _CACHE:
        _CACHE[key] = build(S, NBS, L)
    nc, _ = _CACHE[key]
    w_in = f(w_in)
    kr = w_in[:, :, 1798:1830]
    krs = np.concatenate([kr[:, :, 16:32], kr[:, :, 0:16]], axis=2)
    w_in_p = np.concatenate([w_in[:, :, 0:384], w_in[:, :, 1158:1542], w_in[:, :, 1830:2086], w_in[:, :, 384:768],
                             w_in[:, :, 1152:1158], np.zeros((L, D, 2), np.float32), w_in[:, :, 768:1152],
                             w_in[:, :, 1542:1798], kr, krs, w_in[:, :, 2086:2342], w_in[:, :, 2342:2598]], axis=2)
    assert w_in_p.shape[2] == NCOLS
    uq = f(w_mla_uq).reshape(L, 384, 6, 96)
    uq_p = np.concatenate([uq[..., 0:64], uq[..., 64:96], uq[..., 80:96], uq[..., 64:80]], axis=3).reshape(L, 384, 768)
    up = f(w_ffn_up)
    up_p = np.stack([up[:, :, :DFF].reshape(L, D, 32, 128), up[:, :, DFF:].reshape(L, D, 32, 128)], axis=3).reshape(L, D, 2 * DFF)
    shared = dict(
        w_in=np.ascontiguousarray(w_in_p), w_uq=np.ascontiguousarray(uq_p), w_uk=f(w_mla_uk), w_uv=f(w_mla_uv),
        w_out=f(w_out), w_up=np.ascontiguousarray(up_p), w_down=f(w_ffn_down), w_gate=f(w_ple_gate), w_proj=f(w_ple_proj),
        b_f=f(b_forget), g_q=f(mla_q_norm), g_kv=f(mla_kv_norm),
        lam4=np.ascontiguousarray(np.stack([f(diff_lambda_q1), f(diff_lambda_k1), f(diff_lambda_q2), f(diff_lambda_k2)], axis=1)),
        g_sub=f(diff_subln),
        g_norm=np.ascontiguousarray(np.stack([f(norm_mix_pre), f(norm_mix_post), f(norm_ffn_pre), f(norm_ffn_post),
                                              f(norm_ple_pre), f(norm_ple_post)], axis=1)),
        cw=f(ffn_conv_w), cb=f(ffn_conv_b))
    shared.update(_host_consts(S, NBS))
    xp, xs, pp, ps_ = f(x_prompt), f(x_sample), f(p_prompt), f(p_sample)
    cfk, cfv, clf = f(cache_fox_k), f(cache_fox_v), f(cache_fox_logf)
    cckv, ckr, cdk, cdv, ccs = f(cache_mla_ckv), f(cache_mla_krope), f(cache_diff_k), f(cache_diff_v), f(state_ffn_conv)
    in_maps = []
    for c in range(NCORES):
        b = c % B
        sb = slice(c * NBS, (c + 1) * NBS)
        m = dict(shared)
        m.update(xp=xp[b], xs=np.ascontiguousarray(xs[sb].reshape(NS, D)), pp=np.ascontiguousarray(pp[:, b]),
                 pss=np.ascontiguousarray(ps_[:, sb].reshape(L, NS, PLE)),
                 c_fk=np.ascontiguousarray(cfk[:, sb].reshape(L, NBS, PAST, 384)),
                 c_fv=np.ascontiguousarray(cfv[:, sb].reshape(L, NBS, PAST, 384)),
                 c_lf=np.ascontiguousarray(clf[:, sb]), c_ckv=np.ascontiguousarray(cckv[:, sb]),
                 c_kr=np.ascontiguousarray(ckr[:, sb]),
                 c_dk=np.ascontiguousarray(cdk[:, sb].reshape(L, NBS, PAST, 256)),
                 c_dv=np.ascontiguousarray(cdv[:, sb].reshape(L, NBS, PAST, 256)),
                 c_cs=np.ascontiguousarray(ccs[:, sb]))
        in_maps.append(m)
    res = run_bass_kernel_spmd(nc, in_maps, core_ids=list(range(NCORES)))
    R = res.results
    global _LAST
    _LAST = R
    pc = [R[b] for b in range(B)]
    y_prompt = np.stack([r["y_p"] for r in pc], 0)
    y_sample = np.concatenate([r["y_s"].reshape(NBS, TS, D) for r in R], 0)

    def pstack(name, tail):
        return np.stack([r[name] for r in pc], 1).reshape((L, B, S) + tail)

    def sstack(name, tail):
        return np.concatenate([r[name].reshape((L, NBS, TS) + tail) for r in R], 1)

    outs = (y_prompt, y_sample,
            pstack("o_fk_p", (6, 64)), pstack("o_fv_p", (6, 64)), pstack("o_lf_p", (6,)), pstack("o_ckv_p", (256,)),
            pstack("o_kr_p", (32,)), pstack("o_dk_p", (4, 64)), pstack("o_dv_p", (4, 64)),
            np.stack([r["o_cv_p"][:, 0] for r in pc], 1),
            sstack("o_fk_s", (6, 64)), sstack("o_fv_s", (6, 64)), sstack("o_lf_s", (6,)), sstack("o_ckv_s", (256,)),
            sstack("o_kr_s", (32,)), sstack("o_dk_s", (4, 64)), sstack("o_dv_s", (4, 64)),
            np.concatenate([r["o_cv_s"] for r in R], 1))
    return tuple(np.ascontiguousarray(o, dtype=np.float32) for o in outs)
```

```python
import contextlib
import math
import numpy as np
import concourse.bass as bass
import concourse.mybir as mybir
from concourse.bass_utils import run_bass_kernel_spmd

F32 = mybir.dt.float32
BF16 = mybir.dt.bfloat16
AF = mybir.ActivationFunctionType
ALU = mybir.AluOpType

D = 1024
DFF = 4096
PLE = 256
PAST = 1024
TS = 64
NCOLS = 2632
O_FQ, O_CQ, O_DQ, O_FK, O_FF, O_FV, O_CKV, O_KR, O_KRS, O_DK, O_DV = (
    0, 384, 768, 1024, 1408, 1416, 1800, 2056, 2088, 2120, 2376)
EPS = 1e-6
DEBUG = False
OPT_STQ = False
OPT_P3PIPE = True
_LAST = None
NCORES = 8


class Prog:
    ENG = ('pe', 'act', 'dve', 'pool', 'sp')
    LIM = 30000
    LIMD = 16 * 1800

    def __init__(self, nc):
        self.nc = nc
        self.stack = contextlib.ExitStack()
        self.rec = {e: [] for e in self.ENG}
        self.seq = {e: 0 for e in self.ENG}
        self.dcount = {}
        self.waited = {e: {} for e in self.ENG}
        self.lastw = {}
        self.readers = {}

    def sb(self, name, shape, dtype):
        return self.stack.enter_context(self.nc.sbuf_tensor(name, list(shape), dtype))

    def ps(self, name, shape, dtype):
        return self.stack.enter_context(self.nc.psum_tensor(name, list(shape), dtype))

    def _deps(self, reads, writes, eng):
        deps = []
        for k in reads:
            d = self.lastw.get(k)
            if d is not None:
                deps.append(d)
            if k.startswith('ps'):
                r = self.readers.get(k)
                if r:
                    deps.extend((sk, v) for sk, v in r.items() if sk != ('E', eng))
        for k in writes:
            d = self.lastw.get(k)
            if d is not None:
                deps.append(d)
            r = self.readers.get(k)
            if r:
                deps.extend(r.items())
        return deps

    def _emit_waits(self, eng, deps):
        need = {}
        w = self.waited[eng]
        for (sk, v) in deps:
            if sk[0] == 'E':
                if sk[1] == 'pe' and eng == 'pe':
                    continue
            else:
                v = self.dcount[sk[1]]
            if w.get(sk, 0) >= v:
                continue
            if need.get(sk, 0) < v:
                need[sk] = v
        for sk, v in need.items():
            self.rec[eng].append(('w', sk, v, w.get(sk, 0)))
            w[sk] = v

    def _record(self, dep, reads, writes):
        sk, v = dep
        for k in reads:
            self.readers.setdefault(k, {})[sk] = v
        for k in writes:
            self.lastw[k] = dep
            self.readers[k] = {}

    def op(self, eng, meth, args, kw, reads=(), writes=()):
        self._emit_waits(eng, self._deps(reads, writes, eng))
        self.seq[eng] += 1
        n = self.seq[eng]
        self.rec[eng].append(('o', (meth, args, kw), n))
        self._record((('E', eng), n), reads, writes)

    def dma(self, eng, out, in_, sem, reads=(), writes=(), **kw):
        if sem != 'c0':
            sem = ('S' + reads[0]) if reads else ('L' + writes[0])
        self._emit_waits(eng, self._deps(reads, writes, eng))
        c = self.dcount.get(sem, 0) + 16
        self.dcount[sem] = c
        self.rec[eng].append(('d', out, in_, kw, sem, c))
        self._record((('D', sem), c), reads, writes)

    def barrier(self):
        deps = [(('D', n), c) for n, c in self.dcount.items()]
        deps += [(('E', e), self.seq[e]) for e in ('pe', 'act', 'dve', 'pool') if self.seq[e] > 0]
        for e in self.ENG:
            self._emit_waits(e, [d for d in deps if d[0] != ('E', e)])
        self.lastw = {}
        self.readers = {}

    def finish(self):
        nc = self.nc
        deps = [(('D', n), c) for n, c in self.dcount.items()]
        deps += [(('E', e), self.seq[e]) for e in ('pe', 'act', 'dve', 'pool') if self.seq[e] > 0]
        self._emit_waits('sp', deps)
        sig = {e: set() for e in self.ENG}
        for e in self.ENG:
            for r in self.rec[e]:
                if r[0] == 'w' and r[1][0] == 'E':
                    sig[r[1][1]].add(r[2])
        rank, esems = {}, {}
        for e in self.ENG:
            s = sorted(sig[e])
            rank[e] = {n: i for i, n in enumerate(s)}
            nep = (len(s) + self.LIM - 1) // self.LIM
            esems[e] = [self.stack.enter_context(nc.semaphore("es_%s_%d" % (e, k))) for k in range(nep)]
        dsems = {}
        for n, c in self.dcount.items():
            nep = (c + self.LIMD - 1) // self.LIMD
            dsems[n] = [self.stack.enter_context(nc.semaphore("ds_%s_%d" % (n, k))) for k in range(nep)]
        self.nsem = sum(len(v) for v in esems.values()) + sum(len(v) for v in dsems.values())
        self.ninst = sum(len(v) for v in self.rec.values())
        LIM, LIMD = self.LIM, self.LIMD

        def run(eng, e):
            for r in self.rec[eng]:
                if r[0] == 'w':
                    sk, v, prev = r[1], r[2], r[3]
                    if sk[0] == 'E':
                        i = rank[sk[1]][v]
                        e.wait_ge(esems[sk[1]][i // LIM], (i % LIM) + 1)
                    else:
                        k = (v - 1) // LIMD
                        if k > 0 and prev < k * LIMD:
                            e.wait_ge(dsems[sk[1]][k - 1], LIMD)
                        e.wait_ge(dsems[sk[1]][k], v - k * LIMD)
                elif r[0] == 'o':
                    meth, args, kw = r[1]
                    ins = getattr(e, meth)(*args, **kw)
                    i = rank[eng].get(r[2])
                    if i is not None:
                        ins.then_inc(esems[eng][i // LIM], 1)
                else:
                    _, out, in_, kw, sem, c = r
                    k = (c - 1) // LIMD
                    e.dma_start(out=out, in_=in_, **kw).then_inc(dsems[sem][k], 16)

        with nc.Block() as block:
            @block.tensor
            def _(e):
                run('pe', e)

            @block.scalar
            def _(e):
                run('act', e)

            @block.vector
            def _(e):
                run('dve', e)

            @block.gpsimd
            def _(e):
                run('pool', e)

            @block.sync
            def _(e):
                run('sp', e)
        self.stack.close()


class Arena:
    def __init__(self, t, nwords):
        self.t, self.n, self.off = t, nwords, 0

    def reset(self):
        self.off = 0

    def alloc(self, free_shape, dtype, parts=128):
        nel = 1
        for s in free_shape:
            nel *= s
        nw = (nel * (2 if dtype == BF16 else 4) + 3) // 4
        nw = (nw + 7) // 8 * 8
        o = self.off
        self.off += nw
        assert self.off <= self.n, ("arena overflow", self.off, self.n)
        v = self.t[0:parts, o:o + nw]
        if dtype != F32:
            v = v.bitcast(dtype)
        v = v[:, 0:nel]
        if len(free_shape) == 2:
            v = v.rearrange("p (a b) -> p a b", b=free_shape[1])
        elif len(free_shape) == 3:
            v = v.rearrange("p (a b c) -> p a b c", b=free_shape[1], c=free_shape[2])
        return v


def build(S, NBS, L):
    nc = bass.Bass("TRN2", target_bir_lowering=False)
    P = Prog(nc)
    NS = NBS * TS
    TK = PAST + TS
    NKB_S = (TK + 127) // 128
    NT = S // 128
    NQT = S // 512
    assert S % 512 == 0 and NS % 128 == 0

    def din(name, shape):
        return nc.dram_tensor(name, list(shape), F32, kind="ExternalInput").ap()

    def dout(name, shape):
        return nc.dram_tensor(name, list(shape), F32, kind="ExternalOutput").ap()

    def dscr(name, shape, dt=BF16):
        return nc.dram_tensor(name, list(shape), dt, kind="Internal").ap()

    xp = din("xp", [S, D]); xs = din("xs", [NS, D])
    pp = din("pp", [L, S, PLE]); pss = din("pss", [L, NS, PLE])
    c_fk = din("c_fk", [L, NBS, PAST, 384]); c_fv = din("c_fv", [L, NBS, PAST, 384])
    c_lf = din("c_lf", [L, NBS, PAST, 6]); c_ckv = din("c_ckv", [L, NBS, PAST, 256])
    c_kr = din("c_kr", [L, NBS, PAST, 32]); c_dk = din("c_dk", [L, NBS, PAST, 256])
    c_dv = din("c_dv", [L, NBS, PAST, 256]); c_cs = din("c_cs", [L, NBS, 2, DFF])
    w_in = din("w_in", [L, D, NCOLS]); w_uq = din("w_uq", [L, 384, 768])
    w_uk = din("w_uk", [L, 256, 384]); w_uv = din("w_uv", [L, 256, 384])
    w_out = din("w_out", [L, D, D]); w_up = din("w_up", [L, D, 2 * DFF])
    w_down = din("w_down", [L, DFF, D]); w_gate = din("w_gate", [L, D, D])
    w_proj = din("w_proj", [L, PLE, D])
    b_f = din("b_f", [L, 6]); g_q = din("g_q", [L, 384]); g_kv = din("g_kv", [L, 256])
    lam4 = din("lam4", [L, 4, 32]); g_sub = din("g_sub", [L, 64])
    g_norm = din("g_norm", [L, 6, D])
    cw = din("cw", [L, 3, DFF]); cb = din("cb", [L, DFF])
    k_masks = din("k_masks", [6, 128, 128]); k_U = din("k_U", [128, 128]); k_E = din("k_E", [128, 128])
    k_cs_p = din("k_cs_p", [S, 64]); k_cst_p = din("k_cst_p", [64, S])
    k_cs_s = din("k_cs_s", [NS, 64]); k_cst_s = din("k_cst_s", [64, NS])
    NAL = 4 * NQT + NKB_S
    k_al = din("k_al", [128, 4 * NAL])

    y_p = dout("y_p", [S, D]); y_s = dout("y_s", [NS, D])
    o_fk = [dout("o_fk_p", [L, S, 384]), dout("o_fk_s", [L, NS, 384])]
    o_fv = [dout("o_fv_p", [L, S, 384]), dout("o_fv_s", [L, NS, 384])]
    o_lf = [dout("o_lf_p", [L, S, 6]), dout("o_lf_s", [L, NS, 6])]
    o_ckv = [dout("o_ckv_p", [L, S, 256]), dout("o_ckv_s", [L, NS, 256])]
    o_kr = [dout("o_kr_p", [L, S, 32]), dout("o_kr_s", [L, NS, 32])]
    o_dk = [dout("o_dk_p", [L, S, 256]), dout("o_dk_s", [L, NS, 256])]
    o_dv = [dout("o_dv_p", [L, S, 256]), dout("o_dv_s", [L, NS, 256])]
    o_cv = [dout("o_cv_p", [L, 1, 2, DFF]), dout("o_cv_s", [L, NBS, 2, DFF])]

    wb_in = dscr("wb_in", [L, D, NCOLS]); wb_uq = dscr("wb_uq", [L, 384, 768])
    wb_uk = dscr("wb_uk", [L, 256, 384]); wb_uv = dscr("wb_uv", [L, 256, 384])
    wb_out = dscr("wb_out", [L, D, D]); wb_up = dscr("wb_up", [L, D, 2 * DFF])
    wb_down = dscr("wb_down", [L, DFF, D]); wb_gate = dscr("wb_gate", [L, D, D])
    wb_proj = dscr("wb_proj", [L, PLE, D])

    class G:
        pass
    grp = []
    for gi, (nseq, tk, tq) in enumerate([(1, S, S), (NBS, TK, TS)]):
        g = G()
        g.gi, g.nseq, g.tk, g.tq = gi, nseq, tk, tq
        g.ntok = nseq * tq
        g.qT_fox = dscr("qT_fox%d" % gi, [nseq, 384, tq]); g.kT_fox = dscr("kT_fox%d" % gi, [nseq, 384, tk])
        g.v_fox = dscr("v_fox%d" % gi, [nseq, tk, 6 * 65])
        g.qT_mla = dscr("qT_mla%d" % gi, [nseq, 576, tq]); g.kT_rope = dscr("kT_rope%d" % gi, [nseq, 32, tk])
        g.kT_nope = dscr("kT_nope%d" % gi, [nseq, 384, tk]); g.v_mla = dscr("v_mla%d" % gi, [nseq, tk, 6 * 65])
        g.qT_diff = dscr("qT_diff%d" % gi, [nseq, 256, tq]); g.kT_diff = dscr("kT_diff%d" % gi, [nseq, 256, tk])
        g.v_diff = dscr("v_diff%d" % gi, [nseq, tk, 4 * 65])
        g.mixT = (nc.dram_tensor("mixT%d" % gi, [D, g.ntok], BF16, kind="ExternalOutput").ap() if DEBUG
                  else dscr("mixT%d" % gi, [D, g.ntok]))
        g.hbuf = dscr("hbuf%d" % gi, [g.ntok, D], F32)
        g.x = xp if gi == 0 else xs
        g.y = y_p if gi == 0 else y_s
        g.pin = pp if gi == 0 else pss
        g.cs = k_cs_p if gi == 0 else k_cs_s
        g.cst = k_cst_p if gi == 0 else k_cst_s
        grp.append(g)

    ARW = 50300
    arena_t = P.sb("arena", [128, ARW], F32)
    ar = Arena(arena_t, ARW)
    pers_t = P.sb("pers", [128, 2600], F32)
    pers = Arena(pers_t, 2600)
    banks = [P.ps("bank%d" % i, [128, 512], F32) for i in range(8)]

    def bk(i):
        return banks[i][:, :]

    def bkb(i):
        return banks[i][:, :].bitcast(BF16)

    def pk(i):
        return 'ps%d' % i

    ident = pers.alloc([128], BF16)
    ones_f = pers.alloc([128], F32)
    U_f = pers.alloc([128], F32)
    E_f = pers.alloc([128], F32)
    masks = pers.alloc([6, 128], BF16)
    al_t = pers.alloc([4 * NAL], F32)
    eps_t = pers.alloc([1], F32)
    zero6 = pers.alloc([6], F32)
    ctok_p = pers.alloc([NT, 6], F32)
    ctok_s = pers.alloc([NBS, NKB_S, 6], F32)
    cref = pers.alloc([512], F32)


    def setup():
        A('pool', 'memset', ident, 1.0, w=['ident'])
        A('pool', 'affine_select', out=ident, in_=ident, pattern=[[-1, 128]], compare_op=ALU.is_equal,
                                             fill=0.0, base=0, channel_multiplier=1, r=['ident'], w=['ident'])
        A('pool', 'memset', ones_f, 1.0, w=['ones'])
        A('pool', 'memset', eps_t, EPS, w=['eps'])
        A('pool', 'memset', zero6, 0.0, w=['zero6'])
        P.dma('sp', U_f, k_U[:, :], 'c0', writes=['U'])
        P.dma('sp', E_f, k_E[:, :], 'c0', writes=['E'])
        P.dma('sp', al_t, k_al[:, :], 'c0', writes=['al'])
        ar.reset()
        mtmp = ar.alloc([6, 128], F32)
        P.dma('sp', mtmp, k_masks.rearrange("m k q -> k m q"), 'own', writes=['mtmp'])
        A('dve', 'tensor_copy', out=masks, in_=mtmp, r=['mtmp'], w=['masks'])

    _op = P.op

    def A(eng, meth, *args, r=(), w=(), **kw):
        _op(eng, meth, args, kw, reads=r, writes=w)

    def prep_weights():
        ar.reset()
        NB = 3
        stg = [ar.alloc([4096], F32) for _ in range(NB)]
        stb = [ar.alloc([4096], BF16) for _ in range(NB)]
        gcol = ar.alloc([L, 3, 8], F32)
        for l in range(L):
            for wi, gidx in enumerate((0, 2, 4)):
                P.dma('sp', gcol[:, l, wi, :], g_norm[l, gidx].rearrange("(k p) -> p k", p=128), 'own',
                      writes=['gcol'], allow_slow_non_contiguous=True)
        cnt = [0]

        def conv(src, dst, rows, cols, gain=None):
            for rk in range(rows // 128):
                for c0 in range(0, cols, 4096):
                    c1 = min(cols, c0 + 4096)
                    i = cnt[0] % NB
                    cnt[0] += 1
                    sv, bv = stg[i][:, 0:c1 - c0], stb[i][:, 0:c1 - c0]
                    P.dma('sp', sv, src[rk * 128:(rk + 1) * 128, c0:c1], 'wl%d' % i, writes=['stg%d' % i])
                    eng = ('dve', 'act', 'pool')[cnt[0] % 3] if gain is None else 'dve'
                    if gain is None:
                        if eng == 'act':
                            A('act', 'activation', out=bv, in_=sv, func=AF.Copy,
                              r=['stg%d' % i], w=['stb%d' % i])
                        else:
                            A(eng, 'tensor_copy', out=bv, in_=sv,
                              r=['stg%d' % i], w=['stb%d' % i])
                    else:
                        gc = gain[:, rk:rk + 1]
                        A('dve', 'tensor_scalar_mul', out=bv, in0=sv, scalar1=gc,
                          r=['stg%d' % i, 'gcol'], w=['stb%d' % i])
                    P.dma('pool', dst[rk * 128:(rk + 1) * 128, c0:c1], bv, 'ws%d' % i, reads=['stb%d' % i])

        for l in range(L):
            conv(w_in[l], wb_in[l], D, NCOLS, gcol[:, l, 0, :])
            conv(w_uq[l], wb_uq[l], 384, 768)
            conv(w_uk[l], wb_uk[l], 256, 384)
            conv(w_uv[l], wb_uv[l], 256, 384)
            conv(w_out[l], wb_out[l], D, D)
            conv(w_up[l], wb_up[l], D, 2 * DFF, gcol[:, l, 1, :])
            conv(w_down[l], wb_down[l], DFF, D)
            conv(w_gate[l], wb_gate[l], D, D, gcol[:, l, 2, :])
            conv(w_proj[l], wb_proj[l], PLE, D)

    def rstd_from_ss(ss_ap, n, key):
        A('act', 'activation', out=ss_ap, in_=ss_ap, func=AF.Ln, bias=eps_t[0:ss_ap.shape[0], 0:1],
                                        scale=1.0 / n, r=[key, 'eps'], w=[key])
        A('act', 'activation', out=ss_ap, in_=ss_ap, func=AF.Exp, scale=-0.5, r=[key], w=[key])

    def token_rows(g, t0, n):
        out = []
        if g.gi == 0:
            return [(0, t0, n, 0)]
        t = t0
        while t < t0 + n:
            out.append((t // TS, PAST + (t % TS), TS, t - t0))
            t += TS
        return out

    def phase_p1(l):
        ar.reset()
        wi = ar.alloc([8, NCOLS], BF16)
        wq = ar.alloc([3, 768], BF16)
        wk = ar.alloc([2, 384], BF16)
        wv = ar.alloc([2, 384], BF16)
        P.dma('sp', wi, wb_in[l].rearrange("(k p) n -> p k n", p=128), 'c0', writes=['wi'])
        P.dma('sp', wq, wb_uq[l].rearrange("(k p) n -> p k n", p=128), 'c0', writes=['wq'])
        P.dma('sp', wk, wb_uk[l].rearrange("(k p) n -> p k n", p=128), 'c0', writes=['wk'])
        P.dma('sp', wv, wb_uv[l].rearrange("(k p) n -> p k n", p=128), 'c0', writes=['wv'])
        gq_c = ar.alloc([3], F32); gkv_c = ar.alloc([2], F32)
        P.dma('sp', gq_c, g_q[l].rearrange("(k p) -> p k", p=128), 'c0', writes=['gqc'], allow_slow_non_contiguous=True)
        P.dma('sp', gkv_c, g_kv[l].rearrange("(k p) -> p k", p=128), 'c0', writes=['gkvc'], allow_slow_non_contiguous=True)
        gkv_b = ar.alloc([256], F32); bf_b = ar.alloc([6], F32)
        P.dma('sp', gkv_b, g_kv[l:l + 1, :].to_broadcast([128, 256]), 'c0', writes=['gkvb'])
        P.dma('sp', bf_b, b_f[l:l + 1, :].to_broadcast([128, 6]), 'c0', writes=['bfb'])
        h_ts = [ar.alloc([4, D], F32) for _ in range(2)]
        ssn = ar.alloc([8], F32)
        tcount = [0]
        sq_rr = [0]

        def stq():
            sq_rr[0] += 1
            return ('sp', 'pool')[sq_rr[0] % 2] if OPT_STQ else 'pool'
        xn = [ar.alloc([D], BF16) for _ in range(2)]
        xnT = ar.alloc([8, 512], BF16)
        cs_t = ar.alloc([4, 64], F32)
        cst_t = ar.alloc([512], F32, parts=64)
        ost_a = [ar.alloc([390], F32) for _ in range(2)]
        ost_b = [ar.alloc([384], F32) for _ in range(2)]
        ost_c = [ar.alloc([320], F32) for _ in range(2)]
        ost_d = [ar.alloc([512], F32) for _ in range(2)]
        vst = ar.alloc([4, 16, 65], BF16)
        fst = [ar.alloc([512], BF16) for _ in range(6)]
        fsi = [0]

        def next_fst():
            fsi[0] += 1
            return fsi[0] % 6
        raw = ar.alloc([3, 512], F32)
        sq = [ar.alloc([512], F32) for _ in range(2)]
        rsb = ar.alloc([512], F32)
        cqnT = ar.alloc([3, 512], BF16)
        ckvnT = ar.alloc([2, 512], BF16)
        t1 = ar.alloc([512], F32); t2 = ar.alloc([512], F32)
        sm = ar.alloc([64], F32)
        lf_t = ar.alloc([4, 6], F32)
        A('pool', 'memset', vst, 1.0, w=['vst'])
        rr = [0]

        for g in grp:
            ntile = (g.ntok + 511) // 512
            for ti in range(ntile):
                t0 = ti * 512
                W = min(512, g.ntok - t0)
                nsub = W // 128
                src = g.x if l == 0 else g.hbuf
                h_t = h_ts[tcount[0] % 2]
                hkey = 'h_t%d' % (tcount[0] % 2)
                tcount[0] += 1
                P.dma('sp', h_t[:, 0:nsub, :], src[t0:t0 + W, :].rearrange("(j p) d -> p j d", p=128), 'p1h' + hkey, writes=[hkey])
                P.dma('sp', cs_t[:, 0:nsub, :], g.cs[t0:t0 + W, :].rearrange("(j p) d -> p j d", p=128), 'p1c', writes=['cs_t'])
                P.dma('sp', cst_t[:, 0:W], g.cst[:, t0:t0 + W], 'p1c', writes=['cst_t'])
                A('pool', 'memset', ssn, 0.0, w=['ssn'])
                for j in range(nsub):
                    x_b = xn[j % 2]; xk = 'xn%d' % (j % 2)
                    A('act', 'activation', out=x_b, in_=h_t[:, j, :], func=AF.Square,
                                                                  accum_out=ssn[:, j:j + 1], r=[hkey, 'ssn'], w=[xk, 'ssn'])
                for j in range(nsub):
                    pass
                rstd_from_ss(ssn[:, 0:nsub], D, 'ssn')
                for j in range(nsub):
                    x_b = xn[j % 2]; xk = 'xn%d' % (j % 2)
                    A('dve', 'tensor_scalar_mul', out=x_b, in0=h_t[:, j, :], scalar1=ssn[:, j:j + 1],
                      r=[hkey, 'ssn'], w=[xk])
                    pb = 6 + (j % 2)
                    for kc in range(8):
                        A('pe', 'transpose', bkb(pb)[:, kc * 128:(kc + 1) * 128],
                                                                             x_b[:, kc * 128:(kc + 1) * 128], ident,
                          r=[xk, 'ident'], w=[pk(pb)])
                    A('act', 'activation', out=xnT[:, :, j * 128:(j + 1) * 128],
                                                                in_=bkb(pb).rearrange("p (a b) -> p a b", b=128), func=AF.Copy,
                      r=[pk(pb)], w=['xnT'])
                for j in range(nsub):
                    rows = token_rows(g, t0 + j * 128, 128)
                    oa, ob, oc_, od = ost_a[j % 2], ost_b[j % 2], ost_c[j % 2], ost_d[j % 2]
                    ka, kb_, kc_, kd = 'oa%d' % (j % 2), 'ob%d' % (j % 2), 'oc%d' % (j % 2), 'od%d' % (j % 2)
                    for ci, (c0, c1) in enumerate([(O_FK, O_FK + 390), (O_FV, O_FV + 384), (O_CKV, O_CKV + 320), (O_DK, O_DK + 512)]):
                        pb = rr[0] % 3
                        rr[0] += 1
                        for kc in range(8):
                            A('pe', 'matmul',
                                bk(pb)[:, 0:c1 - c0], lhsT=xnT[:, kc, j * 128:(j + 1) * 128], rhs=wi[:, kc, c0:c1],
                                start=(kc == 0), stop=(kc == 7), r=['xnT', 'wi'], w=[pk(pb)])
                        if ci == 0:
                            A('act', 'activation', out=oa, in_=bk(pb)[:, 0:390], func=AF.Copy,
                              r=[pk(pb)], w=[ka])
                            A('dve', 'tensor_tensor', out=sm[:, 0:6], in0=oa[:, 384:390], in1=bf_b, op=ALU.add,
                              r=[ka, 'bfb'], w=['sm'])
                            A('act', 'activation', out=sm[:, 0:6], in_=sm[:, 0:6], func=AF.Exp, scale=-1.0, r=['sm'], w=['sm'])
                            A('act', 'activation', out=sm[:, 0:6], in_=sm[:, 0:6], func=AF.Ln, bias=1.0, scale=1.0, r=['sm'], w=['sm'])
                            A('dve', 'tensor_scalar_mul', out=lf_t[:, j, :], in0=sm[:, 0:6], scalar1=-1.0, r=['sm'], w=['lf_t'])
                            for (sq_, pos0, cnt_, p0) in rows:
                                d0 = (sq_ * g.tq + pos0 - (g.tk - g.tq))
                                P.dma(stq(), o_fk[g.gi][l, d0:d0 + cnt_, :], oa[p0:p0 + cnt_, 0:384], 'so', reads=[ka])
                                P.dma(stq(), o_lf[g.gi][l, d0:d0 + cnt_, :], lf_t[p0:p0 + cnt_, j, :], 'so', reads=['lf_t'])
                        elif ci == 1:
                            A('act', 'activation', out=ob, in_=bk(pb)[:, 0:384], func=AF.Copy, r=[pk(pb)], w=[kb_])
                            A('dve', 'tensor_copy', out=vst[:, j, 0:6, 0:64],
                                                                         in_=bk(pb)[:, 0:384].rearrange("p (h d) -> p h d", d=64),
                              r=[pk(pb)], w=['vst'])
                            for (sq_, pos0, cnt_, p0) in rows:
                                d0 = (sq_ * g.tq + pos0 - (g.tk - g.tq))
                                P.dma(stq(), o_fv[g.gi][l, d0:d0 + cnt_, :], ob[p0:p0 + cnt_, :], 'so', reads=[kb_])
                        elif ci == 2:
                            A('pool', 'memset', sm[:, 8:9], 0.0, w=['sm8'])
                            A('act', 'activation', out=t1[:, 0:256], in_=bk(pb)[:, 0:256], func=AF.Square,
                                                                   accum_out=sm[:, 8:9], r=[pk(pb), 'sm8'], w=['t1', 'sm8'])
                            rstd_from_ss(sm[:, 8:9], 256, 'sm8')
                            A('dve', 'scalar_tensor_tensor', out=oc_[:, 0:256], in0=bk(pb)[:, 0:256], scalar=sm[:, 8:9],
                                                                                       in1=gkv_b, op0=ALU.mult, op1=ALU.mult,
                              r=[pk(pb), 'sm8', 'gkvb'], w=[kc_])
                            A('dve', 'tensor_tensor', out=t1[:, 256:288], in0=bk(pb)[:, 256:288], in1=cs_t[:, j, 0:32], op=ALU.mult,
                              r=[pk(pb), 'cs_t'], w=['t1'])
                            A('dve', 'tensor_tensor', out=t1[:, 288:320], in0=bk(pb)[:, 288:320], in1=cs_t[:, j, 32:64], op=ALU.mult,
                              r=[pk(pb), 'cs_t'], w=['t1'])
                            A('dve', 'tensor_tensor', out=oc_[:, 256:288], in0=t1[:, 256:288], in1=t1[:, 288:320], op=ALU.add,
                              r=['t1'], w=[kc_])
                            for (sq_, pos0, cnt_, p0) in rows:
                                d0 = (sq_ * g.tq + pos0 - (g.tk - g.tq))
                                P.dma(stq(), o_ckv[g.gi][l, d0:d0 + cnt_, :], oc_[p0:p0 + cnt_, 0:256], 'so', reads=[kc_])
                                P.dma(stq(), o_kr[g.gi][l, d0:d0 + cnt_, :], oc_[p0:p0 + cnt_, 256:288], 'so', reads=[kc_])
                        else:
                            A('act', 'activation', out=od, in_=bk(pb)[:, 0:512], func=AF.Copy, r=[pk(pb)], w=[kd])
                            A('dve', 'tensor_copy', out=vst[:, j, 12:16, 0:64],
                                                                         in_=bk(pb)[:, 256:512].rearrange("p (h d) -> p h d", d=64),
                              r=[pk(pb)], w=['vst'])
                            for (sq_, pos0, cnt_, p0) in rows:
                                d0 = (sq_ * g.tq + pos0 - (g.tk - g.tq))
                                P.dma(stq(), o_dk[g.gi][l, d0:d0 + cnt_, :], od[p0:p0 + cnt_, 0:256], 'so', reads=[kd])
                                P.dma(stq(), o_dv[g.gi][l, d0:d0 + cnt_, :], od[p0:p0 + cnt_, 256:512], 'so', reads=[kd])
                    if g.gi == 0:
                        blk = t0 // 128 + j
                        prev = zero6 if blk == 0 else ctok_p[:, blk - 1, :]
                        A('pe', 'matmul', bk(5)[:, 0:6], lhsT=U_f, rhs=lf_t[:, j, :], start=True, stop=False,
                          r=['U', 'lf_t'], w=[pk(5)])
                        A('pe', 'matmul', bk(5)[:, 0:6], lhsT=E_f, rhs=prev, start=False, stop=True,
                          r=['E', 'ctok', 'zero6'], w=[pk(5)])
                        A('dve', 'tensor_copy', out=ctok_p[:, blk, :], in_=bk(5)[:, 0:6], r=[pk(5)], w=['ctok'])
                    else:
                        for (sq_, pos0, cnt_, p0) in rows:
                            A('pe', 'matmul', bk(5)[0:64, 0:6], lhsT=U_f[p0:p0 + 64, p0:p0 + 64],
                                                                  rhs=lf_t[p0:p0 + 64, j, :], start=True, stop=False,
                              r=['U', 'lf_t'], w=[pk(5)])
                            A('pe', 'matmul', bk(5)[0:64, 0:6], lhsT=E_f[:, 0:64], rhs=ctok_s[:, sq_, NKB_S - 2, :],
                                                               start=False, stop=True, r=['E', 'ctoks'], w=[pk(5)])
                            A('dve', 'tensor_copy', out=ctok_s[0:64, sq_, NKB_S - 1, :], in_=bk(5)[0:64, 0:6],
                              r=[pk(5)], w=['ctoks'])
                trows = token_rows(g, t0, W)

                def fm(c0, m, dst_fn, post=None):
                    pb = 3 + rr[0] % 2
                    rr[0] += 1
                    for kc in range(8):
                        A('pe', 'matmul', bk(pb)[0:m, 0:W], lhsT=wi[:, kc, c0:c0 + m], rhs=xnT[:, kc, 0:W],
                                                                  start=(kc == 0), stop=(kc == 7), r=['xnT', 'wi'], w=[pk(pb)])
                    return pb

                def store_fm(stage, skey, m, dst_fn, is_key):
                    for (sq_, pos0, cnt_, p0) in trows:
                        col = pos0 if is_key else pos0 - (g.tk - g.tq)
                        P.dma(stq(), dst_fn(sq_)[:, col:col + cnt_], stage[0:m, p0:p0 + cnt_], 'sf', reads=[skey])

                def simple_fm(c0, nchunk, dst, is_key):
                    for c in range(nchunk):
                        pb = fm(c0 + c * 128, 128, None)
                        si = next_fst()
                        st, sk = fst[si], 'fst%d' % si
                        eng = 'act' if c % 2 == 0 else 'dve'
                        if eng == 'act':
                            A('act', 'activation', out=st[:, 0:W], in_=bk(pb)[:, 0:W], func=AF.Copy, r=[pk(pb)], w=[sk])
                        else:
                            A('dve', 'tensor_copy', out=st[:, 0:W], in_=bk(pb)[:, 0:W], r=[pk(pb)], w=[sk])
                        store_fm(st, sk, 128, lambda s_, c=c: dst[s_, c * 128:(c + 1) * 128, :], is_key)

                simple_fm(O_FQ, 3, g.qT_fox, False)
                simple_fm(O_FK, 3, g.kT_fox, True)
                simple_fm(O_DQ, 2, g.qT_diff, False)
                simple_fm(O_DK, 2, g.kT_diff, True)

                def norm_fm(c0, nchunk, n, gcol_, gkey, outT, okey):
                    for c in range(nchunk):
                        pb = fm(c0 + c * 128, 128, None)
                        A('act', 'activation', out=raw[:, c, 0:W], in_=bk(pb)[:, 0:W], func=AF.Copy, r=[pk(pb)], w=['raw'])
                        s_ = sq[c % 2]
                        A('dve', 'tensor_tensor', out=s_[:, 0:W], in0=raw[:, c, 0:W], in1=raw[:, c, 0:W], op=ALU.mult,
                          r=['raw'], w=['sq%d' % (c % 2)])
                        A('pe', 'matmul', bk(5)[:, 0:W], lhsT=ones_f, rhs=s_[:, 0:W], start=(c == 0), stop=(c == nchunk - 1),
                          r=['ones', 'sq%d' % (c % 2)], w=[pk(5)])
                    A('act', 'activation', out=rsb[:, 0:W], in_=bk(5)[:, 0:W], func=AF.Ln, bias=eps_t[:, 0:1], scale=1.0 / n,
                      r=[pk(5), 'eps'], w=['rsb'])
                    A('act', 'activation', out=rsb[:, 0:W], in_=rsb[:, 0:W], func=AF.Exp, scale=-0.5, r=['rsb'], w=['rsb'])
                    for c in range(nchunk):
                        A('dve', 'scalar_tensor_tensor', out=outT[:, c, 0:W], in0=raw[:, c, 0:W], scalar=gcol_[:, c:c + 1],
                                                                        in1=rsb[:, 0:W], op0=ALU.mult, op1=ALU.mult,
                          r=['raw', gkey, 'rsb'], w=[okey])

                norm_fm(O_CQ, 3, 384, gq_c, 'gqc', cqnT, 'cqnT')
                norm_fm(O_CKV, 2, 256, gkv_c, 'gkvc', ckvnT, 'ckvnT')

                def rope_rows(pb, st, skey, b0=0):
                    A('dve', 'tensor_tensor', out=t1[b0:b0 + 32, 0:W], in0=bk(pb)[b0:b0 + 32, 0:W], in1=cst_t[0:32, 0:W], op=ALU.mult,
                      r=[pk(pb), 'cst_t'], w=['t1'])
                    A('dve', 'tensor_tensor', out=t2[b0:b0 + 32, 0:W], in0=bk(pb)[b0 + 32:b0 + 64, 0:W], in1=cst_t[32:64, 0:W], op=ALU.mult,
                      r=[pk(pb), 'cst_t'], w=['t2'])
                    A('dve', 'tensor_tensor', out=st[b0:b0 + 32, 0:W], in0=t1[b0:b0 + 32, 0:W], in1=t2[b0:b0 + 32, 0:W], op=ALU.add,
                      r=['t1', 't2'], w=[skey])

                pb = fm(O_KR, 64, None)
                si = next_fst()
                st, sk = fst[si], 'fst%d' % si
                rope_rows(pb, st, sk)
                store_fm(st, sk, 32, lambda s_: g.kT_rope[s_, :, :], True)
                for hh in range(6):
                    pb = 3 + rr[0] % 2
                    rr[0] += 1
                    for kc in range(3):
                        A('pe', 'matmul', bk(pb)[:, 0:W], lhsT=wq[:, kc, hh * 128:(hh + 1) * 128], rhs=cqnT[:, kc, 0:W],
                                                                         start=(kc == 0), stop=(kc == 2), r=['wq', 'cqnT'], w=[pk(pb)])
                    si = next_fst()
                    st, sk = fst[si], 'fst%d' % si
                    rope_rows(pb, st, sk, 64)
                    A('dve', 'tensor_copy', out=st[0:64, 0:W], in_=bk(pb)[0:64, 0:W], r=[pk(pb)], w=[sk])
                    store_fm(st, sk, 96, lambda s_, hh=hh: g.qT_mla[s_, hh * 96:(hh + 1) * 96, :], False)
                for c in range(3):
                    pb = 3 + rr[0] % 2
                    rr[0] += 1
                    for kc in range(2):
                        A('pe', 'matmul', bk(pb)[:, 0:W], lhsT=wk[:, kc, c * 128:(c + 1) * 128], rhs=ckvnT[:, kc, 0:W],
                                                                       start=(kc == 0), stop=(kc == 1), r=['wk', 'ckvnT'], w=[pk(pb)])
                    si = next_fst()
                    st, sk = fst[si], 'fst%d' % si
                    A('act', 'activation', out=st[:, 0:W], in_=bk(pb)[:, 0:W], func=AF.Copy, r=[pk(pb)], w=[sk])
                    store_fm(st, sk, 128, lambda s_, c=c: g.kT_nope[s_, c * 128:(c + 1) * 128, :], True)
                for j in range(nsub):
                    pb = rr[0] % 3
                    rr[0] += 1
                    for kc in range(2):
                        A('pe', 'matmul', bk(pb)[:, 0:384], lhsT=ckvnT[:, kc, j * 128:(j + 1) * 128], rhs=wv[:, kc, :],
                                                                       start=(kc == 0), stop=(kc == 1), r=['wv', 'ckvnT'], w=[pk(pb)])
                    A('dve', 'tensor_copy', out=vst[:, j, 6:12, 0:64], in_=bk(pb)[:, 0:384].rearrange("p (h d) -> p h d", d=64),
                      r=[pk(pb)], w=['vst'])
                for j in range(nsub):
                    for (sq_, pos0, cnt_, p0) in token_rows(g, t0 + j * 128, 128):
                        P.dma(stq(), g.v_fox[sq_, pos0:pos0 + cnt_, :].rearrange("t (h c) -> t h c", c=65), vst[p0:p0 + cnt_, j, 0:6, :], 'sv', reads=['vst'])
                        P.dma(stq(), g.v_mla[sq_, pos0:pos0 + cnt_, :].rearrange("t (h c) -> t h c", c=65), vst[p0:p0 + cnt_, j, 6:12, :], 'sv', reads=['vst'])
                        P.dma(stq(), g.v_diff[sq_, pos0:pos0 + cnt_, :].rearrange("t (h c) -> t h c", c=65), vst[p0:p0 + cnt_, j, 12:16, :], 'sv', reads=['vst'])

    def phase_cache(l):
        ar.reset()
        g = grp[1]
        wk = ar.alloc([2, 384], BF16); wv = ar.alloc([2, 384], BF16)
        P.dma('sp', wk, wb_uk[l].rearrange("(k p) n -> p k n", p=128), 'c0', writes=['wk'])
        P.dma('sp', wv, wb_uv[l].rearrange("(k p) n -> p k n", p=128), 'c0', writes=['wv'])
        ld = [ar.alloc([8, 384], F32) for _ in range(2)]
        lb = [ar.alloc([8, 384], BF16) for _ in range(2)]
        kTs = [ar.alloc([1024], BF16) for _ in range(2)]
        vst = ar.alloc([8, 6, 65], BF16)
        ckT = ar.alloc([2, 1024], BF16)
        lf_c = ar.alloc([8, 6], F32)
        A('pool', 'memset', vst, 1.0, w=['vst'])
        rr = [0]
        for b in range(NBS):
            def load(src, ncol):
                i = rr[0] % 2
                rr[0] += 1
                P.dma('sp', ld[i][:, :, 0:ncol], src[l, b].rearrange("(j p) d -> p j d", p=128), 'cl%d' % i, writes=['ld%d' % i])
                A('dve', 'tensor_copy', out=lb[i][:, :, 0:ncol], in_=ld[i][:, :, 0:ncol], r=['ld%d' % i], w=['lb%d' % i])
                return i

            def transp(i, c0, m, dstT, dkey, dst_dram):
                pb = 6 + rr[0] % 2
                rr[0] += 1
                for j in range(8):
                    A('pe', 'transpose', bkb(pb)[0:m, j * 128:(j + 1) * 128], lb[i][:, j, c0:c0 + m], ident,
                      r=['lb%d' % i, 'ident'], w=[pk(pb)])
                A('act', 'activation', out=dstT[0:m, :], in_=bkb(pb)[0:m, :], func=AF.Copy, r=[pk(pb)], w=[dkey])
                if dst_dram is not None:
                    P.dma('pool', dst_dram, dstT[0:m, :], 'sc', reads=[dkey])

            def vstore(i, nh, dst):
                A('dve', 'tensor_copy', out=vst[:, :, 0:nh, 0:64], in_=lb[i][:, :, 0:nh * 64].rearrange("p j (h d) -> p j h d", d=64),
                  r=['lb%d' % i], w=['vst'])
                P.dma('pool', dst[b, 0:PAST, :].rearrange("(j p) (h c) -> p j h c", p=128, c=65), vst[:, :, 0:nh, :], 'sc', reads=['vst'])

            i = load(c_fk, 384)
            for c in range(3):
                kt = kTs[rr[0] % 2]; kk = 'kTs%d' % (rr[0] % 2)
                transp(i, c * 128, 128, kt, kk, g.kT_fox[b, c * 128:(c + 1) * 128, 0:PAST])
            i = load(c_dk, 256)
            for c in range(2):
                kt = kTs[rr[0] % 2]; kk = 'kTs%d' % (rr[0] % 2)
                transp(i, c * 128, 128, kt, kk, g.kT_diff[b, c * 128:(c + 1) * 128, 0:PAST])
            i = load(c_kr, 32)
            kt = kTs[rr[0] % 2]; kk = 'kTs%d' % (rr[0] % 2)
            transp(i, 0, 32, kt, kk, g.kT_rope[b, :, 0:PAST])
            i = load(c_fv, 384)
            vstore(i, 6, g.v_fox)
            i = load(c_dv, 256)
            vstore(i, 4, g.v_diff)
            i = load(c_ckv, 256)
            for c in range(2):
                transp(i, c * 128, 128, ckT[:, c, :], 'ckT', None)
            for c in range(3):
                for hf in range(2):
                    pb = 3 + rr[0] % 2
                    rr[0] += 1
                    for kc in range(2):
                        A('pe', 'matmul', bk(pb)[:, 0:512], lhsT=wk[:, kc, c * 128:(c + 1) * 128],
                                                                               rhs=ckT[:, kc, hf * 512:(hf + 1) * 512], start=(kc == 0), stop=(kc == 1),
                          r=['wk', 'ckT'], w=[pk(pb)])
                    kt = kTs[rr[0] % 2]; kk = 'kTs%d' % (rr[0] % 2)
                    A('act', 'activation', out=kt[:, 0:512], in_=bk(pb)[:, 0:512], func=AF.Copy, r=[pk(pb)], w=[kk])
                    P.dma('pool', g.kT_nope[b, c * 128:(c + 1) * 128, hf * 512:(hf + 1) * 512], kt[:, 0:512], 'sc', reads=[kk])
            for j in range(8):
                pb = rr[0] % 3
                rr[0] += 1
                for kc in range(2):
                    A('pe', 'matmul', bk(pb)[:, 0:384], lhsT=ckT[:, kc, j * 128:(j + 1) * 128], rhs=wv[:, kc, :],
                                                                   start=(kc == 0), stop=(kc == 1), r=['wv', 'ckT'], w=[pk(pb)])
                A('dve', 'tensor_copy', out=vst[:, j, 0:6, 0:64], in_=bk(pb)[:, 0:384].rearrange("p (h d) -> p h d", d=64),
                  r=[pk(pb)], w=['vst'])
            P.dma('pool', g.v_mla[b, 0:PAST, :].rearrange("(j p) (h c) -> p j h c", p=128, c=65), vst[:, :, 0:6, :], 'sc', reads=['vst'])
            P.dma('sp', lf_c, c_lf[l, b].rearrange("(j p) h -> p j h", p=128), 'cl2', writes=['lf_c'])
            for j in range(8):
                prev = zero6 if j == 0 else ctok_s[:, b, j - 1, :]
                A('pe', 'matmul', bk(5)[:, 0:6], lhsT=U_f, rhs=lf_c[:, j, :], start=True, stop=False, r=['U', 'lf_c'], w=[pk(5)])
                A('pe', 'matmul', bk(5)[:, 0:6], lhsT=E_f, rhs=prev, start=False, stop=True, r=['E', 'ctoks', 'zero6'], w=[pk(5)])
                A('dve', 'tensor_copy', out=ctok_s[:, b, j, :], in_=bk(5)[:, 0:6], r=[pk(5)], w=['ctoks'])

    def phase_p2(l):
        ar.reset()
        lam_init = 0.8 - 0.6 * math.exp(-0.3 * l)
        lq = ar.alloc([4, 32], F32, parts=64)
        lt = ar.alloc([8], F32, parts=64)
        negl = ar.alloc([1], F32, parts=64)
        gsub = ar.alloc([1], F32, parts=64)
        P.dma('sp', lq, lam4[l:l + 1].to_broadcast([64, 4, 32]), 'c0', writes=['lq'])
        P.dma('sp', gsub, g_sub[l].rearrange("(d o) -> d o", o=1), 'c0', writes=['gsub'], allow_slow_non_contiguous=True)
        A('dve', 'tensor_tensor', out=lq[:, 0, :], in0=lq[:, 0, :], in1=lq[:, 1, :], op=ALU.mult, r=['lq'], w=['lq'])
        A('dve', 'tensor_tensor', out=lq[:, 2, :], in0=lq[:, 2, :], in1=lq[:, 3, :], op=ALU.mult, r=['lq'], w=['lq'])
        A('dve', 'reduce_sum', out=lt[:, 0:1], in_=lq[:, 0, :], axis=mybir.AxisListType.X, r=['lq'], w=['lt'])
        A('dve', 'reduce_sum', out=lt[:, 1:2], in_=lq[:, 2, :], axis=mybir.AxisListType.X, r=['lq'], w=['lt'])
        A('act', 'activation', out=lt[:, 0:2], in_=lt[:, 0:2], func=AF.Exp, r=['lt'], w=['lt'])
        A('dve', 'tensor_tensor', out=lt[:, 2:3], in0=lt[:, 1:2], in1=lt[:, 0:1], op=ALU.subtract, r=['lt'], w=['lt'])
        A('dve', 'tensor_scalar_add', out=negl, in0=lt[:, 2:3], scalar1=-lam_init, r=['lt'], w=['negl'])
        A('dve', 'tensor_scalar_mul', out=gsub, in0=gsub, scalar1=(1.0 - lam_init), r=['gsub'], w=['gsub'])

        NKBmax = max(g.tk for g in grp)
        NKBmax = (NKBmax + 127) // 128
        TKmax = max(g.tk for g in grp); TQmax = max(g.tq for g in grp)
        slots = []
        for s in range(2):
            slots.append((ar.alloc([TKmax], BF16), ar.alloc([TQmax], BF16), ar.alloc([NKBmax, 128], BF16)))
            A('pool', 'memset', slots[s][2][:, :, 64:128], 1.0, w=['u%d' % s])
            A('pool', 'memset', slots[s][0], 0.0, w=['u%d' % s])
            A('pool', 'memset', slots[s][1], 0.0, w=['u%d' % s])
        slot_hw = [0, 0]
        pT = [ar.alloc([512], BF16) for _ in range(4)]
        rl = ar.alloc([512], F32)
        on2 = ar.alloc([512], F32, parts=64)
        oc = ar.alloc([512], F32, parts=64)
        sqd = ar.alloc([512], F32, parts=64)
        rs2 = ar.alloc([512], F32, parts=64)
        mst = [ar.alloc([512], BF16, parts=64) for _ in range(2)]
        fbias = ar.alloc([NKBmax], F32)
        crefp = ar.alloc([NT, 6], F32)
        crefs = ar.alloc([max(NBS, 1), 6], F32)
        A('pe', 'matmul', bk(5)[:, 0:NT * 6], lhsT=E_f, rhs=ctok_p.rearrange("p a b -> p (a b)"), start=True, stop=True,
          r=['E', 'ctok'], w=[pk(5)])
        A('dve', 'tensor_copy', out=crefp.rearrange("p a b -> p (a b)"), in_=bk(5)[:, 0:NT * 6], r=[pk(5)], w=['crefp'])
        for b_ in range(NBS):
            A('pe', 'matmul', bk(5)[:, 0:6], lhsT=E_f, rhs=ctok_s[:, b_, NKB_S - 2, :], start=True, stop=True, r=['E', 'ctoks'], w=[pk(5)])
            A('dve', 'tensor_copy', out=crefs[:, b_, :], in_=bk(5)[:, 0:6], r=[pk(5)], w=['crefs'])

        units = []
        for g in grp:
            for sq_ in range(g.nseq):
                for h in range(6):
                    units.append(dict(g=g, s=sq_, kind='fox', h=h, rows=64, scale=0.125, mask=0, row0=64 * h,
                                      kT=[(g.kT_fox[sq_, 64 * h:64 * h + 64, :], 0, 64)], qT=g.qT_fox[sq_, 64 * h:64 * h + 64, :],
                                      v=g.v_fox[sq_, :, 65 * h:65 * h + 65]))
                for h in range(6):
                    units.append(dict(g=g, s=sq_, kind='mla', h=h, rows=96, scale=96 ** -0.5, mask=1, row0=384 + 64 * h,
                                      kT=[(g.kT_nope[sq_, 64 * h:64 * h + 64, :], 0, 64), (g.kT_rope[sq_, :, :], 64, 32)],
                                      qT=g.qT_mla[sq_, 96 * h:96 * h + 96, :], v=g.v_mla[sq_, :, 65 * h:65 * h + 65]))
                for h in range(4):
                    for s2 in range(2):
                        r0 = 64 * h + 32 * s2
                        units.append(dict(g=g, s=sq_, kind='diff', h=h, s2=s2, rows=32, scale=32 ** -0.5, mask=2 + h, row0=768 + 64 * h,
                                          kT=[(g.kT_diff[sq_, r0:r0 + 32, :], 0, 32)], qT=g.qT_diff[sq_, r0:r0 + 32, :],
                                          v=g.v_diff[sq_, :, 65 * h:65 * h + 65]))

        def load_unit(ui):
            u = units[ui]
            g = u['g']
            kt, qt, vt = slots[ui % 2]
            sk = 'u%d' % (ui % 2)
            if u['rows'] < slot_hw[ui % 2]:
                a0 = u['rows']
                while a0 < slot_hw[ui % 2]:
                    a1 = min(128, a0 + (32 if a0 % 64 else 64))
                    A('pool', 'memset', qt[a0:a1, :], 0.0, w=[sk])
                    a0 = a1
            slot_hw[ui % 2] = u['rows']
            for (src, p0, n) in u['kT']:
                P.dma('sp', kt[p0:p0 + n, 0:g.tk], src, 'ul%d' % (ui % 2), writes=[sk])
            P.dma('sp', qt[0:u['rows'], 0:g.tq], u['qT'], 'ul%d' % (ui % 2), writes=[sk])
            nkb = g.tk // 128
            for k0 in range(0, nkb, 16):
                k1 = min(nkb, k0 + 16)
                P.dma('sp', vt[:, k0:k1, 0:64], u['v'][k0 * 128:k1 * 128, 0:64].rearrange("(j p) c -> p j c", p=128), 'ul%d' % (ui % 2), writes=[sk])
            rem = g.tk - nkb * 128
            if rem:
                P.dma('sp', vt[0:rem, nkb, 0:64], u['v'][nkb * 128:g.tk, 0:64], 'ul%d' % (ui % 2), writes=[sk])

        steps = []
        grpno = 0
        for ui, u in enumerate(units):
            g = u['g']
            if g.gi == 0:
                for i in range(NQT):
                    nkb = 4 * i + 4
                    for kb in range(nkb):
                        j = kb - 4 * i
                        c0 = 128 * j if j > 0 else 0
                        steps.append(dict(ui=ui, i=i, kb=kb, nk=128, c0=c0, W=512, first=(kb == 0), last=(kb == nkb - 1),
                                          diag=(j if j >= 0 else None), grp=grpno, q0=512 * i, al=(kb - 4 * i - 2) + (4 * NQT - 2)))
                    grpno += 1
            else:
                nkb = NKB_S
                for kb in range(nkb):
                    nk = min(128, g.tk - 128 * kb)
                    steps.append(dict(ui=ui, i=0, kb=kb, nk=nk, c0=0, W=TS, first=(kb == 0), last=(kb == nkb - 1),
                                      diag=(0 if kb == nkb - 1 else None), grp=grpno, q0=0, al=4 * NQT + kb))
                    grpno += 1 if kb == nkb - 1 else 0

        last_step_of = {}
        for n_, st_ in enumerate(steps):
            last_step_of[st_['ui']] = n_
        for ui_ in range(min(2, len(units))):
            load_unit(ui_)

        fb_state = [None]

        def emit_S(n):
            st = steps[n]
            u = units[st['ui']]
            kt, qt, vt = slots[st['ui'] % 2]
            sk = 'u%d' % (st['ui'] % 2)
            pb = (0, 1, 2, 5)[n % 4]
            rows, nk, c0, W, kb = u['rows'], st['nk'], st['c0'], st['W'], st['kb']
            qc = st['q0'] if u['g'].gi == 0 else 0
            A('pe', 'matmul', bk(pb)[0:nk, c0:W], lhsT=kt[:, kb * 128:kb * 128 + nk], rhs=qt[:, qc + c0:qc + W],
                                       start=True, stop=True, r=[sk], w=[pk(pb)])

        def fox_bias(st, u):
            key = (st['ui'], st['i'])
            if fb_state[0] == key:
                return
            fb_state[0] = key
            g, h = u['g'], u['h']
            if g.gi == 0:
                blk = 4 * st['i'] + 1
                A('dve', 'tensor_scalar', out=fbias[:, 0:NT], in0=ctok_p[:, :, h], scalar1=-1.0, scalar2=crefp[:, blk, h:h + 1],
                                                   op0=ALU.mult, op1=ALU.add, r=['ctok', 'crefp'], w=['fbias'])
            else:
                sq_ = u['s']
                A('dve', 'tensor_scalar', out=fbias[:, 0:NKB_S], in0=ctok_s[:, sq_, :, h], scalar1=-1.0, scalar2=crefs[:, sq_, h:h + 1],
                                                   op0=ALU.mult, op1=ALU.add, r=['ctoks', 'crefs'], w=['fbias'])

        def emit_rest(n):
            st = steps[n]
            u = units[st['ui']]
            g = u['g']
            kt, qt, vt = slots[st['ui'] % 2]
            sk = 'u%d' % (st['ui'] % 2)
            pb = (0, 1, 2, 5)[n % 4]
            nk, c0, W, kb = st['nk'], st['c0'], st['W'], st['kb']
            pt = pT[n % 4]; ptk = 'pT%d' % (n % 4)
            ob = (3, 4, 7)[st['grp'] % 3]
            if u['kind'] == 'fox':
                fox_bias(st, u)
                bias = fbias[0:nk, kb:kb + 1]
                A('act', 'activation', out=pt[0:nk, c0:W], in_=bk(pb)[0:nk, c0:W], func=AF.Exp, bias=bias, scale=u['scale'],
                  r=[pk(pb), 'fbias'], w=[ptk])
            elif u['kind'] == 'diff':
                ai = u['h'] * NAL + st['al']
                bias = al_t[0:nk, ai:ai + 1]
                A('act', 'activation', out=pt[0:nk, c0:W], in_=bk(pb)[0:nk, c0:W], func=AF.Exp, bias=bias, scale=u['scale'],
                  r=[pk(pb), 'al'], w=[ptk])
            else:
                A('act', 'activation', out=pt[0:nk, c0:W], in_=bk(pb)[0:nk, c0:W], func=AF.Exp, scale=u['scale'],
                  r=[pk(pb)], w=[ptk])
            if st['diag'] is not None:
                mw = min(128, W - c0)
                m = masks[0:nk, u['mask'], 0:mw]
                A('dve', 'tensor_tensor', out=pt[0:nk, c0:c0 + mw], in0=pt[0:nk, c0:c0 + mw], in1=m, op=ALU.mult,
                  r=[ptk, 'masks'], w=[ptk])
            A('pe', 'matmul', bk(ob)[:, c0:W], lhsT=vt[0:nk, kb, :], rhs=pt[0:nk, c0:W], start=st['first'], stop=st['last'],
              r=[sk, ptk], w=[pk(ob)])
            if last_step_of[st['ui']] == n and st['ui'] + 2 < len(units):
                load_unit(st['ui'] + 2)
            if not st['last']:
                return
            bcs = rl[64:128, :]
            A('dve', 'reciprocal', out=rl[64:128, 0:W], in_=bk(ob)[64:128, 0:W], r=[pk(ob)], w=['bcs'])
            tok0 = (u['s'] * g.tq if g.gi == 1 else 0) + st['q0']
            if u['kind'] != 'diff':
                ms = mst[st['grp'] % 2]; msk = 'mst%d' % (st['grp'] % 2)
                A('dve', 'tensor_tensor', out=ms[:, 0:W], in0=bk(ob)[0:64, 0:W], in1=bcs[:, 0:W], op=ALU.mult,
                  r=[pk(ob), 'bcs'], w=[msk])
                P.dma('pool', g.mixT[u['row0']:u['row0'] + 64, tok0:tok0 + W], ms[:, 0:W], 'sm', reads=[msk])
            elif u['s2'] == 0:
                A('dve', 'tensor_tensor', out=on1s[st['i']][:, 0:W], in0=bk(ob)[0:64, 0:W], in1=bcs[:, 0:W], op=ALU.mult,
                  r=[pk(ob), 'bcs'], w=['on1%d' % st['i']])
            else:
                A('dve', 'tensor_tensor', out=on2[:, 0:W], in0=bk(ob)[0:64, 0:W], in1=bcs[:, 0:W], op=ALU.mult,
                  r=[pk(ob), 'bcs'], w=['on2'])
                A('dve', 'scalar_tensor_tensor', out=oc[:, 0:W], in0=on2[:, 0:W], scalar=negl[:, 0:1], in1=on1s[st['i']][:, 0:W],
                                                          op0=ALU.mult, op1=ALU.add, r=['on2', 'negl', 'on1%d' % st['i']], w=['oc'])
                A('dve', 'tensor_tensor', out=sqd[:, 0:W], in0=oc[:, 0:W], in1=oc[:, 0:W], op=ALU.mult, r=['oc'], w=['sqd'])
                A('pe', 'matmul', bk(6)[0:64, 0:W], lhsT=ones_f[0:64, 0:64], rhs=sqd[:, 0:W], start=True, stop=True,
                  r=['ones', 'sqd'], w=[pk(6)])
                A('act', 'activation', out=rs2[:, 0:W], in_=bk(6)[0:64, 0:W], func=AF.Ln, bias=eps_t[0:64, 0:1], scale=1.0 / 64,
                  r=[pk(6), 'eps'], w=['rs2'])
                A('act', 'activation', out=rs2[:, 0:W], in_=rs2[:, 0:W], func=AF.Exp, scale=-0.5, r=['rs2'], w=['rs2'])
                ms = mst[st['grp'] % 2]; msk = 'mst%d' % (st['grp'] % 2)
                A('dve', 'scalar_tensor_tensor', out=ms[:, 0:W], in0=oc[:, 0:W], scalar=gsub[:, 0:1], in1=rs2[:, 0:W],
                                                          op0=ALU.mult, op1=ALU.mult, r=['oc', 'gsub', 'rs2'], w=[msk])
                P.dma('pool', g.mixT[u['row0']:u['row0'] + 64, tok0:tok0 + W], ms[:, 0:W], 'sm', reads=[msk])

        on1s = [ar.alloc([512], F32, parts=64) for _ in range(max(NQT, 1))]

        LA = 3
        for n in range(len(steps) + LA):
            if n < len(steps):
                emit_S(n)
            if n - LA >= 0:
                emit_rest(n - LA)

    def phase_p3(l):
        ar.reset()
        wdn = ar.alloc([32, D], BF16)
        wms = ar.alloc([8, D], BF16)
        wpj = ar.alloc([2, D], BF16)
        wus = [ar.alloc([8, 512], BF16) for _ in range(2)]
        P.dma('sp', wdn, wb_down[l].rearrange("(c p) n -> p c n", p=128), 'c0', writes=['wdn'])
        P.dma('sp', wpj, wb_proj[l].rearrange("(c p) n -> p c n", p=128), 'c0', writes=['wpj'])
        gb = [ar.alloc([D], F32) for _ in range(3)]
        for i, gi_ in enumerate((1, 3, 5)):
            P.dma('sp', gb[i], g_norm[l, gi_:gi_ + 1, :].to_broadcast([128, D]), 'c0', writes=['gb%d' % i])
        cwt = ar.alloc([3, 32], F32); cbt = ar.alloc([32], F32)
        for jj in range(3):
            P.dma('sp', cwt[:, jj, :], cw[l, jj].rearrange("(c p) -> p c", p=128), 'c0', writes=['cwt'], allow_slow_non_contiguous=True)
        P.dma('sp', cbt, cb[l].rearrange("(c p) -> p c", p=128), 'c0', writes=['cbt'], allow_slow_non_contiguous=True)
        aT = ar.alloc([32, 512], BF16)
        h_t = ar.alloc([4, D], F32)
        XT = ar.alloc([8, 512], BF16)
        ytmp = ar.alloc([D], F32)
        tpl = ar.alloc([D], F32)
        nseg_max = max(1, NBS)
        gext = [ar.alloc([516 + 2 * nseg_max], F32) for _ in range(2)]
        tcv = ar.alloc([512], F32)
        ugl = ar.alloc([512], F32)
        xnb = [ar.alloc([D], BF16)] * 2
        p_t = ar.alloc([4, PLE], F32)
        p_b = ar.alloc([PLE], BF16)
        pTt = ar.alloc([2, 128], BF16)
        ss = ar.alloc([16], F32)
        carry = ar.alloc([32, nseg_max, 2], F32)
        cso = ar.alloc([32, nseg_max, 2], F32)
        rr = [0]

        def resid_norm(j, src_ap, src_keys, gbi, sscol):
            A('pool', 'memset', ss[:, sscol:sscol + 1], 0.0, w=['ss%d' % sscol])
            A('act', 'activation', out=tpl_junk, in_=src_ap, func=AF.Square, accum_out=ss[:, sscol:sscol + 1],
              r=src_keys + ['ss%d' % sscol], w=['junk', 'ss%d' % sscol])
            rstd_from_ss(ss[:, sscol:sscol + 1], D, 'ss%d' % sscol)
            A('dve', 'scalar_tensor_tensor', out=src_ap, in0=src_ap, scalar=ss[:, sscol:sscol + 1], in1=gb[gbi],
                                                      op0=ALU.mult, op1=ALU.mult, r=src_keys + ['ss%d' % sscol, 'gb%d' % gbi], w=src_keys)
            A('dve', 'tensor_tensor', out=h_t[:, j, :], in0=h_t[:, j, :], in1=src_ap, op=ALU.add, r=src_keys + ['h%d' % j], w=['h%d' % j])

        def prenorm_T(j, sscol):
            x_b = xnb[0]; xk = 'xnb'
            A('pool', 'memset', ss[:, sscol:sscol + 1], 0.0, w=['ss%d' % sscol])
            A('act', 'activation', out=tpl_junk, in_=h_t[:, j, :], func=AF.Square, accum_out=ss[:, sscol:sscol + 1],
              r=['h%d' % j, 'ss%d' % sscol], w=['junk', 'ss%d' % sscol])
            rstd_from_ss(ss[:, sscol:sscol + 1], D, 'ss%d' % sscol)
            A('dve', 'tensor_scalar_mul', out=x_b, in0=h_t[:, j, :], scalar1=ss[:, sscol:sscol + 1], r=['h%d' % j, 'ss%d' % sscol], w=[xk])
            pb = 2 + (j % 2)
            for kc in range(8):
                A('pe', 'transpose', bkb(pb)[:, kc * 128:(kc + 1) * 128], x_b[:, kc * 128:(kc + 1) * 128], ident,
                  r=[xk, 'ident'], w=[pk(pb)])
            A('act', 'activation', out=XT[:, :, j * 128:(j + 1) * 128], in_=bkb(pb).rearrange("p (a b) -> p a b", b=128), func=AF.Copy,
              r=[pk(pb)], w=['XT%d' % j])

        tpl_junk = ar.alloc([D], BF16)

        for g in grp:
            ntile = (g.ntok + 511) // 512
            nseg = 1 if g.gi == 0 else None
            if g.gi == 0:
                A('pool', 'memset', carry, 0.0, w=['carry'])
            else:
                for b in range(NBS):
                    for jj in range(2):
                        P.dma('sp', carry[:, :, b, jj], c_cs[l, b, jj].rearrange("(c p) -> p c", p=128), 'c0', writes=['carry'],
                              allow_slow_non_contiguous=True)
            for ti in range(ntile):
                t0 = ti * 512
                W = min(512, g.ntok - t0)
                nsub = W // 128
                nsg = 1 if g.gi == 0 else W // TS
                segW = W // nsg
                src = g.x if l == 0 else g.hbuf
                dst = g.y if l == L - 1 else g.hbuf
                for j in range(nsub):
                    P.dma('sp', XT[:, :, j * 128:(j + 1) * 128], g.mixT[:, t0 + j * 128:t0 + (j + 1) * 128].rearrange("(k p) t -> p k t", p=128),
                          'p3m%d' % j, writes=['XT%d' % j])
                    P.dma('sp', h_t[:, j, :], src[t0 + j * 128:t0 + (j + 1) * 128, :], 'p3h%d' % j, writes=['h%d' % j])
                P.dma('sp', p_t[:, 0:nsub, :], g.pin[l, t0:t0 + W, :].rearrange("(j p) d -> p j d", p=128), 'p3p', writes=['p_t'])
                P.dma('sp', wms, wb_out[l].rearrange("(k p) n -> p k n", p=128), 'p3w', writes=['wms'])
                for j in range(nsub):
                    for hf in range(2):
                        pb = hf
                        for kc in range(8):
                            A('pe', 'matmul', bk(hf)[:, 0:512], lhsT=XT[:, kc, j * 128:(j + 1) * 128],
                                                                           rhs=wms[:, kc, hf * 512:(hf + 1) * 512], start=(kc == 0), stop=(kc == 7),
                              r=['XT%d' % j, 'wms'], w=[pk(hf)])
                        A('act', 'activation', out=ytmp[:, hf * 512:(hf + 1) * 512], in_=bk(hf)[:, 0:512], func=AF.Copy,
                          r=[pk(hf)], w=['ytmp'])
                    resid_norm(j, ytmp, ['ytmp'], 0, 0)
                for j in range(nsub):
                    prenorm_T(j, 1)
                xkeys = ['XT%d' % j for j in range(nsub)]
                for c in range(32):
                    if c % 2 == 0:
                        gi2 = (c // 2) % 2
                        P.dma('sp', wus[gi2], wb_up[l][:, (c // 2) * 512:(c // 2 + 1) * 512].rearrange("(k p) n -> p k n", p=128),
                              'p3u%d' % gi2, writes=['wus%d' % gi2])
                    wu = wus[(c // 2) % 2]; wuk = 'wus%d' % ((c // 2) % 2)
                    off = (c % 2) * 256
                    pg, pv = 4 + 2 * (c % 2), 5 + 2 * (c % 2)
                    for kc in range(8):
                        A('pe', 'matmul', bk(pg)[:, 0:W], lhsT=wu[:, kc, off:off + 128], rhs=XT[:, kc, 0:W],
                                                                                 start=(kc == 0), stop=(kc == 7), r=[wuk] + xkeys, w=[pk(pg)])
                    for kc in range(8):
                        A('pe', 'matmul', bk(pv)[:, 0:W], lhsT=wu[:, kc, off + 128:off + 256], rhs=XT[:, kc, 0:W],
                                                                                 start=(kc == 0), stop=(kc == 7), r=[wuk] + xkeys, w=[pk(pv)])
                    ge = gext[c % 2]; gk = 'gext%d' % (c % 2)
                    gv = ge[:, 0:nsg * (segW + 2)].rearrange("p (s w) -> p s w", w=segW + 2)
                    A('pool', 'tensor_copy', out=gv[:, :, 0:2], in_=carry[:, c, 0:nsg, :], r=['carry'], w=[gk])
                    A('act', 'activation', out=gv[:, :, 2:2 + segW], in_=bk(pg)[:, 0:W].rearrange("p (s w) -> p s w", w=segW),
                                                                  func=AF.Copy, r=[pk(pg)], w=[gk])
                    A('pool', 'tensor_copy', out=carry[:, c, 0:nsg, :], in_=gv[:, :, segW:segW + 2], r=[gk], w=['carry'])
                    if ti == ntile - 1:
                        A('pool', 'tensor_copy', out=cso[:, c, 0:nsg, :], in_=gv[:, :, segW:segW + 2], r=[gk], w=['cso'])
                    tv = tcv[:, 0:W].rearrange("p (s w) -> p s w", w=segW)
                    A('dve', 'tensor_scalar', out=tv, in0=gv[:, :, 0:segW], scalar1=cwt[:, 0, c:c + 1], scalar2=cbt[:, c:c + 1],
                                                                          op0=ALU.mult, op1=ALU.add, r=[gk, 'cwt', 'cbt'], w=['tcv'])
                    A('dve', 'scalar_tensor_tensor', out=tv, in0=gv[:, :, 1:1 + segW], scalar=cwt[:, 1, c:c + 1], in1=tv,
                                                                                 op0=ALU.mult, op1=ALU.add, r=[gk, 'cwt', 'tcv'], w=['tcv'])
                    A('dve', 'scalar_tensor_tensor', out=tv, in0=gv[:, :, 2:2 + segW], scalar=cwt[:, 2, c:c + 1], in1=tv,
                                                                                 op0=ALU.mult, op1=ALU.add, r=[gk, 'cwt', 'tcv'], w=['tcv'])
                    A('act', 'activation', out=ugl[:, 0:W], in_=tcv[:, 0:W], func=AF.Gelu_apprx_tanh, r=['tcv'], w=['ugl'])
                    A('dve', 'tensor_tensor', out=aT[:, c, 0:W], in0=ugl[:, 0:W], in1=bk(pv)[:, 0:W], op=ALU.mult,
                      r=['ugl', pk(pv)], w=['aT'])
                if ti == ntile - 1:
                    for b in range(nsg if g.gi == 1 else 1):
                        for jj in range(2):
                            P.dma('pool', o_cv[g.gi][l, b, jj].rearrange("(c p) -> p c", p=128), cso[:, :, b, jj], 'so', reads=['cso'],
                                  allow_slow_non_contiguous=True)
                P.dma('sp', wms, wb_gate[l].rearrange("(k p) n -> p k n", p=128), 'p3w', writes=['wms'])
                def st5_mm(j):
                    for hf in range(2):
                        for c in range(32):
                            A('pe', 'matmul', bk(hf)[:, 0:512], lhsT=aT[:, c, j * 128:(j + 1) * 128],
                                                                         rhs=wdn[:, c, hf * 512:(hf + 1) * 512], start=(c == 0), stop=(c == 31),
                              r=['aT', 'wdn'], w=[pk(hf)])

                def st5_ev(j):
                    for hf in range(2):
                        A('act', 'activation', out=ytmp[:, hf * 512:(hf + 1) * 512], in_=bk(hf)[:, 0:512], func=AF.Copy,
                          r=[pk(hf)], w=['ytmp'])
                    resid_norm(j, ytmp, ['ytmp'], 1, 2)

                def st6_rest(j):
                    prenorm_T(j, 3)
                    A('dve', 'tensor_copy', out=p_b, in_=p_t[:, j, :], r=['p_t'], w=['p_b'])
                    for kc in range(2):
                        A('pe', 'transpose', bkb(3)[:, kc * 128:(kc + 1) * 128], p_b[:, kc * 128:(kc + 1) * 128], ident,
                          r=['p_b', 'ident'], w=[pk(3)])
                    A('dve', 'tensor_copy', out=pTt, in_=bkb(3)[:, 0:256].rearrange("p (a b) -> p a b", b=128), r=[pk(3)], w=['pTt'])
                    for hf in range(2):
                        pg, pp_ = 4 + hf, 6 + hf
                        for kc in range(8):
                            A('pe', 'matmul', bk(pg)[:, 0:512], lhsT=XT[:, kc, j * 128:(j + 1) * 128],
                                                                                 rhs=wms[:, kc, hf * 512:(hf + 1) * 512], start=(kc == 0), stop=(kc == 7),
                              r=['XT%d' % j, 'wms'], w=[pk(pg)])
                        for kc in range(2):
                            A('pe', 'matmul', bk(pp_)[:, 0:512], lhsT=pTt[:, kc, :], rhs=wpj[:, kc, hf * 512:(hf + 1) * 512],
                                                                               start=(kc == 0), stop=(kc == 1), r=['pTt', 'wpj'], w=[pk(pp_)])
                        A('act', 'activation', out=tpl[:, hf * 512:(hf + 1) * 512], in_=bk(pg)[:, 0:512], func=AF.Sigmoid,
                          r=[pk(pg)], w=['tpl'])
                        A('dve', 'tensor_tensor', out=tpl[:, hf * 512:(hf + 1) * 512], in0=tpl[:, hf * 512:(hf + 1) * 512],
                                                                            in1=bk(pp_)[:, 0:512], op=ALU.mult, r=['tpl', pk(pp_)], w=['tpl'])
                    resid_norm(j, tpl, ['tpl'], 2, 4)
                    P.dma('pool', dst[t0 + j * 128:t0 + (j + 1) * 128, :], h_t[:, j, :], 'sh', reads=['h%d' % j], writes=['hb'])

                if OPT_P3PIPE:
                    st5_mm(0)
                for j in range(nsub):
                    if not OPT_P3PIPE:
                        st5_mm(j)
                    st5_ev(j)
                    if OPT_P3PIPE and j + 1 < nsub:
                        st5_mm(j + 1)
                    st6_rest(j)

    setup()
    prep_weights()
    P.barrier()
    for l in range(L):
        phase_cache(l)
        P.barrier()
        phase_p1(l)
        P.barrier()
        phase_p2(l)
        P.barrier()
        phase_p3(l)
        P.barrier()
    P.finish()
    return nc, P


_CACHE = {}


def _host_consts(S, NBS):
    NS = NBS * TS
    NQT = S // 512
    TK = PAST + TS
    NKB_S = (TK + 127) // 128
    k = np.arange(128)[:, None]
    q = np.arange(128)[None, :]
    slopes = 2.0 ** (-8.0 * np.arange(1, 5) / 4)
    masks = np.zeros((6, 128, 128), np.float32)
    masks[0] = (k <= q)
    chunk = ((k // 64) <= (q // 64)).astype(np.float32)
    masks[1] = chunk
    for h in range(4):
        masks[2 + h] = chunk * np.exp(-2.0 * slopes[h] * np.maximum(k - q, 0))
    U = (k <= q).astype(np.float32)
    E = np.zeros((128, 128), np.float32)
    E[127, :] = 1.0
    half = 16
    inv_freq = (10000.0 ** (-np.arange(half, dtype=np.float32) / half)).astype(np.float32)

    def cs(pos):
        ang = pos.astype(np.float32)[:, None] * inv_freq[None, :]
        c, s = np.cos(ang), np.sin(ang)
        return np.concatenate([c, c, -s, s], axis=1).astype(np.float32)
    cs_p = cs(np.arange(S))
    cs_s = cs(PAST + (np.arange(NS) % TS))
    NAL = 4 * NQT + NKB_S
    al = np.zeros((128, 4, NAL), np.float32)
    p = np.arange(128)
    for h in range(4):
        for e in range(4 * NQT):
            d = e - (4 * NQT - 2)
            al[:, h, e] = slopes[h] * (p + 128 * d)
        for kb in range(NKB_S):
            al[:, h, 4 * NQT + kb] = slopes[h] * (p + 128 * kb - (PAST + TS // 2))
    return dict(k_masks=masks, k_U=U, k_E=E, k_cs_p=cs_p, k_cst_p=np.ascontiguousarray(cs_p.T),
                k_cs_s=cs_s, k_cst_s=np.ascontiguousarray(cs_s.T), k_al=np.ascontiguousarray(al.reshape(128, 4 * NAL)))


def kernel(x_prompt, x_sample, cache_fox_k, cache_fox_v, cache_fox_logf, cache_mla_ckv,
           cache_mla_krope, cache_diff_k, cache_diff_v, state_ffn_conv, p_prompt, p_sample,
           w_in, b_forget, mla_q_norm, w_mla_uq, mla_kv_norm, w_mla_uk, w_mla_uv,
           diff_lambda_q1, diff_lambda_k1, diff_lambda_q2, diff_lambda_k2, diff_subln, w_out,
           norm_mix_pre, norm_mix_post, norm_ffn_pre, norm_ffn_post, norm_ple_pre, norm_ple_post,
           w_ffn_up, ffn_conv_w, ffn_conv_b, w_ffn_down, w_ple_gate, w_ple_proj):
    f = lambda a: np.ascontiguousarray(np.asarray(a, dtype=np.float32))
    B, S, _ = x_prompt.shape
    DB = x_sample.shape[0]
    L = w_in.shape[0]
    NBS = DB // NCORES
    NS = NBS * TS
    assert B * 2 == NCORES or B <= NCORES
    key = (S, NBS, L)
    if key not in _CACHE:
        _CACHE[key] = build(S, NBS, L)
    nc, _ = _CACHE[key]
    w_in = f(w_in)
    kr = w_in[:, :, 1798:1830]
    krs = np.concatenate([kr[:, :, 16:32], kr[:, :, 0:16]], axis=2)
    w_in_p = np.concatenate([w_in[:, :, 0:384], w_in[:, :, 1158:1542], w_in[:, :, 1830:2086], w_in[:, :, 384:768],
                             w_in[:, :, 1152:1158], np.zeros((L, D, 2), np.float32), w_in[:, :, 768:1152],
                             w_in[:, :, 1542:1798], kr, krs, w_in[:, :, 2086:2342], w_in[:, :, 2342:2598]], axis=2)
    assert w_in_p.shape[2] == NCOLS
    uq = f(w_mla_uq).reshape(L, 384, 6, 96)
    uq_p = np.concatenate([uq[..., 0:64], uq[..., 64:96], uq[..., 80:96], uq[..., 64:80]], axis=3).reshape(L, 384, 768)
    up = f(w_ffn_up)
    up_p = np.stack([up[:, :, :DFF].reshape(L, D, 32, 128), up[:, :, DFF:].reshape(L, D, 32, 128)], axis=3).reshape(L, D, 2 * DFF)
    shared = dict(
        w_in=np.ascontiguousarray(w_in_p), w_uq=np.ascontiguousarray(uq_p), w_uk=f(w_mla_uk), w_uv=f(w_mla_uv),
        w_out=f(w_out), w_up=np.ascontiguousarray(up_p), w_down=f(w_ffn_down), w_gate=f(w_ple_gate), w_proj=f(w_ple_proj),
        b_f=f(b_forget), g_q=f(mla_q_norm), g_kv=f(mla_kv_norm),
        lam4=np.ascontiguousarray(np.stack([f(diff_lambda_q1), f(diff_lambda_k1), f(diff_lambda_q2), f(diff_lambda_k2)], axis=1)),
        g_sub=f(diff_subln),
        g_norm=np.ascontiguousarray(np.stack([f(norm_mix_pre), f(norm_mix_post), f(norm_ffn_pre), f(norm_ffn_post),
                                              f(norm_ple_pre), f(norm_ple_post)], axis=1)),
        cw=f(ffn_conv_w), cb=f(ffn_conv_b))
    shared.update(_host_consts(S, NBS))
    xp, xs, pp, ps_ = f(x_prompt), f(x_sample), f(p_prompt), f(p_sample)
    cfk, cfv, clf = f(cache_fox_k), f(cache_fox_v), f(cache_fox_logf)
    cckv, ckr, cdk, cdv, ccs = f(cache_mla_ckv), f(cache_mla_krope), f(cache_diff_k), f(cache_diff_v), f(state_ffn_conv)
    in_maps = []
    for c in range(NCORES):
        b = c % B
        sb = slice(c * NBS, (c + 1) * NBS)
        m = dict(shared)
        m.update(xp=xp[b], xs=np.ascontiguousarray(xs[sb].reshape(NS, D)), pp=np.ascontiguousarray(pp[:, b]),
                 pss=np.ascontiguousarray(ps_[:, sb].reshape(L, NS, PLE)),
                 c_fk=np.ascontiguousarray(cfk[:, sb].reshape(L, NBS, PAST, 384)),
                 c_fv=np.ascontiguousarray(cfv[:, sb].reshape(L, NBS, PAST, 384)),
                 c_lf=np.ascontiguousarray(clf[:, sb]), c_ckv=np.ascontiguousarray(cckv[:, sb]),
                 c_kr=np.ascontiguousarray(ckr[:, sb]),
                 c_dk=np.ascontiguousarray(cdk[:, sb].reshape(L, NBS, PAST, 256)),
                 c_dv=np.ascontiguousarray(cdv[:, sb].reshape(L, NBS, PAST, 256)),
                 c_cs=np.ascontiguousarray(ccs[:, sb]))
        in_maps.append(m)
    res = run_bass_kernel_spmd(nc, in_maps, core_ids=list(range(NCORES)))
    R = res.results
    global _LAST
    _LAST = R
    pc = [R[b] for b in range(B)]
    y_prompt = np.stack([r["y_p"] for r in pc], 0)
    y_sample = np.concatenate([r["y_s"].reshape(NBS, TS, D) for r in R], 0)

    def pstack(name, tail):
        return np.stack([r[name] for r in pc], 1).reshape((L, B, S) + tail)

    def sstack(name, tail):
        return np.concatenate([r[name].reshape((L, NBS, TS) + tail) for r in R], 1)

    outs = (y_prompt, y_sample,
            pstack("o_fk_p", (6, 64)), pstack("o_fv_p", (6, 64)), pstack("o_lf_p", (6,)), pstack("o_ckv_p", (256,)),
            pstack("o_kr_p", (32,)), pstack("o_dk_p", (4, 64)), pstack("o_dv_p", (4, 64)),
            np.stack([r["o_cv_p"][:, 0] for r in pc], 1),
            sstack("o_fk_s", (6, 64)), sstack("o_fv_s", (6, 64)), sstack("o_lf_s", (6,)), sstack("o_ckv_s", (256,)),
            sstack("o_kr_s", (32,)), sstack("o_dk_s", (4, 64)), sstack("o_dv_s", (4, 64)),
            np.concatenate([r["o_cv_s"] for r in R], 1))
    return tuple(np.ascontiguousarray(o, dtype=np.float32) for o in outs)
```

```python
import contextlib
import math
import numpy as np
import concourse.bass as bass
import concourse.mybir as mybir
from concourse.bass_utils import run_bass_kernel_spmd

F32 = mybir.dt.float32
BF16 = mybir.dt.bfloat16
AF = mybir.ActivationFunctionType
ALU = mybir.AluOpType

D = 1024
DFF = 4096
PLE = 256
PAST = 1024
TS = 64
NCOLS = 2632
O_FQ, O_CQ, O_DQ, O_FK, O_FF, O_FV, O_CKV, O_KR, O_KRS, O_DK, O_DV = (
    0, 384, 768, 1024, 1408, 1416, 1800, 2056, 2088, 2120, 2376)
EPS = 1e-6
DEBUG = False
OPT_STQ = False
OPT_P3PIPE = True
_LAST = None
NCORES = 8


class Prog:
    ENG = ('pe', 'act', 'dve', 'pool', 'sp')
    LIM = 30000
    LIMD = 16 * 1800

    def __init__(self, nc):
        self.nc = nc
        self.stack = contextlib.ExitStack()
        self.rec = {e: [] for e in self.ENG}
        self.seq = {e: 0 for e in self.ENG}
        self.dcount = {}
        self.waited = {e: {} for e in self.ENG}
        self.lastw = {}
        self.readers = {}

    def sb(self, name, shape, dtype):
        return self.stack.enter_context(self.nc.sbuf_tensor(name, list(shape), dtype))

    def ps(self, name, shape, dtype):
        return self.stack.enter_context(self.nc.psum_tensor(name, list(shape), dtype))

    def _deps(self, reads, writes, eng):
        deps = []
        for k in reads:
            d = self.lastw.get(k)
            if d is not None:
                deps.append(d)
            if k.startswith('ps'):
                r = self.readers.get(k)
                if r:
                    deps.extend((sk, v) for sk, v in r.items() if sk != ('E', eng))
        for k in writes:
            d = self.lastw.get(k)
            if d is not None:
                deps.append(d)
            r = self.readers.get(k)
            if r:
                deps.extend(r.items())
        return deps

    def _emit_waits(self, eng, deps):
        need = {}
        w = self.waited[eng]
        for (sk, v) in deps:
            if sk[0] == 'E':
                if sk[1] == 'pe' and eng == 'pe':
                    continue
            else:
                v = self.dcount[sk[1]]
            if w.get(sk, 0) >= v:
                continue
            if need.get(sk, 0) < v:
                need[sk] = v
        for sk, v in need.items():
            self.rec[eng].append(('w', sk, v, w.get(sk, 0)))
            w[sk] = v

    def _record(self, dep, reads, writes):
        sk, v = dep
        for k in reads:
            self.readers.setdefault(k, {})[sk] = v
        for k in writes:
            self.lastw[k] = dep
            self.readers[k] = {}

    def op(self, eng, meth, args, kw, reads=(), writes=()):
        self._emit_waits(eng, self._deps(reads, writes, eng))
        self.seq[eng] += 1
        n = self.seq[eng]
        self.rec[eng].append(('o', (meth, args, kw), n))
        self._record((('E', eng), n), reads, writes)

    def dma(self, eng, out, in_, sem, reads=(), writes=(), **kw):
        if sem != 'c0':
            sem = ('S' + reads[0]) if reads else ('L' + writes[0])
        self._emit_waits(eng, self._deps(reads, writes, eng))
        c = self.dcount.get(sem, 0) + 16
        self.dcount[sem] = c
        self.rec[eng].append(('d', out, in_, kw, sem, c))
        self._record((('D', sem), c), reads, writes)

    def barrier(self):
        deps = [(('D', n), c) for n, c in self.dcount.items()]
        deps += [(('E', e), self.seq[e]) for e in ('pe', 'act', 'dve', 'pool') if self.seq[e] > 0]
        for e in self.ENG:
            self._emit_waits(e, [d for d in deps if d[0] != ('E', e)])
        self.lastw = {}
        self.readers = {}

    def finish(self):
        nc = self.nc
        deps = [(('D', n), c) for n, c in self.dcount.items()]
        deps += [(('E', e), self.seq[e]) for e in ('pe', 'act', 'dve', 'pool') if self.seq[e] > 0]
        self._emit_waits('sp', deps)
        sig = {e: set() for e in self.ENG}
        for e in self.ENG:
            for r in self.rec[e]:
                if r[0] == 'w' and r[1][0] == 'E':
                    sig[r[1][1]].add(r[2])
        rank, esems = {}, {}
        for e in self.ENG:
            s = sorted(sig[e])
            rank[e] = {n: i for i, n in enumerate(s)}
            nep = (len(s) + self.LIM - 1) // self.LIM
            esems[e] = [self.stack.enter_context(nc.semaphore("es_%s_%d" % (e, k))) for k in range(nep)]
        dsems = {}
        for n, c in self.dcount.items():
            nep = (c + self.LIMD - 1) // self.LIMD
            dsems[n] = [self.stack.enter_context(nc.semaphore("ds_%s_%d" % (n, k))) for k in range(nep)]
        self.nsem = sum(len(v) for v in esems.values()) + sum(len(v) for v in dsems.values())
        self.ninst = sum(len(v) for v in self.rec.values())
        LIM, LIMD = self.LIM, self.LIMD

        def run(eng, e):
            for r in self.rec[eng]:
                if r[0] == 'w':
                    sk, v, prev = r[1], r[2], r[3]
                    if sk[0] == 'E':
                        i = rank[sk[1]][v]
                        e.wait_ge(esems[sk[1]][i // LIM], (i % LIM) + 1)
                    else:
                        k = (v - 1) // LIMD
                        if k > 0 and prev < k * LIMD:
                            e.wait_ge(dsems[sk[1]][k - 1], LIMD)
                        e.wait_ge(dsems[sk[1]][k], v - k * LIMD)
                elif r[0] == 'o':
                    meth, args, kw = r[1]
                    ins = getattr(e, meth)(*args, **kw)
                    i = rank[eng].get(r[2])
                    if i is not None:
                        ins.then_inc(esems[eng][i // LIM], 1)
                else:
                    _, out, in_, kw, sem, c = r
                    k = (c - 1) // LIMD
                    e.dma_start(out=out, in_=in_, **kw).then_inc(dsems[sem][k], 16)

        with nc.Block() as block:
            @block.tensor
            def _(e):
                run('pe', e)

            @block.scalar
            def _(e):
                run('act', e)

            @block.vector
            def _(e):
                run('dve', e)

            @block.gpsimd
            def _(e):
                run('pool', e)

            @block.sync
            def _(e):
                run('sp', e)
        self.stack.close()


class Arena:
    def __init__(self, t, nwords):
        self.t, self.n, self.off = t, nwords, 0

    def reset(self):
        self.off = 0

    def alloc(self, free_shape, dtype, parts=128):
        nel = 1
        for s in free_shape:
            nel *= s
        nw = (nel * (2 if dtype == BF16 else 4) + 3) // 4
        nw = (nw + 7) // 8 * 8
        o = self.off
        self.off += nw
        assert self.off <= self.n, ("arena overflow", self.off, self.n)
        v = self.t[0:parts, o:o + nw]
        if dtype != F32:
            v = v.bitcast(dtype)
        v = v[:, 0:nel]
        if len(free_shape) == 2:
            v = v.rearrange("p (a b) -> p a b", b=free_shape[1])
        elif len(free_shape) == 3:
            v = v.rearrange("p (a b c) -> p a b c", b=free_shape[1], c=free_shape[2])
        return v


def build(S, NBS, L):
    nc = bass.Bass("TRN2", target_bir_lowering=False)
    P = Prog(nc)
    NS = NBS * TS
    TK = PAST + TS
    NKB_S = (TK + 127) // 128
    NT = S // 128
    NQT = S // 512
    assert S % 512 == 0 and NS % 128 == 0

    def din(name, shape):
        return nc.dram_tensor(name, list(shape), F32, kind="ExternalInput").ap()

    def dout(name, shape):
        return nc.dram_tensor(name, list(shape), F32, kind="ExternalOutput").ap()

    def dscr(name, shape, dt=BF16):
        return nc.dram_tensor(name, list(shape), dt, kind="Internal").ap()

    xp = din("xp", [S, D]); xs = din("xs", [NS, D])
    pp = din("pp", [L, S, PLE]); pss = din("pss", [L, NS, PLE])
    c_fk = din("c_fk", [L, NBS, PAST, 384]); c_fv = din("c_fv", [L, NBS, PAST, 384])
    c_lf = din("c_lf", [L, NBS, PAST, 6]); c_ckv = din("c_ckv", [L, NBS, PAST, 256])
    c_kr = din("c_kr", [L, NBS, PAST, 32]); c_dk = din("c_dk", [L, NBS, PAST, 256])
    c_dv = din("c_dv", [L, NBS, PAST, 256]); c_cs = din("c_cs", [L, NBS, 2, DFF])
    w_in = din("w_in", [L, D, NCOLS]); w_uq = din("w_uq", [L, 384, 768])
    w_uk = din("w_uk", [L, 256, 384]); w_uv = din("w_uv", [L, 256, 384])
    w_out = din("w_out", [L, D, D]); w_up = din("w_up", [L, D, 2 * DFF])
    w_down = din("w_down", [L, DFF, D]); w_gate = din("w_gate", [L, D, D])
    w_proj = din("w_proj", [L, PLE, D])
    b_f = din("b_f", [L, 6]); g_q = din("g_q", [L, 384]); g_kv = din("g_kv", [L, 256])
    lam4 = din("lam4", [L, 4, 32]); g_sub = din("g_sub", [L, 64])
    g_norm = din("g_norm", [L, 6, D])
    cw = din("cw", [L, 3, DFF]); cb = din("cb", [L, DFF])
    k_masks = din("k_masks", [6, 128, 128]); k_U = din("k_U", [128, 128]); k_E = din("k_E", [128, 128])
    k_cs_p = din("k_cs_p", [S, 64]); k_cst_p = din("k_cst_p", [64, S])
    k_cs_s = din("k_cs_s", [NS, 64]); k_cst_s = din("k_cst_s", [64, NS])
    NAL = 4 * NQT + NKB_S
    k_al = din("k_al", [128, 4 * NAL])

    y_p = dout("y_p", [S, D]); y_s = dout("y_s", [NS, D])
    o_fk = [dout("o_fk_p", [L, S, 384]), dout("o_fk_s", [L, NS, 384])]
    o_fv = [dout("o_fv_p", [L, S, 384]), dout("o_fv_s", [L, NS, 384])]
    o_lf = [dout("o_lf_p", [L, S, 6]), dout("o_lf_s", [L, NS, 6])]
    o_ckv = [dout("o_ckv_p", [L, S, 256]), dout("o_ckv_s", [L, NS, 256])]
    o_kr = [dout("o_kr_p", [L, S, 32]), dout("o_kr_s", [L, NS, 32])]
    o_dk = [dout("o_dk_p", [L, S, 256]), dout("o_dk_s", [L, NS, 256])]
    o_dv = [dout("o_dv_p", [L, S, 256]), dout("o_dv_s", [L, NS, 256])]
    o_cv = [dout("o_cv_p", [L, 1, 2, DFF]), dout("o_cv_s", [L, NBS, 2, DFF])]

    wb_in = dscr("wb_in", [L, D, NCOLS]); wb_uq = dscr("wb_uq", [L, 384, 768])
    wb_uk = dscr("wb_uk", [L, 256, 384]); wb_uv = dscr("wb_uv", [L, 256, 384])
    wb_out = dscr("wb_out", [L, D, D]); wb_up = dscr("wb_up", [L, D, 2 * DFF])
    wb_down = dscr("wb_down", [L, DFF, D]); wb_gate = dscr("wb_gate", [L, D, D])
    wb_proj = dscr("wb_proj", [L, PLE, D])

    class G:
        pass
    grp = []
    for gi, (nseq, tk, tq) in enumerate([(1, S, S), (NBS, TK, TS)]):
        g = G()
        g.gi, g.nseq, g.tk, g.tq = gi, nseq, tk, tq
        g.ntok = nseq * tq
        g.qT_fox = dscr("qT_fox%d" % gi, [nseq, 384, tq]); g.kT_fox = dscr("kT_fox%d" % gi, [nseq, 384, tk])
        g.v_fox = dscr("v_fox%d" % gi, [nseq, tk, 6 * 65])
        g.qT_mla = dscr("qT_mla%d" % gi, [nseq, 576, tq]); g.kT_rope = dscr("kT_rope%d" % gi, [nseq, 32, tk])
        g.kT_nope = dscr("kT_nope%d" % gi, [nseq, 384, tk]); g.v_mla = dscr("v_mla%d" % gi, [nseq, tk, 6 * 65])
        g.qT_diff = dscr("qT_diff%d" % gi, [nseq, 256, tq]); g.kT_diff = dscr("kT_diff%d" % gi, [nseq, 256, tk])
        g.v_diff = dscr("v_diff%d" % gi, [nseq, tk, 4 * 65])
        g.mixT = (nc.dram_tensor("mixT%d" % gi, [D, g.ntok], BF16, kind="ExternalOutput").ap() if DEBUG
                  else dscr("mixT%d" % gi, [D, g.ntok]))
        g.hbuf = dscr("hbuf%d" % gi, [g.ntok, D], F32)
        g.x = xp if gi == 0 else xs
        g.y = y_p if gi == 0 else y_s
        g.pin = pp if gi == 0 else pss
        g.cs = k_cs_p if gi == 0 else k_cs_s
        g.cst = k_cst_p if gi == 0 else k_cst_s
        grp.append(g)

    ARW = 50300
    arena_t = P.sb("arena", [128, ARW], F32)
    ar = Arena(arena_t, ARW)
    pers_t = P.sb("pers", [128, 2600], F32)
    pers = Arena(pers_t, 2600)
    banks = [P.ps("bank%d" % i, [128, 512], F32) for i in range(8)]

    def bk(i):
        return banks[i][:, :]

    def bkb(i):
        return banks[i][:, :].bitcast(BF16)

    def pk(i):
        return 'ps%d' % i

    ident = pers.alloc([128], BF16)
    ones_f = pers.alloc([128], F32)
    U_f = pers.alloc([128], F32)
    E_f = pers.alloc([128], F32)
    masks = pers.alloc([6, 128], BF16)
    al_t = pers.alloc([4 * NAL], F32)
    eps_t = pers.alloc([1], F32)
    zero6 = pers.alloc([6], F32)
    ctok_p = pers.alloc([NT, 6], F32)
    ctok_s = pers.alloc([NBS, NKB_S, 6], F32)
    cref = pers.alloc([512], F32)


    def setup():
        A('pool', 'memset', ident, 1.0, w=['ident'])
        A('pool', 'affine_select', out=ident, in_=ident, pattern=[[-1, 128]], compare_op=ALU.is_equal,
                                             fill=0.0, base=0, channel_multiplier=1, r=['ident'], w=['ident'])
        A('pool', 'memset', ones_f, 1.0, w=['ones'])
        A('pool', 'memset', eps_t, EPS, w=['eps'])
        A('pool', 'memset', zero6, 0.0, w=['zero6'])
        P.dma('sp', U_f, k_U[:, :], 'c0', writes=['U'])
        P.dma('sp', E_f, k_E[:, :], 'c0', writes=['E'])
        P.dma('sp', al_t, k_al[:, :], 'c0', writes=['al'])
        ar.reset()
        mtmp = ar.alloc([6, 128], F32)
        P.dma('sp', mtmp, k_masks.rearrange("m k q -> k m q"), 'own', writes=['mtmp'])
        A('dve', 'tensor_copy', out=masks, in_=mtmp, r=['mtmp'], w=['masks'])

    _op = P.op

    def A(eng, meth, *args, r=(), w=(), **kw):
        _op(eng, meth, args, kw, reads=r, writes=w)

    def prep_weights():
        ar.reset()
        NB = 3
        stg = [ar.alloc([4096], F32) for _ in range(NB)]
        stb = [ar.alloc([4096], BF16) for _ in range(NB)]
        gcol = ar.alloc([L, 3, 8], F32)
        for l in range(L):
            for wi, gidx in enumerate((0, 2, 4)):
                P.dma('sp', gcol[:, l, wi, :], g_norm[l, gidx].rearrange("(k p) -> p k", p=128), 'own',
                      writes=['gcol'], allow_slow_non_contiguous=True)
        cnt = [0]

        def conv(src, dst, rows, cols, gain=None):
            for rk in range(rows // 128):
                for c0 in range(0, cols, 4096):
                    c1 = min(cols, c0 + 4096)
                    i = cnt[0] % NB
                    cnt[0] += 1
                    sv, bv = stg[i][:, 0:c1 - c0], stb[i][:, 0:c1 - c0]
                    P.dma('sp', sv, src[rk * 128:(rk + 1) * 128, c0:c1], 'wl%d' % i, writes=['stg%d' % i])
                    eng = ('dve', 'act', 'pool')[cnt[0] % 3] if gain is None else 'dve'
                    if gain is None:
                        if eng == 'act':
                            A('act', 'activation', out=bv, in_=sv, func=AF.Copy,
                              r=['stg%d' % i], w=['stb%d' % i])
                        else:
                            A(eng, 'tensor_copy', out=bv, in_=sv,
                              r=['stg%d' % i], w=['stb%d' % i])
                    else:
                        gc = gain[:, rk:rk + 1]
                        A('dve', 'tensor_scalar_mul', out=bv, in0=sv, scalar1=gc,
                          r=['stg%d' % i, 'gcol'], w=['stb%d' % i])
                    P.dma('pool', dst[rk * 128:(rk + 1) * 128, c0:c1], bv, 'ws%d' % i, reads=['stb%d' % i])

        for l in range(L):
            conv(w_in[l], wb_in[l], D, NCOLS, gcol[:, l, 0, :])
            conv(w_uq[l], wb_uq[l], 384, 768)
            conv(w_uk[l], wb_uk[l], 256, 384)
            conv(w_uv[l], wb_uv[l], 256, 384)
            conv(w_out[l], wb_out[l], D, D)
            conv(w_up[l], wb_up[l], D, 2 * DFF, gcol[:, l, 1, :])
            conv(w_down[l], wb_down[l], DFF, D)
            conv(w_gate[l], wb_gate[l], D, D, gcol[:, l, 2, :])
            conv(w_proj[l], wb_proj[l], PLE, D)

    def rstd_from_ss(ss_ap, n, key):
        A('act', 'activation', out=ss_ap, in_=ss_ap, func=AF.Ln, bias=eps_t[0:ss_ap.shape[0], 0:1],
                                        scale=1.0 / n, r=[key, 'eps'], w=[key])
        A('act', 'activation', out=ss_ap, in_=ss_ap, func=AF.Exp, scale=-0.5, r=[key], w=[key])

    def token_rows(g, t0, n):
        out = []
        if g.gi == 0:
            return [(0, t0, n, 0)]
        t = t0
        while t < t0 + n:
            out.append((t // TS, PAST + (t % TS), TS, t - t0))
            t += TS
        return out

    def phase_p1(l):
        ar.reset()
        wi = ar.alloc([8, NCOLS], BF16)
        wq = ar.alloc([3, 768], BF16)
        wk = ar.alloc([2, 384], BF16)
        wv = ar.alloc([2, 384], BF16)
        P.dma('sp', wi, wb_in[l].rearrange("(k p) n -> p k n", p=128), 'c0', writes=['wi'])
        P.dma('sp', wq, wb_uq[l].rearrange("(k p) n -> p k n", p=128), 'c0', writes=['wq'])
        P.dma('sp', wk, wb_uk[l].rearrange("(k p) n -> p k n", p=128), 'c0', writes=['wk'])
        P.dma('sp', wv, wb_uv[l].rearrange("(k p) n -> p k n", p=128), 'c0', writes=['wv'])
        gq_c = ar.alloc([3], F32); gkv_c = ar.alloc([2], F32)
        P.dma('sp', gq_c, g_q[l].rearrange("(k p) -> p k", p=128), 'c0', writes=['gqc'], allow_slow_non_contiguous=True)
        P.dma('sp', gkv_c, g_kv[l].rearrange("(k p) -> p k", p=128), 'c0', writes=['gkvc'], allow_slow_non_contiguous=True)
        gkv_b = ar.alloc([256], F32); bf_b = ar.alloc([6], F32)
        P.dma('sp', gkv_b, g_kv[l:l + 1, :].to_broadcast([128, 256]), 'c0', writes=['gkvb'])
        P.dma('sp', bf_b, b_f[l:l + 1, :].to_broadcast([128, 6]), 'c0', writes=['bfb'])
        h_ts = [ar.alloc([4, D], F32) for _ in range(2)]
        ssn = ar.alloc([8], F32)
        tcount = [0]
        sq_rr = [0]

        def stq():
            sq_rr[0] += 1
            return ('sp', 'pool')[sq_rr[0] % 2] if OPT_STQ else 'pool'
        xn = [ar.alloc([D], BF16) for _ in range(2)]
        xnT = ar.alloc([8, 512], BF16)
        cs_t = ar.alloc([4, 64], F32)
        cst_t = ar.alloc([512], F32, parts=64)
        ost_a = [ar.alloc([390], F32) for _ in range(2)]
        ost_b = [ar.alloc([384], F32) for _ in range(2)]
        ost_c = [ar.alloc([320], F32) for _ in range(2)]
        ost_d = [ar.alloc([512], F32) for _ in range(2)]
        vst = ar.alloc([4, 16, 65], BF16)
        fst = [ar.alloc([512], BF16) for _ in range(6)]
        fsi = [0]

        def next_fst():
            fsi[0] += 1
            return fsi[0] % 6
        raw = ar.alloc([3, 512], F32)
        sq = [ar.alloc([512], F32) for _ in range(2)]
        rsb = ar.alloc([512], F32)
        cqnT = ar.alloc([3, 512], BF16)
        ckvnT = ar.alloc([2, 512], BF16)
        t1 = ar.alloc([512], F32); t2 = ar.alloc([512], F32)
        sm = ar.alloc([64], F32)
        lf_t = ar.alloc([4, 6], F32)
        A('pool', 'memset', vst, 1.0, w=['vst'])
        rr = [0]

        for g in grp:
            ntile = (g.ntok + 511) // 512
            for ti in range(ntile):
                t0 = ti * 512
                W = min(512, g.ntok - t0)
                nsub = W // 128
                src = g.x if l == 0 else g.hbuf
                h_t = h_ts[tcount[0] % 2]
                hkey = 'h_t%d' % (tcount[0] % 2)
                tcount[0] += 1
                P.dma('sp', h_t[:, 0:nsub, :], src[t0:t0 + W, :].rearrange("(j p) d -> p j d", p=128), 'p1h' + hkey, writes=[hkey])
                P.dma('sp', cs_t[:, 0:nsub, :], g.cs[t0:t0 + W, :].rearrange("(j p) d -> p j d", p=128), 'p1c', writes=['cs_t'])
                P.dma('sp', cst_t[:, 0:W], g.cst[:, t0:t0 + W], 'p1c', writes=['cst_t'])
                A('pool', 'memset', ssn, 0.0, w=['ssn'])
                for j in range(nsub):
                    x_b = xn[j % 2]; xk = 'xn%d' % (j % 2)
                    A('act', 'activation', out=x_b, in_=h_t[:, j, :], func=AF.Square,
                                                                  accum_out=ssn[:, j:j + 1], r=[hkey, 'ssn'], w=[xk, 'ssn'])
                for j in range(nsub):
                    pass
                rstd_from_ss(ssn[:, 0:nsub], D, 'ssn')
                for j in range(nsub):
                    x_b = xn[j % 2]; xk = 'xn%d' % (j % 2)
                    A('dve', 'tensor_scalar_mul', out=x_b, in0=h_t[:, j, :], scalar1=ssn[:, j:j + 1],
                      r=[hkey, 'ssn'], w=[xk])
                    pb = 6 + (j % 2)
                    for kc in range(8):
                        A('pe', 'transpose', bkb(pb)[:, kc * 128:(kc + 1) * 128],
                                                                             x_b[:, kc * 128:(kc + 1) * 128], ident,
                          r=[xk, 'ident'], w=[pk(pb)])
                    A('act', 'activation', out=xnT[:, :, j * 128:(j + 1) * 128],
                                                                in_=bkb(pb).rearrange("p (a b) -> p a b", b=128), func=AF.Copy,
                      r=[pk(pb)], w=['xnT'])
                for j in range(nsub):
                    rows = token_rows(g, t0 + j * 128, 128)
                    oa, ob, oc_, od = ost_a[j % 2], ost_b[j % 2], ost_c[j % 2], ost_d[j % 2]
                    ka, kb_, kc_, kd = 'oa%d' % (j % 2), 'ob%d' % (j % 2), 'oc%d' % (j % 2), 'od%d' % (j % 2)
                    for ci, (c0, c1) in enumerate([(O_FK, O_FK + 390), (O_FV, O_FV + 384), (O_CKV, O_CKV + 320), (O_DK, O_DK + 512)]):
                        pb = rr[0] % 3
                        rr[0] += 1
                        for kc in range(8):
                            A('pe', 'matmul',
                                bk(pb)[:, 0:c1 - c0], lhsT=xnT[:, kc, j * 128:(j + 1) * 128], rhs=wi[:, kc, c0:c1],
                                start=(kc == 0), stop=(kc == 7), r=['xnT', 'wi'], w=[pk(pb)])
                        if ci == 0:
                            A('act', 'activation', out=oa, in_=bk(pb)[:, 0:390], func=AF.Copy,
                              r=[pk(pb)], w=[ka])
                            A('dve', 'tensor_tensor', out=sm[:, 0:6], in0=oa[:, 384:390], in1=bf_b, op=ALU.add,
                              r=[ka, 'bfb'], w=['sm'])
                            A('act', 'activation', out=sm[:, 0:6], in_=sm[:, 0:6], func=AF.Exp, scale=-1.0, r=['sm'], w=['sm'])
                            A('act', 'activation', out=sm[:, 0:6], in_=sm[:, 0:6], func=AF.Ln, bias=1.0, scale=1.0, r=['sm'], w=['sm'])
                            A('dve', 'tensor_scalar_mul', out=lf_t[:, j, :], in0=sm[:, 0:6], scalar1=-1.0, r=['sm'], w=['lf_t'])
                            for (sq_, pos0, cnt_, p0) in rows:
                                d0 = (sq_ * g.tq + pos0 - (g.tk - g.tq))
                                P.dma(stq(), o_fk[g.gi][l, d0:d0 + cnt_, :], oa[p0:p0 + cnt_, 0:384], 'so', reads=[ka])
                                P.dma(stq(), o_lf[g.gi][l, d0:d0 + cnt_, :], lf_t[p0:p0 + cnt_, j, :], 'so', reads=['lf_t'])
                        elif ci == 1:
                            A('act', 'activation', out=ob, in_=bk(pb)[:, 0:384], func=AF.Copy, r=[pk(pb)], w=[kb_])
                            A('dve', 'tensor_copy', out=vst[:, j, 0:6, 0:64],
                                                                         in_=bk(pb)[:, 0:384].rearrange("p (h d) -> p h d", d=64),
                              r=[pk(pb)], w=['vst'])
                            for (sq_, pos0, cnt_, p0) in rows:
                                d0 = (sq_ * g.tq + pos0 - (g.tk - g.tq))
                                P.dma(stq(), o_fv[g.gi][l, d0:d0 + cnt_, :], ob[p0:p0 + cnt_, :], 'so', reads=[kb_])
                        elif ci == 2:
                            A('pool', 'memset', sm[:, 8:9], 0.0, w=['sm8'])
                            A('act', 'activation', out=t1[:, 0:256], in_=bk(pb)[:, 0:256], func=AF.Square,
                                                                   accum_out=sm[:, 8:9], r=[pk(pb), 'sm8'], w=['t1', 'sm8'])
                            rstd_from_ss(sm[:, 8:9], 256, 'sm8')
                            A('dve', 'scalar_tensor_tensor', out=oc_[:, 0:256], in0=bk(pb)[:, 0:256], scalar=sm[:, 8:9],
                                                                                       in1=gkv_b, op0=ALU.mult, op1=ALU.mult,
                              r=[pk(pb), 'sm8', 'gkvb'], w=[kc_])
                            A('dve', 'tensor_tensor', out=t1[:, 256:288], in0=bk(pb)[:, 256:288], in1=cs_t[:, j, 0:32], op=ALU.mult,
                              r=[pk(pb), 'cs_t'], w=['t1'])
                            A('dve', 'tensor_tensor', out=t1[:, 288:320], in0=bk(pb)[:, 288:320], in1=cs_t[:, j, 32:64], op=ALU.mult,
                              r=[pk(pb), 'cs_t'], w=['t1'])
                            A('dve', 'tensor_tensor', out=oc_[:, 256:288], in0=t1[:, 256:288], in1=t1[:, 288:320], op=ALU.add,
                              r=['t1'], w=[kc_])
                            for (sq_, pos0, cnt_, p0) in rows:
                                d0 = (sq_ * g.tq + pos0 - (g.tk - g.tq))
                                P.dma(stq(), o_ckv[g.gi][l, d0:d0 + cnt_, :], oc_[p0:p0 + cnt_, 0:256], 'so', reads=[kc_])
                                P.dma(stq(), o_kr[g.gi][l, d0:d0 + cnt_, :], oc_[p0:p0 + cnt_, 256:288], 'so', reads=[kc_])
                        else:
                            A('act', 'activation', out=od, in_=bk(pb)[:, 0:512], func=AF.Copy, r=[pk(pb)], w=[kd])
                            A('dve', 'tensor_copy', out=vst[:, j, 12:16, 0:64],
                                                                         in_=bk(pb)[:, 256:512].rearrange("p (h d) -> p h d", d=64),
                              r=[pk(pb)], w=['vst'])
                            for (sq_, pos0, cnt_, p0) in rows:
                                d0 = (sq_ * g.tq + pos0 - (g.tk - g.tq))
                                P.dma(stq(), o_dk[g.gi][l, d0:d0 + cnt_, :], od[p0:p0 + cnt_, 0:256], 'so', reads=[kd])
                                P.dma(stq(), o_dv[g.gi][l, d0:d0 + cnt_, :], od[p0:p0 + cnt_, 256:512], 'so', reads=[kd])
                    if g.gi == 0:
                        blk = t0 // 128 + j
                        prev = zero6 if blk == 0 else ctok_p[:, blk - 1, :]
                        A('pe', 'matmul', bk(5)[:, 0:6], lhsT=U_f, rhs=lf_t[:, j, :], start=True, stop=False,
                          r=['U', 'lf_t'], w=[pk(5)])
                        A('pe', 'matmul', bk(5)[:, 0:6], lhsT=E_f, rhs=prev, start=False, stop=True,
                          r=['E', 'ctok', 'zero6'], w=[pk(5)])
                        A('dve', 'tensor_copy', out=ctok_p[:, blk, :], in_=bk(5)[:, 0:6], r=[pk(5)], w=['ctok'])
                    else:
                        for (sq_, pos0, cnt_, p0) in rows:
                            A('pe', 'matmul', bk(5)[0:64, 0:6], lhsT=U_f[p0:p0 + 64, p0:p0 + 64],
                                                                  rhs=lf_t[p0:p0 + 64, j, :], start=True, stop=False,
                              r=['U', 'lf_t'], w=[pk(5)])
                            A('pe', 'matmul', bk(5)[0:64, 0:6], lhsT=E_f[:, 0:64], rhs=ctok_s[:, sq_, NKB_S - 2, :],
                                                               start=False, stop=True, r=['E', 'ctoks'], w=[pk(5)])
                            A('dve', 'tensor_copy', out=ctok_s[0:64, sq_, NKB_S - 1, :], in_=bk(5)[0:64, 0:6],
                              r=[pk(5)], w=['ctoks'])
                trows = token_rows(g, t0, W)

                def fm(c0, m, dst_fn, post=None):
                    pb = 3 + rr[0] % 2
                    rr[0] += 1
                    for kc in range(8):
                        A('pe', 'matmul', bk(pb)[0:m, 0:W], lhsT=wi[:, kc, c0:c0 + m], rhs=xnT[:, kc, 0:W],
                                                                  start=(kc == 0), stop=(kc == 7), r=['xnT', 'wi'], w=[pk(pb)])
                    return pb

                def store_fm(stage, skey, m, dst_fn, is_key):
                    for (sq_, pos0, cnt_, p0) in trows:
                        col = pos0 if is_key else pos0 - (g.tk - g.tq)
                        P.dma(stq(), dst_fn(sq_)[:, col:col + cnt_], stage[0:m, p0:p0 + cnt_], 'sf', reads=[skey])

                def simple_fm(c0, nchunk, dst, is_key):
                    for c in range(nchunk):
                        pb = fm(c0 + c * 128, 128, None)
                        si = next_fst()
                        st, sk = fst[si], 'fst%d' % si
                        eng = 'act' if c % 2 == 0 else 'dve'
                        if eng == 'act':
                            A('act', 'activation', out=st[:, 0:W], in_=bk(pb)[:, 0:W], func=AF.Copy, r=[pk(pb)], w=[sk])
                        else:
                            A('dve', 'tensor_copy', out=st[:, 0:W], in_=bk(pb)[:, 0:W], r=[pk(pb)], w=[sk])
                        store_fm(st, sk, 128, lambda s_, c=c: dst[s_, c * 128:(c + 1) * 128, :], is_key)

                simple_fm(O_FQ, 3, g.qT_fox, False)
                simple_fm(O_FK, 3, g.kT_fox, True)
                simple_fm(O_DQ, 2, g.qT_diff, False)
                simple_fm(O_DK, 2, g.kT_diff, True)

                def norm_fm(c0, nchunk, n, gcol_, gkey, outT, okey):
                    for c in range(nchunk):
                        pb = fm(c0 + c * 128, 128, None)
                        A('act', 'activation', out=raw[:, c, 0:W], in_=bk(pb)[:, 0:W], func=AF.Copy, r=[pk(pb)], w=['raw'])
                        s_ = sq[c % 2]
                        A('dve', 'tensor_tensor', out=s_[:, 0:W], in0=raw[:, c, 0:W], in1=raw[:, c, 0:W], op=ALU.mult,
                          r=['raw'], w=['sq%d' % (c % 2)])
                        A('pe', 'matmul', bk(5)[:, 0:W], lhsT=ones_f, rhs=s_[:, 0:W], start=(c == 0), stop=(c == nchunk - 1),
                          r=['ones', 'sq%d' % (c % 2)], w=[pk(5)])
                    A('act', 'activation', out=rsb[:, 0:W], in_=bk(5)[:, 0:W], func=AF.Ln, bias=eps_t[:, 0:1], scale=1.0 / n,
                      r=[pk(5), 'eps'], w=['rsb'])
                    A('act', 'activation', out=rsb[:, 0:W], in_=rsb[:, 0:W], func=AF.Exp, scale=-0.5, r=['rsb'], w=['rsb'])
                    for c in range(nchunk):
                        A('dve', 'scalar_tensor_tensor', out=outT[:, c, 0:W], in0=raw[:, c, 0:W], scalar=gcol_[:, c:c + 1],
                                                                        in1=rsb[:, 0:W], op0=ALU.mult, op1=ALU.mult,
                          r=['raw', gkey, 'rsb'], w=[okey])

                norm_fm(O_CQ, 3, 384, gq_c, 'gqc', cqnT, 'cqnT')
                norm_fm(O_CKV, 2, 256, gkv_c, 'gkvc', ckvnT, 'ckvnT')

                def rope_rows(pb, st, skey, b0=0):
                    A('dve', 'tensor_tensor', out=t1[b0:b0 + 32, 0:W], in0=bk(pb)[b0:b0 + 32, 0:W], in1=cst_t[0:32, 0:W], op=ALU.mult,
                      r=[pk(pb), 'cst_t'], w=['t1'])
                    A('dve', 'tensor_tensor', out=t2[b0:b0 + 32, 0:W], in0=bk(pb)[b0 + 32:b0 + 64, 0:W], in1=cst_t[32:64, 0:W], op=ALU.mult,
                      r=[pk(pb), 'cst_t'], w=['t2'])
                    A('dve', 'tensor_tensor', out=st[b0:b0 + 32, 0:W], in0=t1[b0:b0 + 32, 0:W], in1=t2[b0:b0 + 32, 0:W], op=ALU.add,
                      r=['t1', 't2'], w=[skey])

                pb = fm(O_KR, 64, None)
                si = next_fst()
                st, sk = fst[si], 'fst%d' % si
                rope_rows(pb, st, sk)
                store_fm(st, sk, 32, lambda s_: g.kT_rope[s_, :, :], True)
                for hh in range(6):
                    pb = 3 + rr[0] % 2
                    rr[0] += 1
                    for kc in range(3):
                        A('pe', 'matmul', bk(pb)[:, 0:W], lhsT=wq[:, kc, hh * 128:(hh + 1) * 128], rhs=cqnT[:, kc, 0:W],
                                                                         start=(kc == 0), stop=(kc == 2), r=['wq', 'cqnT'], w=[pk(pb)])
                    si = next_fst()
                    st, sk = fst[si], 'fst%d' % si
                    rope_rows(pb, st, sk, 64)
                    A('dve', 'tensor_copy', out=st[0:64, 0:W], in_=bk(pb)[0:64, 0:W], r=[pk(pb)], w=[sk])
                    store_fm(st, sk, 96, lambda s_, hh=hh: g.qT_mla[s_, hh * 96:(hh + 1) * 96, :], False)
                for c in range(3):
                    pb = 3 + rr[0] % 2
                    rr[0] += 1
                    for kc in range(2):
                        A('pe', 'matmul', bk(pb)[:, 0:W], lhsT=wk[:, kc, c * 128:(c + 1) * 128], rhs=ckvnT[:, kc, 0:W],
                                                                       start=(kc == 0), stop=(kc == 1), r=['wk', 'ckvnT'], w=[pk(pb)])
                    si = next_fst()
                    st, sk = fst[si], 'fst%d' % si
                    A('act', 'activation', out=st[:, 0:W], in_=bk(pb)[:, 0:W], func=AF.Copy, r=[pk(pb)], w=[sk])
                    store_fm(st, sk, 128, lambda s_, c=c: g.kT_nope[s_, c * 128:(c + 1) * 128, :], True)
                for j in range(nsub):
                    pb = rr[0] % 3
                    rr[0] += 1
                    for kc in range(2):
                        A('pe', 'matmul', bk(pb)[:, 0:384], lhsT=ckvnT[:, kc, j * 128:(j + 1) * 128], rhs=wv[:, kc, :],
                                                                       start=(kc == 0), stop=(kc == 1), r=['wv', 'ckvnT'], w=[pk(pb)])
                    A('dve', 'tensor_copy', out=vst[:, j, 6:12, 0:64], in_=bk(pb)[:, 0:384].rearrange("p (h d) -> p h d", d=64),
                      r=[pk(pb)], w=['vst'])
                for j in range(nsub):
                    for (sq_, pos0, cnt_, p0) in token_rows(g, t0 + j * 128, 128):
                        P.dma(stq(), g.v_fox[sq_, pos0:pos0 + cnt_, :].rearrange("t (h c) -> t h c", c=65), vst[p0:p0 + cnt_, j, 0:6, :], 'sv', reads=['vst'])
                        P.dma(stq(), g.v_mla[sq_, pos0:pos0 + cnt_, :].rearrange("t (h c) -> t h c", c=65), vst[p0:p0 + cnt_, j, 6:12, :], 'sv', reads=['vst'])
                        P.dma(stq(), g.v_diff[sq_, pos0:pos0 + cnt_, :].rearrange("t (h c) -> t h c", c=65), vst[p0:p0 + cnt_, j, 12:16, :], 'sv', reads=['vst'])

    def phase_cache(l):
        ar.reset()
        g = grp[1]
        wk = ar.alloc([2, 384], BF16); wv = ar.alloc([2, 384], BF16)
        P.dma('sp', wk, wb_uk[l].rearrange("(k p) n -> p k n", p=128), 'c0', writes=['wk'])
        P.dma('sp', wv, wb_uv[l].rearrange("(k p) n -> p k n", p=128), 'c0', writes=['wv'])
        ld = [ar.alloc([8, 384], F32) for _ in range(2)]
        lb = [ar.alloc([8, 384], BF16) for _ in range(2)]
        kTs = [ar.alloc([1024], BF16) for _ in range(2)]
        vst = ar.alloc([8, 6, 65], BF16)
        ckT = ar.alloc([2, 1024], BF16)
        lf_c = ar.alloc([8, 6], F32)
        A('pool', 'memset', vst, 1.0, w=['vst'])
        rr = [0]
        for b in range(NBS):
            def load(src, ncol):
                i = rr[0] % 2
                rr[0] += 1
                P.dma('sp', ld[i][:, :, 0:ncol], src[l, b].rearrange("(j p) d -> p j d", p=128), 'cl%d' % i, writes=['ld%d' % i])
                A('dve', 'tensor_copy', out=lb[i][:, :, 0:ncol], in_=ld[i][:, :, 0:ncol], r=['ld%d' % i], w=['lb%d' % i])
                return i

            def transp(i, c0, m, dstT, dkey, dst_dram):
                pb = 6 + rr[0] % 2
                rr[0] += 1
                for j in range(8):
                    A('pe', 'transpose', bkb(pb)[0:m, j * 128:(j + 1) * 128], lb[i][:, j, c0:c0 + m], ident,
                      r=['lb%d' % i, 'ident'], w=[pk(pb)])
                A('act', 'activation', out=dstT[0:m, :], in_=bkb(pb)[0:m, :], func=AF.Copy, r=[pk(pb)], w=[dkey])
                if dst_dram is not None:
                    P.dma('pool', dst_dram, dstT[0:m, :], 'sc', reads=[dkey])

            def vstore(i, nh, dst):
                A('dve', 'tensor_copy', out=vst[:, :, 0:nh, 0:64], in_=lb[i][:, :, 0:nh * 64].rearrange("p j (h d) -> p j h d", d=64),
                  r=['lb%d' % i], w=['vst'])
                P.dma('pool', dst[b, 0:PAST, :].rearrange("(j p) (h c) -> p j h c", p=128, c=65), vst[:, :, 0:nh, :], 'sc', reads=['vst'])

            i = load(c_fk, 384)
            for c in range(3):
                kt = kTs[rr[0] % 2]; kk = 'kTs%d' % (rr[0] % 2)
                transp(i, c * 128, 128, kt, kk, g.kT_fox[b, c * 128:(c + 1) * 128, 0:PAST])
            i = load(c_dk, 256)
            for c in range(2):
                kt = kTs[rr[0] % 2]; kk = 'kTs%d' % (rr[0] % 2)
                transp(i, c * 128, 128, kt, kk, g.kT_diff[b, c * 128:(c + 1) * 128, 0:PAST])
            i = load(c_kr, 32)
            kt = kTs[rr[0] % 2]; kk = 'kTs%d' % (rr[0] % 2)
            transp(i, 0, 32, kt, kk, g.kT_rope[b, :, 0:PAST])
            i = load(c_fv, 384)
            vstore(i, 6, g.v_fox)
            i = load(c_dv, 256)
            vstore(i, 4, g.v_diff)
            i = load(c_ckv, 256)
            for c in range(2):
                transp(i, c * 128, 128, ckT[:, c, :], 'ckT', None)
            for c in range(3):
                for hf in range(2):
                    pb = 3 + rr[0] % 2
                    rr[0] += 1
                    for kc in range(2):
                        A('pe', 'matmul', bk(pb)[:, 0:512], lhsT=wk[:, kc, c * 128:(c + 1) * 128],
                                                                               rhs=ckT[:, kc, hf * 512:(hf + 1) * 512], start=(kc == 0), stop=(kc == 1),
                          r=['wk', 'ckT'], w=[pk(pb)])
                    kt = kTs[rr[0] % 2]; kk = 'kTs%d' % (rr[0] % 2)
                    A('act', 'activation', out=kt[:, 0:512], in_=bk(pb)[:, 0:512], func=AF.Copy, r=[pk(pb)], w=[kk])
                    P.dma('pool', g.kT_nope[b, c * 128:(c + 1) * 128, hf * 512:(hf + 1) * 512], kt[:, 0:512], 'sc', reads=[kk])
            for j in range(8):
                pb = rr[0] % 3
                rr[0] += 1
                for kc in range(2):
                    A('pe', 'matmul', bk(pb)[:, 0:384], lhsT=ckT[:, kc, j * 128:(j + 1) * 128], rhs=wv[:, kc, :],
                                                                   start=(kc == 0), stop=(kc == 1), r=['wv', 'ckT'], w=[pk(pb)])
                A('dve', 'tensor_copy', out=vst[:, j, 0:6, 0:64], in_=bk(pb)[:, 0:384].rearrange("p (h d) -> p h d", d=64),
                  r=[pk(pb)], w=['vst'])
            P.dma('pool', g.v_mla[b, 0:PAST, :].rearrange("(j p) (h c) -> p j h c", p=128, c=65), vst[:, :, 0:6, :], 'sc', reads=['vst'])
            P.dma('sp', lf_c, c_lf[l, b].rearrange("(j p) h -> p j h", p=128), 'cl2', writes=['lf_c'])
            for j in range(8):
                prev = zero6 if j == 0 else ctok_s[:, b, j - 1, :]
                A('pe', 'matmul', bk(5)[:, 0:6], lhsT=U_f, rhs=lf_c[:, j, :], start=True, stop=False, r=['U', 'lf_c'], w=[pk(5)])
                A('pe', 'matmul', bk(5)[:, 0:6], lhsT=E_f, rhs=prev, start=False, stop=True, r=['E', 'ctoks', 'zero6'], w=[pk(5)])
                A('dve', 'tensor_copy', out=ctok_s[:, b, j, :], in_=bk(5)[:, 0:6], r=[pk(5)], w=['ctoks'])

    def phase_p2(l):
        ar.reset()
        lam_init = 0.8 - 0.6 * math.exp(-0.3 * l)
        lq = ar.alloc([4, 32], F32, parts=64)
        lt = ar.alloc([8], F32, parts=64)
        negl = ar.alloc([1], F32, parts=64)
        gsub = ar.alloc([1], F32, parts=64)
        P.dma('sp', lq, lam4[l:l + 1].to_broadcast([64, 4, 32]), 'c0', writes=['lq'])
        P.dma('sp', gsub, g_sub[l].rearrange("(d o) -> d o", o=1), 'c0', writes=['gsub'], allow_slow_non_contiguous=True)
        A('dve', 'tensor_tensor', out=lq[:, 0, :], in0=lq[:, 0, :], in1=lq[:, 1, :], op=ALU.mult, r=['lq'], w=['lq'])
        A('dve', 'tensor_tensor', out=lq[:, 2, :], in0=lq[:, 2, :], in1=lq[:, 3, :], op=ALU.mult, r=['lq'], w=['lq'])
        A('dve', 'reduce_sum', out=lt[:, 0:1], in_=lq[:, 0, :], axis=mybir.AxisListType.X, r=['lq'], w=['lt'])
        A('dve', 'reduce_sum', out=lt[:, 1:2], in_=lq[:, 2, :], axis=mybir.AxisListType.X, r=['lq'], w=['lt'])
        A('act', 'activation', out=lt[:, 0:2], in_=lt[:, 0:2], func=AF.Exp, r=['lt'], w=['lt'])
        A('dve', 'tensor_tensor', out=lt[:, 2:3], in0=lt[:, 1:2], in1=lt[:, 0:1], op=ALU.subtract, r=['lt'], w=['lt'])
        A('dve', 'tensor_scalar_add', out=negl, in0=lt[:, 2:3], scalar1=-lam_init, r=['lt'], w=['negl'])
        A('dve', 'tensor_scalar_mul', out=gsub, in0=gsub, scalar1=(1.0 - lam_init), r=['gsub'], w=['gsub'])

        NKBmax = max(g.tk for g in grp)
        NKBmax = (NKBmax + 127) // 128
        TKmax = max(g.tk for g in grp); TQmax = max(g.tq for g in grp)
        slots = []
        for s in range(2):
            slots.append((ar.alloc([TKmax], BF16), ar.alloc([TQmax], BF16), ar.alloc([NKBmax, 128], BF16)))
            A('pool', 'memset', slots[s][2][:, :, 64:128], 1.0, w=['u%d' % s])
            A('pool', 'memset', slots[s][0], 0.0, w=['u%d' % s])
            A('pool', 'memset', slots[s][1], 0.0, w=['u%d' % s])
        slot_hw = [0, 0]
        pT = [ar.alloc([512], BF16) for _ in range(4)]
        rl = ar.alloc([512], F32)
        on2 = ar.alloc([512], F32, parts=64)
        oc = ar.alloc([512], F32, parts=64)
        sqd = ar.alloc([512], F32, parts=64)
        rs2 = ar.alloc([512], F32, parts=64)
        mst = [ar.alloc([512], BF16, parts=64) for _ in range(2)]
        fbias = ar.alloc([NKBmax], F32)
        crefp = ar.alloc([NT, 6], F32)
        crefs = ar.alloc([max(NBS, 1), 6], F32)
        A('pe', 'matmul', bk(5)[:, 0:NT * 6], lhsT=E_f, rhs=ctok_p.rearrange("p a b -> p (a b)"), start=True, stop=True,
          r=['E', 'ctok'], w=[pk(5)])
        A('dve', 'tensor_copy', out=crefp.rearrange("p a b -> p (a b)"), in_=bk(5)[:, 0:NT * 6], r=[pk(5)], w=['crefp'])
        for b_ in range(NBS):
            A('pe', 'matmul', bk(5)[:, 0:6], lhsT=E_f, rhs=ctok_s[:, b_, NKB_S - 2, :], start=True, stop=True, r=['E', 'ctoks'], w=[pk(5)])
            A('dve', 'tensor_copy', out=crefs[:, b_, :], in_=bk(5)[:, 0:6], r=[pk(5)], w=['crefs'])

        units = []
        for g in grp:
            for sq_ in range(g.nseq):
                for h in range(6):
                    units.append(dict(g=g, s=sq_, kind='fox', h=h, rows=64, scale=0.125, mask=0, row0=64 * h,
                                      kT=[(g.kT_fox[sq_, 64 * h:64 * h + 64, :], 0, 64)], qT=g.qT_fox[sq_, 64 * h:64 * h + 64, :],
                                      v=g.v_fox[sq_, :, 65 * h:65 * h + 65]))
                for h in range(6):
                    units.append(dict(g=g, s=sq_, kind='mla', h=h, rows=96, scale=96 ** -0.5, mask=1, row0=384 + 64 * h,
                                      kT=[(g.kT_nope[sq_, 64 * h:64 * h + 64, :], 0, 64), (g.kT_rope[sq_, :, :], 64, 32)],
                                      qT=g.qT_mla[sq_, 96 * h:96 * h + 96, :], v=g.v_mla[sq_, :, 65 * h:65 * h + 65]))
                for h in range(4):
                    for s2 in range(2):
                        r0 = 64 * h + 32 * s2
                        units.append(dict(g=g, s=sq_, kind='diff', h=h, s2=s2, rows=32, scale=32 ** -0.5, mask=2 + h, row0=768 + 64 * h,
                                          kT=[(g.kT_diff[sq_, r0:r0 + 32, :], 0, 32)], qT=g.qT_diff[sq_, r0:r0 + 32, :],
                                          v=g.v_diff[sq_, :, 65 * h:65 * h + 65]))

        def load_unit(ui):
            u = units[ui]
            g = u['g']
            kt, qt, vt = slots[ui % 2]
            sk = 'u%d' % (ui % 2)
            if u['rows'] < slot_hw[ui % 2]:
                a0 = u['rows']
                while a0 < slot_hw[ui % 2]:
                    a1 = min(128, a0 + (32 if a0 % 64 else 64))
                    A('pool', 'memset', qt[a0:a1, :], 0.0, w=[sk])
                    a0 = a1
            slot_hw[ui % 2] = u['rows']
            for (src, p0, n) in u['kT']:
                P.dma('sp', kt[p0:p0 + n, 0:g.tk], src, 'ul%d' % (ui % 2), writes=[sk])
            P.dma('sp', qt[0:u['rows'], 0:g.tq], u['qT'], 'ul%d' % (ui % 2), writes=[sk])
            nkb = g.tk // 128
            for k0 in range(0, nkb, 16):
                k1 = min(nkb, k0 + 16)
                P.dma('sp', vt[:, k0:k1, 0:64], u['v'][k0 * 128:k1 * 128, 0:64].rearrange("(j p) c -> p j c", p=128), 'ul%d' % (ui % 2), writes=[sk])
            rem = g.tk - nkb * 128
            if rem:
                P.dma('sp', vt[0:rem, nkb, 0:64], u['v'][nkb * 128:g.tk, 0:64], 'ul%d' % (ui % 2), writes=[sk])

        steps = []
        grpno = 0
        for ui, u in enumerate(units):
            g = u['g']
            if g.gi == 0:
                for i in range(NQT):
                    nkb = 4 * i + 4
                    for kb in range(nkb):
                        j = kb - 4 * i
                        c0 = 128 * j if j > 0 else 0
                        steps.append(dict(ui=ui, i=i, kb=kb, nk=128, c0=c0, W=512, first=(kb == 0), last=(kb == nkb - 1),
                                          diag=(j if j >= 0 else None), grp=grpno, q0=512 * i, al=(kb - 4 * i - 2) + (4 * NQT - 2)))
                    grpno += 1
            else:
                nkb = NKB_S
                for kb in range(nkb):
                    nk = min(128, g.tk - 128 * kb)
                    steps.append(dict(ui=ui, i=0, kb=kb, nk=nk, c0=0, W=TS, first=(kb == 0), last=(kb == nkb - 1),
                                      diag=(0 if kb == nkb - 1 else None), grp=grpno, q0=0, al=4 * NQT + kb))
                    grpno += 1 if kb == nkb - 1 else 0

        last_step_of = {}
        for n_, st_ in enumerate(steps):
            last_step_of[st_['ui']] = n_
        for ui_ in range(min(2, len(units))):
            load_unit(ui_)

        fb_state = [None]

        def emit_S(n):
            st = steps[n]
            u = units[st['ui']]
            kt, qt, vt = slots[st['ui'] % 2]
            sk = 'u%d' % (st['ui'] % 2)
            pb = (0, 1, 2, 5)[n % 4]
            rows, nk, c0, W, kb = u['rows'], st['nk'], st['c0'], st['W'], st['kb']
            qc = st['q0'] if u['g'].gi == 0 else 0
            A('pe', 'matmul', bk(pb)[0:nk, c0:W], lhsT=kt[:, kb * 128:kb * 128 + nk], rhs=qt[:, qc + c0:qc + W],
                                       start=True, stop=True, r=[sk], w=[pk(pb)])

        def fox_bias(st, u):
            key = (st['ui'], st['i'])
            if fb_state[0] == key:
                return
            fb_state[0] = key
            g, h = u['g'], u['h']
            if g.gi == 0:
                blk = 4 * st['i'] + 1
                A('dve', 'tensor_scalar', out=fbias[:, 0:NT], in0=ctok_p[:, :, h], scalar1=-1.0, scalar2=crefp[:, blk, h:h + 1],
                                                   op0=ALU.mult, op1=ALU.add, r=['ctok', 'crefp'], w=['fbias'])
            else:
                sq_ = u['s']
                A('dve', 'tensor_scalar', out=fbias[:, 0:NKB_S], in0=ctok_s[:, sq_, :, h], scalar1=-1.0, scalar2=crefs[:, sq_, h:h + 1],
                                                   op0=ALU.mult, op1=ALU.add, r=['ctoks', 'crefs'], w=['fbias'])

        def emit_rest(n):
            st = steps[n]
            u = units[st['ui']]
            g = u['g']
            kt, qt, vt = slots[st['ui'] % 2]
            sk = 'u%d' % (st['ui'] % 2)
            pb = (0, 1, 2, 5)[n % 4]
            nk, c0, W, kb = st['nk'], st['c0'], st['W'], st['kb']
            pt = pT[n % 4]; ptk = 'pT%d' % (n % 4)
            ob = (3, 4, 7)[st['grp'] % 3]
            if u['kind'] == 'fox':
                fox_bias(st, u)
                bias = fbias[0:nk, kb:kb + 1]
                A('act', 'activation', out=pt[0:nk, c0:W], in_=bk(pb)[0:nk, c0:W], func=AF.Exp, bias=bias, scale=u['scale'],
                  r=[pk(pb), 'fbias'], w=[ptk])
            elif u['kind'] == 'diff':
                ai = u['h'] * NAL + st['al']
                bias = al_t[0:nk, ai:ai + 1]
                A('act', 'activation', out=pt[0:nk, c0:W], in_=bk(pb)[0:nk, c0:W], func=AF.Exp, bias=bias, scale=u['scale'],
                  r=[pk(pb), 'al'], w=[ptk])
            else:
                A('act', 'activation', out=pt[0:nk, c0:W], in_=bk(pb)[0:nk, c0:W], func=AF.Exp, scale=u['scale'],
                  r=[pk(pb)], w=[ptk])
            if st['diag'] is not None:
                mw = min(128, W - c0)
                m = masks[0:nk, u['mask'], 0:mw]
                A('dve', 'tensor_tensor', out=pt[0:nk, c0:c0 + mw], in0=pt[0:nk, c0:c0 + mw], in1=m, op=ALU.mult,
                  r=[ptk, 'masks'], w=[ptk])
            A('pe', 'matmul', bk(ob)[:, c0:W], lhsT=vt[0:nk, kb, :], rhs=pt[0:nk, c0:W], start=st['first'], stop=st['last'],
              r=[sk, ptk], w=[pk(ob)])
            if last_step_of[st['ui']] == n and st['ui'] + 2 < len(units):
                load_unit(st['ui'] + 2)
            if not st['last']:
                return
            bcs = rl[64:128, :]
            A('dve', 'reciprocal', out=rl[64:128, 0:W], in_=bk(ob)[64:128, 0:W], r=[pk(ob)], w=['bcs'])
            tok0 = (u['s'] * g.tq if g.gi == 1 else 0) + st['q0']
            if u['kind'] != 'diff':
                ms = mst[st['grp'] % 2]; msk = 'mst%d' % (st['grp'] % 2)
                A('dve', 'tensor_tensor', out=ms[:, 0:W], in0=bk(ob)[0:64, 0:W], in1=bcs[:, 0:W], op=ALU.mult,
                  r=[pk(ob), 'bcs'], w=[msk])
                P.dma('pool', g.mixT[u['row0']:u['row0'] + 64, tok0:tok0 + W], ms[:, 0:W], 'sm', reads=[msk])
            elif u['s2'] == 0:
                A('dve', 'tensor_tensor', out=on1s[st['i']][:, 0:W], in0=bk(ob)[0:64, 0:W], in1=bcs[:, 0:W], op=ALU.mult,
                  r=[pk(ob), 'bcs'], w=['on1%d' % st['i']])
            else:
                A('dve', 'tensor_tensor', out=on2[:, 0:W], in0=bk(ob)[0:64, 0:W], in1=bcs[:, 0:W], op=ALU.mult,
                  r=[pk(ob), 'bcs'], w=['on2'])
                A('dve', 'scalar_tensor_tensor', out=oc[:, 0:W], in0=on2[:, 0:W], scalar=negl[:, 0:1], in1=on1s[st['i']][:, 0:W],
                                                          op0=ALU.mult, op1=ALU.add, r=['on2', 'negl', 'on1%d' % st['i']], w=['oc'])
                A('dve', 'tensor_tensor', out=sqd[:, 0:W], in0=oc[:, 0:W], in1=oc[:, 0:W], op=ALU.mult, r=['oc'], w=['sqd'])
                A('pe', 'matmul', bk(6)[0:64, 0:W], lhsT=ones_f[0:64, 0:64], rhs=sqd[:, 0:W], start=True, stop=True,
                  r=['ones', 'sqd'], w=[pk(6)])
                A('act', 'activation', out=rs2[:, 0:W], in_=bk(6)[0:64, 0:W], func=AF.Ln, bias=eps_t[0:64, 0:1], scale=1.0 / 64,
                  r=[pk(6), 'eps'], w=['rs2'])
                A('act', 'activation', out=rs2[:, 0:W], in_=rs2[:, 0:W], func=AF.Exp, scale=-0.5, r=['rs2'], w=['rs2'])
                ms = mst[st['grp'] % 2]; msk = 'mst%d' % (st['grp'] % 2)
                A('dve', 'scalar_tensor_tensor', out=ms[:, 0:W], in0=oc[:, 0:W], scalar=gsub[:, 0:1], in1=rs2[:, 0:W],
                                                          op0=ALU.mult, op1=ALU.mult, r=['oc', 'gsub', 'rs2'], w=[msk])
                P.dma('pool', g.mixT[u['row0']:u['row0'] + 64, tok0:tok0 + W], ms[:, 0:W], 'sm', reads=[msk])

        on1s = [ar.alloc([512], F32, parts=64) for _ in range(max(NQT, 1))]

        LA = 3
        for n in range(len(steps) + LA):
            if n < len(steps):
                emit_S(n)
            if n - LA >= 0:
                emit_rest(n - LA)

    def phase_p3(l):
        ar.reset()
        wdn = ar.alloc([32, D], BF16)
        wms = ar.alloc([8, D], BF16)
        wpj = ar.alloc([2, D], BF16)
        wus = [ar.alloc([8, 512], BF16) for _ in range(2)]
        P.dma('sp', wdn, wb_down[l].rearrange("(c p) n -> p c n", p=128), 'c0', writes=['wdn'])
        P.dma('sp', wpj, wb_proj[l].rearrange("(c p) n -> p c n", p=128), 'c0', writes=['wpj'])
        P.dma('sp', wms, wb_out[l].rearrange("(k p) n -> p k n", p=128), 'c0', writes=['wms'])
        gb = [ar.alloc([D], F32) for _ in range(3)]
        for i, gi_ in enumerate((1, 3, 5)):
            P.dma('sp', gb[i], g_norm[l, gi_:gi_ + 1, :].to_broadcast([128, D]), 'c0', writes=['gb%d' % i])
        cwt = ar.alloc([3, 32], F32); cbt = ar.alloc([32], F32)
        for jj in range(3):
            P.dma('sp', cwt[:, jj, :], cw[l, jj].rearrange("(c p) -> p c", p=128), 'c0', writes=['cwt'], allow_slow_non_contiguous=True)
        P.dma('sp', cbt, cb[l].rearrange("(c p) -> p c", p=128), 'c0', writes=['cbt'], allow_slow_non_contiguous=True)
        aT = ar.alloc([32, 512], BF16)
        h_t = ar.alloc([4, D], F32)
        XT = ar.alloc([8, 512], BF16)
        ytmp = ar.alloc([D], F32)
        tpl = ar.alloc([D], F32)
        nseg_max = max(1, NBS)
        gext = [ar.alloc([516 + 2 * nseg_max], F32) for _ in range(2)]
        tcv = ar.alloc([512], F32)
        ugl = ar.alloc([512], F32)
        xnb = [ar.alloc([D], BF16)] * 2
        p_t = ar.alloc([4, PLE], F32)
        p_b = ar.alloc([PLE], BF16)
        pTt = ar.alloc([2, 128], BF16)
        ss = ar.alloc([16], F32)
        carry = ar.alloc([32, nseg_max, 2], F32)
        cso = ar.alloc([32, nseg_max, 2], F32)
        rr = [0]

        def resid_norm(j, src_ap, src_keys, gbi, sscol):
            A('pool', 'memset', ss[:, sscol:sscol + 1], 0.0, w=['ss%d' % sscol])
            A('act', 'activation', out=tpl_junk, in_=src_ap, func=AF.Square, accum_out=ss[:, sscol:sscol + 1],
              r=src_keys + ['ss%d' % sscol], w=['junk', 'ss%d' % sscol])
            rstd_from_ss(ss[:, sscol:sscol + 1], D, 'ss%d' % sscol)
            A('dve', 'scalar_tensor_tensor', out=src_ap, in0=src_ap, scalar=ss[:, sscol:sscol + 1], in1=gb[gbi],
                                                      op0=ALU.mult, op1=ALU.mult, r=src_keys + ['ss%d' % sscol, 'gb%d' % gbi], w=src_keys)
            A('dve', 'tensor_tensor', out=h_t[:, j, :], in0=h_t[:, j, :], in1=src_ap, op=ALU.add, r=src_keys + ['h%d' % j], w=['h%d' % j])

        def prenorm_T(j, sscol):
            x_b = xnb[0]; xk = 'xnb'
            A('pool', 'memset', ss[:, sscol:sscol + 1], 0.0, w=['ss%d' % sscol])
            A('act', 'activation', out=tpl_junk, in_=h_t[:, j, :], func=AF.Square, accum_out=ss[:, sscol:sscol + 1],
              r=['h%d' % j, 'ss%d' % sscol], w=['junk', 'ss%d' % sscol])
            rstd_from_ss(ss[:, sscol:sscol + 1], D, 'ss%d' % sscol)
            A('dve', 'tensor_scalar_mul', out=x_b, in0=h_t[:, j, :], scalar1=ss[:, sscol:sscol + 1], r=['h%d' % j, 'ss%d' % sscol], w=[xk])
            pb = 2 + (j % 2)
            for kc in range(8):
                A('pe', 'transpose', bkb(pb)[:, kc * 128:(kc + 1) * 128], x_b[:, kc * 128:(kc + 1) * 128], ident,
                  r=[xk, 'ident'], w=[pk(pb)])
            A('act', 'activation', out=XT[:, :, j * 128:(j + 1) * 128], in_=bkb(pb).rearrange("p (a b) -> p a b", b=128), func=AF.Copy,
              r=[pk(pb)], w=['XT%d' % j])

        tpl_junk = ar.alloc([D], BF16)

        for g in grp:
            ntile = (g.ntok + 511) // 512
            nseg = 1 if g.gi == 0 else None
            if g.gi == 0:
                A('pool', 'memset', carry, 0.0, w=['carry'])
            else:
                for b in range(NBS):
                    for jj in range(2):
                        P.dma('sp', carry[:, :, b, jj], c_cs[l, b, jj].rearrange("(c p) -> p c", p=128), 'c0', writes=['carry'],
                              allow_slow_non_contiguous=True)
            for ti in range(ntile):
                t0 = ti * 512
                W = min(512, g.ntok - t0)
                nsub = W // 128
                nsg = 1 if g.gi == 0 else W // TS
                segW = W // nsg
                src = g.x if l == 0 else g.hbuf
                dst = g.y if l == L - 1 else g.hbuf
                for j in range(nsub):
                    P.dma('sp', XT[:, :, j * 128:(j + 1) * 128], g.mixT[:, t0 + j * 128:t0 + (j + 1) * 128].rearrange("(k p) t -> p k t", p=128),
                          'p3m%d' % j, writes=['XT%d' % j])
                    P.dma('sp', h_t[:, j, :], src[t0 + j * 128:t0 + (j + 1) * 128, :], 'p3h%d' % j, writes=['h%d' % j])
                P.dma('sp', p_t[:, 0:nsub, :], g.pin[l, t0:t0 + W, :].rearrange("(j p) d -> p j d", p=128), 'p3p', writes=['p_t'])
                for j in range(nsub):
                    for hf in range(2):
                        pb = hf
                        for kc in range(8):
                            A('pe', 'matmul', bk(hf)[:, 0:512], lhsT=XT[:, kc, j * 128:(j + 1) * 128],
                                                                           rhs=wms[:, kc, hf * 512:(hf + 1) * 512], start=(kc == 0), stop=(kc == 7),
                              r=['XT%d' % j, 'wms'], w=[pk(hf)])
                        A('act', 'activation', out=ytmp[:, hf * 512:(hf + 1) * 512], in_=bk(hf)[:, 0:512], func=AF.Copy,
                          r=[pk(hf)], w=['ytmp'])
                    resid_norm(j, ytmp, ['ytmp'], 0, 0)
                for j in range(nsub):
                    prenorm_T(j, 1)
                xkeys = ['XT%d' % j for j in range(nsub)]
                for c in range(32):
                    if c % 2 == 0:
                        gi2 = (c // 2) % 2
                        P.dma('sp', wus[gi2], wb_up[l][:, (c // 2) * 512:(c // 2 + 1) * 512].rearrange("(k p) n -> p k n", p=128),
                              'p3u%d' % gi2, writes=['wus%d' % gi2])
                    wu = wus[(c // 2) % 2]; wuk = 'wus%d' % ((c // 2) % 2)
                    off = (c % 2) * 256
                    pg, pv = 4 + 2 * (c % 2), 5 + 2 * (c % 2)
                    for kc in range(8):
                        A('pe', 'matmul', bk(pg)[:, 0:W], lhsT=wu[:, kc, off:off + 128], rhs=XT[:, kc, 0:W],
                                                                                 start=(kc == 0), stop=(kc == 7), r=[wuk] + xkeys, w=[pk(pg)])
                    for kc in range(8):
                        A('pe', 'matmul', bk(pv)[:, 0:W], lhsT=wu[:, kc, off + 128:off + 256], rhs=XT[:, kc, 0:W],
                                                                                 start=(kc == 0), stop=(kc == 7), r=[wuk] + xkeys, w=[pk(pv)])
                    ge = gext[c % 2]; gk = 'gext%d' % (c % 2)
                    gv = ge[:, 0:nsg * (segW + 2)].rearrange("p (s w) -> p s w", w=segW + 2)
                    A('pool', 'tensor_copy', out=gv[:, :, 0:2], in_=carry[:, c, 0:nsg, :], r=['carry'], w=[gk])
                    A('act', 'activation', out=gv[:, :, 2:2 + segW], in_=bk(pg)[:, 0:W].rearrange("p (s w) -> p s w", w=segW),
                                                                  func=AF.Copy, r=[pk(pg)], w=[gk])
                    A('pool', 'tensor_copy', out=carry[:, c, 0:nsg, :], in_=gv[:, :, segW:segW + 2], r=[gk], w=['carry'])
                    if ti == ntile - 1:
                        A('pool', 'tensor_copy', out=cso[:, c, 0:nsg, :], in_=gv[:, :, segW:segW + 2], r=[gk], w=['cso'])
                    tv = tcv[:, 0:W].rearrange("p (s w) -> p s w", w=segW)
                    A('dve', 'tensor_scalar', out=tv, in0=gv[:, :, 0:segW], scalar1=cwt[:, 0, c:c + 1], scalar2=cbt[:, c:c + 1],
                                                                          op0=ALU.mult, op1=ALU.add, r=[gk, 'cwt', 'cbt'], w=['tcv'])
                    A('dve', 'scalar_tensor_tensor', out=tv, in0=gv[:, :, 1:1 + segW], scalar=cwt[:, 1, c:c + 1], in1=tv,
                                                                                 op0=ALU.mult, op1=ALU.add, r=[gk, 'cwt', 'tcv'], w=['tcv'])
                    A('dve', 'scalar_tensor_tensor', out=tv, in0=gv[:, :, 2:2 + segW], scalar=cwt[:, 2, c:c + 1], in1=tv,
                                                                                 op0=ALU.mult, op1=ALU.add, r=[gk, 'cwt', 'tcv'], w=['tcv'])
                    A('act', 'activation', out=ugl[:, 0:W], in_=tcv[:, 0:W], func=AF.Gelu_apprx_tanh, r=['tcv'], w=['ugl'])
                    A('dve', 'tensor_tensor', out=aT[:, c, 0:W], in0=ugl[:, 0:W], in1=bk(pv)[:, 0:W], op=ALU.mult,
                      r=['ugl', pk(pv)], w=['aT'])
                if ti == ntile - 1:
                    for b in range(nsg if g.gi == 1 else 1):
                        for jj in range(2):
                            P.dma('pool', o_cv[g.gi][l, b, jj].rearrange("(c p) -> p c", p=128), cso[:, :, b, jj], 'so', reads=['cso'],
                                  allow_slow_non_contiguous=True)
                for hf in range(2):
                    P.dma('sp', wus[hf], wb_gate[l][:, hf * 512:(hf + 1) * 512].rearrange("(k p) n -> p k n", p=128),
                          'own', writes=['wus%d' % hf])
                def st5_mm(j):
                    for hf in range(2):
                        for c in range(32):
                            A('pe', 'matmul', bk(hf)[:, 0:512], lhsT=aT[:, c, j * 128:(j + 1) * 128],
                                                                         rhs=wdn[:, c, hf * 512:(hf + 1) * 512], start=(c == 0), stop=(c == 31),
                              r=['aT', 'wdn'], w=[pk(hf)])

                def st5_ev(j):
                    for hf in range(2):
                        A('act', 'activation', out=ytmp[:, hf * 512:(hf + 1) * 512], in_=bk(hf)[:, 0:512], func=AF.Copy,
                          r=[pk(hf)], w=['ytmp'])
                    resid_norm(j, ytmp, ['ytmp'], 1, 2)

                def st6_rest(j):
                    prenorm_T(j, 3)
                    A('dve', 'tensor_copy', out=p_b, in_=p_t[:, j, :], r=['p_t'], w=['p_b'])
                    for kc in range(2):
                        A('pe', 'transpose', bkb(3)[:, kc * 128:(kc + 1) * 128], p_b[:, kc * 128:(kc + 1) * 128], ident,
                          r=['p_b', 'ident'], w=[pk(3)])
                    A('dve', 'tensor_copy', out=pTt, in_=bkb(3)[:, 0:256].rearrange("p (a b) -> p a b", b=128), r=[pk(3)], w=['pTt'])
                    for hf in range(2):
                        pg, pp_ = 4 + hf, 6 + hf
                        for kc in range(8):
                            A('pe', 'matmul', bk(pg)[:, 0:512], lhsT=XT[:, kc, j * 128:(j + 1) * 128],
                                                                                 rhs=wus[hf][:, kc, :], start=(kc == 0), stop=(kc == 7),
                              r=['XT%d' % j, 'wus%d' % hf], w=[pk(pg)])
                        for kc in range(2):
                            A('pe', 'matmul', bk(pp_)[:, 0:512], lhsT=pTt[:, kc, :], rhs=wpj[:, kc, hf * 512:(hf + 1) * 512],
                                                                               start=(kc == 0), stop=(kc == 1), r=['pTt', 'wpj'], w=[pk(pp_)])
                        A('act', 'activation', out=tpl[:, hf * 512:(hf + 1) * 512], in_=bk(pg)[:, 0:512], func=AF.Sigmoid,
                          r=[pk(pg)], w=['tpl'])
                        A('dve', 'tensor_tensor', out=tpl[:, hf * 512:(hf + 1) * 512], in0=tpl[:, hf * 512:(hf + 1) * 512],
                                                                            in1=bk(pp_)[:, 0:512], op=ALU.mult, r=['tpl', pk(pp_)], w=['tpl'])
                    resid_norm(j, tpl, ['tpl'], 2, 4)
                    P.dma('pool', dst[t0 + j * 128:t0 + (j + 1) * 128, :], h_t[:, j, :], 'sh', reads=['h%d' % j], writes=['hb'])

                if OPT_P3PIPE:
                    st5_mm(0)
                for j in range(nsub):
                    if not OPT_P3PIPE:
                        st5_mm(j)
                    st5_ev(j)
                    if OPT_P3PIPE and j + 1 < nsub:
                        st5_mm(j + 1)
                    st6_rest(j)

    setup()
    prep_weights()
    P.barrier()
    for l in range(L):
        phase_cache(l)
        P.barrier()
        phase_p1(l)
        P.barrier()
        phase_p2(l)
        P.barrier()
        phase_p3(l)
        P.barrier()
    P.finish()
    return nc, P


_CACHE = {}


def _host_consts(S, NBS):
    NS = NBS * TS
    NQT = S // 512
    TK = PAST + TS
    NKB_S = (TK + 127) // 128
    k = np.arange(128)[:, None]
    q = np.arange(128)[None, :]
    slopes = 2.0 ** (-8.0 * np.arange(1, 5) / 4)
    masks = np.zeros((6, 128, 128), np.float32)
    masks[0] = (k <= q)
    chunk = ((k // 64) <= (q // 64)).astype(np.float32)
    masks[1] = chunk
    for h in range(4):
        masks[2 + h] = chunk * np.exp(-2.0 * slopes[h] * np.maximum(k - q, 0))
    U = (k <= q).astype(np.float32)
    E = np.zeros((128, 128), np.float32)
    E[127, :] = 1.0
    half = 16
    inv_freq = (10000.0 ** (-np.arange(half, dtype=np.float32) / half)).astype(np.float32)

    def cs(pos):
        ang = pos.astype(np.float32)[:, None] * inv_freq[None, :]
        c, s = np.cos(ang), np.sin(ang)
        return np.concatenate([c, c, -s, s], axis=1).astype(np.float32)
    cs_p = cs(np.arange(S))
    cs_s = cs(PAST + (np.arange(NS) % TS))
    NAL = 4 * NQT + NKB_S
    al = np.zeros((128, 4, NAL), np.float32)
    p = np.arange(128)
    for h in range(4):
        for e in range(4 * NQT):
            d = e - (4 * NQT - 2)
            al[:, h, e] = slopes[h] * (p + 128 * d)
        for kb in range(NKB_S):
            al[:, h, 4 * NQT + kb] = slopes[h] * (p + 128 * kb - (PAST + TS // 2))
    return dict(k_masks=masks, k_U=U, k_E=E, k_cs_p=cs_p, k_cst_p=np.ascontiguousarray(cs_p.T),
                k_cs_s=cs_s, k_cst_s=np.ascontiguousarray(cs_s.T), k_al=np.ascontiguousarray(al.reshape(128, 4 * NAL)))


def kernel(x_prompt, x_sample, cache_fox_k, cache_fox_v, cache_fox_logf, cache_mla_ckv,
           cache_mla_krope, cache_diff_k, cache_diff_v, state_ffn_conv, p_prompt, p_sample,
           w_in, b_forget, mla_q_norm, w_mla_uq, mla_kv_norm, w_mla_uk, w_mla_uv,
           diff_lambda_q1, diff_lambda_k1, diff_lambda_q2, diff_lambda_k2, diff_subln, w_out,
           norm_mix_pre, norm_mix_post, norm_ffn_pre, norm_ffn_post, norm_ple_pre, norm_ple_post,
           w_ffn_up, ffn_conv_w, ffn_conv_b, w_ffn_down, w_ple_gate, w_ple_proj):
    f = lambda a: np.ascontiguousarray(np.asarray(a, dtype=np.float32))
    B, S, _ = x_prompt.shape
    DB = x_sample.shape[0]
    L = w_in.shape[0]
    NBS = DB // NCORES
    NS = NBS * TS
    assert B * 2 == NCORES or B <= NCORES
    key = (S, NBS, L)
    if key not in _CACHE:
        _CACHE[key] = build(S, NBS, L)
    nc, _ = _CACHE[key]
    w_in = f(w_in)
    kr = w_in[:, :, 1798:1830]
    krs = np.concatenate([kr[:, :, 16:32], kr[:, :, 0:16]], axis=2)
    w_in_p = np.concatenate([w_in[:, :, 0:384], w_in[:, :, 1158:1542], w_in[:, :, 1830:2086], w_in[:, :, 384:768],
                             w_in[:, :, 1152:1158], np.zeros((L, D, 2), np.float32), w_in[:, :, 768:1152],
                             w_in[:, :, 1542:1798], kr, krs, w_in[:, :, 2086:2342], w_in[:, :, 2342:2598]], axis=2)
    assert w_in_p.shape[2] == NCOLS
    uq = f(w_mla_uq).reshape(L, 384, 6, 96)
    uq_p = np.concatenate([uq[..., 0:64], uq[..., 64:96], uq[..., 80:96], uq[..., 64:80]], axis=3).reshape(L, 384, 768)
    up = f(w_ffn_up)
    up_p = np.stack([up[:, :, :DFF].reshape(L, D, 32, 128), up[:, :, DFF:].reshape(L, D, 32, 128)], axis=3).reshape(L, D, 2 * DFF)
    shared = dict(
        w_in=np.ascontiguousarray(w_in_p), w_uq=np.ascontiguousarray(uq_p), w_uk=f(w_mla_uk), w_uv=f(w_mla_uv),
        w_out=f(w_out), w_up=np.ascontiguousarray(up_p), w_down=f(w_ffn_down), w_gate=f(w_ple_gate), w_proj=f(w_ple_proj),
        b_f=f(b_forget), g_q=f(mla_q_norm), g_kv=f(mla_kv_norm),
        lam4=np.ascontiguousarray(np.stack([f(diff_lambda_q1), f(diff_lambda_k1), f(diff_lambda_q2), f(diff_lambda_k2)], axis=1)),
        g_sub=f(diff_subln),
        g_norm=np.ascontiguousarray(np.stack([f(norm_mix_pre), f(norm_mix_post), f(norm_ffn_pre), f(norm_ffn_post),
                                              f(norm_ple_pre), f(norm_ple_post)], axis=1)),
        cw=f(ffn_conv_w), cb=f(ffn_conv_b))
    shared.update(_host_consts(S, NBS))
    xp, xs, pp, ps_ = f(x_prompt), f(x_sample), f(p_prompt), f(p_sample)
    cfk, cfv, clf = f(cache_fox_k), f(cache_fox_v), f(cache_fox_logf)
    cckv, ckr, cdk, cdv, ccs = f(cache_mla_ckv), f(cache_mla_krope), f(cache_diff_k), f(cache_diff_v), f(state_ffn_conv)
    in_maps = []
    for c in range(NCORES):
        b = c % B
        sb = slice(c * NBS, (c + 1) * NBS)
        m = dict(shared)
        m.update(xp=xp[b], xs=np.ascontiguousarray(xs[sb].reshape(NS, D)), pp=np.ascontiguousarray(pp[:, b]),
                 pss=np.ascontiguousarray(ps_[:, sb].reshape(L, NS, PLE)),
                 c_fk=np.ascontiguousarray(cfk[:, sb].reshape(L, NBS, PAST, 384)),
                 c_fv=np.ascontiguousarray(cfv[:, sb].reshape(L, NBS, PAST, 384)),
                 c_lf=np.ascontiguousarray(clf[:, sb]), c_ckv=np.ascontiguousarray(cckv[:, sb]),
                 c_kr=np.ascontiguousarray(ckr[:, sb]),
                 c_dk=np.ascontiguousarray(cdk[:, sb].reshape(L, NBS, PAST, 256)),
                 c_dv=np.ascontiguousarray(cdv[:, sb].reshape(L, NBS, PAST, 256)),
                 c_cs=np.ascontiguousarray(ccs[:, sb]))
        in_maps.append(m)
    res = run_bass_kernel_spmd(nc, in_maps, core_ids=list(range(NCORES)))
    R = res.results
    global _LAST
    _LAST = R
    pc = [R[b] for b in range(B)]
    y_prompt = np.stack([r["y_p"] for r in pc], 0)
    y_sample = np.concatenate([r["y_s"].reshape(NBS, TS, D) for r in R], 0)

    def pstack(name, tail):
        return np.stack([r[name] for r in pc], 1).reshape((L, B, S) + tail)

    def sstack(name, tail):
        return np.concatenate([r[name].reshape((L, NBS, TS) + tail) for r in R], 1)

    outs = (y_prompt, y_sample,
            pstack("o_fk_p", (6, 64)), pstack("o_fv_p", (6, 64)), pstack("o_lf_p", (6,)), pstack("o_ckv_p", (256,)),
            pstack("o_kr_p", (32,)), pstack("o_dk_p", (4, 64)), pstack("o_dv_p", (4, 64)),
            np.stack([r["o_cv_p"][:, 0] for r in pc], 1),
            sstack("o_fk_s", (6, 64)), sstack("o_fv_s", (6, 64)), sstack("o_lf_s", (6,)), sstack("o_ckv_s", (256,)),
            sstack("o_kr_s", (32,)), sstack("o_dk_s", (4, 64)), sstack("o_dv_s", (4, 64)),
            np.concatenate([r["o_cv_s"] for r in R], 1))
    return tuple(np.ascontiguousarray(o, dtype=np.float32) for o in outs)
```
